# Optimizing a Trainium2 kernel written in Bass

```python
import jax
import jax.numpy as jnp
from jax import lax
import numpy as np

D_MODEL = 1024
BATCH = 8
SEQ = 4096
DEPTH = 2

GRID_W = 64
CTX_LEN = 256
INNER = 2 * D_MODEL
NA_HEADS = 16
NA_DIM = D_MODEL // NA_HEADS
NA_WIDTH = NA_HEADS * NA_DIM
NA_WIN_H = 8
NA_WIN_W = 16
ML_HEADS = 4
ML_DIM = D_MODEL // ML_HEADS
ML_WIDTH = ML_HEADS * ML_DIM
ML_CHUNK = 64
HG_HEADS = 16
HG_KDIM = 128
HG_VDIM = INNER // HG_HEADS
HG_KW = HG_HEADS * HG_KDIM
HG_VW = HG_HEADS * HG_VDIM
HG_CHUNK = 32
ROPE_BASE = 10000.0
EPS = 1e-6
N_AB = (DEPTH + 1) // 2
N_C = DEPTH // 2
AB_SIZES = (NA_WIDTH,) * 4 + (ML_WIDTH,) * 5 + (4 * ML_HEADS,)
C_SIZES = (HG_KW, HG_KW, HG_KW, HG_VW, HG_VW)
AB_IN = sum(AB_SIZES)
C_IN = sum(C_SIZES)
F32 = jnp.float32

kernel_name = "hybrid_na_mlstm_hgrn2_dit"


def _rms(x, w):
    xf = x.astype(F32)
    y = xf * lax.rsqrt(jnp.mean(xf * xf, axis=-1, keepdims=True) + EPS)
    return (y * w.astype(F32)).astype(x.dtype)


def _heads(t, n):
    return t.reshape(t.shape[:-1] + (n, t.shape[-1] // n))


def _bht(a):
    return jnp.swapaxes(a, 1, 2).astype(F32)


def _split_cols(p, sizes):
    return jnp.split(p, [int(s) for s in np.cumsum(sizes)[:-1]], axis=-1)


def _same(a):
    return a


def _flip_t(a):
    return jnp.flip(a, axis=2)


def _ada(cvec, w, b):
    m = jax.nn.silu(cvec) @ w + b
    return jnp.split(m.reshape((-1, 1, m.shape[-1])), 3, axis=-1)


def _rope_1d(x, pos):
    half = x.shape[-1] // 2
    freqs = ROPE_BASE ** (-jnp.arange(half, dtype=F32) / half)
    ang = pos.astype(F32)[:, None] * freqs[None, :]
    cos, sin = jnp.cos(ang)[:, None, :], jnp.sin(ang)[:, None, :]
    x1, x2 = x[..., :half], x[..., half:]
    return jnp.concatenate([x1 * cos - x2 * sin, x2 * cos + x1 * sin], axis=-1)


def _axial_rope(x):
    t = jnp.arange(x.shape[1])
    half = x.shape[-1] // 2
    xf = x.astype(F32)
    out = jnp.concatenate([_rope_1d(xf[..., :half], t // GRID_W),
                           _rope_1d(xf[..., half:], t % GRID_W)], axis=-1)
    return out.astype(x.dtype)


def _to_chunks(a, L):
    B, H, T = a.shape[:3]
    return jnp.moveaxis(a.reshape((B, H, T // L, L) + a.shape[3:]), 2, 0)


def _from_chunks(a):
    a = jnp.moveaxis(a, 0, 2)
    return a.reshape(a.shape[:2] + (-1,) + a.shape[4:])


def _mlstm_chunkwise(q, k, v, ig, fg, state):
    L = ML_CHUNK
    lf = jax.nn.log_sigmoid(fg)
    tri = jnp.tril(jnp.ones((L, L), bool))

    def step(carry, xs):
        C, n, m = carry
        qc, kc, vc, ic, lfc = xs
        b = jnp.cumsum(lfc, axis=-1)
        dmat = jnp.where(tri, b[..., :, None] - b[..., None, :] + ic[..., None, :], -jnp.inf)
        m_inter = b + m[..., None]
        m_t = jnp.maximum(jnp.max(dmat, axis=-1), m_inter)
        s = jnp.einsum('bhtk,bhsk->bhts', qc, kc) * jnp.exp(dmat - m_t[..., None])
        carry_w = jnp.exp(m_inter - m_t)
        num = jnp.einsum('bhts,bhsv->bhtv', s, vc) + carry_w[..., None] * jnp.einsum('bhtk,bhkv->bhtv', qc, C)
        den = jnp.sum(s, axis=-1) + carry_w * jnp.einsum('bhtk,bhk->bht', qc, n)
        h = num / jnp.maximum(jnp.abs(den), jnp.exp(-m_t))[..., None]
        bL = b[..., -1]
        g = bL[..., None] - b + ic
        m_new = jnp.maximum(bL + m, jnp.max(g, axis=-1))
        kw = kc * jnp.exp(g - m_new[..., None])[..., None]
        dec = jnp.exp(bL + m - m_new)
        C = dec[..., None, None] * C + jnp.einsum('bhsk,bhsv->bhkv', kw, vc)
        n = dec[..., None] * n + jnp.sum(kw, axis=2)
        return (C, n, m_new), h

    xs = tuple(_to_chunks(a, L) for a in (q, k, v, ig, lf))
    state, h = lax.scan(step, state, xs)
    return _from_chunks(h), state


def _mlstm_final_state(k, v, ig, fg):
    b = jnp.cumsum(jax.nn.log_sigmoid(fg), axis=-1)
    g = b[..., -1:] - b + ig
    m = jnp.max(g, axis=-1)
    kw = k * jnp.exp(g - m[..., None])[..., None]
    return (jnp.einsum('bhsk,bhsv->bhkv', kw, v), jnp.sum(kw, axis=2), m)


def _mlstm_bidir(lat, ctx, need_ctx):
    q_l, k_l, v_l, g_l = lat
    q_c, k_c, v_c, g_c = ctx
    B, H, _, dk = q_l.shape
    dv = v_l.shape[-1]
    out_l, out_c = [], []
    for d, tr in enumerate((_same, _flip_t)):
        ic, fc = tr(g_c[..., 2 * d]), tr(g_c[..., 2 * d + 1])
        il, fl = tr(g_l[..., 2 * d]), tr(g_l[..., 2 * d + 1])
        if need_ctx:
            zero = (jnp.zeros((B, H, dk, dv), F32), jnp.zeros((B, H, dk), F32), jnp.zeros((B, H), F32))
            h_c, st = _mlstm_chunkwise(tr(q_c), tr(k_c), tr(v_c), ic, fc, zero)
            out_c.append(tr(h_c))
        else:
            st = _mlstm_final_state(tr(k_c), tr(v_c), ic, fc)
        h_l, _ = _mlstm_chunkwise(tr(q_l), tr(k_l), tr(v_l), il, fl, st)
        out_l.append(tr(h_l))
    return out_l[0] + out_l[1], (out_c[0] + out_c[1] if need_ctx else None)


def _gla_chunkwise(q, k, v, lf, S):
    L = HG_CHUNK
    tri = jnp.tril(jnp.ones((L, L), bool))[..., None]

    def step(S, xs):
        qc, kc, vc, lfc = xs
        b = jnp.cumsum(lfc, axis=2)
        o = jnp.einsum('bhtk,bhkv->bhtv', qc * jnp.exp(b), S)
        pair = jnp.exp(jnp.where(tri, b[:, :, :, None, :] - b[:, :, None, :, :], -jnp.inf))
        a = jnp.einsum('bhtk,bhtsk->bhts', qc, pair * kc[:, :, None, :, :])
        o = o + jnp.einsum('bhts,bhsv->bhtv', a, vc)
        bL = b[:, :, -1:, :]
        S = jnp.exp(bL[:, :, 0])[..., None] * S + jnp.einsum('bhsk,bhsv->bhkv', kc * jnp.exp(bL - b), vc)
        return S, o

    xs = tuple(_to_chunks(a, L) for a in (q, k, v, lf))
    S, o = lax.scan(step, S, xs)
    return _from_chunks(o), S


def _gla_final_state(k, v, lf):
    b = jnp.cumsum(lf, axis=2)
    return jnp.einsum('bhsk,bhsv->bhkv', k * jnp.exp(b[:, :, -1:] - b), v)


def _gla_bidir(lat, ctx, need_ctx):
    q_l, v_l, dirs_l = lat
    q_c, v_c, dirs_c = ctx
    B, H, _, dk = q_l.shape
    dv = v_l.shape[-1]
    out_l, out_c = [], []
    for d, tr in enumerate((_same, _flip_t)):
        (k_l, lf_l), (k_c, lf_c) = dirs_l[d], dirs_c[d]
        if need_ctx:
            o_c, S = _gla_chunkwise(tr(q_c), tr(k_c), tr(v_c), tr(lf_c), jnp.zeros((B, H, dk, dv), F32))
            out_c.append(tr(o_c))
        else:
            S = _gla_final_state(tr(k_c), tr(v_c), tr(lf_c))
        o_l, _ = _gla_chunkwise(tr(q_l), tr(k_l), tr(v_l), tr(lf_l), S)
        out_l.append(tr(o_l))
    return out_l[0] + out_l[1], (out_c[0] + out_c[1] if need_ctx else None)


def _neighbourhood_attention(q, k, v, k_ctx, v_ctx, rpb):
    B, T, H, d = q.shape
    rows = T // GRID_W
    kh, kw = min(NA_WIN_H, rows), NA_WIN_W
    qg = (q * (d ** -0.5)).reshape(B, rows, GRID_W, H, d)
    kg = k.reshape(B, rows, GRID_W, H, d)
    vg = v.reshape(B, rows, GRID_W, H, d)
    col = jnp.arange(GRID_W)
    col_idx = jnp.clip(col - kw // 2, 0, GRID_W - kw)[:, None] + jnp.arange(kw)[None, :]
    col_bias = rpb[:, :, col_idx - col[:, None] + NA_WIN_W - 1]

    def one_row(r):
        r0 = jnp.clip(r - kh // 2, 0, rows - kh)
        q_r = lax.dynamic_index_in_dim(qg, r, axis=1, keepdims=False)
        k_nb = lax.dynamic_slice_in_dim(kg, r0, kh, axis=1)[:, :, col_idx]
        v_nb = lax.dynamic_slice_in_dim(vg, r0, kh, axis=1)[:, :, col_idx]
        row_off = r0 + jnp.arange(kh) - r + NA_WIN_H - 1
        bias = jnp.transpose(jnp.take(col_bias, row_off, axis=1), (0, 2, 1, 3))
        s_nb = jnp.einsum('bqhd,baqwhd->bhqaw', q_r, k_nb).astype(F32) + bias[None].astype(F32)
        s_cx = jnp.einsum('bqhd,bchd->bhqc', q_r, k_ctx).astype(F32)
        p = jax.nn.softmax(jnp.concatenate([s_nb.reshape(B, H, GRID_W, kh * kw), s_cx], axis=-1), axis=-1)
        p = p.astype(v.dtype)
        p_nb = p[..., :kh * kw].reshape(B, H, GRID_W, kh, kw)
        return (jnp.einsum('bhqaw,baqwhd->bqhd', p_nb, v_nb)
                + jnp.einsum('bhqc,bchd->bqhd', p[..., kh * kw:], v_ctx))

    out = lax.map(one_row, jnp.arange(rows))
    return jnp.moveaxis(out, 0, 1).reshape(B, T, H, d)


def _context_attention(q, k, v):
    s = jnp.einsum('bqhd,bkhd->bhqk', q, k).astype(F32) * (q.shape[-1] ** -0.5)
    p = jax.nn.softmax(s, axis=-1).astype(v.dtype)
    return jnp.einsum('bhqk,bkhd->bqhd', p, v)


def _ab_mixer(h_lat, h_ctx, w_in, b_gate, q_norm, k_norm, rpb, h_norm, w_out, need_ctx):
    def project(h, rotary):
        qa, ka, va, za, qb, kb, vb, ob, zb, g = _split_cols(h @ w_in, AB_SIZES)
        qa = _rms(_heads(qa, NA_HEADS), q_norm)
        ka = _rms(_heads(ka, NA_HEADS), k_norm)
        qb, kb = _heads(qb, ML_HEADS), _heads(kb, ML_HEADS) * (ML_DIM ** -0.5)
        if rotary:
            qb, kb = _axial_rope(qb), _axial_rope(kb)
        gates = jnp.transpose(_heads(g + b_gate, 4), (0, 3, 1, 2)).astype(F32)
        mb = (_bht(qb), _bht(kb), _bht(_heads(vb, ML_HEADS)), gates)
        return (qa, ka, _heads(va, NA_HEADS), za), mb, (ob, zb)

    (qa_l, ka_l, va_l, za_l), mb_l, (ob_l, zb_l) = project(h_lat, True)
    (qa_c, ka_c, va_c, za_c), mb_c, (ob_c, zb_c) = project(h_ctx, False)
    oa_l = _neighbourhood_attention(qa_l, ka_l, va_l, ka_c, va_c, rpb)
    hb_l, hb_c = _mlstm_bidir(mb_l, mb_c, need_ctx)

    def merge(oa, hb, za, ob, zb):
        hb = jnp.swapaxes(hb, 1, 2).astype(ob.dtype) * jax.nn.sigmoid(_heads(ob, ML_HEADS))
        hb = _rms(hb, h_norm.reshape(ML_HEADS, ML_DIM))
        ya = oa.reshape(oa.shape[:2] + (NA_WIDTH,)) * jax.nn.silu(za)
        yb = hb.reshape(hb.shape[:2] + (ML_WIDTH,)) * jax.nn.silu(zb)
        return jnp.concatenate([ya, yb], axis=-1) @ w_out

    y_lat = merge(oa_l, hb_l, za_l, ob_l, zb_l)
    y_ctx = merge(_context_attention(qa_c, ka_c, va_c), hb_c, za_c, ob_c, zb_c) if need_ctx else None
    return y_lat, y_ctx


def _c_mixer(h_lat, h_ctx, w_in, lb, h_norm, w_out, need_ctx):
    def project(h):
        q, f_fwd, f_bwd, i, z = _split_cols(h @ w_in, C_SIZES)

        def decay(f_pre):
            f = lb + (1.0 - lb) * jax.nn.sigmoid(f_pre.astype(F32))
            return _bht(_heads(1.0 - f, HG_HEADS)), _bht(_heads(jnp.log(f), HG_HEADS))

        rec = (_bht(_heads(jax.nn.silu(q), HG_HEADS)), _bht(_heads(i, HG_HEADS)), (decay(f_fwd), decay(f_bwd)))
        return rec, z

    rec_l, z_l = project(h_lat)
    rec_c, z_c = project(h_ctx)
    o_l, o_c = _gla_bidir(rec_l, rec_c, need_ctx)

    def merge(o, z):
        o = _rms(jnp.swapaxes(o, 1, 2).astype(z.dtype), h_norm.reshape(HG_HEADS, HG_VDIM))
        return (o.reshape(o.shape[:2] + (HG_VW,)) * jax.nn.silu(z)) @ w_out

    return merge(o_l, z_l), (merge(o_c, z_c) if need_ctx else None)


def _lower_bound(lb_param, layer):
    s = jax.nn.softmax(lb_param.astype(F32), axis=0)
    return (jnp.cumsum(s, axis=0) - s[0])[layer]


def setup_inputs(seed: int = 0) -> dict:
    key = jax.random.key(seed)
    ks = jax.random.split(key, 22)

    def nrm(k, shape, s):
        return jax.random.normal(k, shape, F32) * s

    f_bias = jnp.linspace(3.0, 6.0, ML_HEADS, dtype=F32)
    b_gate_ab = jnp.concatenate([nrm(ks[8], (N_AB, ML_HEADS), 0.1),
                                 f_bias + nrm(ks[9], (N_AB, ML_HEADS), 0.1),
                                 nrm(ks[10], (N_AB, ML_HEADS), 0.1),
                                 f_bias + nrm(ks[11], (N_AB, ML_HEADS), 0.1)], axis=-1)
    return {
        "x": nrm(ks[0], (BATCH, SEQ, D_MODEL), 1.0),
        "c": nrm(ks[1], (BATCH, D_MODEL), 1.0),
        "ctx": nrm(ks[2], (BATCH, CTX_LEN, D_MODEL), 1.0),
        "c_ctx": nrm(ks[3], (D_MODEL,), 1.0),
        "norm_w": 1.0 + nrm(ks[4], (DEPTH, D_MODEL), 0.05),
        "w_ada": nrm(ks[5], (DEPTH, D_MODEL, 3 * D_MODEL), 0.5 * D_MODEL ** -0.5),
        "b_ada": nrm(ks[6], (DEPTH, 3 * D_MODEL), 0.02),
        "w_in_ab": nrm(ks[7], (N_AB, D_MODEL, AB_IN), D_MODEL ** -0.5),
        "b_gate_ab": b_gate_ab,
        "q_norm_a": 1.0 + nrm(ks[12], (N_AB, NA_DIM), 0.05),
        "k_norm_a": 1.0 + nrm(ks[13], (N_AB, NA_DIM), 0.05),
        "rpb_a": nrm(ks[14], (N_AB, NA_HEADS, 2 * NA_WIN_H - 1, 2 * NA_WIN_W - 1), 0.1),
        "h_norm_b": 1.0 + nrm(ks[15], (N_AB, ML_WIDTH), 0.05),
        "w_out_ab": nrm(ks[16], (N_AB, INNER, D_MODEL), INNER ** -0.5),
        "w_in_c": nrm(ks[17], (N_C, D_MODEL, C_IN), D_MODEL ** -0.5),
        "lb_c": nrm(ks[18], (DEPTH, HG_KW), 0.5),
        "h_norm_c": 1.0 + nrm(ks[19], (N_C, HG_VW), 0.05),
        "w_out_c": nrm(ks[20], (N_C, INNER, D_MODEL), INNER ** -0.5),
    }


def reference(x, c, ctx, c_ctx, norm_w, w_ada, b_ada, w_in_ab, b_gate_ab, q_norm_a, k_norm_a, rpb_a,
              h_norm_b, w_out_ab, w_in_c, lb_c, h_norm_c, w_out_c):
    for l in range(DEPTH):
        need_ctx = l < DEPTH - 1
        shift, scale, gate = _ada(c, w_ada[l], b_ada[l])
        shift_c, scale_c, gate_c = _ada(c_ctx, w_ada[l], b_ada[l])
        h_lat = _rms(x, norm_w[l]) * (1.0 + scale) + shift
        h_ctx = _rms(ctx, norm_w[l]) * (1.0 + scale_c) + shift_c
        j = l // 2
        if l % 2 == 0:
            y_lat, y_ctx = _ab_mixer(h_lat, h_ctx, w_in_ab[j], b_gate_ab[j], q_norm_a[j], k_norm_a[j],
                                     rpb_a[j], h_norm_b[j], w_out_ab[j], need_ctx)
        else:
            y_lat, y_ctx = _c_mixer(h_lat, h_ctx, w_in_c[j], _lower_bound(lb_c, l), h_norm_c[j],
                                    w_out_c[j], need_ctx)
        x = x + gate * y_lat
        if need_ctx:
            ctx = ctx + gate_c * y_ctx
    return x
```

```python
import numpy as np
import ml_dtypes
import concourse.bass as bass
import concourse.mybir as mybir
from concourse.bass_utils import run_bass_kernel_spmd

F32 = mybir.dt.float32
BF16 = mybir.dt.bfloat16
AF = mybir.ActivationFunctionType
ALU = mybir.AluOpType
AX = mybir.AxisListType

D = 1024
T_LAT = 4096
T_CTX = 256
T_ALL = T_LAT + T_CTX
NT = T_ALL // 128
EPS = 1e-6


class Prog:
    def __init__(self, nc):
        self.nc = nc
        self.eng = {"pe": nc.tensor, "act": nc.scalar, "dve": nc.vector, "pool": nc.gpsimd, "sp": nc.sync}
        self.csem = {}
        self.ccnt = {}
        for e in ("pe", "act", "dve", "pool"):
            self.csem[e] = nc.alloc_semaphore("c_" + e)
            self.ccnt[e] = 0
        self.seen = {e: {} for e in self.eng}
        self.lastw = {}
        self.readers = {}
        self.dsem = {}
        self.dpool = []
        for i in range(40):
            self.dpool.append([nc.alloc_semaphore("d%d" % i), 0])
        self.nbuf = 0
        self.ninst = 0

    def sb(self, name, shape, dt):
        return self.nc.alloc_sbuf_tensor("s_" + name, list(shape), dt)

    def ps(self, name, shape, dt=F32):
        return self.nc.alloc_psum_tensor("p_" + name, list(shape), dt)

    def _deps(self, reads, writes, wadd=()):
        toks = []
        for k in reads:
            toks.extend(self.lastw.get(k, ()))
        for k in writes:
            toks.extend(self.lastw.get(k, ()))
            toks.extend(self.readers.get(k, {}).values())
        for k in wadd:
            toks.extend(self.readers.get(k, {}).values())
        return toks

    def _wait(self, e, toks, skip_sem=None):
        need = {}
        for (sem, val) in toks:
            if skip_sem is not None and sem.name == skip_sem:
                continue
            if self.seen[e].get(sem.name, 0) >= val:
                continue
            if need.get(sem.name, (None, 0))[1] < val:
                need[sem.name] = (sem, val)
        for name, (sem, val) in need.items():
            self.eng[e].wait_ge(sem, val)
            self.seen[e][name] = val

    def _commit(self, tok, reads, writes, wadd=()):
        for k in writes:
            self.lastw[k] = [tok]
            self.readers[k] = {}
        for k in wadd:
            self.lastw.setdefault(k, []).append(tok)
        for k in reads:
            if k in writes:
                continue
            r = self.readers.setdefault(k, {})
            o = r.get(tok[0].name)
            if o is None or o[1] < tok[1]:
                r[tok[0].name] = tok

    def op(self, e, fn, reads=(), writes=(), pe_chain=False):
        toks = self._deps(reads, writes)
        self._wait(e, toks, skip_sem=(self.csem[e].name if pe_chain else None))
        ins = fn(self.eng[e])
        self.ccnt[e] += 1
        ins.then_inc(self.csem[e], 1)
        tok = (self.csem[e], self.ccnt[e])
        self._commit(tok, reads, writes)
        self.ninst += 1
        return ins

    def dma(self, e, out, in_, reads=(), writes=(), semkey=None, wadd=(), **kw):
        toks = self._deps(reads, writes, wadd)
        if semkey is None:
            semkey = (tuple(writes) + tuple(reads))[0]
        if semkey not in self.dsem:
            self.dsem[semkey] = self.dpool[len(self.dsem) % len(self.dpool)]
        ent = self.dsem[semkey]
        if ent[1] > 0:
            toks.append((ent[0], ent[1]))
        self._wait(e, toks)
        ins = self.eng[e].dma_start(out=out, in_=in_, **kw)
        ent[1] += 16
        ins.then_inc(ent[0], 16)
        tok = (ent[0], ent[1])
        self._commit(tok, reads, writes, wadd)
        self.ninst += 1
        return ins

    def barrier(self):
        toks = [(self.csem[f], self.ccnt[f]) for f in self.csem if self.ccnt[f] > 0]
        toks += [(ent[0], ent[1]) for ent in self.dpool if ent[1] > 0]
        for e in self.eng:
            self._wait(e, toks)

    def finish(self, e, keys):
        toks = []
        for k in keys:
            toks.extend(self.lastw.get(k, ()))
        self._wait(e, toks)


def build(dbg=None):
    from contextlib import ExitStack
    nc = bass.Bass("TRN2", target_bir_lowering=False)
    P = Prog(nc)
    dbg_d = {}
    if dbg:
        for name, shape in dbg.items():
            if name.startswith("_"):
                continue
            dbg_d[name] = nc.dram_tensor("dbg_" + name, list(shape), F32, kind="ExternalOutput").ap()
    dt_in = lambda name, shape, dt=F32: nc.dram_tensor(name, list(shape), dt, kind="ExternalInput").ap()
    x_d = dt_in("x", [T_LAT, D])
    ctx_d = dt_in("ctx", [T_CTX, D])
    cc_d = dt_in("cc", [128, 16])
    wada_d = dt_in("w_ada", [2, D, 3 * D])
    bada_d = dt_in("b_ada", [2, 3 * D])
    normw_d = dt_in("norm_w", [2, D])
    ident_d = dt_in("ident", [128, 128])
    consts_d = dt_in("consts", [128, 6, 128])
    win_ab_d = dt_in("w_in_ab", [D, 9232])
    qkw_d = dt_in("qkw", [128, 2])
    nab_d = dt_in("nab", [16, 21, 128, 128])
    yT_d = nc.dram_tensor("yT_scr", [16, 128, T_ALL], BF16, kind="Internal").ap()
    rope_d = dt_in("rope", [128, 4, 64])
    bgate_d = dt_in("b_gate", [1, 16])
    hnb_d = dt_in("h_norm_b", [1, D])
    Hf_d = nc.dram_tensor("Hf_scr", [NT, 128, 256], F32, kind="Internal").ap()
    wout_ab_d = dt_in("w_out_ab", [2 * D, D])
    win_c_d = dt_in("w_in_c", [D, 10240])
    wout_c_d = dt_in("w_out_c", [2 * D, D])
    lbc_d = dt_in("lbc", [128, 16, 2])
    hnc_d = dt_in("h_norm_c", [1, 2 * D])
    smask_d = dt_in("smask", [128, 512])
    if dbg and "x1" in dbg:
        x1_d, ctx1_d = dbg_d["x1"], dbg_d["ctx1"]
    else:
        x1_d = nc.dram_tensor("x1_scr", [T_LAT, D], F32, kind="Internal").ap()
        ctx1_d = nc.dram_tensor("ctx1_scr", [T_CTX, D], F32, kind="Internal").ap()
    out_d = nc.dram_tensor("out", [T_LAT, D], F32, kind="ExternalOutput").ap()

    uid = [0]

    def SB(es, name, shape, dt):
        uid[0] += 1
        return es.enter_context(nc.sbuf_tensor("s%d_%s" % (uid[0], name), list(shape), dt))

    def PS(es, name, shape, dt=F32):
        uid[0] += 1
        return es.enter_context(nc.psum_tensor("p%d_%s" % (uid[0], name), list(shape), dt))

    G = ExitStack()
    ident_f = SB(G, "ident_f", [128, 128], F32)
    ident_b = SB(G, "ident_b", [128, 128], BF16)
    P.dma("sp", ident_f[:], ident_d[:, :], writes=["ident_f"])
    P.op("dve", lambda e: e.tensor_copy(out=ident_b[:], in_=ident_f[:]), reads=["ident_f"], writes=["ident_b"])
    ones_f = SB(G, "ones_f", [128, 128], F32)
    P.op("dve", lambda e: e.memset(ones_f[:], 1.0), writes=["ones_f"])
    cc = SB(G, "cc", [128, 16], F32)
    sc = SB(G, "sc", [128, 16], F32)
    P.dma("sp", cc[:], cc_d[:, :], writes=["cc"])
    P.op("act", lambda e: e.activation(out=sc[:], in_=cc[:], func=AF.Silu), reads=["cc"], writes=["sc"])
    hT = SB(G, "hT", [128, 8, T_ALL], BF16)
    wstage = SB(G, "wstage", [128, 8, 512], F32)
    consts = SB(G, "consts", [128, 6, 128], F32)
    P.dma("sp", consts[:], consts_d[:, :, :], writes=["consts"])
    bones = consts[:, 0, :]

    def norm_stage(l, xsrc, csrc, gate_tiles):
        with ExitStack() as es:
            screp = SB(es, "screp", [128, 16, 128], F32)
            for kj in range(16):
                P.op("dve", lambda e, kj=kj: e.tensor_scalar(out=screp[:, kj, :], in0=ones_f[:],
                                                             scalar1=sc[:, kj:kj + 1], scalar2=None, op0=ALU.mult),
                     reads=["sc", "ones_f"], writes=[("screp", kj)])
            wada = SB(es, "wada", [128, 8, 512], F32)
            brow = SB(es, "brow", [128, 3 * D], F32)
            nwb = SB(es, "nwb", [128, D], F32)
            shf = [SB(es, "shf%d" % j, [128, D], F32) for j in range(2)]
            mA = [SB(es, "mA%d" % j, [128, D], F32) for j in range(2)]
            mps = PS(es, "mps", [128, 512])
            P.dma("sp", brow[:], bada_d[l:l + 1, :].partition_broadcast(128), writes=["brow"])
            P.dma("act", nwb[:], normw_d[l:l + 1, :].partition_broadcast(128), writes=["nwb"])
            for blk in range(6):
                for k in range(8):
                    P.dma("sp" if k % 2 == 0 else "act", wada[:, k, :],
                          wada_d[l, k * 128:(k + 1) * 128, blk * 512:(blk + 1) * 512], writes=[("wada", k)])
                sec, half = blk // 2, blk % 2
                for j in range(2):
                    if sec == 2 and j not in gate_tiles:
                        continue
                    for k in range(8):
                        P.op("pe", lambda e, k=k, j=j: e.matmul(
                            mps[:], lhsT=screp[:, 2 * k + j, :], rhs=wada[:, k, :], start=(k == 0), stop=(k == 7)),
                            reads=[("screp", 2 * k + j), ("wada", k)], writes=["mps"], pe_chain=True)
                    dst = (shf[j], mA[j], gate_tiles.get(j))[sec]
                    dkey = (("shf", j), ("mA", j), ("gateh", j))[sec]
                    c0 = blk * 512
                    P.op("dve", lambda e, dst=dst, half=half, c0=c0: e.tensor_tensor(
                        out=dst[:, half * 512:(half + 1) * 512], in0=mps[:], in1=brow[:, c0:c0 + 512], op=ALU.add),
                        reads=["mps", "brow"], writes=[dkey + (half,)])
            for j in range(2):
                P.op("dve", lambda e, j=j: e.scalar_tensor_tensor(
                    out=mA[j][:], in0=mA[j][:], scalar=1.0, in1=nwb[:], op0=ALU.add, op1=ALU.mult),
                    reads=[("mA", j, 0), ("mA", j, 1), "nwb"], writes=[("mA", j, 0), ("mA", j, 1)])
            for j in gate_tiles:
                P.op("dve", lambda e, j=j: e.tensor_copy(out=gate_tiles[j][:, 0:1], in_=gate_tiles[j][:, 0:1]),
                     reads=[("gateh", j, 0), ("gateh", j, 1)], writes=[("gate", j)])
            xin = [SB(es, "xin%d" % i, [128, D], F32) for i in range(2)]
            junk = SB(es, "junk", [128, D], F32)
            ssq = SB(es, "ssq", [128, 2], F32)
            rstd = SB(es, "rstd", [128, 2], F32)
            hm = [SB(es, "hm%d" % i, [128, D], F32) for i in range(2)]
            hb = [SB(es, "hbf%d" % i, [128, D], BF16) for i in range(2)]
            tps = [PS(es, "tps%d" % i, [128, 8, 128], BF16) for i in range(2)]
            for t in range(NT):
                i = t % 2
                j = 1 if t < 2 else 0
                src = csrc[t * 128:(t + 1) * 128, :] if t < 2 else xsrc[(t - 2) * 128:(t - 1) * 128, :]
                P.dma("sp" if i == 0 else "act", xin[i][:], src, reads=[("x1", t)] if l == 1 else [],
                      writes=[("xin", i)])
                P.op("act", lambda e, i=i: e.activation(out=junk[:], in_=xin[i][:], func=AF.Square,
                                                        accum_out=ssq[:, i:i + 1]),
                     reads=[("xin", i)], writes=["junk", ("ssq", i)])
                P.op("act", lambda e, i=i: e.activation(out=ssq[:, i:i + 1], in_=ssq[:, i:i + 1], func=AF.Sqrt,
                                                        scale=1.0 / D, bias=EPS),
                     reads=[("ssq", i)], writes=[("ssq", i)])
                P.op("dve", lambda e, i=i: e.reciprocal(out=rstd[:, i:i + 1], in_=ssq[:, i:i + 1]),
                     reads=[("ssq", i)], writes=[("rstd", i)])
                P.op("dve", lambda e, i=i, j=j: e.scalar_tensor_tensor(
                    out=hm[i][:], in0=xin[i][:], scalar=rstd[:, i:i + 1], in1=mA[j][:],
                    op0=ALU.mult, op1=ALU.mult),
                    reads=[("xin", i), ("rstd", i), ("mA", j, 0), ("mA", j, 1)], writes=[("hm", i)])
                P.op("pool", lambda e, i=i, j=j: e.tensor_tensor(out=hb[i][:], in0=hm[i][:], in1=shf[j][:],
                                                                 op=ALU.add),
                     reads=[("hm", i), ("shf", j, 0), ("shf", j, 1)], writes=[("hb", i)])
                for c in range(8):
                    P.op("pe", lambda e, i=i, c=c: e.transpose(out=tps[i][:, c, :],
                                                               in_=hb[i][:, c * 128:(c + 1) * 128],
                                                               identity=ident_b[:]),
                         reads=[("hb", i), "ident_b"], writes=[("tps", i)], pe_chain=True)
                P.op("act", lambda e, i=i, t=t: e.copy(out=hT[:, :, t * 128:(t + 1) * 128], in_=tps[i][:]),
                     reads=[("tps", i)], writes=[("hT", t)])
            P.barrier()

    L0 = ExitStack()
    gate0 = {j: SB(L0, "gate0_%d" % j, [128, D], F32) for j in range(2)}
    norm_stage(0, x_d, ctx_d, gate0)


    def load_w(es_w, wd, col0, ncols, tag, segs=None):
        if segs is None:
            segs = [(col0, ncols)]
        ncols = sum(n for _, n in segs)
        wb = SB(es_w, "wb_" + tag, [128, 8, ncols], BF16)
        for k in range(8):
            o = 0
            for si, (c0, n) in enumerate(segs):
                P.dma("sp" if k % 2 == 0 else "act", wstage[:, k, o:o + n], wd[k * 128:(k + 1) * 128, c0:c0 + n],
                      writes=([("wstage", k)] if si == 0 else []), wadd=([] if si == 0 else [("wstage", k)]),
                      semkey=("wstage", k, si))
                o += n
        for k in range(8):
            P.op("pool" if k % 2 == 0 else "dve",
                 lambda e, k=k: e.tensor_copy(out=wb[:, k, :], in_=wstage[:, k, 0:ncols]),
                 reads=[("wstage", k)], writes=[("wb_" + tag, k)])
        return wb, [("wb_" + tag, k) for k in range(8)]

    TOKG = [(0, 256)] + [(256 + 512 * g, 512) for g in range(8)]

    def proj_fm(wb, wkeys, c0, pst, pkey, t0, n):
        for k in range(8):
            P.op("pe", lambda e, k=k: e.matmul(pst[:, 0:n], lhsT=wb[:, k, c0:c0 + 128], rhs=hT[:, k, t0:t0 + n],
                                               start=(k == 0), stop=(k == 7)),
                 reads=[wkeys[k]] + [("hT", t) for t in range(t0 // 128, (t0 + n) // 128)], writes=[pkey],
                 pe_chain=True)

    def proj_tm(wb, wkeys, c0, ncols, pst, pkey, t):
        for k in range(8):
            P.op("pe", lambda e, k=k: e.matmul(pst[:, 0:ncols], lhsT=hT[:, k, t * 128:(t + 1) * 128],
                                               rhs=wb[:, k, c0:c0 + ncols], start=(k == 0), stop=(k == 7)),
                 reads=[wkeys[k], ("hT", t)], writes=[pkey], pe_chain=True)

    def na_keytiles(t):
        if t < 2:
            return [(0, None), (1, None)]
        i = t - 2
        if i == 0:
            nb = [(j, 5 + j) for j in range(4)]
        elif i == 1:
            nb = [(j, 9 + j) for j in range(4)]
        elif i == 30:
            nb = [(28 + j, 13 + j) for j in range(4)]
        elif i == 31:
            nb = [(28 + j, 17 + j) for j in range(4)]
        else:
            nb = [(i + dj, dj + 2) for dj in range(-2, 3)]
        return [(2 + j, ty) for (j, ty) in nb] + [(0, None), (1, None)]

    def na_stage(chunks, dbg_oa=None):
        with ExitStack() as es:
            qkw = SB(es, "qkw", [128, 2], F32)
            P.dma("sp", qkw[:], qkw_d[:, :], writes=["qkw"])
            P.op("act", lambda e: e.mul(out=qkw[:, 0:1], in_=qkw[:, 0:1], mul=0.125), reads=["qkw"], writes=["qkw"])
            qT = SB(es, "qT", [128, T_ALL], BF16)
            kT = SB(es, "kT", [128, T_ALL], BF16)
            vaug = SB(es, "vaug", [128, NT, 2, 65], BF16)
            yTc = SB(es, "yTc", [128, T_ALL], BF16)
            biasf = SB(es, "biasf", [128, 2, 21, 128], F32)
            biasb = SB(es, "biasb", [128, 2, 21, 128], BF16)
            sq = SB(es, "sq", [128, 512], F32)
            rs = SB(es, "rs", [128, 512], F32)
            PT = SB(es, "PT", [128, 8, 128], BF16)
            rden = SB(es, "rden", [128, 2], F32)
            oa = SB(es, "oa", [128, 128], F32)
            sz = SB(es, "sz", [128, 128], F32)
            yab = SB(es, "yab", [128, 128], BF16)
            pp = PS(es, "pp", [128, 512])
            ssp = PS(es, "ssp", [128, 512])
            st = PS(es, "st", [128, 8, 128])
            num = PS(es, "num", [128, 2, 128])
            tp = PS(es, "tp", [128, 128], BF16)
            P.op("dve", lambda e: e.memset(vaug[:], 1.0), writes=[("vaug", t) for t in range(NT)])
            for c in chunks:
                with ExitStack() as es_w:
                    wq, wqk = load_w(es_w, win_ab_d, c * 128, 128, "q")
                    wk, wkk = load_w(es_w, win_ab_d, 1024 + c * 128, 128, "k")
                    wv, wvk = load_w(es_w, win_ab_d, 2048 + c * 128, 128, "v")
                    wz, wzk = load_w(es_w, win_ab_d, 3072 + c * 128, 128, "z")
                    for hh in range(2):
                        P.dma("sp" if hh == 0 else "act", biasf[:, hh, :, :],
                              nab_d[2 * c + hh].rearrange("t k q -> k t q"), writes=[("biasf", hh)])
                        P.op("pool", lambda e, hh=hh: e.tensor_copy(out=biasb[:, hh, :, :], in_=biasf[:, hh, :, :]),
                             reads=[("biasf", hh)], writes=[("biasb", hh)])
                    for (dst, dname, wb_, wk_, col) in ((qT, "qT", wq, wqk, 0), (kT, "kT", wk, wkk, 1)):
                        for (t0, n) in TOKG:
                            proj_fm(wb_, wk_, 0, pp, "pp", t0, n)
                            P.op("act", lambda e, n=n: e.activation(out=sq[:, 0:n], in_=pp[:, 0:n], func=AF.Square),
                                 reads=["pp"], writes=["sq"])
                            P.op("pe", lambda e, n=n: e.matmul(ssp[:, 0:n], lhsT=bones, rhs=sq[:, 0:n],
                                                               start=True, stop=True),
                                 reads=["sq", "consts"], writes=["ssp"])
                            P.op("act", lambda e, n=n: e.activation(out=rs[:, 0:n], in_=ssp[:, 0:n], func=AF.Sqrt,
                                                                    scale=1.0 / 64, bias=EPS),
                                 reads=["ssp"], writes=["rs"])
                            P.op("dve", lambda e, n=n: e.reciprocal(out=rs[:, 0:n], in_=rs[:, 0:n]),
                                 reads=["rs"], writes=["rs"])
                            P.op("dve", lambda e, n=n, t0=t0, dst=dst, col=col: e.scalar_tensor_tensor(
                                out=dst[:, t0:t0 + n], in0=pp[:, 0:n], scalar=qkw[:, col:col + 1], in1=rs[:, 0:n],
                                op0=ALU.mult, op1=ALU.mult),
                                reads=["pp", "rs", "qkw"],
                                writes=[(dname, t) for t in range(t0 // 128, (t0 + n) // 128)])
                    for t in range(NT):
                        proj_tm(wv, wvk, 0, 128, pp, "pp", t)
                        P.op("act", lambda e, t=t: e.copy(out=vaug[:, t, :, 0:64],
                                                          in_=pp[:, 0:128].rearrange("p (h d) -> p h d", h=2)),
                             reads=["pp"], writes=[("vaug", t)])
                    for t in range(NT):
                        kts = na_keytiles(t)
                        nk = len(kts)
                        proj_tm(wz, wzk, 0, 128, pp, "pp", t)
                        P.op("act", lambda e: e.activation(out=sz[:], in_=pp[:, 0:128], func=AF.Silu),
                             reads=["pp"], writes=["sz"])
                        for hh in range(2):
                            pb = 64 * hh
                            for n_, (kt, ty) in enumerate(kts):
                                P.op("pe", lambda e, n_=n_, kt=kt, ty=ty, pb=pb: e.matmul(
                                    st[:, n_, :], lhsT=kT[pb:pb + 64, kt * 128:(kt + 1) * 128],
                                    rhs=qT[pb:pb + 64, t * 128:(t + 1) * 128], start=True, stop=(ty is None)),
                                    reads=[("kT", kt), ("qT", t)], writes=["st"], pe_chain=True)
                                if ty is not None:
                                    P.op("pe", lambda e, n_=n_, ty=ty, hh=hh: e.matmul(
                                        st[:, n_, :], lhsT=ident_b[:], rhs=biasb[:, hh, ty, :], start=False, stop=True),
                                        reads=["ident_b", ("biasb", hh)], writes=["st"], pe_chain=True)
                            P.op("act", lambda e, nk=nk: e.activation(out=PT[:, 0:nk, :], in_=st[:, 0:nk, :],
                                                                      func=AF.Exp),
                                 reads=["st"], writes=["PT"])
                            for n_, (kt, ty) in enumerate(kts):
                                P.op("pe", lambda e, n_=n_, kt=kt, hh=hh, nk=nk: e.matmul(
                                    num[:, hh, 0:65], lhsT=PT[:, n_, :], rhs=vaug[:, kt, hh, :],
                                    start=(n_ == 0), stop=(n_ == nk - 1)),
                                    reads=["PT", ("vaug", kt)], writes=[("num", hh)], pe_chain=True)
                            P.op("dve", lambda e, hh=hh: e.reciprocal(out=rden[:, hh:hh + 1], in_=num[:, hh, 64:65]),
                                 reads=[("num", hh)], writes=[("rden", hh)])
                            P.op("dve", lambda e, hh=hh: e.tensor_scalar(
                                out=oa[:, hh * 64:(hh + 1) * 64], in0=num[:, hh, 0:64], scalar1=rden[:, hh:hh + 1],
                                scalar2=None, op0=ALU.mult),
                                reads=[("num", hh), ("rden", hh)], writes=[("oa", hh)])
                        if dbg_oa is not None:
                            P.dma("sp", dbg_oa[t * 128:(t + 1) * 128, c * 128:(c + 1) * 128], oa[:],
                                  reads=[("oa", 0), ("oa", 1)], writes=[("dbgoa", t % 4)])
                        P.op("pool", lambda e: e.tensor_tensor(out=yab[:], in0=oa[:], in1=sz[:], op=ALU.mult),
                             reads=[("oa", 0), ("oa", 1), "sz"], writes=["yab"])
                        P.op("pe", lambda e: e.transpose(out=tp[:], in_=yab[:], identity=ident_b[:]),
                             reads=["yab", "ident_b"], writes=["tp"])
                        P.op("act", lambda e, t=t: e.copy(out=yTc[:, t * 128:(t + 1) * 128], in_=tp[:]),
                             reads=["tp"], writes=[("yTc", t)])
                    P.dma("sp", yT_d[c], yTc[:], reads=[("yTc", t) for t in range(NT)], writes=[("yT_d", c)])
                    if dbg and "qk" in dbg:
                        stg = SB(es_w, "dbgstg", [128, T_ALL], F32)
                        for ii, (src, nm) in enumerate(((qT, "qT"), (kT, "kT"))):
                            P.op("dve", lambda e, src=src: e.tensor_copy(out=stg[:], in_=src[:]),
                                 reads=[(nm, t) for t in range(NT)], writes=["dbgstg"])
                            P.dma("sp", dbg_d["qk"][ii * 128:(ii + 1) * 128, :], stg[:], reads=["dbgstg"],
                                  writes=[("dbgqk", ii)])
                        P.finish("sp", [("dbgqk", 0), ("dbgqk", 1), ("dbgqk", 2)])
                    P.barrier()
            if dbg_oa is not None:
                P.finish("sp", [("dbgoa", i) for i in range(4)])
            P.barrier()

    def ml_stage(heads, dbg_hb=None):
        with ExitStack() as es:
            rope = SB(es, "rope", [128, 4, 64], F32)
            P.dma("sp", rope[:], rope_d[:, :, :], writes=["rope"])
            bgate = SB(es, "bgate", [128, 16], F32)
            P.dma("act", bgate[:], bgate_d[0:1, :].partition_broadcast(128), writes=["bgate"])
            hnb = SB(es, "hnb", [128, D], F32)
            P.dma("sp", hnb[:], hnb_d[0:1, :].partition_broadcast(128), writes=["hnb"])
            Gt = SB(es, "Gt", [128, NT, 16], F32)
            LF = SB(es, "LF", [128, NT, 8], F32)
            Wt = SB(es, "Wt", [128, NT, 8], F32)
            Bs = SB(es, "Bs", [128, NT, 16], F32)
            EB = SB(es, "EB", [128, NT, 8], F32)
            EBL = SB(es, "EBL", [128, NT, 8], F32)
            with ExitStack() as es_w:
                pg = PS(es_w, "pg", [128, 16])
                wg, wgk = load_w(es_w, win_ab_d, 9216, 16, "g")
                for t in range(NT):
                    proj_tm(wg, wgk, 0, 16, pg, "pg", t)
                    P.op("dve", lambda e, t=t: e.tensor_tensor(out=Gt[:, t, :], in0=pg[:, 0:16], in1=bgate[:],
                                                               op=ALU.add),
                         reads=["pg", "bgate"], writes=[("Gt", t)])
                gkeys = [("Gt", t) for t in range(NT)]
                for d in range(2):
                    P.op("act", lambda e, d=d: e.activation(out=LF[:, :, 4 * d:4 * d + 4],
                                                            in_=Gt[:, :, 4 + 8 * d:8 + 8 * d], func=AF.Exp, scale=-1.0),
                         reads=gkeys, writes=[("LF", d)])
                    P.op("act", lambda e, d=d: e.activation(out=LF[:, :, 4 * d:4 * d + 4],
                                                            in_=LF[:, :, 4 * d:4 * d + 4], func=AF.Ln, bias=1.0),
                         reads=[("LF", d)], writes=[("LF", d)])
                    P.op("dve", lambda e, d=d: e.tensor_scalar(out=LF[:, :, 4 * d:4 * d + 4],
                                                               in0=LF[:, :, 4 * d:4 * d + 4], scalar1=-1.0,
                                                               scalar2=None, op0=ALU.mult),
                         reads=[("LF", d)], writes=[("LF", d)])
                for t in range(NT):
                    P.op("pe", lambda e, t=t: e.matmul(pg[:, 0:4], lhsT=consts[:, 1, :], rhs=LF[:, t, 0:4],
                                                       start=True, stop=True),
                         reads=["consts", ("LF", 0)], writes=["pg"])
                    P.op("pe", lambda e, t=t: e.matmul(pg[:, 4:8], lhsT=consts[:, 2, :], rhs=LF[:, t, 4:8],
                                                       start=True, stop=True),
                         reads=["consts", ("LF", 1)], writes=["pg"], pe_chain=True)
                    P.op("pe", lambda e, t=t: e.matmul(pg[:, 8:16], lhsT=ones_f[:], rhs=LF[:, t, 0:8],
                                                       start=True, stop=True),
                         reads=["ones_f", ("LF", 0), ("LF", 1)], writes=["pg"], pe_chain=True)
                    for d in range(2):
                        P.op("dve", lambda e, t=t, d=d: e.tensor_tensor(
                            out=Wt[:, t, 4 * d:4 * d + 4], in0=Gt[:, t, 8 * d:8 * d + 4], in1=pg[:, 4 * d:4 * d + 4],
                            op=ALU.subtract),
                            reads=["pg", ("Gt", t)], writes=[("Wt", t)])
                    P.op("act", lambda e, t=t: e.copy(out=Bs[:, t, :], in_=pg[:, 0:16]),
                         reads=["pg"], writes=[("Bs", t)])
                P.op("act", lambda e: e.activation(out=Wt[:], in_=Wt[:], func=AF.Exp),
                     reads=[("Wt", t) for t in range(NT)], writes=["Wt"])
                P.op("act", lambda e: e.activation(out=EB[:], in_=Bs[:, :, 0:8], func=AF.Exp),
                     reads=[("Bs", t) for t in range(NT)], writes=["EB"])
                P.op("act", lambda e: e.activation(out=EBL[:], in_=Bs[:, :, 8:16], func=AF.Exp),
                     reads=[("Bs", t) for t in range(NT)], writes=["EBL"])
                P.barrier()
            for h in heads:
                with ExitStack() as es_h:
                    qT = SB(es_h, "mqT", [128, 2, T_ALL], BF16)
                    kT = SB(es_h, "mkT", [128, 2, T_ALL], BF16)
                    vaug = SB(es_h, "mvaug", [128, NT, 257], BF16)
                    P.op("pool", lambda e: e.memset(vaug[:], 1.0), writes=[("mv", t) for t in range(NT)])
                    with ExitStack() as es_p:
                        pp = PS(es_p, "mpp", [128, 512])
                        pp2 = PS(es_p, "mpp2", [128, 512])
                        t1 = SB(es_p, "t1", [128, 512], F32)
                        t2 = SB(es_p, "t2", [128, 512], F32)
                        for (dst, dn, cbase, ti) in ((qT, "mqT", 4096 + h * 256, 0), (kT, "mkT", 5120 + h * 256, 2)):
                            for cch in range(2):
                                with ExitStack() as es_w:
                                    c0 = cbase + cch * 128
                                    w, wk_ = load_w(es_w, win_ab_d, c0, 128, "a")
                                    wsw, wswk = load_w(es_w, win_ab_d, 0, 0, "b", segs=[(c0 + 64, 64), (c0, 64)])
                                    for (t0, n) in TOKG:
                                        okeys = [(dn, cch, t) for t in range(t0 // 128, (t0 + n) // 128)]
                                        proj_fm(w, wk_, 0, pp, "mpp", t0, n)
                                        if t0 < 256:
                                            P.op("act", lambda e, n=n, t0=t0, dst=dst, cch=cch, ti=ti: e.mul(
                                                out=dst[:, cch, t0:t0 + n], in_=pp[:, 0:n],
                                                mul=(1.0 if ti == 0 else 1.0 / 16)),
                                                reads=["mpp"], writes=okeys)
                                            continue
                                        proj_fm(wsw, wswk, 0, pp2, "mpp2", t0, n)
                                        r0 = (t0 - 256) // 64
                                        if cch == 0:
                                            cosv = rope[:, ti, r0:r0 + 8].unsqueeze(2).broadcast_to([128, 8, 64])
                                            sinv = rope[:, ti + 1, r0:r0 + 8].unsqueeze(2).broadcast_to([128, 8, 64])
                                        else:
                                            cosv = rope[:, ti, :].unsqueeze(1).broadcast_to([128, 8, 64])
                                            sinv = rope[:, ti + 1, :].unsqueeze(1).broadcast_to([128, 8, 64])
                                        P.op("dve", lambda e, cosv=cosv: e.tensor_tensor(
                                            out=t1[:].rearrange("p (r c) -> p r c", r=8),
                                            in0=pp[:].rearrange("p (r c) -> p r c", r=8), in1=cosv, op=ALU.mult),
                                            reads=["mpp", "rope"], writes=["t1"])
                                        P.op("dve", lambda e, sinv=sinv: e.tensor_tensor(
                                            out=t2[:].rearrange("p (r c) -> p r c", r=8),
                                            in0=pp2[:].rearrange("p (r c) -> p r c", r=8), in1=sinv, op=ALU.mult),
                                            reads=["mpp2", "rope"], writes=["t2"])
                                        P.op("pool", lambda e, t0=t0, dst=dst, cch=cch: e.tensor_tensor(
                                            out=dst[:, cch, t0:t0 + 512], in0=t1[:], in1=t2[:], op=ALU.add),
                                            reads=["t1", "t2"], writes=okeys)
                                    P.barrier()
                        with ExitStack() as es_w:
                            wv, wvk = load_w(es_w, win_ab_d, 6144 + h * 256, 256, "a")
                            for t in range(NT):
                                proj_tm(wv, wvk, 0, 256, pp, "mpp", t)
                                P.op("act", lambda e, t=t: e.copy(out=vaug[:, t, 0:256], in_=pp[:, 0:256]),
                                     reads=["mpp"], writes=[("mv", t)])
                            P.barrier()
                    if dbg and "mqk" in dbg:
                        with ExitStack() as es_d:
                            stg = SB(es_d, "dbgstg", [128, T_ALL], F32)
                            for ii, (src, nm, cch) in enumerate(((qT, "mqT", 0), (qT, "mqT", 1), (kT, "mkT", 0), (kT, "mkT", 1))):
                                P.op("dve", lambda e, src=src, cch=cch: e.tensor_copy(out=stg[:], in_=src[:, cch, :]),
                                     reads=[(nm, cch, t) for t in range(NT)], writes=["dbgstg"])
                                P.dma("sp", dbg_d["mqk"][ii * 128:(ii + 1) * 128, :], stg[:], reads=["dbgstg"],
                                      writes=[("dbgqk", ii)])
                            P.finish("sp", [("dbgqk", ii) for ii in range(4)])
                            P.barrier()
                    with ExitStack() as es_c:
                        wo, wok = load_w(es_c, win_ab_d, 0, 0, "oz", segs=[(7168 + h * 256, 256), (8192 + h * 256, 256)])
                        Cf = SB(es_c, "Cf", [128, 2, 257], F32)
                        Ct = SB(es_c, "Ct", [128, 2, 257], F32)
                        Cb = SB(es_c, "Cb", [128, 2, 257], BF16)
                        STm = SB(es_c, "STm", [128, 128], BF16)
                        kt = SB(es_c, "kt", [128, 256], BF16)
                        sm = SB(es_c, "sm", [128, 4], F32)
                        Hin = SB(es_c, "Hin", [128, 256], F32)
                        Hs = SB(es_c, "Hs", [128, 256], F32)
                        sig = SB(es_c, "sig", [128, 256], F32)
                        szb = SB(es_c, "szb", [128, 256], F32)
                        hbg = SB(es_c, "hbg", [128, 256], F32)
                        ybf = SB(es_c, "ybf", [128, 256], BF16)
                        ysb = SB(es_c, "ysb", [128, 2, 128], BF16)
                        STp = PS(es_c, "STp", [128, 128])
                        acc = PS(es_c, "acc", [128, 512])
                        cacc = PS(es_c, "cacc", [128, 2, 512])
                        po = PS(es_c, "po", [128, 512])
                        tpk = PS(es_c, "tpk", [128, 2, 128], BF16)
                        tpy = PS(es_c, "tpy", [128, 2, 128], BF16)
                        for d in range(2):
                            hd = 4 * d + h
                            order = [0, 1] + list(range(2, NT)) if d == 0 else [1, 0] + list(range(NT - 1, 1, -1))
                            P.op("dve", lambda e: e.memset(Cf[:], 0.0), writes=["Cf"])
                            P.op("pool", lambda e: e.memset(Cb[:], 0.0), writes=["Cb"])
                            for t in order:
                                ts_ = slice(t * 128, (t + 1) * 128)
                                for c in range(2):
                                    P.op("pe", lambda e, c=c: e.matmul(STp[:], lhsT=kT[:, c, ts_], rhs=qT[:, c, ts_],
                                                                       start=(c == 0), stop=(c == 1)),
                                         reads=[("mkT", c, t), ("mqT", c, t)], writes=["STp"], pe_chain=True)
                                P.op("dve", lambda e, d=d, t=t, hd=hd: e.scalar_tensor_tensor(
                                    out=STm[:], in0=STp[:], scalar=Wt[:, t, hd:hd + 1], in1=consts[:, 1 + d, :],
                                    op0=ALU.mult, op1=ALU.mult),
                                    reads=["STp", "Wt", "consts"], writes=["STm"])
                                P.op("pe", lambda e, t=t: e.matmul(acc[:, 0:257], lhsT=STm[:], rhs=vaug[:, t, :],
                                                                   start=True, stop=False),
                                     reads=["STm", ("mv", t)], writes=["acc"])
                                for c in range(2):
                                    P.op("pe", lambda e, c=c: e.matmul(acc[:, 0:257], lhsT=qT[:, c, ts_], rhs=Cb[:, c, :],
                                                                       start=False, stop=(c == 1)),
                                         reads=[("mqT", c, t), "Cb"], writes=["acc"], pe_chain=True)
                                P.op("act", lambda e, t=t, hd=hd: e.activation(
                                    out=sm[:, 0:1], in_=acc[:, 256:257], func=AF.Abs, scale=EB[:, t, hd:hd + 1]),
                                    reads=["acc", "EB"], writes=["sm"])
                                P.op("dve", lambda e: e.tensor_scalar(out=sm[:, 1:2], in0=sm[:, 0:1], scalar1=1.0,
                                                                      scalar2=None, op0=ALU.max),
                                     reads=["sm"], writes=["sm"])
                                P.op("dve", lambda e: e.reciprocal(out=sm[:, 2:3], in_=sm[:, 1:2]),
                                     reads=["sm"], writes=["sm"])
                                P.op("dve", lambda e, t=t, hd=hd: e.tensor_tensor(
                                    out=sm[:, 3:4], in0=sm[:, 2:3], in1=EB[:, t, hd:hd + 1], op=ALU.mult),
                                    reads=["sm", "EB"], writes=["sm"])
                                if d == 0:
                                    P.op("act", lambda e: e.activation(out=Hs[:], in_=acc[:, 0:256], func=AF.Copy,
                                                                       scale=sm[:, 3:4]),
                                         reads=["acc", "sm"], writes=["Hs"])
                                    P.dma("sp", Hf_d[t], Hs[:], reads=["Hs"], writes=[("Hf_d", t)])
                                else:
                                    P.dma("sp", Hin[:], Hf_d[t], reads=[("Hf_d", t)], writes=["Hin"])
                                    P.op("dve", lambda e: e.scalar_tensor_tensor(
                                        out=Hs[:], in0=acc[:, 0:256], scalar=sm[:, 3:4], in1=Hin[:],
                                        op0=ALU.mult, op1=ALU.add),
                                        reads=["acc", "sm", "Hin"], writes=["Hs"])
                                    if dbg_hb is not None:
                                        P.dma("act", dbg_hb[t * 128:(t + 1) * 128, h * 256:(h + 1) * 256], Hs[:],
                                              reads=["Hs"], writes=[("dbghb", t % 4)])
                                    proj_tm(wo, wok, 0, 512, po, "po", t)
                                    P.op("act", lambda e: e.activation(out=sig[:], in_=po[:, 0:256], func=AF.Sigmoid),
                                         reads=["po"], writes=["sig"])
                                    P.op("act", lambda e: e.activation(out=szb[:], in_=po[:, 256:512], func=AF.Silu),
                                         reads=["po"], writes=["szb"])
                                    P.op("pool", lambda e: e.tensor_tensor(out=hbg[:], in0=Hs[:], in1=sig[:],
                                                                           op=ALU.mult),
                                         reads=["Hs", "sig"], writes=["hbg"])
                                    P.op("act", lambda e: e.activation(out=sig[:], in_=hbg[:], func=AF.Square,
                                                                       accum_out=sm[:, 0:1]),
                                         reads=["hbg", "sm"], writes=["sig", "sm"])
                                    P.op("act", lambda e: e.activation(out=sm[:, 1:2], in_=sm[:, 0:1], func=AF.Sqrt,
                                                                       scale=1.0 / 256, bias=EPS),
                                         reads=["sm"], writes=["sm"])
                                    P.op("dve", lambda e: e.reciprocal(out=sm[:, 2:3], in_=sm[:, 1:2]),
                                         reads=["sm"], writes=["sm"])
                                    P.op("dve", lambda e, h=h: e.scalar_tensor_tensor(
                                        out=hbg[:], in0=hbg[:], scalar=sm[:, 2:3], in1=hnb[:, h * 256:(h + 1) * 256],
                                        op0=ALU.mult, op1=ALU.mult),
                                        reads=["hbg", "sm", "hnb"], writes=["hbg"])
                                    P.op("pool", lambda e: e.tensor_tensor(out=ybf[:], in0=hbg[:], in1=szb[:],
                                                                           op=ALU.mult),
                                         reads=["hbg", "szb"], writes=["ybf"])
                                    for c in range(2):
                                        P.op("pe", lambda e, c=c: e.transpose(out=tpy[:, c, :],
                                                                              in_=ybf[:, c * 128:(c + 1) * 128],
                                                                              identity=ident_b[:]),
                                             reads=["ybf", "ident_b"], writes=["tpy"], pe_chain=True)
                                    P.op("act", lambda e: e.copy(out=ysb[:], in_=tpy[:]), reads=["tpy"], writes=["ysb"])
                                    P.dma("sp", yT_d[8 + 2 * h:10 + 2 * h, :, ts_].rearrange("c p t -> p c t"), ysb[:],
                                          reads=["ysb"], writes=[("yT_d", 8 + 2 * h, t)])
                                for c in range(2):
                                    P.op("pe", lambda e, c=c: e.transpose(out=tpk[:, c, :], in_=kT[:, c, ts_],
                                                                          identity=ident_b[:]),
                                         reads=[("mkT", c, t), "ident_b"], writes=["tpk"], pe_chain=True)
                                P.op("dve", lambda e, t=t, hd=hd: e.tensor_scalar(
                                    out=kt[:], in0=tpk[:].rearrange("p c k -> p (c k)"), scalar1=Wt[:, t, hd:hd + 1],
                                    scalar2=None, op0=ALU.mult),
                                    reads=["tpk", "Wt"], writes=["kt"])
                                for c in range(2):
                                    P.op("pe", lambda e, c=c, t=t: e.matmul(cacc[:, c, 0:257],
                                                                            lhsT=kt[:, c * 128:(c + 1) * 128],
                                                                            rhs=vaug[:, t, :], start=True, stop=True),
                                         reads=["kt", ("mv", t)], writes=["cacc"], pe_chain=(c == 1))
                                P.op("dve", lambda e: e.tensor_tensor(out=Ct[:], in0=cacc[:, :, 0:257], in1=Cf[:],
                                                                      op=ALU.add),
                                     reads=["cacc", "Cf"], writes=["Ct"])
                                P.op("dve", lambda e, t=t, hd=hd: e.tensor_scalar(
                                    out=Cf[:], in0=Ct[:], scalar1=EBL[:, t, hd:hd + 1], scalar2=None, op0=ALU.mult),
                                    reads=["Ct", "EBL"], writes=["Cf"])
                                P.op("pool", lambda e, t=t, hd=hd: e.tensor_scalar(
                                    out=Cb[:], in0=Ct[:], scalar1=EBL[:, t, hd:hd + 1], scalar2=None, op0=ALU.mult),
                                    reads=["Ct", "EBL"], writes=["Cb"])
                        if dbg_hb is not None:
                            P.finish("act", [("dbghb", i) for i in range(4)])
                        P.barrier()
            P.barrier()

    def out_stage(wout_d, xsrc, csrc, gates, dst_x, dst_c, outkey):
        with ExitStack() as es:
            wo_b = SB(es, "wo_b", [128, 16, D], BF16)
            for half in range(2):
                for kg in range(2):
                    for k in range(8):
                        kc = kg * 8 + k
                        P.dma("sp" if k % 2 == 0 else "act", wstage[:, k, :],
                              wout_d[kc * 128:(kc + 1) * 128, half * 512:(half + 1) * 512], writes=[("wstage", k)],
                              semkey=("wstage", k, 0))
                        P.op("pool" if k % 2 == 0 else "dve", lambda e, k=k, kc=kc, half=half: e.tensor_copy(
                            out=wo_b[:, kc, half * 512:(half + 1) * 512], in_=wstage[:, k, :]),
                            reads=[("wstage", k)], writes=[("wo_b", kc, half)])
            yt = [SB(es, "yt%d" % i, [128, 16, 128], BF16) for i in range(2)]
            xt = [SB(es, "xt%d" % i, [128, D], F32) for i in range(2)]
            tmp = SB(es, "otmp", [128, D], F32)
            xo = [SB(es, "xo%d" % i, [128, D], F32) for i in range(2)]
            py = [PS(es, "py%d" % i, [128, 512]) for i in range(2)]
            tiles = list(range(NT)) if dst_c is not None else list(range(2, NT))
            for n_, t in enumerate(tiles):
                i = n_ % 2
                j = 1 if t < 2 else 0
                ts_ = slice(t * 128, (t + 1) * 128)
                P.dma("sp", yt[i][:], yT_d[:, :, ts_].rearrange("c p t -> p c t"),
                      reads=[("yT_d", c) for c in range(16)], writes=[("yt", i)])
                src = csrc[ts_, :] if t < 2 else xsrc[(t - 2) * 128:(t - 1) * 128, :]
                P.dma("act", xt[i][:], src, reads=[("x1", t)] if outkey == "out" else [], writes=[("xt", i)])
                for half in range(2):
                    for kc in range(16):
                        P.op("pe", lambda e, kc=kc, half=half, i=i: e.matmul(
                            py[half][:], lhsT=yt[i][:, kc, :], rhs=wo_b[:, kc, half * 512:(half + 1) * 512],
                            start=(kc == 0), stop=(kc == 15)),
                            reads=[("yt", i), ("wo_b", kc, half)], writes=[("py", half)], pe_chain=True)
                    hs = slice(half * 512, (half + 1) * 512)
                    P.op("dve", lambda e, half=half, hs=hs, j=j: e.tensor_tensor(
                        out=tmp[:, hs], in0=py[half][:], in1=gates[j][:, hs], op=ALU.mult),
                        reads=[("py", half), ("gate", j)], writes=[("otmp", half)])
                    P.op("pool", lambda e, hs=hs, i=i: e.tensor_tensor(out=xo[i][:, hs], in0=tmp[:, hs],
                                                                       in1=xt[i][:, hs], op=ALU.add),
                         reads=[("otmp", half), ("xt", i)], writes=[("xo", i, half)])
                dst = dst_c[ts_, :] if t < 2 else dst_x[(t - 2) * 128:(t - 1) * 128, :]
                P.dma("sp", dst, xo[i][:], reads=[("xo", i, 0), ("xo", i, 1)], writes=[(outkey, t)],
                      semkey=("xo", i))
            P.barrier()

    def hg_stage(heads, dbg_o=None):
        NCH = T_ALL // 64
        with ExitStack() as es:
            lbc = SB(es, "lbc", [128, 16, 2], F32)
            P.dma("sp", lbc[:], lbc_d[:, :, :], writes=["lbc"])
            lb = SB(es, "lb", [128, 16], F32)
            oml = SB(es, "oml", [128, 16], F32)
            P.op("dve", lambda e: e.tensor_tensor(out=lb[:], in0=lbc[:, :, 1], in1=lbc[:, :, 0], op=ALU.subtract),
                 reads=["lbc"], writes=["lb"])
            P.op("act", lambda e: e.activation(out=lb[:], in_=lb[:], func=AF.Sigmoid), reads=["lb"], writes=["lb"])
            P.op("dve", lambda e: e.tensor_scalar(out=oml[:], in0=lb[:], scalar1=-1.0, scalar2=1.0, op0=ALU.mult,
                                                  op1=ALU.add), reads=["lb"], writes=["oml"])
            hnc = SB(es, "hnc", [128, 2 * D], F32)
            P.dma("act", hnc[:], hnc_d[0:1, :].partition_broadcast(128), writes=["hnc"])
            smask = SB(es, "smask", [128, 512], F32)
            P.dma("sp", smask[:], smask_d[:, :], writes=["smask"])
            for h in heads:
                with ExitStack() as es_h:
                    qf = SB(es_h, "gqf", [128, T_ALL], BF16)
                    kf = SB(es_h, "gkf", [128, T_ALL], BF16)
                    qb = SB(es_h, "gqb", [128, T_ALL], BF16)
                    kb = SB(es_h, "gkb", [128, T_ALL], BF16)
                    ebL = SB(es_h, "ebL", [128, 2, NCH], F32)
                    vtok = SB(es_h, "gv", [128, NT, 128], BF16)
                    SfH = SB(es_h, "SfH", [128, NCH + 1, 128], BF16)
                    yTc = SB(es_h, "gyTc", [128, T_ALL], BF16)
                    with ExitStack() as es_p:
                        pp = PS(es_p, "gpp", [128, 512])
                        wq, wqk = load_w(es_p, win_c_d, h * 128, 128, "gq")
                        wff, wffk = load_w(es_p, win_c_d, 2048 + h * 128, 128, "gff")
                        wfb, wfbk = load_w(es_p, win_c_d, 4096 + h * 128, 128, "gfb")
                        qs = SB(es_p, "gqs", [128, 512], F32)
                        A = SB(es_p, "gA", [128, 512], F32)
                        B = SB(es_p, "gB", [128, 512], F32)
                        C = SB(es_p, "gC", [128, 512], F32)
                        TL = SB(es_p, "gTL", [128, 8], F32)
                        for (t0, n) in TOKG:
                            nch = n // 64
                            j0 = t0 // 64
                            tk = [t for t in range(t0 // 128, (t0 + n) // 128)]
                            proj_fm(wq, wqk, 0, pp, "gpp", t0, n)
                            P.op("act", lambda e, n=n: e.activation(out=qs[:, 0:n], in_=pp[:, 0:n], func=AF.Silu),
                                 reads=["gpp"], writes=["gqs"])
                            for d, (w_, wk_, qd, qn, kd, kn) in enumerate(((wff, wffk, qf, "gqf", kf, "gkf"),
                                                                            (wfb, wfbk, qb, "gqb", kb, "gkb"))):
                                proj_fm(w_, wk_, 0, pp, "gpp", t0, n)
                                P.op("act", lambda e, n=n: e.activation(out=A[:, 0:n], in_=pp[:, 0:n], func=AF.Sigmoid),
                                     reads=["gpp"], writes=["gA"])
                                P.op("dve", lambda e, n=n, h=h: e.tensor_scalar(
                                    out=A[:, 0:n], in0=A[:, 0:n], scalar1=oml[:, h:h + 1], scalar2=lb[:, h:h + 1],
                                    op0=ALU.mult, op1=ALU.add), reads=["gA", "oml", "lb"], writes=["gA"])
                                P.op("act", lambda e, n=n: e.activation(out=B[:, 0:n], in_=A[:, 0:n], func=AF.Ln),
                                     reads=["gA"], writes=["gB"])
                                P.op("pool", lambda e, n=n: e.tensor_scalar(out=A[:, 0:n], in0=A[:, 0:n], scalar1=-1.0,
                                                                            scalar2=1.0, op0=ALU.mult, op1=ALU.add),
                                     reads=["gA", "gB"], writes=["gA"])
                                P.op("dve", lambda e, n=n: e.tensor_tensor_scan(
                                    out=C[:, 0:n], data0=smask[:, 0:n], data1=B[:, 0:n], initial=0.0,
                                    op0=ALU.mult, op1=ALU.add), reads=["smask", "gB"], writes=["gC"])
                                C3 = C[:, 0:n].rearrange("p (c l) -> p c l", l=64)
                                if d == 1:
                                    P.op("pool", lambda e, n=n: e.tensor_tensor(out=B[:, 0:n], in0=B[:, 0:n],
                                                                                in1=C[:, 0:n], op=ALU.subtract),
                                         reads=["gB", "gC"], writes=["gB"])
                                    P.op("act", lambda e, nch=nch, C3=C3: e.copy(out=TL[:, 0:nch], in_=C3[:, :, 63]),
                                         reads=["gC"], writes=["gTL"])
                                    P.op("dve", lambda e, n=n, nch=nch, C3=C3: e.tensor_tensor(
                                        out=C3, in0=B[:, 0:n].rearrange("p (c l) -> p c l", l=64),
                                        in1=TL[:, 0:nch].unsqueeze(2).broadcast_to([128, nch, 64]), op=ALU.add),
                                        reads=["gB", "gTL"], writes=["gC"])
                                    P.op("act", lambda e, nch=nch, j0=j0: e.activation(
                                        out=ebL[:, 1, j0:j0 + nch], in_=TL[:, 0:nch], func=AF.Exp),
                                        reads=["gTL"], writes=[("ebL", 1, t0)])
                                else:
                                    P.op("act", lambda e, nch=nch, j0=j0, C3=C3: e.activation(
                                        out=ebL[:, 0, j0:j0 + nch], in_=C3[:, :, 63], func=AF.Exp),
                                        reads=["gC"], writes=[("ebL", 0, t0)])
                                P.op("act", lambda e, n=n: e.activation(out=B[:, 0:n], in_=C[:, 0:n], func=AF.Exp),
                                     reads=["gC", "gB"], writes=["gB"])
                                P.op("dve", lambda e, n=n, t0=t0, qd=qd: e.tensor_tensor(
                                    out=qd[:, t0:t0 + n], in0=qs[:, 0:n], in1=B[:, 0:n], op=ALU.mult),
                                    reads=["gqs", "gB"], writes=[(qn, t) for t in tk])
                                P.op("act", lambda e, n=n: e.activation(out=B[:, 0:n], in_=C[:, 0:n], func=AF.Exp,
                                                                        scale=-1.0),
                                     reads=["gC", "gB"], writes=["gB"])
                                P.op("pool", lambda e, n=n, t0=t0, kd=kd: e.tensor_tensor(
                                    out=kd[:, t0:t0 + n], in0=A[:, 0:n], in1=B[:, 0:n], op=ALU.mult),
                                    reads=["gA", "gB"], writes=[(kn, t) for t in tk])
                        P.barrier()
                    with ExitStack() as es_p:
                        pv = PS(es_p, "gpv", [128, 128])
                        wv, wvk = load_w(es_p, win_c_d, 6144 + h * 128, 128, "gv")
                        for t in range(NT):
                            proj_tm(wv, wvk, 0, 128, pv, "gpv", t)
                            P.op("act", lambda e, t=t: e.copy(out=vtok[:, t, :], in_=pv[:, 0:128]),
                                 reads=["gpv"], writes=[("gv", t)])
                        P.barrier()
                    ebkeys = [("ebL", d, t0) for d in range(2) for (t0, n) in TOKG]
                    with ExitStack() as es_c:
                        wz, wzk = load_w(es_c, win_c_d, 8192 + h * 128, 128, "gz")
                        Sf = SB(es_c, "gSf", [128, 128], F32)
                        St = SB(es_c, "gSt", [128, 128], F32)
                        Sbb = SB(es_c, "gSbb", [128, 128], BF16)
                        ktk = SB(es_c, "gktk", [128, 128], BF16)
                        AfT = SB(es_c, "gAfT", [128, 128], BF16)
                        AbT = SB(es_c, "gAbT", [128, 128], BF16)
                        szl = SB(es_c, "gszl", [128, 128], F32)
                        yv = SB(es_c, "gyv", [128, 128], F32)
                        junk = SB(es_c, "gjunk", [128, 128], F32)
                        ybf = SB(es_c, "gybf", [128, 128], BF16)
                        sm = SB(es_c, "gsm", [128, 4], F32)
                        tpk = PS(es_c, "gtpk", [128, 128], BF16)
                        Sacc = PS(es_c, "gSacc", [128, 128])
                        Af = PS(es_c, "gAf", [128, 128])
                        Ab = PS(es_c, "gAb", [128, 128])
                        o = PS(es_c, "go", [128, 128])
                        pz = PS(es_c, "gpz", [128, 128])
                        tpy = PS(es_c, "gtpy", [128, 128], BF16)

                        def state_update(d, t, hf, hist):
                            j = 2 * t + hf
                            rs_ = slice(64 * hf, 64 * hf + 64)
                            P.op("pe", lambda e: e.matmul(Sacc[:], lhsT=ktk[rs_, :], rhs=vtok[rs_, t, :],
                                                          start=True, stop=True),
                                 reads=["gktk", ("gv", t)], writes=["gSacc"])
                            P.op("dve", lambda e: e.tensor_tensor(out=St[:], in0=Sacc[:], in1=Sf[:], op=ALU.add),
                                 reads=["gSacc", "gSf"], writes=["gSt"])
                            P.op("dve", lambda e: e.tensor_scalar(out=Sf[:], in0=St[:], scalar1=ebL[:, d, j:j + 1],
                                                                  scalar2=None, op0=ALU.mult),
                                 reads=["gSt"] + ebkeys, writes=["gSf"])
                            dst, dkey = (SfH[:, j + 1, :], ("SfH", j + 1)) if hist else (Sbb[:], "gSbb")
                            P.op("pool", lambda e: e.tensor_scalar(out=dst, in0=St[:], scalar1=ebL[:, d, j:j + 1],
                                                                   scalar2=None, op0=ALU.mult),
                                 reads=["gSt"] + ebkeys, writes=[dkey])

                        def ktrans(src, sn, t):
                            ts_ = slice(t * 128, (t + 1) * 128)
                            P.op("pe", lambda e: e.transpose(out=tpk[:], in_=src[:, ts_], identity=ident_b[:]),
                                 reads=[(sn, t), "ident_b"], writes=["gtpk"])
                            P.op("act", lambda e: e.copy(out=ktk[:], in_=tpk[:]), reads=["gtpk"], writes=["gktk"])

                        P.op("dve", lambda e: e.memset(Sf[:], 0.0), writes=["gSf"])
                        P.op("pool", lambda e: e.memset(SfH[:, 0, :], 0.0), writes=[("SfH", 0)])
                        for t in range(NT):
                            ktrans(kf, "gkf", t)
                            for hf in range(2):
                                state_update(0, t, hf, True)
                        P.op("dve", lambda e: e.memset(Sf[:], 0.0), writes=["gSf"])
                        P.op("pool", lambda e: e.memset(Sbb[:], 0.0), writes=["gSbb"])
                        for t in [1, 0] + list(range(NT - 1, 1, -1)):
                            ts_ = slice(t * 128, (t + 1) * 128)
                            ktrans(kb, "gkb", t)
                            if t >= 2:
                                P.op("pe", lambda e: e.matmul(Af[:], lhsT=kf[:, ts_], rhs=qf[:, ts_], start=True, stop=True),
                                     reads=[("gkf", t), ("gqf", t)], writes=["gAf"])
                                P.op("pe", lambda e: e.matmul(Ab[:], lhsT=kb[:, ts_], rhs=qb[:, ts_], start=True, stop=True),
                                     reads=[("gkb", t), ("gqb", t)], writes=["gAb"])
                                P.op("dve", lambda e: e.tensor_tensor(out=AfT[:], in0=Af[:], in1=consts[:, 3, :],
                                                                      op=ALU.mult),
                                     reads=["gAf", "consts"], writes=["gAfT"])
                                P.op("dve", lambda e: e.tensor_tensor(out=AbT[:], in0=Ab[:], in1=consts[:, 4, :],
                                                                      op=ALU.mult),
                                     reads=["gAb", "consts"], writes=["gAbT"])
                            for hf in (1, 0):
                                j = 2 * t + hf
                                cs_ = slice(j * 64, (j + 1) * 64)
                                rs_ = slice(64 * hf, 64 * hf + 64)
                                if t >= 2:
                                    P.op("pe", lambda e, rs_=rs_: e.matmul(o[0:64, :], lhsT=AfT[rs_, rs_], rhs=vtok[rs_, t, :],
                                                                           start=True, stop=False),
                                         reads=["gAfT", ("gv", t)], writes=["go"])
                                    P.op("pe", lambda e, rs_=rs_: e.matmul(o[0:64, :], lhsT=AbT[rs_, rs_], rhs=vtok[rs_, t, :],
                                                                           start=False, stop=False),
                                         reads=["gAbT", ("gv", t)], writes=["go"], pe_chain=True)
                                    P.op("pe", lambda e, j=j, cs_=cs_: e.matmul(o[0:64, :], lhsT=qf[:, cs_], rhs=SfH[:, j, :],
                                                                                start=False, stop=False),
                                         reads=[("gqf", t), ("SfH", j)], writes=["go"], pe_chain=True)
                                    P.op("pe", lambda e, cs_=cs_: e.matmul(o[0:64, :], lhsT=qb[:, cs_], rhs=Sbb[:],
                                                                           start=False, stop=True),
                                         reads=[("gqb", t), "gSbb"], writes=["go"], pe_chain=True)
                                state_update(1, t, hf, False)
                                if t < 2:
                                    continue
                                if dbg_o is not None:
                                    P.op("act", lambda e: e.copy(out=junk[0:64, :], in_=o[0:64, :]), reads=["go"],
                                         writes=["gjunk"])
                                    P.dma("sp", dbg_o[j * 64:(j + 1) * 64, h * 128:(h + 1) * 128], junk[0:64, :],
                                          reads=["gjunk"], writes=[("dbgo1", j % 4)])
                                P.op("act", lambda e: e.activation(out=junk[0:64, :], in_=o[0:64, :], func=AF.Square,
                                                                   accum_out=sm[0:64, 0:1]),
                                     reads=["go"], writes=["gjunk", "gsm"])
                                P.op("act", lambda e: e.activation(out=sm[0:64, 1:2], in_=sm[0:64, 0:1], func=AF.Sqrt,
                                                                   scale=1.0 / 128, bias=EPS),
                                     reads=["gsm"], writes=["gsm"])
                                P.op("dve", lambda e: e.reciprocal(out=sm[0:64, 2:3], in_=sm[0:64, 1:2]),
                                     reads=["gsm"], writes=["gsm"])
                                for k in range(8):
                                    P.op("pe", lambda e, k=k, cs_=cs_: e.matmul(pz[0:64, :], lhsT=hT[:, k, cs_], rhs=wz[:, k, :],
                                                                                start=(k == 0), stop=(k == 7)),
                                         reads=[wzk[k], ("hT", t)], writes=["gpz"], pe_chain=True)
                                P.op("act", lambda e: e.activation(out=szl[0:64, :], in_=pz[0:64, :], func=AF.Silu),
                                     reads=["gpz"], writes=["gszl"])
                                P.op("dve", lambda e, h=h: e.scalar_tensor_tensor(
                                    out=yv[0:64, :], in0=o[0:64, :], scalar=sm[0:64, 2:3],
                                    in1=hnc[0:64, h * 128:(h + 1) * 128], op0=ALU.mult, op1=ALU.mult),
                                    reads=["go", "gsm", "hnc"], writes=["gyv"])
                                P.op("pool", lambda e: e.tensor_tensor(out=ybf[0:64, :], in0=yv[0:64, :], in1=szl[0:64, :],
                                                                       op=ALU.mult),
                                     reads=["gyv", "gszl"], writes=["gybf"])
                                P.op("pe", lambda e: e.transpose(out=tpy[:, 0:64], in_=ybf[0:64, :], identity=ident_b[0:64, 0:64]),
                                     reads=["gybf", "ident_b"], writes=["gtpy"])
                                P.op("act", lambda e, cs_=cs_: e.copy(out=yTc[:, cs_], in_=tpy[:, 0:64]), reads=["gtpy"],
                                     writes=[("gyTc", j)])
                        P.dma("sp", yT_d[h, :, 256:T_ALL], yTc[:, 256:T_ALL], reads=[("gyTc", j) for j in range(4, NCH)],
                              writes=[("yT_d", h)])
                        if dbg_o is not None:
                            P.finish("sp", [("dbgo1", i) for i in range(4)])
                        P.barrier()
            P.barrier()

    if dbg and "oa" in dbg:
        na_stage(dbg.get("_chunks", [0]), dbg_d["oa"])
    elif dbg and "hb" in dbg:
        ml_stage([0], dbg_d["hb"])
    elif dbg and "x1" in dbg:
        na_stage(list(range(8)))
        ml_stage(list(range(4)))
        out_stage(wout_ab_d, x_d, ctx_d, gate0, x1_d, ctx1_d, "x1")
        P.finish("sp", [("x1", t) for t in range(NT)])
    elif dbg and "o1" in dbg:
        pass
    elif not dbg:
        na_stage(list(range(8)))
        ml_stage(list(range(4)))
        out_stage(wout_ab_d, x_d, ctx_d, gate0, x1_d, ctx1_d, "x1")

    if dbg and "hT" in dbg:
        with ExitStack() as es:
            stg = SB(es, "dbgstg", [128, T_ALL], F32)
            for c in range(8):
                P.op("dve", lambda e, c=c: e.tensor_copy(out=stg[:], in_=hT[:, c, :]),
                     reads=[("hT", t) for t in range(NT)], writes=["dbgstg"])
                P.dma("sp", dbg_d["hT"][c * 128:(c + 1) * 128, :], stg[:], reads=["dbgstg"], writes=[("dbgo", c)])
            P.finish("sp", [("dbgo", c) for c in range(8)])
    L0.close()
    P.barrier()
    L1 = ExitStack()
    gate1 = {0: SB(L1, "gate1_0", [128, D], F32)}
    if dbg and "o1" in dbg:
        x1_in = dt_in("x1_in", [T_LAT, D])
        ctx1_in = dt_in("ctx1_in", [T_CTX, D])
        norm_stage(1, x1_in, ctx1_in, gate1)
        hg_stage(dbg.get("_heads", [0]), dbg_d["o1"])
    elif not dbg:
        norm_stage(1, x1_d, ctx1_d, gate1)
        hg_stage(list(range(16)))
        out_stage(wout_c_d, x1_d, None, gate1, out_d, None, "out")
        P.finish("sp", [("out", t) for t in range(2, NT)])
    L1.close()
    G.close()
    return nc


def host_inputs(inputs, b):
    cc = np.zeros((128, 16), np.float32)
    cb = np.asarray(inputs["c"][b], np.float32).reshape(8, 128)
    cx = np.asarray(inputs["c_ctx"], np.float32).reshape(8, 128)
    for k in range(8):
        cc[:, 2 * k] = cb[k]
        cc[:, 2 * k + 1] = cx[k]
    m = {
        "x": np.ascontiguousarray(inputs["x"][b], dtype=np.float32),
        "ctx": np.ascontiguousarray(inputs["ctx"][b], dtype=np.float32),
        "cc": cc,
        "w_ada": np.ascontiguousarray(inputs["w_ada"], dtype=np.float32),
        "b_ada": np.ascontiguousarray(inputs["b_ada"], dtype=np.float32),
        "norm_w": np.ascontiguousarray(inputs["norm_w"], dtype=np.float32),
        "ident": np.eye(128, dtype=np.float32),
        "consts": host_consts(),
        "w_in_ab": np.ascontiguousarray(inputs["w_in_ab"][0], dtype=np.float32),
        "qkw": np.stack([np.tile(np.asarray(inputs["q_norm_a"][0], np.float32), 2),
                         np.tile(np.asarray(inputs["k_norm_a"][0], np.float32), 2)], axis=1),
        "nab": host_nab(np.asarray(inputs["rpb_a"][0], np.float32)),
        "rope": host_rope(),
        "b_gate": np.asarray(inputs["b_gate_ab"], np.float32).reshape(1, 16),
        "h_norm_b": np.asarray(inputs["h_norm_b"], np.float32).reshape(1, D),
        "w_out_ab": np.ascontiguousarray(inputs["w_out_ab"][0], dtype=np.float32),
        "w_in_c": np.ascontiguousarray(inputs["w_in_c"][0], dtype=np.float32),
        "w_out_c": np.ascontiguousarray(inputs["w_out_c"][0], dtype=np.float32),
        "lbc": np.ascontiguousarray(np.asarray(inputs["lb_c"], np.float32).reshape(2, 16, 128).transpose(2, 1, 0)),
        "h_norm_c": np.asarray(inputs["h_norm_c"], np.float32).reshape(1, 2 * D),
        "smask": np.tile((np.arange(512) % 64 != 0).astype(np.float32)[None, :], (128, 1)),
    }
    return m


def host_rope():
    p = np.arange(128)
    freq = (10000.0 ** (-(p % 64).astype(np.float64) / 64.0))[:, None]
    pos = np.arange(64, dtype=np.float64)[None, :]
    ang = (pos.astype(np.float32) * freq.astype(np.float32)).astype(np.float32)
    cos = np.cos(ang).astype(np.float32)
    sin = np.sin(ang).astype(np.float32) * np.where(p < 64, -1.0, 1.0)[:, None].astype(np.float32)
    return np.stack([cos, sin, cos / 16, sin / 16], axis=1).astype(np.float32)


def host_consts():
    c = np.zeros((128, 6, 128), np.float32)
    p = np.arange(128)
    same = (p[:, None] // 64 == p[None, :] // 64)
    c[:, 0, :] = same
    c[:, 1, :] = (p[:, None] <= p[None, :])
    c[:, 2, :] = (p[:, None] >= p[None, :])
    c[:, 3, :] = same & (p[:, None] <= p[None, :])
    c[:, 4, :] = same & (p[:, None] >= p[None, :])
    return c


_NAB_CACHE = {}


def host_nab(rpb):
    NEG = np.float32(-30000.0)
    types = [(10, 10 + dj) for dj in range(-2, 3)]
    types += [(0, j) for j in range(4)] + [(1, j) for j in range(4)]
    types += [(30, 28 + j) for j in range(4)] + [(31, 28 + j) for j in range(4)]
    out = np.empty((16, len(types), 128, 128), np.float32)
    p = np.arange(128)
    for ti, (i, j) in enumerate(types):
        kr = (2 * j + p // 64)[:, None]
        kc = (p % 64)[:, None]
        qr = (2 * i + p // 64)[None, :]
        qc = (p % 64)[None, :]
        r0 = np.clip(qr - 4, 0, 56)
        c0 = np.clip(qc - 8, 0, 48)
        valid = (kr >= r0) & (kr < r0 + 8) & (kc >= c0) & (kc < c0 + 16)
        ri = np.clip(kr - qr + 7, 0, 14)
        ci = np.clip(kc - qc + 15, 0, 30)
        g = rpb[:, ri, ci]
        out[:, ti] = np.where(valid[None], g, NEG)
    return out


def kernel(**inputs):
    nc = build()
    in_maps = [host_inputs(inputs, b) for b in range(8)]
    res = run_bass_kernel_spmd(nc, in_maps, core_ids=list(range(8)))
    return np.stack([np.asarray(r["out"], np.float32) for r in res.results], axis=0)
```

```python
import numpy as np
import ml_dtypes
import concourse.bass as bass
import concourse.mybir as mybir
from concourse.bass_utils import run_bass_kernel_spmd

F32 = mybir.dt.float32
BF16 = mybir.dt.bfloat16
AF = mybir.ActivationFunctionType
ALU = mybir.AluOpType
AX = mybir.AxisListType

D = 1024
T_LAT = 4096
T_CTX = 256
T_ALL = T_LAT + T_CTX
NT = T_ALL // 128
EPS = 1e-6


class KeyList(list):
    pass


def _flat(keys):
    out = []
    for k in keys:
        if isinstance(k, KeyList):
            out.extend(k)
        else:
            out.append(k)
    return out


class Prog:
    def __init__(self, nc):
        self.nc = nc
        self.eng = {"pe": nc.tensor, "act": nc.scalar, "dve": nc.vector, "pool": nc.gpsimd, "sp": nc.sync}
        self.csem = {}
        self.ccnt = {}
        for e in ("pe", "act", "dve", "pool"):
            self.csem[e] = nc.alloc_semaphore("c_" + e)
            self.ccnt[e] = 0
        self.seen = {e: {} for e in self.eng}
        self.lastw = {}
        self.readers = {}
        self.dsem = {}
        self.dpool = []
        for i in range(40):
            self.dpool.append([nc.alloc_semaphore("d%d" % i), 0])
        self.nbuf = 0
        self.ninst = 0

    def sb(self, name, shape, dt):
        return self.nc.alloc_sbuf_tensor("s_" + name, list(shape), dt)

    def ps(self, name, shape, dt=F32):
        return self.nc.alloc_psum_tensor("p_" + name, list(shape), dt)

    def _deps(self, reads, writes, wadd=()):
        toks = []
        for k in reads:
            toks.extend(self.lastw.get(k, ()))
        for k in writes:
            toks.extend(self.lastw.get(k, ()))
            toks.extend(self.readers.get(k, {}).values())
        for k in wadd:
            toks.extend(self.readers.get(k, {}).values())
        return toks

    def _wait(self, e, toks, skip_sem=None):
        need = {}
        for (sem, val) in toks:
            if skip_sem is not None and sem.name == skip_sem:
                continue
            if self.seen[e].get(sem.name, 0) >= val:
                continue
            if need.get(sem.name, (None, 0))[1] < val:
                need[sem.name] = (sem, val)
        for name, (sem, val) in need.items():
            self.eng[e].wait_ge(sem, val)
            self.seen[e][name] = val

    def _commit(self, tok, reads, writes, wadd=()):
        for k in writes:
            self.lastw[k] = [tok]
            self.readers[k] = {}
        for k in wadd:
            self.lastw.setdefault(k, []).append(tok)
        for k in reads:
            if k in writes:
                continue
            r = self.readers.setdefault(k, {})
            o = r.get(tok[0].name)
            if o is None or o[1] < tok[1]:
                r[tok[0].name] = tok

    def op(self, e, fn, reads=(), writes=(), pe_chain=False):
        reads = _flat(reads)
        toks = self._deps(reads, writes)
        self._wait(e, toks, skip_sem=(self.csem[e].name if pe_chain else None))
        ins = fn(self.eng[e])
        self.ccnt[e] += 1
        ins.then_inc(self.csem[e], 1)
        tok = (self.csem[e], self.ccnt[e])
        self._commit(tok, reads, writes)
        self.ninst += 1
        return ins

    def dma(self, e, out, in_, reads=(), writes=(), semkey=None, wadd=(), **kw):
        toks = self._deps(reads, writes, wadd)
        if semkey is None:
            semkey = (tuple(writes) + tuple(reads))[0]
        if semkey not in self.dsem:
            self.dsem[semkey] = self.dpool[len(self.dsem) % len(self.dpool)]
        ent = self.dsem[semkey]
        if ent[1] > 0:
            toks.append((ent[0], ent[1]))
        self._wait(e, toks)
        ins = self.eng[e].dma_start(out=out, in_=in_, **kw)
        ent[1] += 16
        ins.then_inc(ent[0], 16)
        tok = (ent[0], ent[1])
        self._commit(tok, reads, writes, wadd)
        self.ninst += 1
        return ins

    def barrier(self):
        toks = [(self.csem[f], self.ccnt[f]) for f in self.csem if self.ccnt[f] > 0]
        toks += [(ent[0], ent[1]) for ent in self.dpool if ent[1] > 0]
        for e in self.eng:
            self._wait(e, toks)

    def finish(self, e, keys):
        toks = []
        for k in keys:
            toks.extend(self.lastw.get(k, ()))
        self._wait(e, toks)


def build(dbg=None):
    from contextlib import ExitStack
    nc = bass.Bass("TRN2", target_bir_lowering=False)
    P = Prog(nc)
    dbg_d = {}
    if dbg:
        for name, shape in dbg.items():
            if name.startswith("_"):
                continue
            dbg_d[name] = nc.dram_tensor("dbg_" + name, list(shape), F32, kind="ExternalOutput").ap()
    dt_in = lambda name, shape, dt=F32: nc.dram_tensor(name, list(shape), dt, kind="ExternalInput").ap()
    x_d = dt_in("x", [T_LAT, D])
    ctx_d = dt_in("ctx", [T_CTX, D])
    cc_d = dt_in("cc", [128, 16])
    wada_d = dt_in("w_ada", [2, D, 3 * D])
    bada_d = dt_in("b_ada", [2, 3 * D])
    normw_d = dt_in("norm_w", [2, D])
    ident_d = dt_in("ident", [128, 128])
    consts_d = dt_in("consts", [128, 6, 128])
    win_ab_d = dt_in("w_in_ab", [D, 9232])
    qkw_d = dt_in("qkw", [128, 2])
    nab_d = dt_in("nab", [16, 21, 128, 128])
    yT_d = nc.dram_tensor("yT_scr", [16, 128, T_ALL], BF16, kind="Internal").ap()
    rope_d = dt_in("rope", [128, 4, 64])
    bgate_d = dt_in("b_gate", [1, 16])
    hnb_d = dt_in("h_norm_b", [1, D])
    Hf_d = nc.dram_tensor("Hf_scr", [NT, 128, 256], F32, kind="Internal").ap()
    wout_ab_d = dt_in("w_out_ab", [2 * D, D])
    win_c_d = dt_in("w_in_c", [D, 10240])
    wout_c_d = dt_in("w_out_c", [2 * D, D])
    lbc_d = dt_in("lbc", [128, 16, 2])
    hnc_d = dt_in("h_norm_c", [1, 2 * D])
    smask_d = dt_in("smask", [128, 512])
    if dbg and "x1" in dbg:
        x1_d, ctx1_d = dbg_d["x1"], dbg_d["ctx1"]
    else:
        x1_d = nc.dram_tensor("x1_scr", [T_LAT, D], F32, kind="Internal").ap()
        ctx1_d = nc.dram_tensor("ctx1_scr", [T_CTX, D], F32, kind="Internal").ap()
    out_d = nc.dram_tensor("out", [T_LAT, D], F32, kind="ExternalOutput").ap()

    uid = [0]

    def SB(es, name, shape, dt):
        uid[0] += 1
        return es.enter_context(nc.sbuf_tensor("s%d_%s" % (uid[0], name), list(shape), dt))

    def PS(es, name, shape, dt=F32):
        uid[0] += 1
        return es.enter_context(nc.psum_tensor("p%d_%s" % (uid[0], name), list(shape), dt))

    G = ExitStack()
    ident_f = SB(G, "ident_f", [128, 128], F32)
    ident_b = SB(G, "ident_b", [128, 128], BF16)
    P.dma("sp", ident_f[:], ident_d[:, :], writes=["ident_f"])
    P.op("dve", lambda e: e.tensor_copy(out=ident_b[:], in_=ident_f[:]), reads=["ident_f"], writes=["ident_b"])
    ones_f = SB(G, "ones_f", [128, 128], F32)
    P.op("dve", lambda e: e.memset(ones_f[:], 1.0), writes=["ones_f"])
    cc = SB(G, "cc", [128, 16], F32)
    sc = SB(G, "sc", [128, 16], F32)
    P.dma("sp", cc[:], cc_d[:, :], writes=["cc"])
    P.op("act", lambda e: e.activation(out=sc[:], in_=cc[:], func=AF.Silu), reads=["cc"], writes=["sc"])
    hT = SB(G, "hT", [128, 8, T_ALL], BF16)
    wstage = SB(G, "wstage", [128, 8, 256], F32)
    consts = SB(G, "consts", [128, 6, 128], F32)
    P.dma("sp", consts[:], consts_d[:, :, :], writes=["consts"])
    bones = consts[:, 0, :]

    def norm_stage(l, xsrc, csrc, gate_tiles):
        with ExitStack() as es:
            screp = SB(es, "screp", [128, 16, 128], F32)
            for kj in range(16):
                P.op("dve", lambda e, kj=kj: e.tensor_scalar(out=screp[:, kj, :], in0=ones_f[:],
                                                             scalar1=sc[:, kj:kj + 1], scalar2=None, op0=ALU.mult),
                     reads=["sc", "ones_f"], writes=[("screp", kj)])
            wada = SB(es, "wada", [128, 8, 512], F32)
            brow = SB(es, "brow", [128, 3 * D], F32)
            nwb = SB(es, "nwb", [128, D], F32)
            shf = [SB(es, "shf%d" % j, [128, D], F32) for j in range(2)]
            mA = [SB(es, "mA%d" % j, [128, D], F32) for j in range(2)]
            mps = PS(es, "mps", [128, 512])
            P.dma("sp", brow[:], bada_d[l:l + 1, :].partition_broadcast(128), writes=["brow"])
            P.dma("act", nwb[:], normw_d[l:l + 1, :].partition_broadcast(128), writes=["nwb"])
            for blk in range(6):
                for k in range(8):
                    P.dma("sp" if k % 2 == 0 else "act", wada[:, k, :],
                          wada_d[l, k * 128:(k + 1) * 128, blk * 512:(blk + 1) * 512], writes=[("wada", k)])
                sec, half = blk // 2, blk % 2
                for j in range(2):
                    if sec == 2 and j not in gate_tiles:
                        continue
                    for k in range(8):
                        P.op("pe", lambda e, k=k, j=j: e.matmul(
                            mps[:], lhsT=screp[:, 2 * k + j, :], rhs=wada[:, k, :], start=(k == 0), stop=(k == 7)),
                            reads=[("screp", 2 * k + j), ("wada", k)], writes=["mps"], pe_chain=True)
                    dst = (shf[j], mA[j], gate_tiles.get(j))[sec]
                    dkey = (("shf", j), ("mA", j), ("gateh", j))[sec]
                    c0 = blk * 512
                    P.op("dve", lambda e, dst=dst, half=half, c0=c0: e.tensor_tensor(
                        out=dst[:, half * 512:(half + 1) * 512], in0=mps[:], in1=brow[:, c0:c0 + 512], op=ALU.add),
                        reads=["mps", "brow"], writes=[dkey + (half,)])
            for j in range(2):
                P.op("dve", lambda e, j=j: e.scalar_tensor_tensor(
                    out=mA[j][:], in0=mA[j][:], scalar=1.0, in1=nwb[:], op0=ALU.add, op1=ALU.mult),
                    reads=[("mA", j, 0), ("mA", j, 1), "nwb"], writes=[("mA", j, 0), ("mA", j, 1)])
            for j in gate_tiles:
                P.op("dve", lambda e, j=j: e.tensor_copy(out=gate_tiles[j][:, 0:1], in_=gate_tiles[j][:, 0:1]),
                     reads=[("gateh", j, 0), ("gateh", j, 1)], writes=[("gate", j)])
            xin = [SB(es, "xin%d" % i, [128, D], F32) for i in range(2)]
            junk = SB(es, "junk", [128, D], F32)
            ssq = SB(es, "ssq", [128, 2], F32)
            rstd = SB(es, "rstd", [128, 2], F32)
            hm = [SB(es, "hm%d" % i, [128, D], F32) for i in range(2)]
            hb = [SB(es, "hbf%d" % i, [128, D], BF16) for i in range(2)]
            tps = [PS(es, "tps%d" % i, [128, 8, 128], BF16) for i in range(2)]
            for t in range(NT):
                i = t % 2
                j = 1 if t < 2 else 0
                src = csrc[t * 128:(t + 1) * 128, :] if t < 2 else xsrc[(t - 2) * 128:(t - 1) * 128, :]
                P.dma("sp" if i == 0 else "act", xin[i][:], src, reads=[("x1", t)] if l == 1 else [],
                      writes=[("xin", i)])
                P.op("act", lambda e, i=i: e.activation(out=junk[:], in_=xin[i][:], func=AF.Square,
                                                        accum_out=ssq[:, i:i + 1]),
                     reads=[("xin", i)], writes=["junk", ("ssq", i)])
                P.op("act", lambda e, i=i: e.activation(out=ssq[:, i:i + 1], in_=ssq[:, i:i + 1], func=AF.Sqrt,
                                                        scale=1.0 / D, bias=EPS),
                     reads=[("ssq", i)], writes=[("ssq", i)])
                P.op("dve", lambda e, i=i: e.reciprocal(out=rstd[:, i:i + 1], in_=ssq[:, i:i + 1]),
                     reads=[("ssq", i)], writes=[("rstd", i)])
                P.op("dve", lambda e, i=i, j=j: e.scalar_tensor_tensor(
                    out=hm[i][:], in0=xin[i][:], scalar=rstd[:, i:i + 1], in1=mA[j][:],
                    op0=ALU.mult, op1=ALU.mult),
                    reads=[("xin", i), ("rstd", i), ("mA", j, 0), ("mA", j, 1)], writes=[("hm", i)])
                P.op("pool", lambda e, i=i, j=j: e.tensor_tensor(out=hb[i][:], in0=hm[i][:], in1=shf[j][:],
                                                                 op=ALU.add),
                     reads=[("hm", i), ("shf", j, 0), ("shf", j, 1)], writes=[("hb", i)])
                for c in range(8):
                    P.op("pe", lambda e, i=i, c=c: e.transpose(out=tps[i][:, c, :],
                                                               in_=hb[i][:, c * 128:(c + 1) * 128],
                                                               identity=ident_b[:]),
                         reads=[("hb", i), "ident_b"], writes=[("tps", i)], pe_chain=True)
                P.op("act", lambda e, i=i, t=t: e.copy(out=hT[:, :, t * 128:(t + 1) * 128], in_=tps[i][:]),
                     reads=[("tps", i)], writes=[("hT", t)])
            P.barrier()

    L0 = ExitStack()
    gate0 = {j: SB(L0, "gate0_%d" % j, [128, D], F32) for j in range(2)}
    norm_stage(0, x_d, ctx_d, gate0)


    def load_w(es_w, wd, col0, ncols, tag, segs=None):
        if segs is None:
            segs = [(col0, ncols)]
        pieces = []
        for (c0, n) in segs:
            while n > 0:
                m_ = min(n, 256)
                pieces.append((c0, m_))
                c0 += m_
                n -= m_
        ncols = sum(n for _, n in pieces)
        wb = SB(es_w, "wb_" + tag, [128, 8, ncols], BF16)
        groups, cur, curn = [], [], 0
        for pc in pieces:
            if curn + pc[1] > 256:
                groups.append(cur)
                cur, curn = [], 0
            cur.append(pc)
            curn += pc[1]
        groups.append(cur)
        o0 = 0
        for gi, grp in enumerate(groups):
            gn = sum(n for _, n in grp)
            for k in range(8):
                o = 0
                for si, (c0, n) in enumerate(grp):
                    P.dma("sp" if k % 2 == 0 else "act", wstage[:, k, o:o + n], wd[k * 128:(k + 1) * 128, c0:c0 + n],
                          writes=([("wstage", k)] if si == 0 else []), wadd=([] if si == 0 else [("wstage", k)]),
                          semkey=("wstage", k, si))
                    o += n
            for k in range(8):
                P.op("pool" if k % 2 == 0 else "dve",
                     lambda e, k=k, o0=o0, gn=gn: e.tensor_copy(out=wb[:, k, o0:o0 + gn], in_=wstage[:, k, 0:gn]),
                     reads=[("wstage", k)], writes=([("wb_" + tag, k)] if gi == 0 else []),
                     ) if gi == 0 else P.op("pool" if k % 2 == 0 else "dve",
                     lambda e, k=k, o0=o0, gn=gn: e.tensor_copy(out=wb[:, k, o0:o0 + gn], in_=wstage[:, k, 0:gn]),
                     reads=[("wstage", k)], writes=[("wb_" + tag, k, gi)])
            o0 += gn
        keys = [("wb_" + tag, k) for k in range(8)]
        extra = [("wb_" + tag, k, gi) for k in range(8) for gi in range(1, len(groups))]
        return wb, [KeyList([("wb_" + tag, k)] + [("wb_" + tag, k, gi) for gi in range(1, len(groups))]) for k in range(8)]

    TOKG = [(0, 256)] + [(256 + 512 * g, 512) for g in range(8)]

    def proj_fm(wb, wkeys, c0, pst, pkey, t0, n):
        for k in range(8):
            P.op("pe", lambda e, k=k: e.matmul(pst[:, 0:n], lhsT=wb[:, k, c0:c0 + 128], rhs=hT[:, k, t0:t0 + n],
                                               start=(k == 0), stop=(k == 7)),
                 reads=[wkeys[k]] + [("hT", t) for t in range(t0 // 128, (t0 + n) // 128)], writes=[pkey],
                 pe_chain=True)

    def proj_tm(wb, wkeys, c0, ncols, pst, pkey, t):
        for k in range(8):
            P.op("pe", lambda e, k=k: e.matmul(pst[:, 0:ncols], lhsT=hT[:, k, t * 128:(t + 1) * 128],
                                               rhs=wb[:, k, c0:c0 + ncols], start=(k == 0), stop=(k == 7)),
                 reads=[wkeys[k], ("hT", t)], writes=[pkey], pe_chain=True)

    def na_keytiles(t):
        if t < 2:
            return [(0, None), (1, None)]
        i = t - 2
        if i == 0:
            nb = [(j, 5 + j) for j in range(4)]
        elif i == 1:
            nb = [(j, 9 + j) for j in range(4)]
        elif i == 30:
            nb = [(28 + j, 13 + j) for j in range(4)]
        elif i == 31:
            nb = [(28 + j, 17 + j) for j in range(4)]
        else:
            nb = [(i + dj, dj + 2) for dj in range(-2, 3)]
        return [(2 + j, ty) for (j, ty) in nb] + [(0, None), (1, None)]

    def na_stage(chunks, dbg_oa=None):
        with ExitStack() as es:
            qkw = SB(es, "qkw", [128, 2], F32)
            P.dma("sp", qkw[:], qkw_d[:, :], writes=["qkw"])
            P.op("act", lambda e: e.mul(out=qkw[:, 0:1], in_=qkw[:, 0:1], mul=0.125), reads=["qkw"], writes=["qkw"])
            qT = SB(es, "qT", [128, T_ALL], BF16)
            kT = SB(es, "kT", [128, T_ALL], BF16)
            vaug = SB(es, "vaug", [128, NT, 2, 65], BF16)
            yTc = SB(es, "yTc", [128, T_ALL], BF16)
            biasf = SB(es, "biasf", [128, 2, 21, 128], F32)
            biasb = SB(es, "biasb", [128, 2, 21, 128], BF16)
            sq = SB(es, "sq", [128, 512], F32)
            rs = SB(es, "rs", [128, 512], F32)
            PT = SB(es, "PT", [128, 8, 128], BF16)
            rden = SB(es, "rden", [128, 2], F32)
            oa = SB(es, "oa", [128, 128], F32)
            sz = SB(es, "sz", [128, 128], F32)
            yab = SB(es, "yab", [128, 128], BF16)
            pp = PS(es, "pp", [128, 512])
            ssp = PS(es, "ssp", [128, 512])
            st = PS(es, "st", [128, 8, 128])
            num = PS(es, "num", [128, 2, 128])
            tp = PS(es, "tp", [128, 128], BF16)
            P.op("dve", lambda e: e.memset(vaug[:], 1.0), writes=[("vaug", t) for t in range(NT)])
            for c in chunks:
                with ExitStack() as es_w:
                    wq, wqk = load_w(es_w, win_ab_d, c * 128, 128, "q")
                    wk, wkk = load_w(es_w, win_ab_d, 1024 + c * 128, 128, "k")
                    wv, wvk = load_w(es_w, win_ab_d, 2048 + c * 128, 128, "v")
                    wz, wzk = load_w(es_w, win_ab_d, 3072 + c * 128, 128, "z")
                    for hh in range(2):
                        P.dma("sp" if hh == 0 else "act", biasf[:, hh, :, :],
                              nab_d[2 * c + hh].rearrange("t k q -> k t q"), writes=[("biasf", hh)])
                        P.op("pool", lambda e, hh=hh: e.tensor_copy(out=biasb[:, hh, :, :], in_=biasf[:, hh, :, :]),
                             reads=[("biasf", hh)], writes=[("biasb", hh)])
                    for (dst, dname, wb_, wk_, col) in ((qT, "qT", wq, wqk, 0), (kT, "kT", wk, wkk, 1)):
                        for (t0, n) in TOKG:
                            proj_fm(wb_, wk_, 0, pp, "pp", t0, n)
                            P.op("act", lambda e, n=n: e.activation(out=sq[:, 0:n], in_=pp[:, 0:n], func=AF.Square),
                                 reads=["pp"], writes=["sq"])
                            P.op("pe", lambda e, n=n: e.matmul(ssp[:, 0:n], lhsT=bones, rhs=sq[:, 0:n],
                                                               start=True, stop=True),
                                 reads=["sq", "consts"], writes=["ssp"])
                            P.op("act", lambda e, n=n: e.activation(out=rs[:, 0:n], in_=ssp[:, 0:n], func=AF.Sqrt,
                                                                    scale=1.0 / 64, bias=EPS),
                                 reads=["ssp"], writes=["rs"])
                            P.op("dve", lambda e, n=n: e.reciprocal(out=rs[:, 0:n], in_=rs[:, 0:n]),
                                 reads=["rs"], writes=["rs"])
                            P.op("dve", lambda e, n=n, t0=t0, dst=dst, col=col: e.scalar_tensor_tensor(
                                out=dst[:, t0:t0 + n], in0=pp[:, 0:n], scalar=qkw[:, col:col + 1], in1=rs[:, 0:n],
                                op0=ALU.mult, op1=ALU.mult),
                                reads=["pp", "rs", "qkw"],
                                writes=[(dname, t) for t in range(t0 // 128, (t0 + n) // 128)])
                    for t in range(NT):
                        proj_tm(wv, wvk, 0, 128, pp, "pp", t)
                        P.op("act", lambda e, t=t: e.copy(out=vaug[:, t, :, 0:64],
                                                          in_=pp[:, 0:128].rearrange("p (h d) -> p h d", h=2)),
                             reads=["pp"], writes=[("vaug", t)])
                    for t in range(NT):
                        kts = na_keytiles(t)
                        nk = len(kts)
                        proj_tm(wz, wzk, 0, 128, pp, "pp", t)
                        P.op("act", lambda e: e.activation(out=sz[:], in_=pp[:, 0:128], func=AF.Silu),
                             reads=["pp"], writes=["sz"])
                        for hh in range(2):
                            pb = 64 * hh
                            for n_, (kt, ty) in enumerate(kts):
                                P.op("pe", lambda e, n_=n_, kt=kt, ty=ty, pb=pb: e.matmul(
                                    st[:, n_, :], lhsT=kT[pb:pb + 64, kt * 128:(kt + 1) * 128],
                                    rhs=qT[pb:pb + 64, t * 128:(t + 1) * 128], start=True, stop=(ty is None)),
                                    reads=[("kT", kt), ("qT", t)], writes=["st"], pe_chain=True)
                                if ty is not None:
                                    P.op("pe", lambda e, n_=n_, ty=ty, hh=hh: e.matmul(
                                        st[:, n_, :], lhsT=ident_b[:], rhs=biasb[:, hh, ty, :], start=False, stop=True),
                                        reads=["ident_b", ("biasb", hh)], writes=["st"], pe_chain=True)
                            P.op("act", lambda e, nk=nk: e.activation(out=PT[:, 0:nk, :], in_=st[:, 0:nk, :],
                                                                      func=AF.Exp),
                                 reads=["st"], writes=["PT"])
                            for n_, (kt, ty) in enumerate(kts):
                                P.op("pe", lambda e, n_=n_, kt=kt, hh=hh, nk=nk: e.matmul(
                                    num[:, hh, 0:65], lhsT=PT[:, n_, :], rhs=vaug[:, kt, hh, :],
                                    start=(n_ == 0), stop=(n_ == nk - 1)),
                                    reads=["PT", ("vaug", kt)], writes=["num"], pe_chain=True)
                            P.op("dve", lambda e, hh=hh: e.reciprocal(out=rden[:, hh:hh + 1], in_=num[:, hh, 64:65]),
                                 reads=["num"], writes=[("rden", hh)])
                            P.op("dve", lambda e, hh=hh: e.tensor_scalar(
                                out=oa[:, hh * 64:(hh + 1) * 64], in0=num[:, hh, 0:64], scalar1=rden[:, hh:hh + 1],
                                scalar2=None, op0=ALU.mult),
                                reads=["num", ("rden", hh)], writes=[("oa", hh)])
                        if dbg_oa is not None:
                            P.dma("sp", dbg_oa[t * 128:(t + 1) * 128, c * 128:(c + 1) * 128], oa[:],
                                  reads=[("oa", 0), ("oa", 1)], writes=[("dbgoa", t % 4)])
                        P.op("pool", lambda e: e.tensor_tensor(out=yab[:], in0=oa[:], in1=sz[:], op=ALU.mult),
                             reads=[("oa", 0), ("oa", 1), "sz"], writes=["yab"])
                        P.op("pe", lambda e: e.transpose(out=tp[:], in_=yab[:], identity=ident_b[:]),
                             reads=["yab", "ident_b"], writes=["tp"])
                        P.op("act", lambda e, t=t: e.copy(out=yTc[:, t * 128:(t + 1) * 128], in_=tp[:]),
                             reads=["tp"], writes=[("yTc", t)])
                    P.dma("sp", yT_d[c], yTc[:], reads=[("yTc", t) for t in range(NT)], writes=[("yT_d", c)])
                    if dbg and "qk" in dbg:
                        stg = SB(es_w, "dbgstg", [128, T_ALL], F32)
                        for ii, (src, nm) in enumerate(((qT, "qT"), (kT, "kT"))):
                            P.op("dve", lambda e, src=src: e.tensor_copy(out=stg[:], in_=src[:]),
                                 reads=[(nm, t) for t in range(NT)], writes=["dbgstg"])
                            P.dma("sp", dbg_d["qk"][ii * 128:(ii + 1) * 128, :], stg[:], reads=["dbgstg"],
                                  writes=[("dbgqk", ii)])
                        P.finish("sp", [("dbgqk", 0), ("dbgqk", 1), ("dbgqk", 2)])
                    P.barrier()
            if dbg_oa is not None:
                P.finish("sp", [("dbgoa", i) for i in range(4)])
            P.barrier()

    def ml_stage(heads, dbg_hb=None):
        with ExitStack() as es:
            rope = SB(es, "rope", [128, 4, 64], F32)
            P.dma("sp", rope[:], rope_d[:, :, :], writes=["rope"])
            bgate = SB(es, "bgate", [128, 16], F32)
            P.dma("act", bgate[:], bgate_d[0:1, :].partition_broadcast(128), writes=["bgate"])
            hnb = SB(es, "hnb", [128, D], F32)
            P.dma("sp", hnb[:], hnb_d[0:1, :].partition_broadcast(128), writes=["hnb"])
            Gt = SB(es, "Gt", [128, NT, 16], F32)
            LF = SB(es, "LF", [128, NT, 8], F32)
            Wt = SB(es, "Wt", [128, NT, 8], F32)
            Bs = SB(es, "Bs", [128, NT, 16], F32)
            EB = SB(es, "EB", [128, NT, 8], F32)
            EBL = SB(es, "EBL", [128, NT, 8], F32)
            with ExitStack() as es_w:
                pg = PS(es_w, "pg", [128, 16])
                wg, wgk = load_w(es_w, win_ab_d, 9216, 16, "g")
                for t in range(NT):
                    proj_tm(wg, wgk, 0, 16, pg, "pg", t)
                    P.op("dve", lambda e, t=t: e.tensor_tensor(out=Gt[:, t, :], in0=pg[:, 0:16], in1=bgate[:],
                                                               op=ALU.add),
                         reads=["pg", "bgate"], writes=[("Gt", t)])
                gkeys = [("Gt", t) for t in range(NT)]
                for d in range(2):
                    P.op("act", lambda e, d=d: e.activation(out=LF[:, :, 4 * d:4 * d + 4],
                                                            in_=Gt[:, :, 4 + 8 * d:8 + 8 * d], func=AF.Exp, scale=-1.0),
                         reads=gkeys, writes=[("LF", d)])
                    P.op("act", lambda e, d=d: e.activation(out=LF[:, :, 4 * d:4 * d + 4],
                                                            in_=LF[:, :, 4 * d:4 * d + 4], func=AF.Ln, bias=1.0),
                         reads=[("LF", d)], writes=[("LF", d)])
                    P.op("dve", lambda e, d=d: e.tensor_scalar(out=LF[:, :, 4 * d:4 * d + 4],
                                                               in0=LF[:, :, 4 * d:4 * d + 4], scalar1=-1.0,
                                                               scalar2=None, op0=ALU.mult),
                         reads=[("LF", d)], writes=[("LF", d)])
                for t in range(NT):
                    P.op("pe", lambda e, t=t: e.matmul(pg[:, 0:4], lhsT=consts[:, 1, :], rhs=LF[:, t, 0:4],
                                                       start=True, stop=True),
                         reads=["consts", ("LF", 0)], writes=["pg"])
                    P.op("pe", lambda e, t=t: e.matmul(pg[:, 4:8], lhsT=consts[:, 2, :], rhs=LF[:, t, 4:8],
                                                       start=True, stop=True),
                         reads=["consts", ("LF", 1)], writes=["pg"], pe_chain=True)
                    P.op("pe", lambda e, t=t: e.matmul(pg[:, 8:16], lhsT=ones_f[:], rhs=LF[:, t, 0:8],
                                                       start=True, stop=True),
                         reads=["ones_f", ("LF", 0), ("LF", 1)], writes=["pg"], pe_chain=True)
                    for d in range(2):
                        P.op("dve", lambda e, t=t, d=d: e.tensor_tensor(
                            out=Wt[:, t, 4 * d:4 * d + 4], in0=Gt[:, t, 8 * d:8 * d + 4], in1=pg[:, 4 * d:4 * d + 4],
                            op=ALU.subtract),
                            reads=["pg", ("Gt", t)], writes=[("Wt", t)])
                    P.op("act", lambda e, t=t: e.copy(out=Bs[:, t, :], in_=pg[:, 0:16]),
                         reads=["pg"], writes=[("Bs", t)])
                P.op("act", lambda e: e.activation(out=Wt[:], in_=Wt[:], func=AF.Exp),
                     reads=[("Wt", t) for t in range(NT)], writes=["Wt"])
                P.op("act", lambda e: e.activation(out=EB[:], in_=Bs[:, :, 0:8], func=AF.Exp),
                     reads=[("Bs", t) for t in range(NT)], writes=["EB"])
                P.op("act", lambda e: e.activation(out=EBL[:], in_=Bs[:, :, 8:16], func=AF.Exp),
                     reads=[("Bs", t) for t in range(NT)], writes=["EBL"])
                P.barrier()
            for h in heads:
                with ExitStack() as es_h:
                    qT = SB(es_h, "mqT", [128, 2, T_ALL], BF16)
                    kT = SB(es_h, "mkT", [128, 2, T_ALL], BF16)
                    vaug = SB(es_h, "mvaug", [128, NT, 257], BF16)
                    P.op("pool", lambda e: e.memset(vaug[:], 1.0), writes=[("mv", t) for t in range(NT)])
                    with ExitStack() as es_p:
                        pp = PS(es_p, "mpp", [128, 512])
                        pp2 = PS(es_p, "mpp2", [128, 512])
                        t1 = SB(es_p, "t1", [128, 512], F32)
                        t2 = SB(es_p, "t2", [128, 512], F32)
                        for (dst, dn, cbase, ti) in ((qT, "mqT", 4096 + h * 256, 0), (kT, "mkT", 5120 + h * 256, 2)):
                            for cch in range(2):
                                with ExitStack() as es_w:
                                    c0 = cbase + cch * 128
                                    w, wk_ = load_w(es_w, win_ab_d, c0, 128, "a")
                                    wsw, wswk = load_w(es_w, win_ab_d, 0, 0, "b", segs=[(c0 + 64, 64), (c0, 64)])
                                    for (t0, n) in TOKG:
                                        okeys = [(dn, cch, t) for t in range(t0 // 128, (t0 + n) // 128)]
                                        proj_fm(w, wk_, 0, pp, "mpp", t0, n)
                                        if t0 < 256:
                                            P.op("act", lambda e, n=n, t0=t0, dst=dst, cch=cch, ti=ti: e.mul(
                                                out=dst[:, cch, t0:t0 + n], in_=pp[:, 0:n],
                                                mul=(1.0 if ti == 0 else 1.0 / 16)),
                                                reads=["mpp"], writes=okeys)
                                            continue
                                        proj_fm(wsw, wswk, 0, pp2, "mpp2", t0, n)
                                        r0 = (t0 - 256) // 64
                                        if cch == 0:
                                            cosv = rope[:, ti, r0:r0 + 8].unsqueeze(2).broadcast_to([128, 8, 64])
                                            sinv = rope[:, ti + 1, r0:r0 + 8].unsqueeze(2).broadcast_to([128, 8, 64])
                                        else:
                                            cosv = rope[:, ti, :].unsqueeze(1).broadcast_to([128, 8, 64])
                                            sinv = rope[:, ti + 1, :].unsqueeze(1).broadcast_to([128, 8, 64])
                                        P.op("dve", lambda e, cosv=cosv: e.tensor_tensor(
                                            out=t1[:].rearrange("p (r c) -> p r c", r=8),
                                            in0=pp[:].rearrange("p (r c) -> p r c", r=8), in1=cosv, op=ALU.mult),
                                            reads=["mpp", "rope"], writes=["t1"])
                                        P.op("dve", lambda e, sinv=sinv: e.tensor_tensor(
                                            out=t2[:].rearrange("p (r c) -> p r c", r=8),
                                            in0=pp2[:].rearrange("p (r c) -> p r c", r=8), in1=sinv, op=ALU.mult),
                                            reads=["mpp2", "rope"], writes=["t2"])
                                        P.op("pool", lambda e, t0=t0, dst=dst, cch=cch: e.tensor_tensor(
                                            out=dst[:, cch, t0:t0 + 512], in0=t1[:], in1=t2[:], op=ALU.add),
                                            reads=["t1", "t2"], writes=okeys)
                                    P.barrier()
                        with ExitStack() as es_w:
                            wv, wvk = load_w(es_w, win_ab_d, 6144 + h * 256, 256, "a")
                            for t in range(NT):
                                proj_tm(wv, wvk, 0, 256, pp, "mpp", t)
                                P.op("act", lambda e, t=t: e.copy(out=vaug[:, t, 0:256], in_=pp[:, 0:256]),
                                     reads=["mpp"], writes=[("mv", t)])
                            P.barrier()
                    if dbg and "mqk" in dbg:
                        with ExitStack() as es_d:
                            stg = SB(es_d, "dbgstg", [128, T_ALL], F32)
                            for ii, (src, nm, cch) in enumerate(((qT, "mqT", 0), (qT, "mqT", 1), (kT, "mkT", 0), (kT, "mkT", 1))):
                                P.op("dve", lambda e, src=src, cch=cch: e.tensor_copy(out=stg[:], in_=src[:, cch, :]),
                                     reads=[(nm, cch, t) for t in range(NT)], writes=["dbgstg"])
                                P.dma("sp", dbg_d["mqk"][ii * 128:(ii + 1) * 128, :], stg[:], reads=["dbgstg"],
                                      writes=[("dbgqk", ii)])
                            P.finish("sp", [("dbgqk", ii) for ii in range(4)])
                            P.barrier()
                    with ExitStack() as es_c:
                        wo, wok = load_w(es_c, win_ab_d, 0, 0, "oz", segs=[(7168 + h * 256, 256), (8192 + h * 256, 256)])
                        Cf = SB(es_c, "Cf", [128, 2, 257], F32)
                        Ct = SB(es_c, "Ct", [128, 2, 257], F32)
                        Cb = SB(es_c, "Cb", [128, 2, 257], BF16)
                        STm = SB(es_c, "STm", [128, 128], BF16)
                        kt = SB(es_c, "kt", [128, 256], BF16)
                        sm = SB(es_c, "sm", [128, 4], F32)
                        Hin = SB(es_c, "Hin", [128, 256], F32)
                        Hs = SB(es_c, "Hs", [128, 256], F32)
                        sig = SB(es_c, "sig", [128, 256], F32)
                        szb = SB(es_c, "szb", [128, 256], F32)
                        hbg = SB(es_c, "hbg", [128, 256], F32)
                        ybf = SB(es_c, "ybf", [128, 256], BF16)
                        ysb = SB(es_c, "ysb", [128, 2, 128], BF16)
                        STp = PS(es_c, "STp", [128, 128])
                        acc = PS(es_c, "acc", [128, 512])
                        cacc = PS(es_c, "cacc", [128, 2, 512])
                        po = PS(es_c, "po", [128, 512])
                        tpk = PS(es_c, "tpk", [128, 2, 128], BF16)
                        tpy = PS(es_c, "tpy", [128, 2, 128], BF16)
                        for d in range(2):
                            hd = 4 * d + h
                            order = [0, 1] + list(range(2, NT)) if d == 0 else [1, 0] + list(range(NT - 1, 1, -1))
                            P.op("dve", lambda e: e.memset(Cf[:], 0.0), writes=["Cf"])
                            P.op("pool", lambda e: e.memset(Cb[:], 0.0), writes=["Cb"])
                            for t in order:
                                ts_ = slice(t * 128, (t + 1) * 128)
                                for c in range(2):
                                    P.op("pe", lambda e, c=c: e.matmul(STp[:], lhsT=kT[:, c, ts_], rhs=qT[:, c, ts_],
                                                                       start=(c == 0), stop=(c == 1)),
                                         reads=[("mkT", c, t), ("mqT", c, t)], writes=["STp"], pe_chain=True)
                                P.op("dve", lambda e, d=d, t=t, hd=hd: e.scalar_tensor_tensor(
                                    out=STm[:], in0=STp[:], scalar=Wt[:, t, hd:hd + 1], in1=consts[:, 1 + d, :],
                                    op0=ALU.mult, op1=ALU.mult),
                                    reads=["STp", "Wt", "consts"], writes=["STm"])
                                P.op("pe", lambda e, t=t: e.matmul(acc[:, 0:257], lhsT=STm[:], rhs=vaug[:, t, :],
                                                                   start=True, stop=False),
                                     reads=["STm", ("mv", t)], writes=["acc"])
                                for c in range(2):
                                    P.op("pe", lambda e, c=c: e.matmul(acc[:, 0:257], lhsT=qT[:, c, ts_], rhs=Cb[:, c, :],
                                                                       start=False, stop=(c == 1)),
                                         reads=[("mqT", c, t), "Cb"], writes=["acc"], pe_chain=True)
                                P.op("act", lambda e, t=t, hd=hd: e.activation(
                                    out=sm[:, 0:1], in_=acc[:, 256:257], func=AF.Abs, scale=EB[:, t, hd:hd + 1]),
                                    reads=["acc", "EB"], writes=["sm"])
                                P.op("dve", lambda e: e.tensor_scalar(out=sm[:, 1:2], in0=sm[:, 0:1], scalar1=1.0,
                                                                      scalar2=None, op0=ALU.max),
                                     reads=["sm"], writes=["sm"])
                                P.op("dve", lambda e: e.reciprocal(out=sm[:, 2:3], in_=sm[:, 1:2]),
                                     reads=["sm"], writes=["sm"])
                                P.op("dve", lambda e, t=t, hd=hd: e.tensor_tensor(
                                    out=sm[:, 3:4], in0=sm[:, 2:3], in1=EB[:, t, hd:hd + 1], op=ALU.mult),
                                    reads=["sm", "EB"], writes=["sm"])
                                if d == 0:
                                    P.op("act", lambda e: e.activation(out=Hs[:], in_=acc[:, 0:256], func=AF.Copy,
                                                                       scale=sm[:, 3:4]),
                                         reads=["acc", "sm"], writes=["Hs"])
                                    P.dma("sp", Hf_d[t], Hs[:], reads=["Hs"], writes=[("Hf_d", t)])
                                else:
                                    P.dma("sp", Hin[:], Hf_d[t], reads=[("Hf_d", t)], writes=["Hin"])
                                    P.op("dve", lambda e: e.scalar_tensor_tensor(
                                        out=Hs[:], in0=acc[:, 0:256], scalar=sm[:, 3:4], in1=Hin[:],
                                        op0=ALU.mult, op1=ALU.add),
                                        reads=["acc", "sm", "Hin"], writes=["Hs"])
                                    if dbg_hb is not None:
                                        P.dma("act", dbg_hb[t * 128:(t + 1) * 128, h * 256:(h + 1) * 256], Hs[:],
                                              reads=["Hs"], writes=[("dbghb", t % 4)])
                                    proj_tm(wo, wok, 0, 512, po, "po", t)
                                    P.op("act", lambda e: e.activation(out=sig[:], in_=po[:, 0:256], func=AF.Sigmoid),
                                         reads=["po"], writes=["sig"])
                                    P.op("act", lambda e: e.activation(out=szb[:], in_=po[:, 256:512], func=AF.Silu),
                                         reads=["po"], writes=["szb"])
                                    P.op("pool", lambda e: e.tensor_tensor(out=hbg[:], in0=Hs[:], in1=sig[:],
                                                                           op=ALU.mult),
                                         reads=["Hs", "sig"], writes=["hbg"])
                                    P.op("act", lambda e: e.activation(out=sig[:], in_=hbg[:], func=AF.Square,
                                                                       accum_out=sm[:, 0:1]),
                                         reads=["hbg", "sm"], writes=["sig", "sm"])
                                    P.op("act", lambda e: e.activation(out=sm[:, 1:2], in_=sm[:, 0:1], func=AF.Sqrt,
                                                                       scale=1.0 / 256, bias=EPS),
                                         reads=["sm"], writes=["sm"])
                                    P.op("dve", lambda e: e.reciprocal(out=sm[:, 2:3], in_=sm[:, 1:2]),
                                         reads=["sm"], writes=["sm"])
                                    P.op("dve", lambda e, h=h: e.scalar_tensor_tensor(
                                        out=hbg[:], in0=hbg[:], scalar=sm[:, 2:3], in1=hnb[:, h * 256:(h + 1) * 256],
                                        op0=ALU.mult, op1=ALU.mult),
                                        reads=["hbg", "sm", "hnb"], writes=["hbg"])
                                    P.op("pool", lambda e: e.tensor_tensor(out=ybf[:], in0=hbg[:], in1=szb[:],
                                                                           op=ALU.mult),
                                         reads=["hbg", "szb"], writes=["ybf"])
                                    for c in range(2):
                                        P.op("pe", lambda e, c=c: e.transpose(out=tpy[:, c, :],
                                                                              in_=ybf[:, c * 128:(c + 1) * 128],
                                                                              identity=ident_b[:]),
                                             reads=["ybf", "ident_b"], writes=["tpy"], pe_chain=True)
                                    P.op("act", lambda e: e.copy(out=ysb[:], in_=tpy[:]), reads=["tpy"], writes=["ysb"])
                                    P.dma("sp", yT_d[8 + 2 * h:10 + 2 * h, :, ts_].rearrange("c p t -> p c t"), ysb[:],
                                          reads=["ysb"], writes=[("yT_d", 8 + 2 * h, t)])
                                for c in range(2):
                                    P.op("pe", lambda e, c=c: e.transpose(out=tpk[:, c, :], in_=kT[:, c, ts_],
                                                                          identity=ident_b[:]),
                                         reads=[("mkT", c, t), "ident_b"], writes=["tpk"], pe_chain=True)
                                P.op("dve", lambda e, t=t, hd=hd: e.tensor_scalar(
                                    out=kt[:], in0=tpk[:].rearrange("p c k -> p (c k)"), scalar1=Wt[:, t, hd:hd + 1],
                                    scalar2=None, op0=ALU.mult),
                                    reads=["tpk", "Wt"], writes=["kt"])
                                for c in range(2):
                                    P.op("pe", lambda e, c=c, t=t: e.matmul(cacc[:, c, 0:257],
                                                                            lhsT=kt[:, c * 128:(c + 1) * 128],
                                                                            rhs=vaug[:, t, :], start=True, stop=True),
                                         reads=["kt", ("mv", t)], writes=["cacc"], pe_chain=(c == 1))
                                P.op("dve", lambda e: e.tensor_tensor(out=Ct[:], in0=cacc[:, :, 0:257], in1=Cf[:],
                                                                      op=ALU.add),
                                     reads=["cacc", "Cf"], writes=["Ct"])
                                P.op("dve", lambda e, t=t, hd=hd: e.tensor_scalar(
                                    out=Cf[:], in0=Ct[:], scalar1=EBL[:, t, hd:hd + 1], scalar2=None, op0=ALU.mult),
                                    reads=["Ct", "EBL"], writes=["Cf"])
                                P.op("pool", lambda e, t=t, hd=hd: e.tensor_scalar(
                                    out=Cb[:], in0=Ct[:], scalar1=EBL[:, t, hd:hd + 1], scalar2=None, op0=ALU.mult),
                                    reads=["Ct", "EBL"], writes=["Cb"])
                        if dbg_hb is not None:
                            P.finish("act", [("dbghb", i) for i in range(4)])
                        P.barrier()
            P.barrier()

    def out_stage(wout_d, xsrc, csrc, gates, dst_x, dst_c, outkey):
        with ExitStack() as es:
            wo_b = SB(es, "wo_b", [128, 16, D], BF16)
            for q4 in range(4):
                for kg in range(2):
                    for k in range(8):
                        kc = kg * 8 + k
                        P.dma("sp" if k % 2 == 0 else "act", wstage[:, k, :],
                              wout_d[kc * 128:(kc + 1) * 128, q4 * 256:(q4 + 1) * 256], writes=[("wstage", k)],
                              semkey=("wstage", k, 0))
                        P.op("pool" if k % 2 == 0 else "dve", lambda e, k=k, kc=kc, q4=q4: e.tensor_copy(
                            out=wo_b[:, kc, q4 * 256:(q4 + 1) * 256], in_=wstage[:, k, :]),
                            reads=[("wstage", k)], writes=[("wo_b", kc, q4)])
            yt = [SB(es, "yt%d" % i, [128, 16, 128], BF16) for i in range(2)]
            xt = [SB(es, "xt%d" % i, [128, D], F32) for i in range(2)]
            tmp = SB(es, "otmp", [128, D], F32)
            xo = [SB(es, "xo%d" % i, [128, D], F32) for i in range(2)]
            py = [PS(es, "py%d" % i, [128, 512]) for i in range(2)]
            tiles = list(range(NT)) if dst_c is not None else list(range(2, NT))
            for n_, t in enumerate(tiles):
                i = n_ % 2
                j = 1 if t < 2 else 0
                ts_ = slice(t * 128, (t + 1) * 128)
                P.dma("sp", yt[i][:], yT_d[:, :, ts_].rearrange("c p t -> p c t"),
                      reads=[("yT_d", c) for c in range(16)], writes=[("yt", i)])
                src = csrc[ts_, :] if t < 2 else xsrc[(t - 2) * 128:(t - 1) * 128, :]
                P.dma("act", xt[i][:], src, reads=[("x1", t)] if outkey == "out" else [], writes=[("xt", i)])
                for half in range(2):
                    for kc in range(16):
                        P.op("pe", lambda e, kc=kc, half=half, i=i: e.matmul(
                            py[half][:], lhsT=yt[i][:, kc, :], rhs=wo_b[:, kc, half * 512:(half + 1) * 512],
                            start=(kc == 0), stop=(kc == 15)),
                            reads=[("yt", i), ("wo_b", kc, 2 * half), ("wo_b", kc, 2 * half + 1)], writes=[("py", half)], pe_chain=True)
                    hs = slice(half * 512, (half + 1) * 512)
                    P.op("dve", lambda e, half=half, hs=hs, j=j: e.tensor_tensor(
                        out=tmp[:, hs], in0=py[half][:], in1=gates[j][:, hs], op=ALU.mult),
                        reads=[("py", half), ("gate", j)], writes=[("otmp", half)])
                    P.op("pool", lambda e, hs=hs, i=i: e.tensor_tensor(out=xo[i][:, hs], in0=tmp[:, hs],
                                                                       in1=xt[i][:, hs], op=ALU.add),
                         reads=[("otmp", half), ("xt", i)], writes=[("xo", i, half)])
                dst = dst_c[ts_, :] if t < 2 else dst_x[(t - 2) * 128:(t - 1) * 128, :]
                P.dma("sp", dst, xo[i][:], reads=[("xo", i, 0), ("xo", i, 1)], writes=[(outkey, t)],
                      semkey=("xo", i))
            P.barrier()

    def hg_stage(heads, dbg_o=None):
        NCH = T_ALL // 64
        ORD = [list(range(NCH)), [3, 2, 1, 0] + list(range(NCH - 1, 3, -1))]
        POS = [{j: p for p, j in enumerate(o_)} for o_ in ORD]
        with ExitStack() as es:
            lbc = SB(es, "lbc", [128, 16, 2], F32)
            P.dma("sp", lbc[:], lbc_d[:, :, :], writes=["lbc"])
            lb = SB(es, "lb", [128, 16], F32)
            oml = SB(es, "oml", [128, 16], F32)
            P.op("dve", lambda e: e.tensor_tensor(out=lb[:], in0=lbc[:, :, 1], in1=lbc[:, :, 0], op=ALU.subtract),
                 reads=["lbc"], writes=["lb"])
            P.op("act", lambda e: e.activation(out=lb[:], in_=lb[:], func=AF.Sigmoid), reads=["lb"], writes=["lb"])
            P.op("dve", lambda e: e.tensor_scalar(out=oml[:], in0=lb[:], scalar1=-1.0, scalar2=1.0, op0=ALU.mult,
                                                  op1=ALU.add), reads=["lb"], writes=["oml"])
            hnc = SB(es, "hnc", [128, 2 * D], F32)
            P.dma("act", hnc[:], hnc_d[0:1, :].partition_broadcast(128), writes=["hnc"])
            smask = SB(es, "smask", [128, 512], F32)
            P.dma("sp", smask[:], smask_d[:, :], writes=["smask"])
            for h in heads:
                with ExitStack() as es_h:
                    qf = SB(es_h, "gqf", [128, T_ALL], BF16)
                    kf = SB(es_h, "gkf", [128, T_ALL], BF16)
                    qb = SB(es_h, "gqb", [128, T_ALL], BF16)
                    kb = SB(es_h, "gkb", [128, T_ALL], BF16)
                    QK = ((qf, "gqf", kf, "gkf"), (qb, "gqb", kb, "gkb"))
                    ebp = SB(es_h, "ebp", [128, 2, NCH], F32)
                    vtok = SB(es_h, "gv", [128, NT, 128], BF16)
                    SH = [SB(es_h, "SH%d" % d, [128, NCH, 128], BF16) for d in range(2)]
                    with ExitStack() as es_p:
                        pp = [PS(es_p, "gpp%d" % i, [128, 512]) for i in range(3)]
                        wq, wqk = load_w(es_p, win_c_d, h * 128, 128, "gq")
                        wf_ = [load_w(es_p, win_c_d, 2048 + h * 128, 128, "gff"),
                               load_w(es_p, win_c_d, 4096 + h * 128, 128, "gfb")]
                        qs = SB(es_p, "gqs", [128, 512], F32)
                        A = [SB(es_p, "gA%d" % d, [128, 512], F32) for d in range(2)]
                        B = [SB(es_p, "gB%d" % d, [128, 512], F32) for d in range(2)]
                        C = [SB(es_p, "gC%d" % d, [128, 512], F32) for d in range(2)]
                        TL = SB(es_p, "gTL", [128, 8], F32)
                        for (t0, n) in TOKG:
                            nch = n // 64
                            j0 = t0 // 64
                            tk = [t for t in range(t0 // 128, (t0 + n) // 128)]
                            proj_fm(wq, wqk, 0, pp[2], "gpp2", t0, n)
                            P.op("act", lambda e, n=n: e.activation(out=qs[:, 0:n], in_=pp[2][:, 0:n], func=AF.Silu),
                                 reads=["gpp2"], writes=["gqs"])
                            for d in range(2):
                                (w_, wk_) = wf_[d]
                                (qd, qn, kd, kn) = QK[d]
                                Ad, Bd, Cd = A[d], B[d], C[d]
                                kA, kB, kC, kP = "gA%d" % d, "gB%d" % d, "gC%d" % d, "gpp%d" % d
                                proj_fm(w_, wk_, 0, pp[d], kP, t0, n)
                                P.op("act", lambda e, n=n: e.activation(out=Ad[:, 0:n], in_=pp[d][:, 0:n],
                                                                        func=AF.Sigmoid), reads=[kP], writes=[kA])
                                P.op("dve", lambda e, n=n, h=h: e.tensor_scalar(
                                    out=Ad[:, 0:n], in0=Ad[:, 0:n], scalar1=oml[:, h:h + 1], scalar2=lb[:, h:h + 1],
                                    op0=ALU.mult, op1=ALU.add), reads=[kA, "oml", "lb"], writes=[kA])
                                P.op("act", lambda e, n=n: e.activation(out=Bd[:, 0:n], in_=Ad[:, 0:n], func=AF.Ln),
                                     reads=[kA], writes=[kB])
                                P.op("pool", lambda e, n=n: e.tensor_scalar(out=Ad[:, 0:n], in0=Ad[:, 0:n], scalar1=-1.0,
                                                                            scalar2=1.0, op0=ALU.mult, op1=ALU.add),
                                     reads=[kA, kB], writes=[kA])
                                P.op("dve", lambda e, n=n: e.tensor_tensor_scan(
                                    out=Cd[:, 0:n], data0=smask[:, 0:n], data1=Bd[:, 0:n], initial=0.0,
                                    op0=ALU.mult, op1=ALU.add), reads=["smask", kB], writes=[kC])
                                C3 = Cd[:, 0:n].rearrange("p (c l) -> p c l", l=64)
                                if d == 1:
                                    P.op("pool", lambda e, n=n: e.tensor_tensor(out=Bd[:, 0:n], in0=Bd[:, 0:n],
                                                                                in1=Cd[:, 0:n], op=ALU.subtract),
                                         reads=[kB, kC], writes=[kB])
                                    P.op("act", lambda e, nch=nch, C3=C3: e.copy(out=TL[:, 0:nch], in_=C3[:, :, 63]),
                                         reads=[kC], writes=["gTL"])
                                    P.op("dve", lambda e, n=n, nch=nch, C3=C3: e.tensor_tensor(
                                        out=C3, in0=Bd[:, 0:n].rearrange("p (c l) -> p c l", l=64),
                                        in1=TL[:, 0:nch].unsqueeze(2).broadcast_to([128, nch, 64]), op=ALU.add),
                                        reads=[kB, "gTL"], writes=[kC])
                                    for c in range(nch):
                                        p_ = POS[1][j0 + c]
                                        P.op("act", lambda e, c=c, p_=p_: e.activation(
                                            out=ebp[:, 1, p_:p_ + 1], in_=TL[:, c:c + 1], func=AF.Exp),
                                            reads=["gTL"], writes=[("ebp", 1, t0, c)])
                                else:
                                    P.op("act", lambda e, nch=nch, j0=j0, C3=C3: e.activation(
                                        out=ebp[:, 0, j0:j0 + nch], in_=C3[:, :, 63], func=AF.Exp),
                                        reads=[kC], writes=[("ebp", 0, t0, 0)])
                                P.op("act", lambda e, n=n: e.activation(out=Bd[:, 0:n], in_=Cd[:, 0:n], func=AF.Exp),
                                     reads=[kC, kB], writes=[kB])
                                P.op("dve", lambda e, n=n, t0=t0, qd=qd: e.tensor_tensor(
                                    out=qd[:, t0:t0 + n], in0=qs[:, 0:n], in1=Bd[:, 0:n], op=ALU.mult),
                                    reads=["gqs", kB], writes=[(qn, t) for t in tk])
                                P.op("act", lambda e, n=n: e.activation(out=Bd[:, 0:n], in_=Cd[:, 0:n], func=AF.Exp,
                                                                        scale=-1.0),
                                     reads=[kC, kB], writes=[kB])
                                P.op("pool", lambda e, n=n, t0=t0, kd=kd: e.tensor_tensor(
                                    out=kd[:, t0:t0 + n], in0=Ad[:, 0:n], in1=Bd[:, 0:n], op=ALU.mult),
                                    reads=[kA, kB], writes=[(kn, t) for t in tk])
                        P.barrier()
                    with ExitStack() as es_p:
                        pv = [PS(es_p, "gpv%d" % i, [128, 128]) for i in range(2)]
                        wv, wvk = load_w(es_p, win_c_d, 6144 + h * 128, 128, "gv")
                        for t in range(NT):
                            sl = t % 2
                            proj_tm(wv, wvk, 0, 128, pv[sl], ("gpv", sl), t)
                            P.op("act", lambda e, t=t, sl=sl: e.copy(out=vtok[:, t, :], in_=pv[sl][:, 0:128]),
                                 reads=[("gpv", sl)], writes=[("gv", t)])
                        P.barrier()
                    with ExitStack() as es_u:
                        U = SB(es_u, "gU", [128, NCH, 128], F32)
                        ktk = [SB(es_u, "gktk%d" % i, [128, 128], BF16) for i in range(2)]
                        tpk = [PS(es_u, "gtpk%d" % i, [128, 128], BF16) for i in range(2)]
                        Ups = [[PS(es_u, "gUps%d%d" % (i, hf), [128, 128]) for hf in range(2)] for i in range(2)]
                        for d in range(2):
                            (qd, qn, kd, kn) = QK[d]
                            for t in range(NT):
                                i = t % 2
                                ts_ = slice(t * 128, (t + 1) * 128)
                                P.op("pe", lambda e, i=i: e.transpose(out=tpk[i][:], in_=kd[:, ts_], identity=ident_b[:]),
                                     reads=[(kn, t), "ident_b"], writes=[("gtpk", i)])
                                P.op("act", lambda e, i=i: e.copy(out=ktk[i][:], in_=tpk[i][:]),
                                     reads=[("gtpk", i)], writes=[("gktk", i)])
                                for hf in range(2):
                                    rs_ = slice(64 * hf, 64 * hf + 64)
                                    P.op("pe", lambda e, i=i, hf=hf, rs_=rs_: e.matmul(
                                        Ups[i][hf][:], lhsT=ktk[i][rs_, :], rhs=vtok[rs_, t, :], start=True,
                                        stop=True),
                                        reads=[("gktk", i), ("gv", t)], writes=[("gUps", i, hf)])
                                for hf in range(2):
                                    p_ = POS[d][2 * t + hf]
                                    P.op("dve", lambda e, i=i, hf=hf, p_=p_, d=d: e.tensor_scalar(
                                        out=U[:, p_, :], in0=Ups[i][hf][:], scalar1=ebp[:, d, p_:p_ + 1],
                                        scalar2=None, op0=ALU.mult),
                                        reads=[("gUps", i, hf)], writes=[("gUp", p_)])
                            ukeys = [("gUp", p_) for p_ in range(NCH)]
                            for v in range(128):
                                P.op("dve", lambda e, v=v, d=d: e.tensor_tensor_scan(
                                    out=U[:, :, v], data0=ebp[:, d, :], data1=U[:, :, v], initial=0.0,
                                    op0=ALU.mult, op1=ALU.add),
                                    reads=(ukeys if v < 1 else []), writes=[("gUs", v)])
                            skeys = [("gUs", v) for v in range(128)]
                            hn = NCH // 2
                            P.op("act", lambda e, d=d: e.copy(out=SH[d][:, 0:hn, :], in_=U[:, 0:hn, :]),
                                 reads=skeys, writes=[("SH", d, 0)])
                            P.op("pool", lambda e, d=d: e.tensor_copy(out=SH[d][:, hn:NCH, :], in_=U[:, hn:NCH, :]),
                                 reads=skeys, writes=[("SH", d, 1)])
                            for p_ in range(NCH):
                                P.lastw[("gUp", p_)] = list(P.lastw.get(("SH", d, 0), [])) + list(P.lastw.get(("SH", d, 1), []))
                                P.readers[("gUp", p_)] = {}
                        P.barrier()
                    with ExitStack() as es_c:
                        wz, wzk = load_w(es_c, win_c_d, 8192 + h * 128, 128, "gz")
                        NB = 2
                        AT = [SB(es_c, "gAT%d" % i, [128, 2, 128], BF16) for i in range(NB)]
                        ot = [SB(es_c, "got%d" % i, [128, 128], F32) for i in range(NB)]
                        szl = [SB(es_c, "gszl%d" % i, [128, 128], F32) for i in range(NB)]
                        yv = [SB(es_c, "gyv%d" % i, [128, 128], F32) for i in range(NB)]
                        junk = SB(es_c, "gjunk", [128, 128], F32)
                        ybf = [SB(es_c, "gybf%d" % i, [128, 128], BF16) for i in range(NB)]
                        ysb = [SB(es_c, "gysb%d" % i, [128, 128], BF16) for i in range(NB)]
                        sm = [SB(es_c, "gsm%d" % i, [128, 4], F32) for i in range(NB)]
                        pA = [PS(es_c, "gpA%d" % i, [128, 2, 128]) for i in range(NB)]
                        po = [PS(es_c, "gpo%d" % i, [128, 2, 128]) for i in range(NB)]
                        pz = [PS(es_c, "gpz%d" % i, [128, 128]) for i in range(NB)]
                        tpy = [PS(es_c, "gtpy%d" % i, [128, 128], BF16) for i in range(NB)]
                        for t in range(2, NT):
                            i = t % NB
                            ts_ = slice(t * 128, (t + 1) * 128)
                            for d in range(2):
                                (qd, qn, kd, kn) = QK[d]
                                P.op("pe", lambda e, d=d, i=i, kd=kd, qd=qd: e.matmul(
                                    pA[i][:, d, :], lhsT=kd[:, ts_], rhs=qd[:, ts_], start=True, stop=True),
                                    reads=[(kn, t), (qn, t)], writes=[("gpA", i)], pe_chain=(d == 1))
                            for d in range(2):
                                P.op("dve", lambda e, d=d, i=i: e.tensor_tensor(
                                    out=AT[i][:, d, :], in0=pA[i][:, d, :], in1=consts[:, 3 + d, :], op=ALU.mult),
                                    reads=[("gpA", i), "consts"], writes=[("gAT", i, d)])
                            for hf in range(2):
                                j = 2 * t + hf
                                cs_ = slice(j * 64, (j + 1) * 64)
                                rs_ = slice(64 * hf, 64 * hf + 64)
                                for d in range(2):
                                    P.op("pe", lambda e, d=d, i=i, hf=hf, rs_=rs_: e.matmul(
                                        po[i][0:64, hf, :], lhsT=AT[i][rs_, d, rs_], rhs=vtok[rs_, t, :],
                                        start=(d == 0), stop=False),
                                        reads=[("gAT", i, 0), ("gAT", i, 1), ("gv", t)], writes=[("gpo", i)],
                                        pe_chain=(d == 1 or hf == 1))
                                for d in range(2):
                                    (qd, qn, kd, kn) = QK[d]
                                    pm = POS[d][j] - 1
                                    P.op("pe", lambda e, d=d, i=i, hf=hf, cs_=cs_, pm=pm, qd=qd: e.matmul(
                                        po[i][0:64, hf, :], lhsT=qd[:, cs_], rhs=SH[d][:, pm, :],
                                        start=False, stop=(d == 1)),
                                        reads=[(qn, t), ("SH", d, 0), ("SH", d, 1)], writes=[("gpo", i)],
                                        pe_chain=True)
                            for hf in range(2):
                                rs_ = slice(64 * hf, 64 * hf + 64)
                                P.op("act", lambda e, i=i, hf=hf, rs_=rs_: e.copy(out=ot[i][rs_, :], in_=po[i][0:64, hf, :]),
                                     reads=[("gpo", i)], writes=[("got", i, hf)])
                            okeys = [("got", i, 0), ("got", i, 1)]
                            if dbg_o is not None:
                                P.dma("sp", dbg_o[ts_, h * 128:(h + 1) * 128], ot[i][:], reads=okeys,
                                      writes=[("dbgo1", t % 4)])
                            P.op("act", lambda e, i=i: e.activation(out=junk[:], in_=ot[i][:], func=AF.Square,
                                                                    accum_out=sm[i][:, 0:1]),
                                 reads=okeys, writes=["gjunk", ("gsm", i)])
                            P.op("act", lambda e, i=i: e.activation(out=sm[i][:, 1:2], in_=sm[i][:, 0:1], func=AF.Sqrt,
                                                                    scale=1.0 / 128, bias=EPS),
                                 reads=[("gsm", i)], writes=[("gsm", i)])
                            P.op("dve", lambda e, i=i: e.reciprocal(out=sm[i][:, 2:3], in_=sm[i][:, 1:2]),
                                 reads=[("gsm", i)], writes=[("gsm", i)])
                            proj_tm(wz, wzk, 0, 128, pz[i], ("gpz", i), t)
                            P.op("act", lambda e, i=i: e.activation(out=szl[i][:], in_=pz[i][:, 0:128], func=AF.Silu),
                                 reads=[("gpz", i)], writes=[("gszl", i)])
                            P.op("dve", lambda e, h=h, i=i: e.scalar_tensor_tensor(
                                out=yv[i][:], in0=ot[i][:], scalar=sm[i][:, 2:3], in1=hnc[:, h * 128:(h + 1) * 128],
                                op0=ALU.mult, op1=ALU.mult),
                                reads=okeys + [("gsm", i), "hnc"], writes=[("gyv", i)])
                            P.op("pool", lambda e, i=i: e.tensor_tensor(out=ybf[i][:], in0=yv[i][:], in1=szl[i][:],
                                                                        op=ALU.mult),
                                 reads=[("gyv", i), ("gszl", i)], writes=[("gybf", i)])
                            P.op("pe", lambda e, i=i: e.transpose(out=tpy[i][:], in_=ybf[i][:], identity=ident_b[:]),
                                 reads=[("gybf", i), "ident_b"], writes=[("gtpy", i)])
                            P.op("act", lambda e, i=i: e.copy(out=ysb[i][:], in_=tpy[i][:]), reads=[("gtpy", i)],
                                 writes=[("gysb", i)])
                            P.dma("sp" if i == 0 else "act", yT_d[h, :, ts_], ysb[i][:], reads=[("gysb", i)],
                                  writes=[("yT_d", h, t)], semkey=("gysb", i))
                        if dbg_o is not None:
                            P.finish("sp", [("dbgo1", i) for i in range(4)])
                        P.barrier()
            P.barrier()

    if dbg and "oa" in dbg:
        na_stage(dbg.get("_chunks", [0]), dbg_d["oa"])
    elif dbg and "hb" in dbg:
        ml_stage([0], dbg_d["hb"])
    elif dbg and "x1" in dbg:
        na_stage(list(range(8)))
        ml_stage(list(range(4)))
        out_stage(wout_ab_d, x_d, ctx_d, gate0, x1_d, ctx1_d, "x1")
        P.finish("sp", [("x1", t) for t in range(NT)])
    elif dbg and "o1" in dbg:
        pass
    elif not dbg:
        na_stage(list(range(8)))
        ml_stage(list(range(4)))
        out_stage(wout_ab_d, x_d, ctx_d, gate0, x1_d, ctx1_d, "x1")

    if dbg and "hT" in dbg:
        with ExitStack() as es:
            stg = SB(es, "dbgstg", [128, T_ALL], F32)
            for c in range(8):
                P.op("dve", lambda e, c=c: e.tensor_copy(out=stg[:], in_=hT[:, c, :]),
                     reads=[("hT", t) for t in range(NT)], writes=["dbgstg"])
                P.dma("sp", dbg_d["hT"][c * 128:(c + 1) * 128, :], stg[:], reads=["dbgstg"], writes=[("dbgo", c)])
            P.finish("sp", [("dbgo", c) for c in range(8)])
    L0.close()
    P.barrier()
    L1 = ExitStack()
    gate1 = {0: SB(L1, "gate1_0", [128, D], F32)}
    if dbg and "o1" in dbg:
        x1_in = dt_in("x1_in", [T_LAT, D])
        ctx1_in = dt_in("ctx1_in", [T_CTX, D])
        norm_stage(1, x1_in, ctx1_in, gate1)
        hg_stage(dbg.get("_heads", [0]), dbg_d["o1"])
    elif not dbg:
        norm_stage(1, x1_d, ctx1_d, gate1)
        hg_stage(list(range(16)))
        out_stage(wout_c_d, x1_d, None, gate1, out_d, None, "out")
        P.finish("sp", [("out", t) for t in range(2, NT)])
    L1.close()
    G.close()
    return nc


def host_inputs(inputs, b):
    cc = np.zeros((128, 16), np.float32)
    cb = np.asarray(inputs["c"][b], np.float32).reshape(8, 128)
    cx = np.asarray(inputs["c_ctx"], np.float32).reshape(8, 128)
    for k in range(8):
        cc[:, 2 * k] = cb[k]
        cc[:, 2 * k + 1] = cx[k]
    m = {
        "x": np.ascontiguousarray(inputs["x"][b], dtype=np.float32),
        "ctx": np.ascontiguousarray(inputs["ctx"][b], dtype=np.float32),
        "cc": cc,
        "w_ada": np.ascontiguousarray(inputs["w_ada"], dtype=np.float32),
        "b_ada": np.ascontiguousarray(inputs["b_ada"], dtype=np.float32),
        "norm_w": np.ascontiguousarray(inputs["norm_w"], dtype=np.float32),
        "ident": np.eye(128, dtype=np.float32),
        "consts": host_consts(),
        "w_in_ab": np.ascontiguousarray(inputs["w_in_ab"][0], dtype=np.float32),
        "qkw": np.stack([np.tile(np.asarray(inputs["q_norm_a"][0], np.float32), 2),
                         np.tile(np.asarray(inputs["k_norm_a"][0], np.float32), 2)], axis=1),
        "nab": host_nab(np.asarray(inputs["rpb_a"][0], np.float32)),
        "rope": host_rope(),
        "b_gate": np.asarray(inputs["b_gate_ab"], np.float32).reshape(1, 16),
        "h_norm_b": np.asarray(inputs["h_norm_b"], np.float32).reshape(1, D),
        "w_out_ab": np.ascontiguousarray(inputs["w_out_ab"][0], dtype=np.float32),
        "w_in_c": np.ascontiguousarray(inputs["w_in_c"][0], dtype=np.float32),
        "w_out_c": np.ascontiguousarray(inputs["w_out_c"][0], dtype=np.float32),
        "lbc": np.ascontiguousarray(np.asarray(inputs["lb_c"], np.float32).reshape(2, 16, 128).transpose(2, 1, 0)),
        "h_norm_c": np.asarray(inputs["h_norm_c"], np.float32).reshape(1, 2 * D),
        "smask": np.tile((np.arange(512) % 64 != 0).astype(np.float32)[None, :], (128, 1)),
    }
    return m


def host_rope():
    p = np.arange(128)
    freq = (10000.0 ** (-(p % 64).astype(np.float64) / 64.0))[:, None]
    pos = np.arange(64, dtype=np.float64)[None, :]
    ang = (pos.astype(np.float32) * freq.astype(np.float32)).astype(np.float32)
    cos = np.cos(ang).astype(np.float32)
    sin = np.sin(ang).astype(np.float32) * np.where(p < 64, -1.0, 1.0)[:, None].astype(np.float32)
    return np.stack([cos, sin, cos / 16, sin / 16], axis=1).astype(np.float32)


def host_consts():
    c = np.zeros((128, 6, 128), np.float32)
    p = np.arange(128)
    same = (p[:, None] // 64 == p[None, :] // 64)
    c[:, 0, :] = same
    c[:, 1, :] = (p[:, None] <= p[None, :])
    c[:, 2, :] = (p[:, None] >= p[None, :])
    c[:, 3, :] = same & (p[:, None] <= p[None, :])
    c[:, 4, :] = same & (p[:, None] >= p[None, :])
    return c


_NAB_CACHE = {}


def host_nab(rpb):
    NEG = np.float32(-30000.0)
    types = [(10, 10 + dj) for dj in range(-2, 3)]
    types += [(0, j) for j in range(4)] + [(1, j) for j in range(4)]
    types += [(30, 28 + j) for j in range(4)] + [(31, 28 + j) for j in range(4)]
    out = np.empty((16, len(types), 128, 128), np.float32)
    p = np.arange(128)
    for ti, (i, j) in enumerate(types):
        kr = (2 * j + p // 64)[:, None]
        kc = (p % 64)[:, None]
        qr = (2 * i + p // 64)[None, :]
        qc = (p % 64)[None, :]
        r0 = np.clip(qr - 4, 0, 56)
        c0 = np.clip(qc - 8, 0, 48)
        valid = (kr >= r0) & (kr < r0 + 8) & (kc >= c0) & (kc < c0 + 16)
        ri = np.clip(kr - qr + 7, 0, 14)
        ci = np.clip(kc - qc + 15, 0, 30)
        g = rpb[:, ri, ci]
        out[:, ti] = np.where(valid[None], g, NEG)
    return out


def kernel(**inputs):
    nc = build()
    in_maps = [host_inputs(inputs, b) for b in range(8)]
    res = run_bass_kernel_spmd(nc, in_maps, core_ids=list(range(8)))
    return np.stack([np.asarray(r["out"], np.float32) for r in res.results], axis=0)
```

```python
import numpy as np
import ml_dtypes
import concourse.bass as bass
import concourse.mybir as mybir
from concourse.bass_utils import run_bass_kernel_spmd

F32 = mybir.dt.float32
BF16 = mybir.dt.bfloat16
AF = mybir.ActivationFunctionType
ALU = mybir.AluOpType
AX = mybir.AxisListType

D = 1024
T_LAT = 4096
T_CTX = 256
T_ALL = T_LAT + T_CTX
NT = T_ALL // 128
EPS = 1e-6


class KeyList(list):
    pass


def _flat(keys):
    out = []
    for k in keys:
        if isinstance(k, KeyList):
            out.extend(k)
        else:
            out.append(k)
    return out


class Prog:
    def __init__(self, nc):
        self.nc = nc
        self.eng = {"pe": nc.tensor, "act": nc.scalar, "dve": nc.vector, "pool": nc.gpsimd, "sp": nc.sync}
        self.csem = {}
        self.ccnt = {}
        for e in ("pe", "act", "dve", "pool"):
            self.csem[e] = nc.alloc_semaphore("c_" + e)
            self.ccnt[e] = 0
        self.seen = {e: {} for e in self.eng}
        self.lastw = {}
        self.readers = {}
        self.dsem = {}
        self.dpool = []
        for i in range(40):
            self.dpool.append([nc.alloc_semaphore("d%d" % i), 0])
        self.nbuf = 0
        self.ninst = 0

    def sb(self, name, shape, dt):
        return self.nc.alloc_sbuf_tensor("s_" + name, list(shape), dt)

    def ps(self, name, shape, dt=F32):
        return self.nc.alloc_psum_tensor("p_" + name, list(shape), dt)

    def _deps(self, reads, writes, wadd=()):
        toks = []
        for k in reads:
            toks.extend(self.lastw.get(k, ()))
        for k in writes:
            toks.extend(self.lastw.get(k, ()))
            toks.extend(self.readers.get(k, {}).values())
        for k in wadd:
            toks.extend(self.readers.get(k, {}).values())
        return toks

    def _wait(self, e, toks, skip_sem=None):
        need = {}
        for (sem, val) in toks:
            if skip_sem is not None and sem.name == skip_sem:
                continue
            if self.seen[e].get(sem.name, 0) >= val:
                continue
            if need.get(sem.name, (None, 0))[1] < val:
                need[sem.name] = (sem, val)
        for name, (sem, val) in need.items():
            self.eng[e].wait_ge(sem, val)
            self.seen[e][name] = val

    def _commit(self, tok, reads, writes, wadd=()):
        for k in writes:
            self.lastw[k] = [tok]
            self.readers[k] = {}
        for k in wadd:
            self.lastw.setdefault(k, []).append(tok)
        for k in reads:
            if k in writes:
                continue
            r = self.readers.setdefault(k, {})
            o = r.get(tok[0].name)
            if o is None or o[1] < tok[1]:
                r[tok[0].name] = tok

    def op(self, e, fn, reads=(), writes=(), pe_chain=False):
        reads = _flat(reads)
        toks = self._deps(reads, writes)
        self._wait(e, toks, skip_sem=(self.csem[e].name if pe_chain else None))
        ins = fn(self.eng[e])
        self.ccnt[e] += 1
        ins.then_inc(self.csem[e], 1)
        tok = (self.csem[e], self.ccnt[e])
        self._commit(tok, reads, writes)
        self.ninst += 1
        return ins

    def dma(self, e, out, in_, reads=(), writes=(), semkey=None, wadd=(), **kw):
        toks = self._deps(reads, writes, wadd)
        if semkey is None:
            semkey = (tuple(writes) + tuple(reads))[0]
        if semkey not in self.dsem:
            self.dsem[semkey] = self.dpool[len(self.dsem) % len(self.dpool)]
        ent = self.dsem[semkey]
        if ent[1] > 0:
            toks.append((ent[0], ent[1]))
        self._wait(e, toks)
        ins = self.eng[e].dma_start(out=out, in_=in_, **kw)
        ent[1] += 16
        ins.then_inc(ent[0], 16)
        tok = (ent[0], ent[1])
        self._commit(tok, reads, writes, wadd)
        self.ninst += 1
        return ins

    def barrier(self):
        toks = [(self.csem[f], self.ccnt[f]) for f in self.csem if self.ccnt[f] > 0]
        toks += [(ent[0], ent[1]) for ent in self.dpool if ent[1] > 0]
        for e in self.eng:
            self._wait(e, toks)

    def finish(self, e, keys):
        toks = []
        for k in keys:
            toks.extend(self.lastw.get(k, ()))
        self._wait(e, toks)


def build(dbg=None):
    from contextlib import ExitStack
    nc = bass.Bass("TRN2", target_bir_lowering=False)
    P = Prog(nc)
    dbg_d = {}
    if dbg:
        for name, shape in dbg.items():
            if name.startswith("_"):
                continue
            dbg_d[name] = nc.dram_tensor("dbg_" + name, list(shape), F32, kind="ExternalOutput").ap()
    dt_in = lambda name, shape, dt=F32: nc.dram_tensor(name, list(shape), dt, kind="ExternalInput").ap()
    x_d = dt_in("x", [T_LAT, D])
    ctx_d = dt_in("ctx", [T_CTX, D])
    cc_d = dt_in("cc", [128, 16])
    wada_d = dt_in("w_ada", [2, D, 3 * D])
    bada_d = dt_in("b_ada", [2, 3 * D])
    normw_d = dt_in("norm_w", [2, D])
    ident_d = dt_in("ident", [128, 128])
    consts_d = dt_in("consts", [128, 6, 128])
    win_ab_d = dt_in("w_in_ab", [D, 9232])
    qkw_d = dt_in("qkw", [128, 2])
    nab_d = dt_in("nab", [16, 21, 128, 128])
    yT_d = nc.dram_tensor("yT_scr", [16, 128, T_ALL], BF16, kind="Internal").ap()
    rope_d = dt_in("rope", [128, 4, 64])
    bgate_d = dt_in("b_gate", [1, 16])
    hnb_d = dt_in("h_norm_b", [1, D])
    Hf_d = nc.dram_tensor("Hf_scr", [NT, 128, 256], F32, kind="Internal").ap()
    wout_ab_d = dt_in("w_out_ab", [2 * D, D])
    win_c_d = dt_in("w_in_c", [D, 10240])
    wout_c_d = dt_in("w_out_c", [2 * D, D])
    lbc_d = dt_in("lbc", [128, 16, 2])
    hnc_d = dt_in("h_norm_c", [1, 2 * D])
    smask_d = dt_in("smask", [128, 512])
    if dbg and "x1" in dbg:
        x1_d, ctx1_d = dbg_d["x1"], dbg_d["ctx1"]
    else:
        x1_d = nc.dram_tensor("x1_scr", [T_LAT, D], F32, kind="Internal").ap()
        ctx1_d = nc.dram_tensor("ctx1_scr", [T_CTX, D], F32, kind="Internal").ap()
    out_d = nc.dram_tensor("out", [T_LAT, D], F32, kind="ExternalOutput").ap()

    uid = [0]

    def SB(es, name, shape, dt):
        uid[0] += 1
        return es.enter_context(nc.sbuf_tensor("s%d_%s" % (uid[0], name), list(shape), dt))

    def PS(es, name, shape, dt=F32):
        uid[0] += 1
        return es.enter_context(nc.psum_tensor("p%d_%s" % (uid[0], name), list(shape), dt))

    G = ExitStack()
    ident_f = SB(G, "ident_f", [128, 128], F32)
    ident_b = SB(G, "ident_b", [128, 128], BF16)
    P.dma("sp", ident_f[:], ident_d[:, :], writes=["ident_f"])
    P.op("dve", lambda e: e.tensor_copy(out=ident_b[:], in_=ident_f[:]), reads=["ident_f"], writes=["ident_b"])
    ones_f = SB(G, "ones_f", [128, 128], F32)
    P.op("dve", lambda e: e.memset(ones_f[:], 1.0), writes=["ones_f"])
    cc = SB(G, "cc", [128, 16], F32)
    sc = SB(G, "sc", [128, 16], F32)
    P.dma("sp", cc[:], cc_d[:, :], writes=["cc"])
    P.op("act", lambda e: e.activation(out=sc[:], in_=cc[:], func=AF.Silu), reads=["cc"], writes=["sc"])
    hT = SB(G, "hT", [128, 8, T_ALL], BF16)
    wstage = SB(G, "wstage", [128, 8, 256], F32)
    consts = SB(G, "consts", [128, 6, 128], F32)
    P.dma("sp", consts[:], consts_d[:, :, :], writes=["consts"])
    bones = consts[:, 0, :]

    def norm_stage(l, xsrc, csrc, gate_tiles):
        with ExitStack() as es:
            screp = SB(es, "screp", [128, 16, 128], F32)
            for kj in range(16):
                P.op("dve", lambda e, kj=kj: e.tensor_scalar(out=screp[:, kj, :], in0=ones_f[:],
                                                             scalar1=sc[:, kj:kj + 1], scalar2=None, op0=ALU.mult),
                     reads=["sc", "ones_f"], writes=[("screp", kj)])
            wada = SB(es, "wada", [128, 8, 512], F32)
            brow = SB(es, "brow", [128, 3 * D], F32)
            nwb = SB(es, "nwb", [128, D], F32)
            shf = [SB(es, "shf%d" % j, [128, D], F32) for j in range(2)]
            mA = [SB(es, "mA%d" % j, [128, D], F32) for j in range(2)]
            mps = PS(es, "mps", [128, 512])
            P.dma("sp", brow[:], bada_d[l:l + 1, :].partition_broadcast(128), writes=["brow"])
            P.dma("act", nwb[:], normw_d[l:l + 1, :].partition_broadcast(128), writes=["nwb"])
            for blk in range(6):
                for k in range(8):
                    P.dma("sp" if k % 2 == 0 else "act", wada[:, k, :],
                          wada_d[l, k * 128:(k + 1) * 128, blk * 512:(blk + 1) * 512], writes=[("wada", k)])
                sec, half = blk // 2, blk % 2
                for j in range(2):
                    if sec == 2 and j not in gate_tiles:
                        continue
                    for k in range(8):
                        P.op("pe", lambda e, k=k, j=j: e.matmul(
                            mps[:], lhsT=screp[:, 2 * k + j, :], rhs=wada[:, k, :], start=(k == 0), stop=(k == 7)),
                            reads=[("screp", 2 * k + j), ("wada", k)], writes=["mps"], pe_chain=True)
                    dst = (shf[j], mA[j], gate_tiles.get(j))[sec]
                    dkey = (("shf", j), ("mA", j), ("gateh", j))[sec]
                    c0 = blk * 512
                    P.op("dve", lambda e, dst=dst, half=half, c0=c0: e.tensor_tensor(
                        out=dst[:, half * 512:(half + 1) * 512], in0=mps[:], in1=brow[:, c0:c0 + 512], op=ALU.add),
                        reads=["mps", "brow"], writes=[dkey + (half,)])
            for j in range(2):
                P.op("dve", lambda e, j=j: e.scalar_tensor_tensor(
                    out=mA[j][:], in0=mA[j][:], scalar=1.0, in1=nwb[:], op0=ALU.add, op1=ALU.mult),
                    reads=[("mA", j, 0), ("mA", j, 1), "nwb"], writes=[("mA", j, 0), ("mA", j, 1)])
            for j in gate_tiles:
                P.op("dve", lambda e, j=j: e.tensor_copy(out=gate_tiles[j][:, 0:1], in_=gate_tiles[j][:, 0:1]),
                     reads=[("gateh", j, 0), ("gateh", j, 1)], writes=[("gate", j)])
            xin = [SB(es, "xin%d" % i, [128, D], F32) for i in range(2)]
            junk = SB(es, "junk", [128, D], F32)
            ssq = SB(es, "ssq", [128, 2], F32)
            rstd = SB(es, "rstd", [128, 2], F32)
            hm = [SB(es, "hm%d" % i, [128, D], F32) for i in range(2)]
            hb = [SB(es, "hbf%d" % i, [128, D], BF16) for i in range(2)]
            tps = [PS(es, "tps%d" % i, [128, 8, 128], BF16) for i in range(2)]
            for t in range(NT):
                i = t % 2
                j = 1 if t < 2 else 0
                src = csrc[t * 128:(t + 1) * 128, :] if t < 2 else xsrc[(t - 2) * 128:(t - 1) * 128, :]
                P.dma("sp" if i == 0 else "act", xin[i][:], src, reads=[("x1", t)] if l == 1 else [],
                      writes=[("xin", i)])
                P.op("act", lambda e, i=i: e.activation(out=junk[:], in_=xin[i][:], func=AF.Square,
                                                        accum_out=ssq[:, i:i + 1]),
                     reads=[("xin", i)], writes=["junk", ("ssq", i)])
                P.op("act", lambda e, i=i: e.activation(out=ssq[:, i:i + 1], in_=ssq[:, i:i + 1], func=AF.Sqrt,
                                                        scale=1.0 / D, bias=EPS),
                     reads=[("ssq", i)], writes=[("ssq", i)])
                P.op("dve", lambda e, i=i: e.reciprocal(out=rstd[:, i:i + 1], in_=ssq[:, i:i + 1]),
                     reads=[("ssq", i)], writes=[("rstd", i)])
                P.op("dve", lambda e, i=i, j=j: e.scalar_tensor_tensor(
                    out=hm[i][:], in0=xin[i][:], scalar=rstd[:, i:i + 1], in1=mA[j][:],
                    op0=ALU.mult, op1=ALU.mult),
                    reads=[("xin", i), ("rstd", i), ("mA", j, 0), ("mA", j, 1)], writes=[("hm", i)])
                P.op("pool", lambda e, i=i, j=j: e.tensor_tensor(out=hb[i][:], in0=hm[i][:], in1=shf[j][:],
                                                                 op=ALU.add),
                     reads=[("hm", i), ("shf", j, 0), ("shf", j, 1)], writes=[("hb", i)])
                for c in range(8):
                    P.op("pe", lambda e, i=i, c=c: e.transpose(out=tps[i][:, c, :],
                                                               in_=hb[i][:, c * 128:(c + 1) * 128],
                                                               identity=ident_b[:]),
                         reads=[("hb", i), "ident_b"], writes=[("tps", i)], pe_chain=True)
                P.op("act", lambda e, i=i, t=t: e.copy(out=hT[:, :, t * 128:(t + 1) * 128], in_=tps[i][:]),
                     reads=[("tps", i)], writes=[("hT", t)])
            P.barrier()

    L0 = ExitStack()
    gate0 = {j: SB(L0, "gate0_%d" % j, [128, D], F32) for j in range(2)}
    norm_stage(0, x_d, ctx_d, gate0)


    def load_w(es_w, wd, col0, ncols, tag, segs=None):
        if segs is None:
            segs = [(col0, ncols)]
        pieces = []
        for (c0, n) in segs:
            while n > 0:
                m_ = min(n, 256)
                pieces.append((c0, m_))
                c0 += m_
                n -= m_
        ncols = sum(n for _, n in pieces)
        wb = SB(es_w, "wb_" + tag, [128, 8, ncols], BF16)
        groups, cur, curn = [], [], 0
        for pc in pieces:
            if curn + pc[1] > 256:
                groups.append(cur)
                cur, curn = [], 0
            cur.append(pc)
            curn += pc[1]
        groups.append(cur)
        o0 = 0
        for gi, grp in enumerate(groups):
            gn = sum(n for _, n in grp)
            for k in range(8):
                o = 0
                for si, (c0, n) in enumerate(grp):
                    P.dma("sp" if k % 2 == 0 else "act", wstage[:, k, o:o + n], wd[k * 128:(k + 1) * 128, c0:c0 + n],
                          writes=([("wstage", k)] if si == 0 else []), wadd=([] if si == 0 else [("wstage", k)]),
                          semkey=("wstage", k, si))
                    o += n
            for k in range(8):
                P.op("pool" if k % 2 == 0 else "dve",
                     lambda e, k=k, o0=o0, gn=gn: e.tensor_copy(out=wb[:, k, o0:o0 + gn], in_=wstage[:, k, 0:gn]),
                     reads=[("wstage", k)], writes=([("wb_" + tag, k)] if gi == 0 else []),
                     ) if gi == 0 else P.op("pool" if k % 2 == 0 else "dve",
                     lambda e, k=k, o0=o0, gn=gn: e.tensor_copy(out=wb[:, k, o0:o0 + gn], in_=wstage[:, k, 0:gn]),
                     reads=[("wstage", k)], writes=[("wb_" + tag, k, gi)])
            o0 += gn
        keys = [("wb_" + tag, k) for k in range(8)]
        extra = [("wb_" + tag, k, gi) for k in range(8) for gi in range(1, len(groups))]
        return wb, [KeyList([("wb_" + tag, k)] + [("wb_" + tag, k, gi) for gi in range(1, len(groups))]) for k in range(8)]

    TOKG = [(0, 256)] + [(256 + 512 * g, 512) for g in range(8)]

    def proj_fm(wb, wkeys, c0, pst, pkey, t0, n):
        for k in range(8):
            P.op("pe", lambda e, k=k: e.matmul(pst[:, 0:n], lhsT=wb[:, k, c0:c0 + 128], rhs=hT[:, k, t0:t0 + n],
                                               start=(k == 0), stop=(k == 7)),
                 reads=[wkeys[k]] + [("hT", t) for t in range(t0 // 128, (t0 + n) // 128)], writes=[pkey],
                 pe_chain=True)

    def proj_tm(wb, wkeys, c0, ncols, pst, pkey, t):
        for k in range(8):
            P.op("pe", lambda e, k=k: e.matmul(pst[:, 0:ncols], lhsT=hT[:, k, t * 128:(t + 1) * 128],
                                               rhs=wb[:, k, c0:c0 + ncols], start=(k == 0), stop=(k == 7)),
                 reads=[wkeys[k], ("hT", t)], writes=[pkey], pe_chain=True)

    def na_keytiles(t):
        if t < 2:
            return [(0, None), (1, None)]
        i = t - 2
        if i == 0:
            nb = [(j, 5 + j) for j in range(4)]
        elif i == 1:
            nb = [(j, 9 + j) for j in range(4)]
        elif i == 30:
            nb = [(28 + j, 13 + j) for j in range(4)]
        elif i == 31:
            nb = [(28 + j, 17 + j) for j in range(4)]
        else:
            nb = [(i + dj, dj + 2) for dj in range(-2, 3)]
        return [(2 + j, ty) for (j, ty) in nb] + [(0, None), (1, None)]

    def na_stage(chunks, dbg_oa=None):
        with ExitStack() as es:
            qkw = SB(es, "qkw", [128, 2], F32)
            P.dma("sp", qkw[:], qkw_d[:, :], writes=["qkw"])
            P.op("act", lambda e: e.mul(out=qkw[:, 0:1], in_=qkw[:, 0:1], mul=0.125), reads=["qkw"], writes=["qkw"])
            qT = SB(es, "qT", [128, T_ALL], BF16)
            kT = SB(es, "kT", [128, T_ALL], BF16)
            vaug = SB(es, "vaug", [128, NT, 2, 65], BF16)
            yTc = SB(es, "yTc", [128, T_ALL], BF16)
            biasf = SB(es, "biasf", [128, 2, 21, 128], F32)
            biasb = SB(es, "biasb", [128, 2, 21, 128], BF16)
            sq = SB(es, "sq", [128, 512], F32)
            rs = SB(es, "rs", [128, 512], F32)
            PT = SB(es, "PT", [128, 8, 128], BF16)
            rden = SB(es, "rden", [128, 2], F32)
            oa = SB(es, "oa", [128, 128], F32)
            sz = SB(es, "sz", [128, 128], F32)
            yab = SB(es, "yab", [128, 128], BF16)
            pp = PS(es, "pp", [128, 512])
            ssp = PS(es, "ssp", [128, 512])
            st = PS(es, "st", [128, 8, 128])
            num = PS(es, "num", [128, 2, 128])
            tp = PS(es, "tp", [128, 128], BF16)
            P.op("dve", lambda e: e.memset(vaug[:], 1.0), writes=[("vaug", t) for t in range(NT)])
            for c in chunks:
                with ExitStack() as es_w:
                    wq, wqk = load_w(es_w, win_ab_d, c * 128, 128, "q")
                    wk, wkk = load_w(es_w, win_ab_d, 1024 + c * 128, 128, "k")
                    wv, wvk = load_w(es_w, win_ab_d, 2048 + c * 128, 128, "v")
                    wz, wzk = load_w(es_w, win_ab_d, 3072 + c * 128, 128, "z")
                    for hh in range(2):
                        P.dma("sp" if hh == 0 else "act", biasf[:, hh, :, :],
                              nab_d[2 * c + hh].rearrange("t k q -> k t q"), writes=[("biasf", hh)])
                        P.op("pool", lambda e, hh=hh: e.tensor_copy(out=biasb[:, hh, :, :], in_=biasf[:, hh, :, :]),
                             reads=[("biasf", hh)], writes=[("biasb", hh)])
                    for (dst, dname, wb_, wk_, col) in ((qT, "qT", wq, wqk, 0), (kT, "kT", wk, wkk, 1)):
                        for (t0, n) in TOKG:
                            proj_fm(wb_, wk_, 0, pp, "pp", t0, n)
                            P.op("act", lambda e, n=n: e.activation(out=sq[:, 0:n], in_=pp[:, 0:n], func=AF.Square),
                                 reads=["pp"], writes=["sq"])
                            P.op("pe", lambda e, n=n: e.matmul(ssp[:, 0:n], lhsT=bones, rhs=sq[:, 0:n],
                                                               start=True, stop=True),
                                 reads=["sq", "consts"], writes=["ssp"])
                            P.op("act", lambda e, n=n: e.activation(out=rs[:, 0:n], in_=ssp[:, 0:n], func=AF.Sqrt,
                                                                    scale=1.0 / 64, bias=EPS),
                                 reads=["ssp"], writes=["rs"])
                            P.op("dve", lambda e, n=n: e.reciprocal(out=rs[:, 0:n], in_=rs[:, 0:n]),
                                 reads=["rs"], writes=["rs"])
                            P.op("dve", lambda e, n=n, t0=t0, dst=dst, col=col: e.scalar_tensor_tensor(
                                out=dst[:, t0:t0 + n], in0=pp[:, 0:n], scalar=qkw[:, col:col + 1], in1=rs[:, 0:n],
                                op0=ALU.mult, op1=ALU.mult),
                                reads=["pp", "rs", "qkw"],
                                writes=[(dname, t) for t in range(t0 // 128, (t0 + n) // 128)])
                    for t in range(NT):
                        proj_tm(wv, wvk, 0, 128, pp, "pp", t)
                        P.op("act", lambda e, t=t: e.copy(out=vaug[:, t, :, 0:64],
                                                          in_=pp[:, 0:128].rearrange("p (h d) -> p h d", h=2)),
                             reads=["pp"], writes=[("vaug", t)])
                    for t in range(NT):
                        kts = na_keytiles(t)
                        nk = len(kts)
                        proj_tm(wz, wzk, 0, 128, pp, "pp", t)
                        P.op("act", lambda e: e.activation(out=sz[:], in_=pp[:, 0:128], func=AF.Silu),
                             reads=["pp"], writes=["sz"])
                        for hh in range(2):
                            pb = 64 * hh
                            for n_, (kt, ty) in enumerate(kts):
                                P.op("pe", lambda e, n_=n_, kt=kt, ty=ty, pb=pb: e.matmul(
                                    st[:, n_, :], lhsT=kT[pb:pb + 64, kt * 128:(kt + 1) * 128],
                                    rhs=qT[pb:pb + 64, t * 128:(t + 1) * 128], start=True, stop=(ty is None)),
                                    reads=[("kT", kt), ("qT", t)], writes=["st"], pe_chain=True)
                                if ty is not None:
                                    P.op("pe", lambda e, n_=n_, ty=ty, hh=hh: e.matmul(
                                        st[:, n_, :], lhsT=ident_b[:], rhs=biasb[:, hh, ty, :], start=False, stop=True),
                                        reads=["ident_b", ("biasb", hh)], writes=["st"], pe_chain=True)
                            P.op("act", lambda e, nk=nk: e.activation(out=PT[:, 0:nk, :], in_=st[:, 0:nk, :],
                                                                      func=AF.Exp),
                                 reads=["st"], writes=["PT"])
                            for n_, (kt, ty) in enumerate(kts):
                                P.op("pe", lambda e, n_=n_, kt=kt, hh=hh, nk=nk: e.matmul(
                                    num[:, hh, 0:65], lhsT=PT[:, n_, :], rhs=vaug[:, kt, hh, :],
                                    start=(n_ == 0), stop=(n_ == nk - 1)),
                                    reads=["PT", ("vaug", kt)], writes=["num"], pe_chain=True)
                            P.op("dve", lambda e, hh=hh: e.reciprocal(out=rden[:, hh:hh + 1], in_=num[:, hh, 64:65]),
                                 reads=["num"], writes=[("rden", hh)])
                            P.op("dve", lambda e, hh=hh: e.tensor_scalar(
                                out=oa[:, hh * 64:(hh + 1) * 64], in0=num[:, hh, 0:64], scalar1=rden[:, hh:hh + 1],
                                scalar2=None, op0=ALU.mult),
                                reads=["num", ("rden", hh)], writes=[("oa", hh)])
                        if dbg_oa is not None:
                            P.dma("sp", dbg_oa[t * 128:(t + 1) * 128, c * 128:(c + 1) * 128], oa[:],
                                  reads=[("oa", 0), ("oa", 1)], writes=[("dbgoa", t % 4)])
                        P.op("pool", lambda e: e.tensor_tensor(out=yab[:], in0=oa[:], in1=sz[:], op=ALU.mult),
                             reads=[("oa", 0), ("oa", 1), "sz"], writes=["yab"])
                        P.op("pe", lambda e: e.transpose(out=tp[:], in_=yab[:], identity=ident_b[:]),
                             reads=["yab", "ident_b"], writes=["tp"])
                        P.op("act", lambda e, t=t: e.copy(out=yTc[:, t * 128:(t + 1) * 128], in_=tp[:]),
                             reads=["tp"], writes=[("yTc", t)])
                    P.dma("sp", yT_d[c], yTc[:], reads=[("yTc", t) for t in range(NT)], writes=[("yT_d", c)])
                    if dbg and "qk" in dbg:
                        stg = SB(es_w, "dbgstg", [128, T_ALL], F32)
                        for ii, (src, nm) in enumerate(((qT, "qT"), (kT, "kT"))):
                            P.op("dve", lambda e, src=src: e.tensor_copy(out=stg[:], in_=src[:]),
                                 reads=[(nm, t) for t in range(NT)], writes=["dbgstg"])
                            P.dma("sp", dbg_d["qk"][ii * 128:(ii + 1) * 128, :], stg[:], reads=["dbgstg"],
                                  writes=[("dbgqk", ii)])
                        P.finish("sp", [("dbgqk", 0), ("dbgqk", 1), ("dbgqk", 2)])
                    P.barrier()
            if dbg_oa is not None:
                P.finish("sp", [("dbgoa", i) for i in range(4)])
            P.barrier()

    def ml_stage(heads, dbg_hb=None):
        with ExitStack() as es:
            rope = SB(es, "rope", [128, 4, 64], F32)
            P.dma("sp", rope[:], rope_d[:, :, :], writes=["rope"])
            bgate = SB(es, "bgate", [128, 16], F32)
            P.dma("act", bgate[:], bgate_d[0:1, :].partition_broadcast(128), writes=["bgate"])
            hnb = SB(es, "hnb", [128, D], F32)
            P.dma("sp", hnb[:], hnb_d[0:1, :].partition_broadcast(128), writes=["hnb"])
            Gt = SB(es, "Gt", [128, NT, 16], F32)
            LF = SB(es, "LF", [128, NT, 8], F32)
            Wt = SB(es, "Wt", [128, NT, 8], F32)
            Bs = SB(es, "Bs", [128, NT, 16], F32)
            EB = SB(es, "EB", [128, NT, 8], F32)
            EBL = SB(es, "EBL", [128, NT, 8], F32)
            with ExitStack() as es_w:
                pg = PS(es_w, "pg", [128, 16])
                wg, wgk = load_w(es_w, win_ab_d, 9216, 16, "g")
                for t in range(NT):
                    proj_tm(wg, wgk, 0, 16, pg, "pg", t)
                    P.op("dve", lambda e, t=t: e.tensor_tensor(out=Gt[:, t, :], in0=pg[:, 0:16], in1=bgate[:],
                                                               op=ALU.add),
                         reads=["pg", "bgate"], writes=[("Gt", t)])
                gkeys = [("Gt", t) for t in range(NT)]
                for d in range(2):
                    P.op("act", lambda e, d=d: e.activation(out=LF[:, :, 4 * d:4 * d + 4],
                                                            in_=Gt[:, :, 4 + 8 * d:8 + 8 * d], func=AF.Exp, scale=-1.0),
                         reads=gkeys, writes=[("LF", d)])
                    P.op("act", lambda e, d=d: e.activation(out=LF[:, :, 4 * d:4 * d + 4],
                                                            in_=LF[:, :, 4 * d:4 * d + 4], func=AF.Ln, bias=1.0),
                         reads=[("LF", d)], writes=[("LF", d)])
                    P.op("dve", lambda e, d=d: e.tensor_scalar(out=LF[:, :, 4 * d:4 * d + 4],
                                                               in0=LF[:, :, 4 * d:4 * d + 4], scalar1=-1.0,
                                                               scalar2=None, op0=ALU.mult),
                         reads=[("LF", d)], writes=[("LF", d)])
                for t in range(NT):
                    P.op("pe", lambda e, t=t: e.matmul(pg[:, 0:4], lhsT=consts[:, 1, :], rhs=LF[:, t, 0:4],
                                                       start=True, stop=True),
                         reads=["consts", ("LF", 0)], writes=["pg"])
                    P.op("pe", lambda e, t=t: e.matmul(pg[:, 4:8], lhsT=consts[:, 2, :], rhs=LF[:, t, 4:8],
                                                       start=True, stop=True),
                         reads=["consts", ("LF", 1)], writes=["pg"], pe_chain=True)
                    P.op("pe", lambda e, t=t: e.matmul(pg[:, 8:16], lhsT=ones_f[:], rhs=LF[:, t, 0:8],
                                                       start=True, stop=True),
                         reads=["ones_f", ("LF", 0), ("LF", 1)], writes=["pg"], pe_chain=True)
                    for d in range(2):
                        P.op("dve", lambda e, t=t, d=d: e.tensor_tensor(
                            out=Wt[:, t, 4 * d:4 * d + 4], in0=Gt[:, t, 8 * d:8 * d + 4], in1=pg[:, 4 * d:4 * d + 4],
                            op=ALU.subtract),
                            reads=["pg", ("Gt", t)], writes=[("Wt", t)])
                    P.op("act", lambda e, t=t: e.copy(out=Bs[:, t, :], in_=pg[:, 0:16]),
                         reads=["pg"], writes=[("Bs", t)])
                P.op("act", lambda e: e.activation(out=Wt[:], in_=Wt[:], func=AF.Exp),
                     reads=[("Wt", t) for t in range(NT)], writes=["Wt"])
                P.op("act", lambda e: e.activation(out=EB[:], in_=Bs[:, :, 0:8], func=AF.Exp),
                     reads=[("Bs", t) for t in range(NT)], writes=["EB"])
                P.op("act", lambda e: e.activation(out=EBL[:], in_=Bs[:, :, 8:16], func=AF.Exp),
                     reads=[("Bs", t) for t in range(NT)], writes=["EBL"])
                P.barrier()
            for h in heads:
                with ExitStack() as es_h:
                    qT = SB(es_h, "mqT", [128, 2, T_ALL], BF16)
                    kT = SB(es_h, "mkT", [128, 2, T_ALL], BF16)
                    vaug = SB(es_h, "mvaug", [128, NT, 257], BF16)
                    P.op("pool", lambda e: e.memset(vaug[:], 1.0), writes=[("mv", t) for t in range(NT)])
                    with ExitStack() as es_p:
                        pp = PS(es_p, "mpp", [128, 512])
                        pp2 = PS(es_p, "mpp2", [128, 512])
                        t1 = SB(es_p, "t1", [128, 512], F32)
                        t2 = SB(es_p, "t2", [128, 512], F32)
                        for (dst, dn, cbase, ti) in ((qT, "mqT", 4096 + h * 256, 0), (kT, "mkT", 5120 + h * 256, 2)):
                            for cch in range(2):
                                with ExitStack() as es_w:
                                    c0 = cbase + cch * 128
                                    w, wk_ = load_w(es_w, win_ab_d, c0, 128, "a")
                                    wsw, wswk = load_w(es_w, win_ab_d, 0, 0, "b", segs=[(c0 + 64, 64), (c0, 64)])
                                    for (t0, n) in TOKG:
                                        okeys = [(dn, cch, t) for t in range(t0 // 128, (t0 + n) // 128)]
                                        proj_fm(w, wk_, 0, pp, "mpp", t0, n)
                                        if t0 < 256:
                                            P.op("act", lambda e, n=n, t0=t0, dst=dst, cch=cch, ti=ti: e.mul(
                                                out=dst[:, cch, t0:t0 + n], in_=pp[:, 0:n],
                                                mul=(1.0 if ti == 0 else 1.0 / 16)),
                                                reads=["mpp"], writes=okeys)
                                            continue
                                        proj_fm(wsw, wswk, 0, pp2, "mpp2", t0, n)
                                        r0 = (t0 - 256) // 64
                                        if cch == 0:
                                            cosv = rope[:, ti, r0:r0 + 8].unsqueeze(2).broadcast_to([128, 8, 64])
                                            sinv = rope[:, ti + 1, r0:r0 + 8].unsqueeze(2).broadcast_to([128, 8, 64])
                                        else:
                                            cosv = rope[:, ti, :].unsqueeze(1).broadcast_to([128, 8, 64])
                                            sinv = rope[:, ti + 1, :].unsqueeze(1).broadcast_to([128, 8, 64])
                                        P.op("dve", lambda e, cosv=cosv: e.tensor_tensor(
                                            out=t1[:].rearrange("p (r c) -> p r c", r=8),
                                            in0=pp[:].rearrange("p (r c) -> p r c", r=8), in1=cosv, op=ALU.mult),
                                            reads=["mpp", "rope"], writes=["t1"])
                                        P.op("dve", lambda e, sinv=sinv: e.tensor_tensor(
                                            out=t2[:].rearrange("p (r c) -> p r c", r=8),
                                            in0=pp2[:].rearrange("p (r c) -> p r c", r=8), in1=sinv, op=ALU.mult),
                                            reads=["mpp2", "rope"], writes=["t2"])
                                        P.op("pool", lambda e, t0=t0, dst=dst, cch=cch: e.tensor_tensor(
                                            out=dst[:, cch, t0:t0 + 512], in0=t1[:], in1=t2[:], op=ALU.add),
                                            reads=["t1", "t2"], writes=okeys)
                                    P.barrier()
                        with ExitStack() as es_w:
                            wv, wvk = load_w(es_w, win_ab_d, 6144 + h * 256, 256, "a")
                            for t in range(NT):
                                proj_tm(wv, wvk, 0, 256, pp, "mpp", t)
                                P.op("act", lambda e, t=t: e.copy(out=vaug[:, t, 0:256], in_=pp[:, 0:256]),
                                     reads=["mpp"], writes=[("mv", t)])
                            P.barrier()
                    if dbg and "mqk" in dbg:
                        with ExitStack() as es_d:
                            stg = SB(es_d, "dbgstg", [128, T_ALL], F32)
                            for ii, (src, nm, cch) in enumerate(((qT, "mqT", 0), (qT, "mqT", 1), (kT, "mkT", 0), (kT, "mkT", 1))):
                                P.op("dve", lambda e, src=src, cch=cch: e.tensor_copy(out=stg[:], in_=src[:, cch, :]),
                                     reads=[(nm, cch, t) for t in range(NT)], writes=["dbgstg"])
                                P.dma("sp", dbg_d["mqk"][ii * 128:(ii + 1) * 128, :], stg[:], reads=["dbgstg"],
                                      writes=[("dbgqk", ii)])
                            P.finish("sp", [("dbgqk", ii) for ii in range(4)])
                            P.barrier()
                    with ExitStack() as es_c:
                        wo, wok = load_w(es_c, win_ab_d, 0, 0, "oz", segs=[(7168 + h * 256, 256), (8192 + h * 256, 256)])
                        Cf = SB(es_c, "Cf", [128, 2, 257], F32)
                        Ct = SB(es_c, "Ct", [128, 2, 257], F32)
                        Cb = SB(es_c, "Cb", [128, 2, 257], BF16)
                        STm = SB(es_c, "STm", [128, 128], BF16)
                        kt = SB(es_c, "kt", [128, 256], BF16)
                        sm = SB(es_c, "sm", [128, 4], F32)
                        Hin = SB(es_c, "Hin", [128, 256], F32)
                        Hs = SB(es_c, "Hs", [128, 256], F32)
                        sig = SB(es_c, "sig", [128, 256], F32)
                        szb = SB(es_c, "szb", [128, 256], F32)
                        hbg = SB(es_c, "hbg", [128, 256], F32)
                        ybf = SB(es_c, "ybf", [128, 256], BF16)
                        ysb = SB(es_c, "ysb", [128, 2, 128], BF16)
                        STp = PS(es_c, "STp", [128, 128])
                        acc = PS(es_c, "acc", [128, 512])
                        cacc = PS(es_c, "cacc", [128, 2, 512])
                        po = PS(es_c, "po", [128, 512])
                        tpk = PS(es_c, "tpk", [128, 2, 128], BF16)
                        tpy = PS(es_c, "tpy", [128, 2, 128], BF16)
                        for d in range(2):
                            hd = 4 * d + h
                            order = [0, 1] + list(range(2, NT)) if d == 0 else [1, 0] + list(range(NT - 1, 1, -1))
                            P.op("dve", lambda e: e.memset(Cf[:], 0.0), writes=["Cf"])
                            P.op("pool", lambda e: e.memset(Cb[:], 0.0), writes=["Cb"])
                            for t in order:
                                ts_ = slice(t * 128, (t + 1) * 128)
                                for c in range(2):
                                    P.op("pe", lambda e, c=c: e.matmul(STp[:], lhsT=kT[:, c, ts_], rhs=qT[:, c, ts_],
                                                                       start=(c == 0), stop=(c == 1)),
                                         reads=[("mkT", c, t), ("mqT", c, t)], writes=["STp"], pe_chain=True)
                                P.op("dve", lambda e, d=d, t=t, hd=hd: e.scalar_tensor_tensor(
                                    out=STm[:], in0=STp[:], scalar=Wt[:, t, hd:hd + 1], in1=consts[:, 1 + d, :],
                                    op0=ALU.mult, op1=ALU.mult),
                                    reads=["STp", "Wt", "consts"], writes=["STm"])
                                P.op("pe", lambda e, t=t: e.matmul(acc[:, 0:257], lhsT=STm[:], rhs=vaug[:, t, :],
                                                                   start=True, stop=False),
                                     reads=["STm", ("mv", t)], writes=["acc"])
                                for c in range(2):
                                    P.op("pe", lambda e, c=c: e.matmul(acc[:, 0:257], lhsT=qT[:, c, ts_], rhs=Cb[:, c, :],
                                                                       start=False, stop=(c == 1)),
                                         reads=[("mqT", c, t), "Cb"], writes=["acc"], pe_chain=True)
                                P.op("act", lambda e, t=t, hd=hd: e.activation(
                                    out=sm[:, 0:1], in_=acc[:, 256:257], func=AF.Abs, scale=EB[:, t, hd:hd + 1]),
                                    reads=["acc", "EB"], writes=["sm"])
                                P.op("dve", lambda e: e.tensor_scalar(out=sm[:, 1:2], in0=sm[:, 0:1], scalar1=1.0,
                                                                      scalar2=None, op0=ALU.max),
                                     reads=["sm"], writes=["sm"])
                                P.op("dve", lambda e: e.reciprocal(out=sm[:, 2:3], in_=sm[:, 1:2]),
                                     reads=["sm"], writes=["sm"])
                                P.op("dve", lambda e, t=t, hd=hd: e.tensor_tensor(
                                    out=sm[:, 3:4], in0=sm[:, 2:3], in1=EB[:, t, hd:hd + 1], op=ALU.mult),
                                    reads=["sm", "EB"], writes=["sm"])
                                if d == 0:
                                    P.op("act", lambda e: e.activation(out=Hs[:], in_=acc[:, 0:256], func=AF.Copy,
                                                                       scale=sm[:, 3:4]),
                                         reads=["acc", "sm"], writes=["Hs"])
                                    P.dma("sp", Hf_d[t], Hs[:], reads=["Hs"], writes=[("Hf_d", t)])
                                else:
                                    P.dma("sp", Hin[:], Hf_d[t], reads=[("Hf_d", t)], writes=["Hin"])
                                    P.op("dve", lambda e: e.scalar_tensor_tensor(
                                        out=Hs[:], in0=acc[:, 0:256], scalar=sm[:, 3:4], in1=Hin[:],
                                        op0=ALU.mult, op1=ALU.add),
                                        reads=["acc", "sm", "Hin"], writes=["Hs"])
                                    if dbg_hb is not None:
                                        P.dma("act", dbg_hb[t * 128:(t + 1) * 128, h * 256:(h + 1) * 256], Hs[:],
                                              reads=["Hs"], writes=[("dbghb", t % 4)])
                                    proj_tm(wo, wok, 0, 512, po, "po", t)
                                    P.op("act", lambda e: e.activation(out=sig[:], in_=po[:, 0:256], func=AF.Sigmoid),
                                         reads=["po"], writes=["sig"])
                                    P.op("act", lambda e: e.activation(out=szb[:], in_=po[:, 256:512], func=AF.Silu),
                                         reads=["po"], writes=["szb"])
                                    P.op("pool", lambda e: e.tensor_tensor(out=hbg[:], in0=Hs[:], in1=sig[:],
                                                                           op=ALU.mult),
                                         reads=["Hs", "sig"], writes=["hbg"])
                                    P.op("act", lambda e: e.activation(out=sig[:], in_=hbg[:], func=AF.Square,
                                                                       accum_out=sm[:, 0:1]),
                                         reads=["hbg", "sm"], writes=["sig", "sm"])
                                    P.op("act", lambda e: e.activation(out=sm[:, 1:2], in_=sm[:, 0:1], func=AF.Sqrt,
                                                                       scale=1.0 / 256, bias=EPS),
                                         reads=["sm"], writes=["sm"])
                                    P.op("dve", lambda e: e.reciprocal(out=sm[:, 2:3], in_=sm[:, 1:2]),
                                         reads=["sm"], writes=["sm"])
                                    P.op("dve", lambda e, h=h: e.scalar_tensor_tensor(
                                        out=hbg[:], in0=hbg[:], scalar=sm[:, 2:3], in1=hnb[:, h * 256:(h + 1) * 256],
                                        op0=ALU.mult, op1=ALU.mult),
                                        reads=["hbg", "sm", "hnb"], writes=["hbg"])
                                    P.op("pool", lambda e: e.tensor_tensor(out=ybf[:], in0=hbg[:], in1=szb[:],
                                                                           op=ALU.mult),
                                         reads=["hbg", "szb"], writes=["ybf"])
                                    for c in range(2):
                                        P.op("pe", lambda e, c=c: e.transpose(out=tpy[:, c, :],
                                                                              in_=ybf[:, c * 128:(c + 1) * 128],
                                                                              identity=ident_b[:]),
                                             reads=["ybf", "ident_b"], writes=["tpy"], pe_chain=True)
                                    P.op("act", lambda e: e.copy(out=ysb[:], in_=tpy[:]), reads=["tpy"], writes=["ysb"])
                                    P.dma("sp", yT_d[8 + 2 * h:10 + 2 * h, :, ts_].rearrange("c p t -> p c t"), ysb[:],
                                          reads=["ysb"], writes=[("yT_d", 8 + 2 * h, t)])
                                for c in range(2):
                                    P.op("pe", lambda e, c=c: e.transpose(out=tpk[:, c, :], in_=kT[:, c, ts_],
                                                                          identity=ident_b[:]),
                                         reads=[("mkT", c, t), "ident_b"], writes=["tpk"], pe_chain=True)
                                P.op("dve", lambda e, t=t, hd=hd: e.tensor_scalar(
                                    out=kt[:], in0=tpk[:].rearrange("p c k -> p (c k)"), scalar1=Wt[:, t, hd:hd + 1],
                                    scalar2=None, op0=ALU.mult),
                                    reads=["tpk", "Wt"], writes=["kt"])
                                for c in range(2):
                                    P.op("pe", lambda e, c=c, t=t: e.matmul(cacc[:, c, 0:257],
                                                                            lhsT=kt[:, c * 128:(c + 1) * 128],
                                                                            rhs=vaug[:, t, :], start=True, stop=True),
                                         reads=["kt", ("mv", t)], writes=["cacc"], pe_chain=(c == 1))
                                P.op("dve", lambda e: e.tensor_tensor(out=Ct[:], in0=cacc[:, :, 0:257], in1=Cf[:],
                                                                      op=ALU.add),
                                     reads=["cacc", "Cf"], writes=["Ct"])
                                P.op("dve", lambda e, t=t, hd=hd: e.tensor_scalar(
                                    out=Cf[:], in0=Ct[:], scalar1=EBL[:, t, hd:hd + 1], scalar2=None, op0=ALU.mult),
                                    reads=["Ct", "EBL"], writes=["Cf"])
                                P.op("pool", lambda e, t=t, hd=hd: e.tensor_scalar(
                                    out=Cb[:], in0=Ct[:], scalar1=EBL[:, t, hd:hd + 1], scalar2=None, op0=ALU.mult),
                                    reads=["Ct", "EBL"], writes=["Cb"])
                        if dbg_hb is not None:
                            P.finish("act", [("dbghb", i) for i in range(4)])
                        P.barrier()
            P.barrier()

    def out_stage(wout_d, xsrc, csrc, gates, dst_x, dst_c, outkey):
        with ExitStack() as es:
            wo_b = SB(es, "wo_b", [128, 16, D], BF16)
            for q4 in range(4):
                for kg in range(2):
                    for k in range(8):
                        kc = kg * 8 + k
                        P.dma("sp" if k % 2 == 0 else "act", wstage[:, k, :],
                              wout_d[kc * 128:(kc + 1) * 128, q4 * 256:(q4 + 1) * 256], writes=[("wstage", k)],
                              semkey=("wstage", k, 0))
                        P.op("pool" if k % 2 == 0 else "dve", lambda e, k=k, kc=kc, q4=q4: e.tensor_copy(
                            out=wo_b[:, kc, q4 * 256:(q4 + 1) * 256], in_=wstage[:, k, :]),
                            reads=[("wstage", k)], writes=[("wo_b", kc, q4)])
            yt = [SB(es, "yt%d" % i, [128, 16, 128], BF16) for i in range(2)]
            xt = [SB(es, "xt%d" % i, [128, D], F32) for i in range(2)]
            tmp = SB(es, "otmp", [128, D], F32)
            xo = [SB(es, "xo%d" % i, [128, D], F32) for i in range(2)]
            py = [PS(es, "py%d" % i, [128, 512]) for i in range(2)]
            tiles = list(range(NT)) if dst_c is not None else list(range(2, NT))
            for n_, t in enumerate(tiles):
                i = n_ % 2
                j = 1 if t < 2 else 0
                ts_ = slice(t * 128, (t + 1) * 128)
                P.dma("sp", yt[i][:], yT_d[:, :, ts_].rearrange("c p t -> p c t"),
                      reads=[("yT_d", c) for c in range(16)], writes=[("yt", i)])
                src = csrc[ts_, :] if t < 2 else xsrc[(t - 2) * 128:(t - 1) * 128, :]
                P.dma("act", xt[i][:], src, reads=[("x1", t)] if outkey == "out" else [], writes=[("xt", i)])
                for half in range(2):
                    for kc in range(16):
                        P.op("pe", lambda e, kc=kc, half=half, i=i: e.matmul(
                            py[half][:], lhsT=yt[i][:, kc, :], rhs=wo_b[:, kc, half * 512:(half + 1) * 512],
                            start=(kc == 0), stop=(kc == 15)),
                            reads=[("yt", i), ("wo_b", kc, 2 * half), ("wo_b", kc, 2 * half + 1)], writes=[("py", half)], pe_chain=True)
                    hs = slice(half * 512, (half + 1) * 512)
                    P.op("dve", lambda e, half=half, hs=hs, j=j: e.tensor_tensor(
                        out=tmp[:, hs], in0=py[half][:], in1=gates[j][:, hs], op=ALU.mult),
                        reads=[("py", half), ("gate", j)], writes=[("otmp", half)])
                    P.op("pool", lambda e, hs=hs, i=i: e.tensor_tensor(out=xo[i][:, hs], in0=tmp[:, hs],
                                                                       in1=xt[i][:, hs], op=ALU.add),
                         reads=[("otmp", half), ("xt", i)], writes=[("xo", i, half)])
                dst = dst_c[ts_, :] if t < 2 else dst_x[(t - 2) * 128:(t - 1) * 128, :]
                P.dma("sp", dst, xo[i][:], reads=[("xo", i, 0), ("xo", i, 1)], writes=[(outkey, t)],
                      semkey=("xo", i))
            P.barrier()

    def hg_stage(heads, dbg_o=None):
        from itertools import zip_longest
        NCH = T_ALL // 64
        ORD = [list(range(NCH)), [3, 2, 1, 0] + list(range(NCH - 1, 3, -1))]
        POS = [{j: p for p, j in enumerate(o_)} for o_ in ORD]
        hgstop = (dbg or {}).get("_hgstop")

        def rev(ap2d, lo, n):
            v = ap2d[:, lo:lo + n]
            return bass.AP(v.tensor, v.offset + (n - 1) * v.ap[-1][0], [list(v.ap[0]), [-v.ap[-1][0], n]])

        with ExitStack() as es:
            lbc = SB(es, "lbc", [128, 16, 2], F32)
            P.dma("sp", lbc[:], lbc_d[:, :, :], writes=["lbc"])
            lb = SB(es, "lb", [128, 16], F32)
            oml = SB(es, "oml", [128, 16], F32)
            P.op("dve", lambda e: e.tensor_tensor(out=lb[:], in0=lbc[:, :, 1], in1=lbc[:, :, 0], op=ALU.subtract),
                 reads=["lbc"], writes=["lb"])
            P.op("act", lambda e: e.activation(out=lb[:], in_=lb[:], func=AF.Sigmoid), reads=["lb"], writes=["lb"])
            P.op("dve", lambda e: e.tensor_scalar(out=oml[:], in0=lb[:], scalar1=-1.0, scalar2=1.0, op0=ALU.mult,
                                                  op1=ALU.add), reads=["lb"], writes=["oml"])
            smask = SB(es, "smask", [128, 512], F32)
            P.dma("sp", smask[:], smask_d[:, :], writes=["smask"])
            for h in heads:
                with ExitStack() as es_h:
                    hnh = SB(es_h, "hnh", [128, 128], F32)
                    P.dma("act", hnh[:], hnc_d[0:1, h * 128:(h + 1) * 128].partition_broadcast(128), writes=["hnh"])
                    qf = SB(es_h, "gqf", [128, T_ALL], BF16)
                    kf = SB(es_h, "gkf", [128, T_ALL], BF16)
                    qb = SB(es_h, "gqb", [128, T_ALL], BF16)
                    kb = SB(es_h, "gkb", [128, T_ALL], BF16)
                    QK = ((qf, "gqf", kf, "gkf"), (qb, "gqb", kb, "gkb"))
                    ebj = SB(es_h, "ebj", [128, 2, NCH], F32)
                    vtok = SB(es_h, "gv", [128, NT, 128], BF16)
                    SH = [SB(es_h, "SH%d" % d, [128, NCH, 128], BF16) for d in range(2)]
                    with ExitStack() as es_p:
                        pp = [PS(es_p, "gpp%d" % i, [128, 512]) for i in range(3)]
                        wq, wqk = load_w(es_p, win_c_d, h * 128, 128, "gq")
                        wf_ = [load_w(es_p, win_c_d, 2048 + h * 128, 128, "gff"),
                               load_w(es_p, win_c_d, 4096 + h * 128, 128, "gfb")]
                        qs = SB(es_p, "gqs", [128, 512], F32)
                        A = [SB(es_p, "gA%d" % d, [128, 512], F32) for d in range(2)]
                        B = [SB(es_p, "gB%d" % d, [128, 512], F32) for d in range(2)]
                        C = [SB(es_p, "gC%d" % d, [128, 512], F32) for d in range(2)]
                        E = [SB(es_p, "gE%d" % d, [128, 512], F32) for d in range(2)]
                        TL = SB(es_p, "gTL", [128, 8], F32)
                        for (t0, n) in TOKG:
                            nch = n // 64
                            j0 = t0 // 64
                            tk = [t for t in range(t0 // 128, (t0 + n) // 128)]
                            proj_fm(wq, wqk, 0, pp[2], "gpp2", t0, n)
                            P.op("act", lambda e, n=n: e.activation(out=qs[:, 0:n], in_=pp[2][:, 0:n], func=AF.Silu),
                                 reads=["gpp2"], writes=["gqs"])
                            steps = [[], []]
                            for d in range(2):
                                (w_, wk_) = wf_[d]
                                (qd, qn, kd, kn) = QK[d]
                                Ad, Bd, Cd, Ed = A[d], B[d], C[d], E[d]
                                kA, kB, kC, kE, kP = "gA%d" % d, "gB%d" % d, "gC%d" % d, "gE%d" % d, "gpp%d" % d
                                C3 = Cd[:, 0:n].rearrange("p (c l) -> p c l", l=64)
                                L = steps[d]
                                L.append(lambda w_=w_, wk_=wk_, d=d, kP=kP: proj_fm(w_, wk_, 0, pp[d], kP, t0, n))
                                L.append(lambda Ad=Ad, d=d, kP=kP, kA=kA: P.op(
                                    "act", lambda e: e.activation(out=Ad[:, 0:n], in_=pp[d][:, 0:n], func=AF.Sigmoid),
                                    reads=[kP], writes=[kA]))
                                L.append(lambda Ad=Ad, kA=kA: P.op("dve", lambda e: e.tensor_scalar(
                                    out=Ad[:, 0:n], in0=Ad[:, 0:n], scalar1=oml[:, h:h + 1], scalar2=lb[:, h:h + 1],
                                    op0=ALU.mult, op1=ALU.add), reads=[kA, "oml", "lb"], writes=[kA]))
                                L.append(lambda Ad=Ad, Bd=Bd, kA=kA, kB=kB: P.op(
                                    "act", lambda e: e.activation(out=Bd[:, 0:n], in_=Ad[:, 0:n], func=AF.Ln),
                                    reads=[kA], writes=[kB]))
                                L.append(lambda Ad=Ad, kA=kA, kB=kB: P.op("pool", lambda e: e.tensor_scalar(
                                    out=Ad[:, 0:n], in0=Ad[:, 0:n], scalar1=-1.0, scalar2=1.0, op0=ALU.mult,
                                    op1=ALU.add), reads=[kA, kB], writes=[kA]))
                                L.append(lambda Bd=Bd, Cd=Cd, kB=kB, kC=kC: P.op("dve", lambda e: e.tensor_tensor_scan(
                                    out=Cd[:, 0:n], data0=smask[:, 0:n], data1=Bd[:, 0:n], initial=0.0,
                                    op0=ALU.mult, op1=ALU.add), reads=["smask", kB], writes=[kC]))
                                if d == 1:
                                    L.append(lambda Bd=Bd, Cd=Cd, kB=kB, kC=kC: P.op("pool", lambda e: e.tensor_tensor(
                                        out=Bd[:, 0:n], in0=Bd[:, 0:n], in1=Cd[:, 0:n], op=ALU.subtract),
                                        reads=[kB, kC], writes=[kB]))
                                    L.append(lambda C3=C3, kC=kC: P.op(
                                        "act", lambda e: e.copy(out=TL[:, 0:nch], in_=C3[:, :, 63]),
                                        reads=[kC], writes=["gTL"]))
                                    L.append(lambda Bd=Bd, C3=C3, kB=kB, kC=kC: P.op("dve", lambda e: e.tensor_tensor(
                                        out=C3, in0=Bd[:, 0:n].rearrange("p (c l) -> p c l", l=64),
                                        in1=TL[:, 0:nch].unsqueeze(2).broadcast_to([128, nch, 64]), op=ALU.add),
                                        reads=[kB, "gTL"], writes=[kC]))
                                    L.append(lambda: P.op("act", lambda e: e.activation(
                                        out=ebj[:, 1, j0:j0 + nch], in_=TL[:, 0:nch], func=AF.Exp),
                                        reads=["gTL"], writes=[("ebj", 1, t0)]))
                                else:
                                    L.append(lambda C3=C3, kC=kC: P.op("act", lambda e: e.activation(
                                        out=ebj[:, 0, j0:j0 + nch], in_=C3[:, :, 63], func=AF.Exp),
                                        reads=[kC], writes=[("ebj", 0, t0)]))
                                L.append(lambda Cd=Cd, Ed=Ed, kC=kC, kE=kE: P.op(
                                    "act", lambda e: e.activation(out=Ed[:, 0:n], in_=Cd[:, 0:n], func=AF.Exp),
                                    reads=[kC], writes=[kE]))
                                L.append(lambda Ed=Ed, qd=qd, qn=qn, kE=kE: P.op("dve", lambda e: e.tensor_tensor(
                                    out=qd[:, t0:t0 + n], in0=qs[:, 0:n], in1=Ed[:, 0:n], op=ALU.mult),
                                    reads=["gqs", kE], writes=[(qn, t) for t in tk]))
                                L.append(lambda Bd=Bd, Ed=Ed, kB=kB, kE=kE: P.op(
                                    "dve", lambda e: e.reciprocal(out=Bd[:, 0:n], in_=Ed[:, 0:n]),
                                    reads=[kE], writes=[kB]))
                                L.append(lambda Ad=Ad, Bd=Bd, kd=kd, kn=kn, kA=kA, kB=kB: P.op(
                                    "pool", lambda e: e.tensor_tensor(out=kd[:, t0:t0 + n], in0=Ad[:, 0:n],
                                                                      in1=Bd[:, 0:n], op=ALU.mult),
                                    reads=[kA, kB], writes=[(kn, t) for t in tk]))
                            for fa, fb in zip_longest(steps[0], steps[1]):
                                if fa is not None:
                                    fa()
                                if fb is not None:
                                    fb()
                        P.barrier()
                    if hgstop == "prep":
                        continue
                    with ExitStack() as es_p:
                        pv = [PS(es_p, "gpv%d" % i, [128, 128]) for i in range(2)]
                        wv, wvk = load_w(es_p, win_c_d, 6144 + h * 128, 128, "gv")
                        for t in range(NT):
                            sl = t % 2
                            proj_tm(wv, wvk, 0, 128, pv[sl], ("gpv", sl), t)
                            P.op("act", lambda e, t=t, sl=sl: e.copy(out=vtok[:, t, :], in_=pv[sl][:, 0:128]),
                                 reads=[("gpv", sl)], writes=[("gv", t)])
                        P.barrier()
                    if hgstop == "v":
                        continue
                    with ExitStack() as es_u:
                        NVB = 16
                        U = SB(es_u, "gU", [128, 128, NCH], F32)
                        ebpo = SB(es_u, "ebpo", [128, NCH], F32)
                        ebrep = SB(es_u, "ebrep", [128, NVB, NCH], F32)
                        ktk = [SB(es_u, "gktk%d" % i, [128, 128], BF16) for i in range(2)]
                        tpk = [PS(es_u, "gtpk%d" % i, [128, 128], BF16) for i in range(2)]
                        Ups = [[PS(es_u, "gUps%d%d" % (i, hf), [128, 128]) for hf in range(2)] for i in range(2)]
                        for d in range(2):
                            (qd, qn, kd, kn) = QK[d]
                            if d == 0:
                                P.op("pool", lambda e: e.tensor_copy(out=ebpo[:], in_=ebj[:, 0, :]), writes=["ebpo"])
                            else:
                                P.op("pool", lambda e: e.tensor_copy(out=ebpo[:, 0:4], in_=rev(ebj[:, 1, :], 0, 4)),
                                     writes=["ebpo"])
                                P.op("pool", lambda e: e.tensor_copy(out=ebpo[:, 4:NCH], in_=rev(ebj[:, 1, :], 4, NCH - 4)),
                                     reads=["ebpo"], writes=["ebpo"])
                            P.op("pool", lambda e: e.memset(ebpo[:, 0:1], 0.0), reads=["ebpo"], writes=["ebpo"])
                            P.op("pool", lambda e: e.tensor_copy(
                                out=ebrep[:], in_=ebpo[:].unsqueeze(1).broadcast_to([128, NVB, NCH])),
                                reads=["ebpo"], writes=["ebrep"])
                            for t in range(NT):
                                i = t % 2
                                ts_ = slice(t * 128, (t + 1) * 128)
                                P.op("pe", lambda e, i=i: e.transpose(out=tpk[i][:], in_=kd[:, ts_], identity=ident_b[:]),
                                     reads=[(kn, t), "ident_b"], writes=[("gtpk", i)])
                                P.op("act", lambda e, i=i: e.copy(out=ktk[i][:], in_=tpk[i][:]),
                                     reads=[("gtpk", i)], writes=[("gktk", i)])
                                for hf in range(2):
                                    rs_ = slice(64 * hf, 64 * hf + 64)
                                    P.op("pe", lambda e, i=i, hf=hf, rs_=rs_: e.matmul(
                                        Ups[i][hf][:], lhsT=ktk[i][rs_, :], rhs=vtok[rs_, t, :], start=True,
                                        stop=True),
                                        reads=[("gktk", i), ("gv", t)], writes=[("gUps", i, hf)])
                                for hf in range(2):
                                    j = 2 * t + hf
                                    p_ = POS[d][j]
                                    if hf == 0:
                                        P.op("act", lambda e, i=i, hf=hf, p_=p_, d=d, j=j: e.activation(
                                            out=U[:, :, p_], in_=Ups[i][hf][:], func=AF.Copy, scale=ebj[:, d, j:j + 1]),
                                            reads=[("gUps", i, hf)], writes=[("gUp", p_)])
                                    else:
                                        P.op("dve", lambda e, i=i, hf=hf, p_=p_, d=d, j=j: e.tensor_scalar(
                                            out=U[:, :, p_], in0=Ups[i][hf][:], scalar1=ebj[:, d, j:j + 1],
                                            scalar2=None, op0=ALU.mult),
                                            reads=[("gUps", i, hf)], writes=[("gUp", p_)])
                            ukeys = [("gUp", p_) for p_ in range(NCH)]
                            for vb in range(128 // NVB):
                                Ub = U[:, vb * NVB:(vb + 1) * NVB, :].rearrange("p v j -> p (v j)")
                                P.op("dve", lambda e, Ub=Ub: e.tensor_tensor_scan(
                                    out=Ub, data0=ebrep[:].rearrange("p v j -> p (v j)"), data1=Ub, initial=0.0,
                                    op0=ALU.mult, op1=ALU.add),
                                    reads=(ukeys + ["ebrep"] if vb == 0 else []), writes=[("gUs", vb)])
                            skeys = [("gUs", vb) for vb in range(128 // NVB)]
                            hn = NCH // 2
                            Uperm = U[:].rearrange("p v j -> p j v")
                            P.op("act", lambda e, d=d: e.copy(out=SH[d][:, 0:hn, :], in_=Uperm[:, 0:hn, :]),
                                 reads=skeys, writes=[("SH", d, 0)])
                            P.op("pool", lambda e, d=d: e.tensor_copy(out=SH[d][:, hn:NCH, :], in_=Uperm[:, hn:NCH, :]),
                                 reads=skeys, writes=[("SH", d, 1)])
                            tk_ = list(P.lastw.get(("SH", d, 0), [])) + list(P.lastw.get(("SH", d, 1), []))
                            for p_ in range(NCH):
                                P.lastw[("gUp", p_)] = list(tk_)
                                P.readers[("gUp", p_)] = {}
                        P.barrier()
                    if hgstop == "u":
                        continue
                    with ExitStack() as es_c:
                        wz, wzk = load_w(es_c, win_c_d, 8192 + h * 128, 128, "gz")
                        NB = 2
                        AT = [SB(es_c, "gAT%d" % i, [128, 2, 4, 128], BF16) for i in range(NB)]
                        ot = [SB(es_c, "got%d" % i, [128, 4, 128], F32) for i in range(NB)]
                        sq = [SB(es_c, "gsq%d" % i, [128, 4, 128], F32) for i in range(NB)]
                        szl = [SB(es_c, "gszl%d" % i, [128, 4, 128], F32) for i in range(NB)]
                        yv = [SB(es_c, "gyv%d" % i, [128, 4, 128], F32) for i in range(NB)]
                        ybf = [SB(es_c, "gybf%d" % i, [128, 4, 128], BF16) for i in range(NB)]
                        ysb = [SB(es_c, "gysb%d" % i, [128, 4, 128], BF16) for i in range(NB)]
                        sm = [SB(es_c, "gsm%d" % i, [128, 3, 4], F32) for i in range(NB)]
                        pA = PS(es_c, "gpA", [128, 2, 4, 128])
                        po = PS(es_c, "gpo", [128, 4, 2, 128])
                        pz = PS(es_c, "gpz", [128, 4, 128])
                        tpy = PS(es_c, "gtpy", [128, 4, 128], BF16)
                        for g in range(8):
                            i = g % NB
                            tl = [2 + 4 * g + tt for tt in range(4)]
                            t0 = tl[0] * 128
                            for d in range(2):
                                (qd, qn, kd, kn) = QK[d]
                                for tt, t in enumerate(tl):
                                    ts_ = slice(t * 128, (t + 1) * 128)
                                    P.op("pe", lambda e, d=d, tt=tt, ts_=ts_, kd=kd, qd=qd: e.matmul(
                                        pA[:, d, tt, :], lhsT=kd[:, ts_], rhs=qd[:, ts_], start=True, stop=True),
                                        reads=[(kn, t), (qn, t)], writes=["gpA"], pe_chain=(d + tt > 0))
                            for d in range(2):
                                P.op("dve", lambda e, d=d, i=i: e.tensor_tensor(
                                    out=AT[i][:, d, :, :], in0=pA[:, d, :, :],
                                    in1=consts[:, 3 + d, :].unsqueeze(1).broadcast_to([128, 4, 128]), op=ALU.mult),
                                    reads=["gpA", "consts"], writes=[("gAT", i, d)])
                            first = True
                            for tt, t in enumerate(tl):
                                for hf in range(2):
                                    j = 2 * t + hf
                                    cs_ = slice(j * 64, (j + 1) * 64)
                                    rs_ = slice(64 * hf, 64 * hf + 64)
                                    for d in range(2):
                                        P.op("pe", lambda e, d=d, i=i, hf=hf, tt=tt, rs_=rs_, t=t: e.matmul(
                                            po[0:64, tt, hf, :], lhsT=AT[i][rs_, d, tt, rs_], rhs=vtok[rs_, t, :],
                                            start=(d == 0), stop=False),
                                            reads=[("gAT", i, 0), ("gAT", i, 1), ("gv", t)], writes=["gpo"],
                                            pe_chain=(not first))
                                        first = False
                                    for d in range(2):
                                        (qd, qn, kd, kn) = QK[d]
                                        pm = POS[d][j] - 1
                                        P.op("pe", lambda e, d=d, hf=hf, tt=tt, cs_=cs_, pm=pm, qd=qd: e.matmul(
                                            po[0:64, tt, hf, :], lhsT=qd[:, cs_], rhs=SH[d][:, pm, :],
                                            start=False, stop=(d == 1)),
                                            reads=[(qn, t), ("SH", d, 0), ("SH", d, 1)], writes=["gpo"],
                                            pe_chain=True)
                            for hf in range(2):
                                rs_ = slice(64 * hf, 64 * hf + 64)
                                P.op("act", lambda e, i=i, hf=hf, rs_=rs_: e.copy(out=ot[i][rs_, :, :],
                                                                                  in_=po[0:64, :, hf, :]),
                                     reads=["gpo"], writes=[("got", i, hf)])
                            okeys = [("got", i, 0), ("got", i, 1)]
                            if dbg_o is not None:
                                P.dma("sp", dbg_o[t0:t0 + 512, h * 128:(h + 1) * 128].rearrange("(a p) v -> p a v", p=128),
                                      ot[i][:], reads=okeys, writes=[("dbgo1", g % 4)])
                            P.op("dve", lambda e, i=i: e.tensor_tensor(out=sq[i][:], in0=ot[i][:], in1=ot[i][:],
                                                                       op=ALU.mult),
                                 reads=okeys, writes=[("gsq", i)])
                            P.op("dve", lambda e, i=i: e.tensor_reduce(out=sm[i][:, 0, :], in_=sq[i][:], axis=AX.X,
                                                                       op=ALU.add),
                                 reads=[("gsq", i)], writes=[("gsm", i)])
                            P.op("act", lambda e, i=i: e.activation(out=sm[i][:, 1, :], in_=sm[i][:, 0, :], func=AF.Sqrt,
                                                                    scale=1.0 / 128, bias=EPS),
                                 reads=[("gsm", i)], writes=[("gsm", i)])
                            P.op("dve", lambda e, i=i: e.reciprocal(out=sm[i][:, 2, :], in_=sm[i][:, 1, :]),
                                 reads=[("gsm", i)], writes=[("gsm", i)])
                            for tt, t in enumerate(tl):
                                proj_tm(wz, wzk, 0, 128, pz[:, tt, :], "gpz", t)
                            P.op("act", lambda e, i=i: e.activation(out=szl[i][:], in_=pz[:], func=AF.Silu),
                                 reads=["gpz"], writes=[("gszl", i)])
                            P.op("dve", lambda e, i=i: e.tensor_tensor(
                                out=yv[i][:], in0=ot[i][:], in1=sm[i][:, 2, :].unsqueeze(2).broadcast_to([128, 4, 128]),
                                op=ALU.mult),
                                reads=okeys + [("gsm", i)], writes=[("gyv", i)])
                            P.op("pool", lambda e, i=i: e.tensor_tensor(
                                out=yv[i][:], in0=yv[i][:], in1=hnh[:].unsqueeze(1).broadcast_to([128, 4, 128]),
                                op=ALU.mult),
                                reads=[("gyv", i), "hnh"], writes=[("gyv", i)])
                            P.op("pool", lambda e, i=i: e.tensor_tensor(out=ybf[i][:], in0=yv[i][:], in1=szl[i][:],
                                                                        op=ALU.mult),
                                 reads=[("gyv", i), ("gszl", i)], writes=[("gybf", i)])
                            for tt in range(4):
                                P.op("pe", lambda e, i=i, tt=tt: e.transpose(out=tpy[:, tt, :], in_=ybf[i][:, tt, :],
                                                                             identity=ident_b[:]),
                                     reads=[("gybf", i), "ident_b"], writes=["gtpy"], pe_chain=(tt > 0))
                            P.op("act", lambda e, i=i: e.copy(out=ysb[i][:], in_=tpy[:]), reads=["gtpy"],
                                 writes=[("gysb", i)])
                            P.dma("sp" if i == 0 else "act", yT_d[h, :, t0:t0 + 512],
                                  ysb[i][:].rearrange("p a t -> p (a t)"), reads=[("gysb", i)],
                                  writes=[("yT_d", h, g)], semkey=("gysb", i))
                        if dbg_o is not None:
                            P.finish("sp", [("dbgo1", i) for i in range(4)])
                        P.barrier()
            P.barrier()

    if dbg and "oa" in dbg:
        na_stage(dbg.get("_chunks", [0]), dbg_d["oa"])
    elif dbg and "hb" in dbg:
        ml_stage([0], dbg_d["hb"])
    elif dbg and "x1" in dbg:
        na_stage(list(range(8)))
        ml_stage(list(range(4)))
        out_stage(wout_ab_d, x_d, ctx_d, gate0, x1_d, ctx1_d, "x1")
        P.finish("sp", [("x1", t) for t in range(NT)])
    elif dbg and "o1" in dbg:
        pass
    elif not dbg or "_stop" in dbg:
        stop = (dbg or {}).get("_stop", "end")
        order = ["norm0", "na", "ml", "out0", "norm1", "hg", "end"]
        lvl = order.index(stop)
        if lvl >= 1:
            na_stage(list(range(8)))
        if lvl >= 2:
            ml_stage(list(range(4)))
        if lvl >= 3:
            out_stage(wout_ab_d, x_d, ctx_d, gate0, x1_d, ctx1_d, "x1")
    L0.close()
    P.barrier()
    L1 = ExitStack()
    gate1 = {0: SB(L1, "gate1_0", [128, D], F32)}
    if dbg and "o1" in dbg:
        x1_in = dt_in("x1_in", [T_LAT, D])
        ctx1_in = dt_in("ctx1_in", [T_CTX, D])
        norm_stage(1, x1_in, ctx1_in, gate1)
        hg_stage(dbg.get("_heads", [0]), dbg_d["o1"])
    elif not dbg or "_stop" in dbg:
        stop = (dbg or {}).get("_stop", "end")
        lvl = ["norm0", "na", "ml", "out0", "norm1", "hg", "end"].index(stop)
        if lvl >= 4:
            norm_stage(1, x1_d, ctx1_d, gate1)
        if lvl >= 5:
            hg_stage((dbg or {}).get("_heads", list(range(16))))
        if lvl >= 6:
            out_stage(wout_c_d, x1_d, None, gate1, out_d, None, "out")
            P.finish("sp", [("out", t) for t in range(2, NT)])
        else:
            P.dma("sp", out_d[0:128, :], x_d[0:128, :], writes=["probe_out"])
            P.finish("sp", ["probe_out"])
    L1.close()
    G.close()
    return nc


def host_inputs(inputs, b):
    cc = np.zeros((128, 16), np.float32)
    cb = np.asarray(inputs["c"][b], np.float32).reshape(8, 128)
    cx = np.asarray(inputs["c_ctx"], np.float32).reshape(8, 128)
    for k in range(8):
        cc[:, 2 * k] = cb[k]
        cc[:, 2 * k + 1] = cx[k]
    m = {
        "x": np.ascontiguousarray(inputs["x"][b], dtype=np.float32),
        "ctx": np.ascontiguousarray(inputs["ctx"][b], dtype=np.float32),
        "cc": cc,
        "w_ada": np.ascontiguousarray(inputs["w_ada"], dtype=np.float32),
        "b_ada": np.ascontiguousarray(inputs["b_ada"], dtype=np.float32),
        "norm_w": np.ascontiguousarray(inputs["norm_w"], dtype=np.float32),
        "ident": np.eye(128, dtype=np.float32),
        "consts": host_consts(),
        "w_in_ab": np.ascontiguousarray(inputs["w_in_ab"][0], dtype=np.float32),
        "qkw": np.stack([np.tile(np.asarray(inputs["q_norm_a"][0], np.float32), 2),
                         np.tile(np.asarray(inputs["k_norm_a"][0], np.float32), 2)], axis=1),
        "nab": host_nab(np.asarray(inputs["rpb_a"][0], np.float32)),
        "rope": host_rope(),
        "b_gate": np.asarray(inputs["b_gate_ab"], np.float32).reshape(1, 16),
        "h_norm_b": np.asarray(inputs["h_norm_b"], np.float32).reshape(1, D),
        "w_out_ab": np.ascontiguousarray(inputs["w_out_ab"][0], dtype=np.float32),
        "w_in_c": np.ascontiguousarray(inputs["w_in_c"][0], dtype=np.float32),
        "w_out_c": np.ascontiguousarray(inputs["w_out_c"][0], dtype=np.float32),
        "lbc": np.ascontiguousarray(np.asarray(inputs["lb_c"], np.float32).reshape(2, 16, 128).transpose(2, 1, 0)),
        "h_norm_c": np.asarray(inputs["h_norm_c"], np.float32).reshape(1, 2 * D),
        "smask": np.tile((np.arange(512) % 64 != 0).astype(np.float32)[None, :], (128, 1)),
    }
    return m


def host_rope():
    p = np.arange(128)
    freq = (10000.0 ** (-(p % 64).astype(np.float64) / 64.0))[:, None]
    pos = np.arange(64, dtype=np.float64)[None, :]
    ang = (pos.astype(np.float32) * freq.astype(np.float32)).astype(np.float32)
    cos = np.cos(ang).astype(np.float32)
    sin = np.sin(ang).astype(np.float32) * np.where(p < 64, -1.0, 1.0)[:, None].astype(np.float32)
    return np.stack([cos, sin, cos / 16, sin / 16], axis=1).astype(np.float32)


def host_consts():
    c = np.zeros((128, 6, 128), np.float32)
    p = np.arange(128)
    same = (p[:, None] // 64 == p[None, :] // 64)
    c[:, 0, :] = same
    c[:, 1, :] = (p[:, None] <= p[None, :])
    c[:, 2, :] = (p[:, None] >= p[None, :])
    c[:, 3, :] = same & (p[:, None] <= p[None, :])
    c[:, 4, :] = same & (p[:, None] >= p[None, :])
    return c


_NAB_CACHE = {}


def host_nab(rpb):
    NEG = np.float32(-30000.0)
    types = [(10, 10 + dj) for dj in range(-2, 3)]
    types += [(0, j) for j in range(4)] + [(1, j) for j in range(4)]
    types += [(30, 28 + j) for j in range(4)] + [(31, 28 + j) for j in range(4)]
    out = np.empty((16, len(types), 128, 128), np.float32)
    p = np.arange(128)
    for ti, (i, j) in enumerate(types):
        kr = (2 * j + p // 64)[:, None]
        kc = (p % 64)[:, None]
        qr = (2 * i + p // 64)[None, :]
        qc = (p % 64)[None, :]
        r0 = np.clip(qr - 4, 0, 56)
        c0 = np.clip(qc - 8, 0, 48)
        valid = (kr >= r0) & (kr < r0 + 8) & (kc >= c0) & (kc < c0 + 16)
        ri = np.clip(kr - qr + 7, 0, 14)
        ci = np.clip(kc - qc + 15, 0, 30)
        g = rpb[:, ri, ci]
        out[:, ti] = np.where(valid[None], g, NEG)
    return out


def kernel(**inputs):
    nc = build()
    in_maps = [host_inputs(inputs, b) for b in range(8)]
    res = run_bass_kernel_spmd(nc, in_maps, core_ids=list(range(8)))
    return np.stack([np.asarray(r["out"], np.float32) for r in res.results], axis=0)
```

```python
import numpy as np
import ml_dtypes
import concourse.bass as bass
import concourse.mybir as mybir
from concourse.bass_utils import run_bass_kernel_spmd

F32 = mybir.dt.float32
BF16 = mybir.dt.bfloat16
AF = mybir.ActivationFunctionType
ALU = mybir.AluOpType
AX = mybir.AxisListType

D = 1024
T_LAT = 4096
T_CTX = 256
T_ALL = T_LAT + T_CTX
NT = T_ALL // 128
EPS = 1e-6


class KeyList(list):
    pass


def _flat(keys):
    out = []
    for k in keys:
        if isinstance(k, KeyList):
            out.extend(k)
        else:
            out.append(k)
    return out


class Prog:
    def __init__(self, nc):
        self.nc = nc
        self.eng = {"pe": nc.tensor, "act": nc.scalar, "dve": nc.vector, "pool": nc.gpsimd, "sp": nc.sync}
        self.csem = {}
        self.ccnt = {}
        for e in ("pe", "act", "dve", "pool"):
            self.csem[e] = nc.alloc_semaphore("c_" + e)
            self.ccnt[e] = 0
        self.seen = {e: {} for e in self.eng}
        self.lastw = {}
        self.readers = {}
        self.dsem = {}
        self.dpool = []
        for i in range(40):
            self.dpool.append([nc.alloc_semaphore("d%d" % i), 0])
        self.nbuf = 0
        self.ninst = 0

    def sb(self, name, shape, dt):
        return self.nc.alloc_sbuf_tensor("s_" + name, list(shape), dt)

    def ps(self, name, shape, dt=F32):
        return self.nc.alloc_psum_tensor("p_" + name, list(shape), dt)

    def _deps(self, reads, writes, wadd=()):
        toks = []
        for k in reads:
            toks.extend(self.lastw.get(k, ()))
        for k in writes:
            toks.extend(self.lastw.get(k, ()))
            toks.extend(self.readers.get(k, {}).values())
        for k in wadd:
            toks.extend(self.readers.get(k, {}).values())
        return toks

    def _wait(self, e, toks, skip_sem=None):
        need = {}
        for (sem, val) in toks:
            if skip_sem is not None and sem.name == skip_sem:
                continue
            if self.seen[e].get(sem.name, 0) >= val:
                continue
            if need.get(sem.name, (None, 0))[1] < val:
                need[sem.name] = (sem, val)
        for name, (sem, val) in need.items():
            self.eng[e].wait_ge(sem, val)
            self.seen[e][name] = val

    def _commit(self, tok, reads, writes, wadd=()):
        for k in writes:
            self.lastw[k] = [tok]
            self.readers[k] = {}
        for k in wadd:
            self.lastw.setdefault(k, []).append(tok)
        for k in reads:
            if k in writes:
                continue
            r = self.readers.setdefault(k, {})
            o = r.get(tok[0].name)
            if o is None or o[1] < tok[1]:
                r[tok[0].name] = tok

    def op(self, e, fn, reads=(), writes=(), pe_chain=False):
        reads = _flat(reads)
        toks = self._deps(reads, writes)
        self._wait(e, toks, skip_sem=(self.csem[e].name if pe_chain else None))
        ins = fn(self.eng[e])
        self.ccnt[e] += 1
        ins.then_inc(self.csem[e], 1)
        tok = (self.csem[e], self.ccnt[e])
        self._commit(tok, reads, writes)
        self.ninst += 1
        return ins

    def dma(self, e, out, in_, reads=(), writes=(), semkey=None, wadd=(), **kw):
        toks = self._deps(reads, writes, wadd)
        if semkey is None:
            semkey = (tuple(writes) + tuple(reads))[0]
        if semkey not in self.dsem:
            self.dsem[semkey] = self.dpool[len(self.dsem) % len(self.dpool)]
        ent = self.dsem[semkey]
        if ent[1] > 0:
            toks.append((ent[0], ent[1]))
        self._wait(e, toks)
        ins = self.eng[e].dma_start(out=out, in_=in_, **kw)
        ent[1] += 16
        ins.then_inc(ent[0], 16)
        tok = (ent[0], ent[1])
        self._commit(tok, reads, writes, wadd)
        self.ninst += 1
        return ins

    def barrier(self):
        toks = [(self.csem[f], self.ccnt[f]) for f in self.csem if self.ccnt[f] > 0]
        toks += [(ent[0], ent[1]) for ent in self.dpool if ent[1] > 0]
        for e in self.eng:
            self._wait(e, toks)

    def finish(self, e, keys):
        toks = []
        for k in keys:
            toks.extend(self.lastw.get(k, ()))
        self._wait(e, toks)


def interleave(gens):
    active = list(gens)
    while active:
        for g in list(active):
            try:
                next(g)
            except StopIteration:
                active.remove(g)


def build(dbg=None):
    from contextlib import ExitStack
    nc = bass.Bass("TRN2", target_bir_lowering=False)
    P = Prog(nc)
    dbg_d = {}
    if dbg:
        for name, shape in dbg.items():
            if name.startswith("_"):
                continue
            dbg_d[name] = nc.dram_tensor("dbg_" + name, list(shape), F32, kind="ExternalOutput").ap()
    dt_in = lambda name, shape, dt=F32: nc.dram_tensor(name, list(shape), dt, kind="ExternalInput").ap()
    x_d = dt_in("x", [T_LAT, D])
    ctx_d = dt_in("ctx", [T_CTX, D])
    cc_d = dt_in("cc", [128, 16])
    wada_d = dt_in("w_ada", [2, D, 3 * D])
    bada_d = dt_in("b_ada", [2, 3 * D])
    normw_d = dt_in("norm_w", [2, D])
    ident_d = dt_in("ident", [128, 128])
    consts_d = dt_in("consts", [128, 6, 128])
    win_ab_d = dt_in("w_in_ab", [D, 9232])
    qkw_d = dt_in("qkw", [128, 2])
    nab_d = dt_in("nab", [16, 21, 128, 128])
    yT_d = nc.dram_tensor("yT_scr", [16, 128, T_ALL], BF16, kind="Internal").ap()
    rope_d = dt_in("rope", [128, 4, 64])
    bgate_d = dt_in("b_gate", [1, 16])
    hnb_d = dt_in("h_norm_b", [1, D])
    Hf_d = nc.dram_tensor("Hf_scr", [NT, 128, 256], F32, kind="Internal").ap()
    wout_ab_d = dt_in("w_out_ab", [2 * D, D])
    win_c_d = dt_in("w_in_c", [D, 10240])
    wout_c_d = dt_in("w_out_c", [2 * D, D])
    lbc_d = dt_in("lbc", [128, 16, 2])
    hnc_d = dt_in("h_norm_c", [1, 2 * D])
    smask_d = dt_in("smask", [128, 512])
    if dbg and "x1" in dbg:
        x1_d, ctx1_d = dbg_d["x1"], dbg_d["ctx1"]
    else:
        x1_d = nc.dram_tensor("x1_scr", [T_LAT, D], F32, kind="Internal").ap()
        ctx1_d = nc.dram_tensor("ctx1_scr", [T_CTX, D], F32, kind="Internal").ap()
    out_d = nc.dram_tensor("out", [T_LAT, D], F32, kind="ExternalOutput").ap()

    uid = [0]

    def SB(es, name, shape, dt):
        uid[0] += 1
        return es.enter_context(nc.sbuf_tensor("s%d_%s" % (uid[0], name), list(shape), dt))

    def PS(es, name, shape, dt=F32):
        uid[0] += 1
        return es.enter_context(nc.psum_tensor("p%d_%s" % (uid[0], name), list(shape), dt))

    G = ExitStack()
    ident_f = SB(G, "ident_f", [128, 128], F32)
    ident_b = SB(G, "ident_b", [128, 128], BF16)
    P.dma("sp", ident_f[:], ident_d[:, :], writes=["ident_f"])
    P.op("dve", lambda e: e.tensor_copy(out=ident_b[:], in_=ident_f[:]), reads=["ident_f"], writes=["ident_b"])
    ones_f = SB(G, "ones_f", [128, 128], F32)
    P.op("dve", lambda e: e.memset(ones_f[:], 1.0), writes=["ones_f"])
    cc = SB(G, "cc", [128, 16], F32)
    sc = SB(G, "sc", [128, 16], F32)
    P.dma("sp", cc[:], cc_d[:, :], writes=["cc"])
    P.op("act", lambda e: e.activation(out=sc[:], in_=cc[:], func=AF.Silu), reads=["cc"], writes=["sc"])
    hT = SB(G, "hT", [128, 8, T_ALL], BF16)
    wstage = SB(G, "wstage", [128, 8, 256], F32)
    consts = SB(G, "consts", [128, 6, 128], F32)
    P.dma("sp", consts[:], consts_d[:, :, :], writes=["consts"])
    bones = consts[:, 0, :]

    def norm_stage(l, xsrc, csrc, gate_tiles):
        with ExitStack() as es:
            screp = SB(es, "screp", [128, 16, 128], F32)
            for kj in range(16):
                P.op("dve", lambda e, kj=kj: e.tensor_scalar(out=screp[:, kj, :], in0=ones_f[:],
                                                             scalar1=sc[:, kj:kj + 1], scalar2=None, op0=ALU.mult),
                     reads=["sc", "ones_f"], writes=[("screp", kj)])
            wada = SB(es, "wada", [128, 8, 512], F32)
            brow = SB(es, "brow", [128, 3 * D], F32)
            nwb = SB(es, "nwb", [128, D], F32)
            shf = [SB(es, "shf%d" % j, [128, D], F32) for j in range(2)]
            mA = [SB(es, "mA%d" % j, [128, D], F32) for j in range(2)]
            mps = PS(es, "mps", [128, 512])
            P.dma("sp", brow[:], bada_d[l:l + 1, :].partition_broadcast(128), writes=["brow"])
            P.dma("act", nwb[:], normw_d[l:l + 1, :].partition_broadcast(128), writes=["nwb"])
            for blk in range(6):
                for k in range(8):
                    P.dma("sp" if k % 2 == 0 else "act", wada[:, k, :],
                          wada_d[l, k * 128:(k + 1) * 128, blk * 512:(blk + 1) * 512], writes=[("wada", k)])
                sec, half = blk // 2, blk % 2
                for j in range(2):
                    if sec == 2 and j not in gate_tiles:
                        continue
                    for k in range(8):
                        P.op("pe", lambda e, k=k, j=j: e.matmul(
                            mps[:], lhsT=screp[:, 2 * k + j, :], rhs=wada[:, k, :], start=(k == 0), stop=(k == 7)),
                            reads=[("screp", 2 * k + j), ("wada", k)], writes=["mps"], pe_chain=True)
                    dst = (shf[j], mA[j], gate_tiles.get(j))[sec]
                    dkey = (("shf", j), ("mA", j), ("gateh", j))[sec]
                    c0 = blk * 512
                    P.op("dve", lambda e, dst=dst, half=half, c0=c0: e.tensor_tensor(
                        out=dst[:, half * 512:(half + 1) * 512], in0=mps[:], in1=brow[:, c0:c0 + 512], op=ALU.add),
                        reads=["mps", "brow"], writes=[dkey + (half,)])
            for j in range(2):
                P.op("dve", lambda e, j=j: e.scalar_tensor_tensor(
                    out=mA[j][:], in0=mA[j][:], scalar=1.0, in1=nwb[:], op0=ALU.add, op1=ALU.mult),
                    reads=[("mA", j, 0), ("mA", j, 1), "nwb"], writes=[("mA", j, 0), ("mA", j, 1)])
            for j in gate_tiles:
                P.op("dve", lambda e, j=j: e.tensor_copy(out=gate_tiles[j][:, 0:1], in_=gate_tiles[j][:, 0:1]),
                     reads=[("gateh", j, 0), ("gateh", j, 1)], writes=[("gate", j)])
            xin = [SB(es, "xin%d" % i, [128, D], F32) for i in range(2)]
            junk = SB(es, "junk", [128, D], F32)
            ssq = SB(es, "ssq", [128, 2], F32)
            rstd = SB(es, "rstd", [128, 2], F32)
            hm = [SB(es, "hm%d" % i, [128, D], F32) for i in range(2)]
            hb = [SB(es, "hbf%d" % i, [128, D], BF16) for i in range(2)]
            tps = [PS(es, "tps%d" % i, [128, 8, 128], BF16) for i in range(2)]
            for t in range(NT):
                i = t % 2
                j = 1 if t < 2 else 0
                src = csrc[t * 128:(t + 1) * 128, :] if t < 2 else xsrc[(t - 2) * 128:(t - 1) * 128, :]
                P.dma("sp" if i == 0 else "act", xin[i][:], src, reads=[("x1", t)] if l == 1 else [],
                      writes=[("xin", i)])
                P.op("act", lambda e, i=i: e.activation(out=junk[:], in_=xin[i][:], func=AF.Square,
                                                        accum_out=ssq[:, i:i + 1]),
                     reads=[("xin", i)], writes=["junk", ("ssq", i)])
                P.op("act", lambda e, i=i: e.activation(out=ssq[:, i:i + 1], in_=ssq[:, i:i + 1], func=AF.Sqrt,
                                                        scale=1.0 / D, bias=EPS),
                     reads=[("ssq", i)], writes=[("ssq", i)])
                P.op("dve", lambda e, i=i: e.reciprocal(out=rstd[:, i:i + 1], in_=ssq[:, i:i + 1]),
                     reads=[("ssq", i)], writes=[("rstd", i)])
                P.op("dve", lambda e, i=i, j=j: e.scalar_tensor_tensor(
                    out=hm[i][:], in0=xin[i][:], scalar=rstd[:, i:i + 1], in1=mA[j][:],
                    op0=ALU.mult, op1=ALU.mult),
                    reads=[("xin", i), ("rstd", i), ("mA", j, 0), ("mA", j, 1)], writes=[("hm", i)])
                P.op("pool", lambda e, i=i, j=j: e.tensor_tensor(out=hb[i][:], in0=hm[i][:], in1=shf[j][:],
                                                                 op=ALU.add),
                     reads=[("hm", i), ("shf", j, 0), ("shf", j, 1)], writes=[("hb", i)])
                for c in range(8):
                    P.op("pe", lambda e, i=i, c=c: e.transpose(out=tps[i][:, c, :],
                                                               in_=hb[i][:, c * 128:(c + 1) * 128],
                                                               identity=ident_b[:]),
                         reads=[("hb", i), "ident_b"], writes=[("tps", i)], pe_chain=True)
                P.op("act", lambda e, i=i, t=t: e.copy(out=hT[:, :, t * 128:(t + 1) * 128], in_=tps[i][:]),
                     reads=[("tps", i)], writes=[("hT", t)])
            P.barrier()

    L0 = ExitStack()
    gate0 = {j: SB(L0, "gate0_%d" % j, [128, D], F32) for j in range(2)}
    norm_stage(0, x_d, ctx_d, gate0)


    def load_w(es_w, wd, col0, ncols, tag, segs=None):
        if segs is None:
            segs = [(col0, ncols)]
        pieces = []
        for (c0, n) in segs:
            while n > 0:
                m_ = min(n, 256)
                pieces.append((c0, m_))
                c0 += m_
                n -= m_
        ncols = sum(n for _, n in pieces)
        wb = SB(es_w, "wb_" + tag, [128, 8, ncols], BF16)
        groups, cur, curn = [], [], 0
        for pc in pieces:
            if curn + pc[1] > 256:
                groups.append(cur)
                cur, curn = [], 0
            cur.append(pc)
            curn += pc[1]
        groups.append(cur)
        o0 = 0
        for gi, grp in enumerate(groups):
            gn = sum(n for _, n in grp)
            for k in range(8):
                o = 0
                for si, (c0, n) in enumerate(grp):
                    P.dma("sp" if k % 2 == 0 else "act", wstage[:, k, o:o + n], wd[k * 128:(k + 1) * 128, c0:c0 + n],
                          writes=([("wstage", k)] if si == 0 else []), wadd=([] if si == 0 else [("wstage", k)]),
                          semkey=("wstage", k, si))
                    o += n
            for k in range(8):
                P.op("pool" if k % 2 == 0 else "dve",
                     lambda e, k=k, o0=o0, gn=gn: e.tensor_copy(out=wb[:, k, o0:o0 + gn], in_=wstage[:, k, 0:gn]),
                     reads=[("wstage", k)], writes=([("wb_" + tag, k)] if gi == 0 else []),
                     ) if gi == 0 else P.op("pool" if k % 2 == 0 else "dve",
                     lambda e, k=k, o0=o0, gn=gn: e.tensor_copy(out=wb[:, k, o0:o0 + gn], in_=wstage[:, k, 0:gn]),
                     reads=[("wstage", k)], writes=[("wb_" + tag, k, gi)])
            o0 += gn
        keys = [("wb_" + tag, k) for k in range(8)]
        extra = [("wb_" + tag, k, gi) for k in range(8) for gi in range(1, len(groups))]
        return wb, [KeyList([("wb_" + tag, k)] + [("wb_" + tag, k, gi) for gi in range(1, len(groups))]) for k in range(8)]

    TOKG = [(0, 256)] + [(256 + 512 * g, 512) for g in range(8)]

    def proj_fm(wb, wkeys, c0, pst, pkey, t0, n):
        for k in range(8):
            P.op("pe", lambda e, k=k: e.matmul(pst[:, 0:n], lhsT=wb[:, k, c0:c0 + 128], rhs=hT[:, k, t0:t0 + n],
                                               start=(k == 0), stop=(k == 7)),
                 reads=[wkeys[k]] + [("hT", t) for t in range(t0 // 128, (t0 + n) // 128)], writes=[pkey],
                 pe_chain=True)

    def proj_tm(wb, wkeys, c0, ncols, pst, pkey, t):
        for k in range(8):
            P.op("pe", lambda e, k=k: e.matmul(pst[:, 0:ncols], lhsT=hT[:, k, t * 128:(t + 1) * 128],
                                               rhs=wb[:, k, c0:c0 + ncols], start=(k == 0), stop=(k == 7)),
                 reads=[wkeys[k], ("hT", t)], writes=[pkey], pe_chain=True)

    def na_keytiles(t):
        if t < 2:
            return [(0, None), (1, None)]
        i = t - 2
        if i == 0:
            nb = [(j, 5 + j) for j in range(4)]
        elif i == 1:
            nb = [(j, 9 + j) for j in range(4)]
        elif i == 30:
            nb = [(28 + j, 13 + j) for j in range(4)]
        elif i == 31:
            nb = [(28 + j, 17 + j) for j in range(4)]
        else:
            nb = [(i + dj, dj + 2) for dj in range(-2, 3)]
        return [(2 + j, ty) for (j, ty) in nb] + [(0, None), (1, None)]

    def na_stage(chunks, dbg_oa=None):
        with ExitStack() as es:
            qkw = SB(es, "qkw", [128, 2], F32)
            P.dma("sp", qkw[:], qkw_d[:, :], writes=["qkw"])
            P.op("act", lambda e: e.mul(out=qkw[:, 0:1], in_=qkw[:, 0:1], mul=0.125), reads=["qkw"], writes=["qkw"])
            qT = SB(es, "qT", [128, T_ALL], BF16)
            kT = SB(es, "kT", [128, T_ALL], BF16)
            vaug = SB(es, "vaug", [128, NT, 2, 65], BF16)
            yTc = SB(es, "yTc", [128, T_ALL], BF16)
            biasf = SB(es, "biasf", [128, 2, 21, 128], F32)
            biasb = SB(es, "biasb", [128, 2, 21, 128], BF16)
            sq = SB(es, "sq", [128, 512], F32)
            rs = SB(es, "rs", [128, 512], F32)
            PT = SB(es, "PT", [128, 8, 128], BF16)
            rden = SB(es, "rden", [128, 2], F32)
            oa = SB(es, "oa", [128, 128], F32)
            sz = SB(es, "sz", [128, 128], F32)
            yab = SB(es, "yab", [128, 128], BF16)
            pp = PS(es, "pp", [128, 512])
            P.op("dve", lambda e: e.memset(vaug[:], 1.0), writes=[("vaug", t) for t in range(NT)])
            for c in chunks:
                with ExitStack() as es_w:
                    wq, wqk = load_w(es_w, win_ab_d, c * 128, 128, "q")
                    wk, wkk = load_w(es_w, win_ab_d, 1024 + c * 128, 128, "k")
                    wv, wvk = load_w(es_w, win_ab_d, 2048 + c * 128, 128, "v")
                    wz, wzk = load_w(es_w, win_ab_d, 3072 + c * 128, 128, "z")
                    for hh in range(2):
                        P.dma("sp" if hh == 0 else "act", biasf[:, hh, :, :],
                              nab_d[2 * c + hh].rearrange("t k q -> k t q"), writes=[("biasf", hh)])
                        P.op("pool", lambda e, hh=hh: e.tensor_copy(out=biasb[:, hh, :, :], in_=biasf[:, hh, :, :]),
                             reads=[("biasf", hh)], writes=[("biasb", hh)])
                    es_qk = ExitStack()
                    ssp = PS(es_qk, "ssp", [128, 512])
                    for (dst, dname, wb_, wk_, col) in ((qT, "qT", wq, wqk, 0), (kT, "kT", wk, wkk, 1)):
                        for (t0, n) in TOKG:
                            proj_fm(wb_, wk_, 0, pp, "pp", t0, n)
                            P.op("act", lambda e, n=n: e.activation(out=sq[:, 0:n], in_=pp[:, 0:n], func=AF.Square),
                                 reads=["pp"], writes=["sq"])
                            P.op("pe", lambda e, n=n: e.matmul(ssp[:, 0:n], lhsT=bones, rhs=sq[:, 0:n],
                                                               start=True, stop=True),
                                 reads=["sq", "consts"], writes=["ssp"])
                            P.op("act", lambda e, n=n: e.activation(out=rs[:, 0:n], in_=ssp[:, 0:n], func=AF.Sqrt,
                                                                    scale=1.0 / 64, bias=EPS),
                                 reads=["ssp"], writes=["rs"])
                            P.op("dve", lambda e, n=n: e.reciprocal(out=rs[:, 0:n], in_=rs[:, 0:n]),
                                 reads=["rs"], writes=["rs"])
                            P.op("dve", lambda e, n=n, t0=t0, dst=dst, col=col: e.scalar_tensor_tensor(
                                out=dst[:, t0:t0 + n], in0=pp[:, 0:n], scalar=qkw[:, col:col + 1], in1=rs[:, 0:n],
                                op0=ALU.mult, op1=ALU.mult),
                                reads=["pp", "rs", "qkw"],
                                writes=[(dname, t) for t in range(t0 // 128, (t0 + n) // 128)])
                    P.barrier()
                    es_qk.close()
                    for t in range(NT):
                        proj_tm(wv, wvk, 0, 128, pp, "pp", t)
                        P.op("act", lambda e, t=t: e.copy(out=vaug[:, t, :, 0:64],
                                                          in_=pp[:, 0:128].rearrange("p (h d) -> p h d", h=2)),
                             reads=["pp"], writes=[("vaug", t)])
                    P.barrier()
                    with ExitStack() as es_a:
                        st2 = [PS(es_a, "st%d" % i, [128, 8, 128]) for i in range(2)]
                        num2 = [PS(es_a, "num%d" % i, [128, 128]) for i in range(2)]
                        tp = PS(es_a, "tp", [128, 128], BF16)
                        PT2 = [SB(es_a, "PT%d" % i, [128, 8, 128], BF16) for i in range(2)]
                        oa2 = [SB(es_a, "oa%d" % i, [128, 128], F32) for i in range(2)]
                        sz2 = [SB(es_a, "sz%d" % i, [128, 128], F32) for i in range(2)]
                        yab2 = [SB(es_a, "yab%d" % i, [128, 128], BF16) for i in range(2)]
                        rden2 = SB(es_a, "rden2", [128, 4], F32)

                        def head_gen(t, hh, kts):
                            nk = len(kts)
                            pb = 64 * hh
                            st_, num_, PT_ = st2[hh], num2[hh], PT2[hh]
                            i = t % 2
                            for n_, (kt, ty) in enumerate(kts):
                                P.op("pe", lambda e: e.matmul(
                                    st_[:, n_, :], lhsT=kT[pb:pb + 64, kt * 128:(kt + 1) * 128],
                                    rhs=qT[pb:pb + 64, t * 128:(t + 1) * 128], start=True, stop=(ty is None)),
                                    reads=[("kT", kt), ("qT", t)], writes=[("st", hh)], pe_chain=True)
                                yield
                                if ty is not None:
                                    P.op("pe", lambda e: e.matmul(
                                        st_[:, n_, :], lhsT=ident_b[:], rhs=biasb[:, hh, ty, :], start=False, stop=True),
                                        reads=["ident_b", ("biasb", hh)], writes=[("st", hh)], pe_chain=True)
                                    yield
                            P.op("act", lambda e: e.activation(out=PT_[:, 0:nk, :], in_=st_[:, 0:nk, :], func=AF.Exp),
                                 reads=[("st", hh)], writes=[("PT", hh)])
                            yield
                            for n_, (kt, ty) in enumerate(kts):
                                P.op("pe", lambda e: e.matmul(
                                    num_[:, 0:65], lhsT=PT_[:, n_, :], rhs=vaug[:, kt, hh, :],
                                    start=(n_ == 0), stop=(n_ == nk - 1)),
                                    reads=[("PT", hh), ("vaug", kt)], writes=[("num", hh)], pe_chain=True)
                                yield
                            rd = rden2[:, 2 * i + hh:2 * i + hh + 1]
                            P.op("dve", lambda e: e.reciprocal(out=rd, in_=num_[:, 64:65]),
                                 reads=[("num", hh)], writes=[("rden", i, hh)])
                            yield
                            P.op("dve", lambda e: e.tensor_scalar(
                                out=oa2[i][:, hh * 64:(hh + 1) * 64], in0=num_[:, 0:64], scalar1=rd,
                                scalar2=None, op0=ALU.mult),
                                reads=[("num", hh), ("rden", i, hh)], writes=[("oa", i, hh)])
                            yield

                        def tail(t):
                            i = t % 2
                            if dbg_oa is not None:
                                P.dma("sp", dbg_oa[t * 128:(t + 1) * 128, c * 128:(c + 1) * 128], oa2[i][:],
                                      reads=[("oa", i, 0), ("oa", i, 1)], writes=[("dbgoa", t % 4)])
                            P.op("pool", lambda e: e.tensor_tensor(out=yab2[i][:], in0=oa2[i][:], in1=sz2[i][:],
                                                                   op=ALU.mult),
                                 reads=[("oa", i, 0), ("oa", i, 1), ("sz", i)], writes=[("yab", i)])
                            P.op("pe", lambda e: e.transpose(out=tp[:], in_=yab2[i][:], identity=ident_b[:]),
                                 reads=[("yab", i), "ident_b"], writes=["tp"])
                            P.op("act", lambda e: e.copy(out=yTc[:, t * 128:(t + 1) * 128], in_=tp[:]),
                                 reads=["tp"], writes=[("yTc", t)])

                        pend = None
                        for t in range(NT):
                            kts = na_keytiles(t)
                            i = t % 2
                            proj_tm(wz, wzk, 0, 128, pp, "pp", t)
                            P.op("act", lambda e, i=i: e.activation(out=sz2[i][:], in_=pp[:, 0:128], func=AF.Tanh,
                                                                    scale=0.5),
                                 reads=["pp"], writes=[("sz", i)])
                            P.op("dve", lambda e, i=i: e.tensor_scalar(out=sz2[i][:], in0=sz2[i][:], scalar1=0.5,
                                                                       scalar2=0.5, op0=ALU.mult, op1=ALU.add),
                                 reads=[("sz", i)], writes=[("sz", i)])
                            P.op("dve", lambda e, i=i: e.tensor_tensor(out=sz2[i][:], in0=sz2[i][:], in1=pp[:, 0:128],
                                                                       op=ALU.mult),
                                 reads=[("sz", i), "pp"], writes=[("sz", i)])
                            interleave([head_gen(t, 0, kts), head_gen(t, 1, kts)])
                            if pend is not None:
                                tail(pend)
                            pend = t
                        tail(pend)
                        P.barrier()
                    P.dma("sp", yT_d[c], yTc[:], reads=[("yTc", t) for t in range(NT)], writes=[("yT_d", c)])
                    if dbg and "qk" in dbg:
                        stg = SB(es_w, "dbgstg", [128, T_ALL], F32)
                        for ii, (src, nm) in enumerate(((qT, "qT"), (kT, "kT"))):
                            P.op("dve", lambda e, src=src: e.tensor_copy(out=stg[:], in_=src[:]),
                                 reads=[(nm, t) for t in range(NT)], writes=["dbgstg"])
                            P.dma("sp", dbg_d["qk"][ii * 128:(ii + 1) * 128, :], stg[:], reads=["dbgstg"],
                                  writes=[("dbgqk", ii)])
                        P.finish("sp", [("dbgqk", 0), ("dbgqk", 1), ("dbgqk", 2)])
                    P.barrier()
            if dbg_oa is not None:
                P.finish("sp", [("dbgoa", i) for i in range(4)])
            P.barrier()

    def ml_stage(heads, dbg_hb=None):
        with ExitStack() as es:
            rope = SB(es, "rope", [128, 4, 64], F32)
            P.dma("sp", rope[:], rope_d[:, :, :], writes=["rope"])
            bgate = SB(es, "bgate", [128, 16], F32)
            P.dma("act", bgate[:], bgate_d[0:1, :].partition_broadcast(128), writes=["bgate"])
            hnb = SB(es, "hnb", [128, D], F32)
            P.dma("sp", hnb[:], hnb_d[0:1, :].partition_broadcast(128), writes=["hnb"])
            Gt = SB(es, "Gt", [128, NT, 16], F32)
            LF = SB(es, "LF", [128, NT, 8], F32)
            Wt = SB(es, "Wt", [128, NT, 8], F32)
            Bs = SB(es, "Bs", [128, NT, 16], F32)
            EB = SB(es, "EB", [128, NT, 8], F32)
            EBL = SB(es, "EBL", [128, NT, 8], F32)
            with ExitStack() as es_w:
                pg = PS(es_w, "pg", [128, 16])
                wg, wgk = load_w(es_w, win_ab_d, 9216, 16, "g")
                for t in range(NT):
                    proj_tm(wg, wgk, 0, 16, pg, "pg", t)
                    P.op("dve", lambda e, t=t: e.tensor_tensor(out=Gt[:, t, :], in0=pg[:, 0:16], in1=bgate[:],
                                                               op=ALU.add),
                         reads=["pg", "bgate"], writes=[("Gt", t)])
                gkeys = [("Gt", t) for t in range(NT)]
                for d in range(2):
                    P.op("act", lambda e, d=d: e.activation(out=LF[:, :, 4 * d:4 * d + 4],
                                                            in_=Gt[:, :, 4 + 8 * d:8 + 8 * d], func=AF.Exp, scale=-1.0),
                         reads=gkeys, writes=[("LF", d)])
                    P.op("act", lambda e, d=d: e.activation(out=LF[:, :, 4 * d:4 * d + 4],
                                                            in_=LF[:, :, 4 * d:4 * d + 4], func=AF.Ln, bias=1.0),
                         reads=[("LF", d)], writes=[("LF", d)])
                    P.op("dve", lambda e, d=d: e.tensor_scalar(out=LF[:, :, 4 * d:4 * d + 4],
                                                               in0=LF[:, :, 4 * d:4 * d + 4], scalar1=-1.0,
                                                               scalar2=None, op0=ALU.mult),
                         reads=[("LF", d)], writes=[("LF", d)])
                for t in range(NT):
                    P.op("pe", lambda e, t=t: e.matmul(pg[:, 0:4], lhsT=consts[:, 1, :], rhs=LF[:, t, 0:4],
                                                       start=True, stop=True),
                         reads=["consts", ("LF", 0)], writes=["pg"])
                    P.op("pe", lambda e, t=t: e.matmul(pg[:, 4:8], lhsT=consts[:, 2, :], rhs=LF[:, t, 4:8],
                                                       start=True, stop=True),
                         reads=["consts", ("LF", 1)], writes=["pg"], pe_chain=True)
                    P.op("pe", lambda e, t=t: e.matmul(pg[:, 8:16], lhsT=ones_f[:], rhs=LF[:, t, 0:8],
                                                       start=True, stop=True),
                         reads=["ones_f", ("LF", 0), ("LF", 1)], writes=["pg"], pe_chain=True)
                    for d in range(2):
                        P.op("dve", lambda e, t=t, d=d: e.tensor_tensor(
                            out=Wt[:, t, 4 * d:4 * d + 4], in0=Gt[:, t, 8 * d:8 * d + 4], in1=pg[:, 4 * d:4 * d + 4],
                            op=ALU.subtract),
                            reads=["pg", ("Gt", t)], writes=[("Wt", t)])
                    P.op("act", lambda e, t=t: e.copy(out=Bs[:, t, :], in_=pg[:, 0:16]),
                         reads=["pg"], writes=[("Bs", t)])
                P.op("act", lambda e: e.activation(out=Wt[:], in_=Wt[:], func=AF.Exp),
                     reads=[("Wt", t) for t in range(NT)], writes=["Wt"])
                P.op("act", lambda e: e.activation(out=EB[:], in_=Bs[:, :, 0:8], func=AF.Exp),
                     reads=[("Bs", t) for t in range(NT)], writes=["EB"])
                P.op("act", lambda e: e.activation(out=EBL[:], in_=Bs[:, :, 8:16], func=AF.Exp),
                     reads=[("Bs", t) for t in range(NT)], writes=["EBL"])
                P.barrier()
            for h in heads:
                with ExitStack() as es_h:
                    qT = SB(es_h, "mqT", [128, 2, T_ALL], BF16)
                    kT = SB(es_h, "mkT", [128, 2, T_ALL], BF16)
                    vaug = SB(es_h, "mvaug", [128, NT, 257], BF16)
                    P.op("pool", lambda e: e.memset(vaug[:], 1.0), writes=[("mv", t) for t in range(NT)])
                    with ExitStack() as es_p:
                        pp = PS(es_p, "mpp", [128, 512])
                        pp2 = PS(es_p, "mpp2", [128, 512])
                        t1 = SB(es_p, "t1", [128, 512], F32)
                        t2 = SB(es_p, "t2", [128, 512], F32)
                        for (dst, dn, cbase, ti) in ((qT, "mqT", 4096 + h * 256, 0), (kT, "mkT", 5120 + h * 256, 2)):
                            for cch in range(2):
                                with ExitStack() as es_w:
                                    c0 = cbase + cch * 128
                                    w, wk_ = load_w(es_w, win_ab_d, c0, 128, "a")
                                    wsw, wswk = load_w(es_w, win_ab_d, 0, 0, "b", segs=[(c0 + 64, 64), (c0, 64)])
                                    for (t0, n) in TOKG:
                                        okeys = [(dn, cch, t) for t in range(t0 // 128, (t0 + n) // 128)]
                                        proj_fm(w, wk_, 0, pp, "mpp", t0, n)
                                        if t0 < 256:
                                            P.op("act", lambda e, n=n, t0=t0, dst=dst, cch=cch, ti=ti: e.mul(
                                                out=dst[:, cch, t0:t0 + n], in_=pp[:, 0:n],
                                                mul=(1.0 if ti == 0 else 1.0 / 16)),
                                                reads=["mpp"], writes=okeys)
                                            continue
                                        proj_fm(wsw, wswk, 0, pp2, "mpp2", t0, n)
                                        r0 = (t0 - 256) // 64
                                        if cch == 0:
                                            cosv = rope[:, ti, r0:r0 + 8].unsqueeze(2).broadcast_to([128, 8, 64])
                                            sinv = rope[:, ti + 1, r0:r0 + 8].unsqueeze(2).broadcast_to([128, 8, 64])
                                        else:
                                            cosv = rope[:, ti, :].unsqueeze(1).broadcast_to([128, 8, 64])
                                            sinv = rope[:, ti + 1, :].unsqueeze(1).broadcast_to([128, 8, 64])
                                        P.op("dve", lambda e, cosv=cosv: e.tensor_tensor(
                                            out=t1[:].rearrange("p (r c) -> p r c", r=8),
                                            in0=pp[:].rearrange("p (r c) -> p r c", r=8), in1=cosv, op=ALU.mult),
                                            reads=["mpp", "rope"], writes=["t1"])
                                        P.op("dve", lambda e, sinv=sinv: e.tensor_tensor(
                                            out=t2[:].rearrange("p (r c) -> p r c", r=8),
                                            in0=pp2[:].rearrange("p (r c) -> p r c", r=8), in1=sinv, op=ALU.mult),
                                            reads=["mpp2", "rope"], writes=["t2"])
                                        P.op("pool", lambda e, t0=t0, dst=dst, cch=cch: e.tensor_tensor(
                                            out=dst[:, cch, t0:t0 + 512], in0=t1[:], in1=t2[:], op=ALU.add),
                                            reads=["t1", "t2"], writes=okeys)
                                    P.barrier()
                        with ExitStack() as es_w:
                            wv, wvk = load_w(es_w, win_ab_d, 6144 + h * 256, 256, "a")
                            for t in range(NT):
                                proj_tm(wv, wvk, 0, 256, pp, "mpp", t)
                                P.op("act", lambda e, t=t: e.copy(out=vaug[:, t, 0:256], in_=pp[:, 0:256]),
                                     reads=["mpp"], writes=[("mv", t)])
                            P.barrier()
                    if dbg and "mqk" in dbg:
                        with ExitStack() as es_d:
                            stg = SB(es_d, "dbgstg", [128, T_ALL], F32)
                            for ii, (src, nm, cch) in enumerate(((qT, "mqT", 0), (qT, "mqT", 1), (kT, "mkT", 0), (kT, "mkT", 1))):
                                P.op("dve", lambda e, src=src, cch=cch: e.tensor_copy(out=stg[:], in_=src[:, cch, :]),
                                     reads=[(nm, cch, t) for t in range(NT)], writes=["dbgstg"])
                                P.dma("sp", dbg_d["mqk"][ii * 128:(ii + 1) * 128, :], stg[:], reads=["dbgstg"],
                                      writes=[("dbgqk", ii)])
                            P.finish("sp", [("dbgqk", ii) for ii in range(4)])
                            P.barrier()
                    with ExitStack() as es_c:
                        wo, wok = load_w(es_c, win_ab_d, 0, 0, "oz", segs=[(7168 + h * 256, 256), (8192 + h * 256, 256)])
                        Cf = SB(es_c, "Cf", [128, 2, 257], F32)
                        Ct = SB(es_c, "Ct", [128, 2, 257], F32)
                        Cb = SB(es_c, "Cb", [128, 2, 257], BF16)
                        STm = SB(es_c, "STm", [128, 128], BF16)
                        kt = SB(es_c, "kt", [128, 256], BF16)
                        sm = SB(es_c, "sm", [128, 4], F32)
                        Hin = SB(es_c, "Hin", [128, 256], F32)
                        Hs = SB(es_c, "Hs", [128, 256], F32)
                        sig = SB(es_c, "sig", [128, 256], F32)
                        szb = SB(es_c, "szb", [128, 256], F32)
                        hbg = SB(es_c, "hbg", [128, 256], F32)
                        ybf = SB(es_c, "ybf", [128, 256], BF16)
                        ysb = SB(es_c, "ysb", [128, 2, 128], BF16)
                        STp = PS(es_c, "STp", [128, 128])
                        acc = PS(es_c, "acc", [128, 512])
                        cacc = PS(es_c, "cacc", [128, 2, 512])
                        po = PS(es_c, "po", [128, 512])
                        tpk = PS(es_c, "tpk", [128, 2, 128], BF16)
                        tpy = PS(es_c, "tpy", [128, 2, 128], BF16)
                        for d in range(2):
                            hd = 4 * d + h
                            order = [0, 1] + list(range(2, NT)) if d == 0 else [1, 0] + list(range(NT - 1, 1, -1))
                            P.op("dve", lambda e: e.memset(Cf[:], 0.0), writes=["Cf"])
                            P.op("pool", lambda e: e.memset(Cb[:], 0.0), writes=["Cb"])
                            for t in order:
                                ts_ = slice(t * 128, (t + 1) * 128)
                                for c in range(2):
                                    P.op("pe", lambda e, c=c: e.matmul(STp[:], lhsT=kT[:, c, ts_], rhs=qT[:, c, ts_],
                                                                       start=(c == 0), stop=(c == 1)),
                                         reads=[("mkT", c, t), ("mqT", c, t)], writes=["STp"], pe_chain=True)
                                P.op("dve", lambda e, d=d, t=t, hd=hd: e.scalar_tensor_tensor(
                                    out=STm[:], in0=STp[:], scalar=Wt[:, t, hd:hd + 1], in1=consts[:, 1 + d, :],
                                    op0=ALU.mult, op1=ALU.mult),
                                    reads=["STp", "Wt", "consts"], writes=["STm"])
                                P.op("pe", lambda e, t=t: e.matmul(acc[:, 0:257], lhsT=STm[:], rhs=vaug[:, t, :],
                                                                   start=True, stop=False),
                                     reads=["STm", ("mv", t)], writes=["acc"])
                                for c in range(2):
                                    P.op("pe", lambda e, c=c: e.matmul(acc[:, 0:257], lhsT=qT[:, c, ts_], rhs=Cb[:, c, :],
                                                                       start=False, stop=(c == 1)),
                                         reads=[("mqT", c, t), "Cb"], writes=["acc"], pe_chain=True)
                                P.op("act", lambda e, t=t, hd=hd: e.activation(
                                    out=sm[:, 0:1], in_=acc[:, 256:257], func=AF.Abs, scale=EB[:, t, hd:hd + 1]),
                                    reads=["acc", "EB"], writes=["sm"])
                                P.op("dve", lambda e: e.tensor_scalar(out=sm[:, 1:2], in0=sm[:, 0:1], scalar1=1.0,
                                                                      scalar2=None, op0=ALU.max),
                                     reads=["sm"], writes=["sm"])
                                P.op("dve", lambda e: e.reciprocal(out=sm[:, 2:3], in_=sm[:, 1:2]),
                                     reads=["sm"], writes=["sm"])
                                P.op("dve", lambda e, t=t, hd=hd: e.tensor_tensor(
                                    out=sm[:, 3:4], in0=sm[:, 2:3], in1=EB[:, t, hd:hd + 1], op=ALU.mult),
                                    reads=["sm", "EB"], writes=["sm"])
                                if d == 0:
                                    P.op("act", lambda e: e.activation(out=Hs[:], in_=acc[:, 0:256], func=AF.Copy,
                                                                       scale=sm[:, 3:4]),
                                         reads=["acc", "sm"], writes=["Hs"])
                                    P.dma("sp", Hf_d[t], Hs[:], reads=["Hs"], writes=[("Hf_d", t)])
                                else:
                                    P.dma("sp", Hin[:], Hf_d[t], reads=[("Hf_d", t)], writes=["Hin"])
                                    P.op("dve", lambda e: e.scalar_tensor_tensor(
                                        out=Hs[:], in0=acc[:, 0:256], scalar=sm[:, 3:4], in1=Hin[:],
                                        op0=ALU.mult, op1=ALU.add),
                                        reads=["acc", "sm", "Hin"], writes=["Hs"])
                                    if dbg_hb is not None:
                                        P.dma("act", dbg_hb[t * 128:(t + 1) * 128, h * 256:(h + 1) * 256], Hs[:],
                                              reads=["Hs"], writes=[("dbghb", t % 4)])
                                    proj_tm(wo, wok, 0, 512, po, "po", t)
                                    P.op("act", lambda e: e.activation(out=sig[:], in_=po[:, 0:256], func=AF.Sigmoid),
                                         reads=["po"], writes=["sig"])
                                    P.op("act", lambda e: e.activation(out=szb[:], in_=po[:, 256:512], func=AF.Silu),
                                         reads=["po"], writes=["szb"])
                                    P.op("pool", lambda e: e.tensor_tensor(out=hbg[:], in0=Hs[:], in1=sig[:],
                                                                           op=ALU.mult),
                                         reads=["Hs", "sig"], writes=["hbg"])
                                    P.op("act", lambda e: e.activation(out=sig[:], in_=hbg[:], func=AF.Square,
                                                                       accum_out=sm[:, 0:1]),
                                         reads=["hbg", "sm"], writes=["sig", "sm"])
                                    P.op("act", lambda e: e.activation(out=sm[:, 1:2], in_=sm[:, 0:1], func=AF.Sqrt,
                                                                       scale=1.0 / 256, bias=EPS),
                                         reads=["sm"], writes=["sm"])
                                    P.op("dve", lambda e: e.reciprocal(out=sm[:, 2:3], in_=sm[:, 1:2]),
                                         reads=["sm"], writes=["sm"])
                                    P.op("dve", lambda e, h=h: e.scalar_tensor_tensor(
                                        out=hbg[:], in0=hbg[:], scalar=sm[:, 2:3], in1=hnb[:, h * 256:(h + 1) * 256],
                                        op0=ALU.mult, op1=ALU.mult),
                                        reads=["hbg", "sm", "hnb"], writes=["hbg"])
                                    P.op("pool", lambda e: e.tensor_tensor(out=ybf[:], in0=hbg[:], in1=szb[:],
                                                                           op=ALU.mult),
                                         reads=["hbg", "szb"], writes=["ybf"])
                                    for c in range(2):
                                        P.op("pe", lambda e, c=c: e.transpose(out=tpy[:, c, :],
                                                                              in_=ybf[:, c * 128:(c + 1) * 128],
                                                                              identity=ident_b[:]),
                                             reads=["ybf", "ident_b"], writes=["tpy"], pe_chain=True)
                                    P.op("act", lambda e: e.copy(out=ysb[:], in_=tpy[:]), reads=["tpy"], writes=["ysb"])
                                    P.dma("sp", yT_d[8 + 2 * h:10 + 2 * h, :, ts_].rearrange("c p t -> p c t"), ysb[:],
                                          reads=["ysb"], writes=[("yT_d", 8 + 2 * h, t)])
                                for c in range(2):
                                    P.op("pe", lambda e, c=c: e.transpose(out=tpk[:, c, :], in_=kT[:, c, ts_],
                                                                          identity=ident_b[:]),
                                         reads=[("mkT", c, t), "ident_b"], writes=["tpk"], pe_chain=True)
                                P.op("dve", lambda e, t=t, hd=hd: e.tensor_scalar(
                                    out=kt[:], in0=tpk[:].rearrange("p c k -> p (c k)"), scalar1=Wt[:, t, hd:hd + 1],
                                    scalar2=None, op0=ALU.mult),
                                    reads=["tpk", "Wt"], writes=["kt"])
                                for c in range(2):
                                    P.op("pe", lambda e, c=c, t=t: e.matmul(cacc[:, c, 0:257],
                                                                            lhsT=kt[:, c * 128:(c + 1) * 128],
                                                                            rhs=vaug[:, t, :], start=True, stop=True),
                                         reads=["kt", ("mv", t)], writes=["cacc"], pe_chain=(c == 1))
                                P.op("dve", lambda e: e.tensor_tensor(out=Ct[:], in0=cacc[:, :, 0:257], in1=Cf[:],
                                                                      op=ALU.add),
                                     reads=["cacc", "Cf"], writes=["Ct"])
                                P.op("dve", lambda e, t=t, hd=hd: e.tensor_scalar(
                                    out=Cf[:], in0=Ct[:], scalar1=EBL[:, t, hd:hd + 1], scalar2=None, op0=ALU.mult),
                                    reads=["Ct", "EBL"], writes=["Cf"])
                                P.op("pool", lambda e, t=t, hd=hd: e.tensor_scalar(
                                    out=Cb[:], in0=Ct[:], scalar1=EBL[:, t, hd:hd + 1], scalar2=None, op0=ALU.mult),
                                    reads=["Ct", "EBL"], writes=["Cb"])
                        if dbg_hb is not None:
                            P.finish("act", [("dbghb", i) for i in range(4)])
                        P.barrier()
            P.barrier()

    def out_stage(wout_d, xsrc, csrc, gates, dst_x, dst_c, outkey):
        with ExitStack() as es:
            wo_b = SB(es, "wo_b", [128, 16, D], BF16)
            for q4 in range(4):
                for kg in range(2):
                    for k in range(8):
                        kc = kg * 8 + k
                        P.dma("sp" if k % 2 == 0 else "act", wstage[:, k, :],
                              wout_d[kc * 128:(kc + 1) * 128, q4 * 256:(q4 + 1) * 256], writes=[("wstage", k)],
                              semkey=("wstage", k, 0))
                        P.op("pool" if k % 2 == 0 else "dve", lambda e, k=k, kc=kc, q4=q4: e.tensor_copy(
                            out=wo_b[:, kc, q4 * 256:(q4 + 1) * 256], in_=wstage[:, k, :]),
                            reads=[("wstage", k)], writes=[("wo_b", kc, q4)])
            yt = [SB(es, "yt%d" % i, [128, 16, 128], BF16) for i in range(2)]
            xt = [SB(es, "xt%d" % i, [128, D], F32) for i in range(2)]
            tmp = SB(es, "otmp", [128, D], F32)
            xo = [SB(es, "xo%d" % i, [128, D], F32) for i in range(2)]
            py = [PS(es, "py%d" % i, [128, 512]) for i in range(2)]
            tiles = list(range(NT)) if dst_c is not None else list(range(2, NT))
            for n_, t in enumerate(tiles):
                i = n_ % 2
                j = 1 if t < 2 else 0
                ts_ = slice(t * 128, (t + 1) * 128)
                P.dma("sp", yt[i][:], yT_d[:, :, ts_].rearrange("c p t -> p c t"),
                      reads=[("yT_d", c) for c in range(16)], writes=[("yt", i)])
                src = csrc[ts_, :] if t < 2 else xsrc[(t - 2) * 128:(t - 1) * 128, :]
                P.dma("act", xt[i][:], src, reads=[("x1", t)] if outkey == "out" else [], writes=[("xt", i)])
                for half in range(2):
                    for kc in range(16):
                        P.op("pe", lambda e, kc=kc, half=half, i=i: e.matmul(
                            py[half][:], lhsT=yt[i][:, kc, :], rhs=wo_b[:, kc, half * 512:(half + 1) * 512],
                            start=(kc == 0), stop=(kc == 15)),
                            reads=[("yt", i), ("wo_b", kc, 2 * half), ("wo_b", kc, 2 * half + 1)], writes=[("py", half)], pe_chain=True)
                    hs = slice(half * 512, (half + 1) * 512)
                    P.op("dve", lambda e, half=half, hs=hs, j=j: e.tensor_tensor(
                        out=tmp[:, hs], in0=py[half][:], in1=gates[j][:, hs], op=ALU.mult),
                        reads=[("py", half), ("gate", j)], writes=[("otmp", half)])
                    P.op("pool", lambda e, hs=hs, i=i: e.tensor_tensor(out=xo[i][:, hs], in0=tmp[:, hs],
                                                                       in1=xt[i][:, hs], op=ALU.add),
                         reads=[("otmp", half), ("xt", i)], writes=[("xo", i, half)])
                dst = dst_c[ts_, :] if t < 2 else dst_x[(t - 2) * 128:(t - 1) * 128, :]
                P.dma("sp", dst, xo[i][:], reads=[("xo", i, 0), ("xo", i, 1)], writes=[(outkey, t)],
                      semkey=("xo", i))
            P.barrier()

    def hg_stage(heads, dbg_o=None):
        from itertools import zip_longest
        NCH = T_ALL // 64
        ORD = [list(range(NCH)), [3, 2, 1, 0] + list(range(NCH - 1, 3, -1))]
        POS = [{j: p for p, j in enumerate(o_)} for o_ in ORD]
        hgstop = (dbg or {}).get("_hgstop")

        def rev(ap2d, lo, n):
            v = ap2d[:, lo:lo + n]
            return bass.AP(v.tensor, v.offset + (n - 1) * v.ap[-1][0], [list(v.ap[0]), [-v.ap[-1][0], n]])

        with ExitStack() as es:
            lbc = SB(es, "lbc", [128, 16, 2], F32)
            P.dma("sp", lbc[:], lbc_d[:, :, :], writes=["lbc"])
            lb = SB(es, "lb", [128, 16], F32)
            oml = SB(es, "oml", [128, 16], F32)
            P.op("dve", lambda e: e.tensor_tensor(out=lb[:], in0=lbc[:, :, 1], in1=lbc[:, :, 0], op=ALU.subtract),
                 reads=["lbc"], writes=["lb"])
            P.op("act", lambda e: e.activation(out=lb[:], in_=lb[:], func=AF.Sigmoid), reads=["lb"], writes=["lb"])
            P.op("dve", lambda e: e.tensor_scalar(out=oml[:], in0=lb[:], scalar1=-1.0, scalar2=1.0, op0=ALU.mult,
                                                  op1=ALU.add), reads=["lb"], writes=["oml"])
            smask = SB(es, "smask", [128, 512], F32)
            P.dma("sp", smask[:], smask_d[:, :], writes=["smask"])
            omh = SB(es, "omh", [128, 16], F32)
            lbh = SB(es, "lbh", [128, 16], F32)
            P.op("dve", lambda e: e.tensor_scalar(out=omh[:], in0=oml[:], scalar1=0.5, scalar2=None, op0=ALU.mult),
                 reads=["oml"], writes=["omh"])
            P.op("dve", lambda e: e.tensor_tensor(out=lbh[:], in0=lb[:], in1=omh[:], op=ALU.add),
                 reads=["lb", "omh"], writes=["lbh"])
            for h in heads:
                with ExitStack() as es_h:
                    hnh = SB(es_h, "hnh", [128, 128], F32)
                    P.dma("act", hnh[:], hnc_d[0:1, h * 128:(h + 1) * 128].partition_broadcast(128), writes=["hnh"])
                    qf = SB(es_h, "gqf", [128, T_ALL], BF16)
                    kf = SB(es_h, "gkf", [128, T_ALL], BF16)
                    qb = SB(es_h, "gqb", [128, T_ALL], BF16)
                    kb = SB(es_h, "gkb", [128, T_ALL], BF16)
                    QK = ((qf, "gqf", kf, "gkf"), (qb, "gqb", kb, "gkb"))
                    ebj = SB(es_h, "ebj", [128, 2, NCH], F32)
                    vtok = SB(es_h, "gv", [128, NT, 128], BF16)
                    SH = [SB(es_h, "SH%d" % d, [128, NCH, 128], BF16) for d in range(2)]
                    with ExitStack() as es_p:
                        pp = [PS(es_p, "gpp%d" % i, [128, 512]) for i in range(3)]
                        wq, wqk = load_w(es_p, win_c_d, h * 128, 128, "gq")
                        wf_ = [load_w(es_p, win_c_d, 2048 + h * 128, 128, "gff"),
                               load_w(es_p, win_c_d, 4096 + h * 128, 128, "gfb")]
                        qs = SB(es_p, "gqs", [128, 512], F32)
                        A = [SB(es_p, "gA%d" % d, [128, 512], F32) for d in range(2)]
                        B = [SB(es_p, "gB%d" % d, [128, 512], F32) for d in range(2)]
                        C = [SB(es_p, "gC%d" % d, [128, 512], F32) for d in range(2)]
                        E = [SB(es_p, "gE%d" % d, [128, 512], F32) for d in range(2)]
                        TL = SB(es_p, "gTL", [128, 8], F32)
                        for (t0, n) in TOKG:
                            nch = n // 64
                            j0 = t0 // 64
                            tk = [t for t in range(t0 // 128, (t0 + n) // 128)]
                            proj_fm(wq, wqk, 0, pp[2], "gpp2", t0, n)
                            P.op("act", lambda e, n=n: e.activation(out=qs[:, 0:n], in_=pp[2][:, 0:n], func=AF.Tanh,
                                                                    scale=0.5),
                                 reads=["gpp2"], writes=["gqs"])
                            P.op("dve", lambda e, n=n: e.scalar_tensor_tensor(
                                out=qs[:, 0:n], in0=qs[:, 0:n], scalar=1.0, in1=pp[2][:, 0:n], op0=ALU.add, op1=ALU.mult),
                                reads=["gqs", "gpp2"], writes=["gqs"])
                            steps = [[], []]
                            for d in range(2):
                                (w_, wk_) = wf_[d]
                                (qd, qn, kd, kn) = QK[d]
                                Ad, Bd, Cd, Ed = A[d], B[d], C[d], E[d]
                                kA, kB, kC, kE, kP = "gA%d" % d, "gB%d" % d, "gC%d" % d, "gE%d" % d, "gpp%d" % d
                                C3 = Cd[:, 0:n].rearrange("p (c l) -> p c l", l=64)
                                L = steps[d]
                                L.append(lambda w_=w_, wk_=wk_, d=d, kP=kP: proj_fm(w_, wk_, 0, pp[d], kP, t0, n))
                                L.append(lambda Ad=Ad, d=d, kP=kP, kA=kA: P.op(
                                    "act", lambda e: e.activation(out=Ad[:, 0:n], in_=pp[d][:, 0:n], func=AF.Tanh,
                                                                  scale=0.5),
                                    reads=[kP], writes=[kA]))
                                L.append(lambda Ad=Ad, kA=kA: P.op("dve", lambda e: e.tensor_scalar(
                                    out=Ad[:, 0:n], in0=Ad[:, 0:n], scalar1=omh[:, h:h + 1], scalar2=lbh[:, h:h + 1],
                                    op0=ALU.mult, op1=ALU.add), reads=[kA, "omh", "lbh"], writes=[kA]))
                                L.append(lambda Ad=Ad, Bd=Bd, kA=kA, kB=kB: P.op(
                                    "act", lambda e: e.activation(out=Bd[:, 0:n], in_=Ad[:, 0:n], func=AF.Ln),
                                    reads=[kA], writes=[kB]))
                                L.append(lambda Ad=Ad, kA=kA, kB=kB: P.op("pool", lambda e: e.tensor_scalar(
                                    out=Ad[:, 0:n], in0=Ad[:, 0:n], scalar1=-1.0, scalar2=1.0, op0=ALU.mult,
                                    op1=ALU.add), reads=[kA, kB], writes=[kA]))
                                L.append(lambda Bd=Bd, Cd=Cd, kB=kB, kC=kC: P.op("dve", lambda e: e.tensor_tensor_scan(
                                    out=Cd[:, 0:n], data0=smask[:, 0:n], data1=Bd[:, 0:n], initial=0.0,
                                    op0=ALU.mult, op1=ALU.add), reads=["smask", kB], writes=[kC]))
                                if d == 1:
                                    L.append(lambda Bd=Bd, Cd=Cd, kB=kB, kC=kC: P.op("pool", lambda e: e.tensor_tensor(
                                        out=Bd[:, 0:n], in0=Bd[:, 0:n], in1=Cd[:, 0:n], op=ALU.subtract),
                                        reads=[kB, kC], writes=[kB]))
                                    L.append(lambda C3=C3, kC=kC: P.op(
                                        "act", lambda e: e.copy(out=TL[:, 0:nch], in_=C3[:, :, 63]),
                                        reads=[kC], writes=["gTL"]))
                                    L.append(lambda Bd=Bd, C3=C3, kB=kB, kC=kC: P.op("dve", lambda e: e.tensor_tensor(
                                        out=C3, in0=Bd[:, 0:n].rearrange("p (c l) -> p c l", l=64),
                                        in1=TL[:, 0:nch].unsqueeze(2).broadcast_to([128, nch, 64]), op=ALU.add),
                                        reads=[kB, "gTL"], writes=[kC]))
                                    L.append(lambda: P.op("act", lambda e: e.activation(
                                        out=ebj[:, 1, j0:j0 + nch], in_=TL[:, 0:nch], func=AF.Exp),
                                        reads=["gTL"], writes=[("ebj", 1, t0)]))
                                else:
                                    L.append(lambda C3=C3, kC=kC: P.op("act", lambda e: e.activation(
                                        out=ebj[:, 0, j0:j0 + nch], in_=C3[:, :, 63], func=AF.Exp),
                                        reads=[kC], writes=[("ebj", 0, t0)]))
                                L.append(lambda Cd=Cd, Ed=Ed, kC=kC, kE=kE: P.op(
                                    "act", lambda e: e.activation(out=Ed[:, 0:n], in_=Cd[:, 0:n], func=AF.Exp),
                                    reads=[kC], writes=[kE]))
                                L.append(lambda Ed=Ed, qd=qd, qn=qn, kE=kE: P.op("dve", lambda e: e.scalar_tensor_tensor(
                                    out=qd[:, t0:t0 + n], in0=qs[:, 0:n], scalar=0.5, in1=Ed[:, 0:n], op0=ALU.mult,
                                    op1=ALU.mult),
                                    reads=["gqs", kE], writes=[(qn, t) for t in tk]))
                                L.append(lambda Bd=Bd, Cd=Cd, kB=kB, kC=kC: P.op(
                                    "act", lambda e: e.activation(out=Bd[:, 0:n], in_=Cd[:, 0:n], func=AF.Exp, scale=-1.0),
                                    reads=[kC], writes=[kB]))
                                L.append(lambda Ad=Ad, Bd=Bd, kd=kd, kn=kn, kA=kA, kB=kB: P.op(
                                    "pool", lambda e: e.tensor_tensor(out=kd[:, t0:t0 + n], in0=Ad[:, 0:n],
                                                                      in1=Bd[:, 0:n], op=ALU.mult),
                                    reads=[kA, kB], writes=[(kn, t) for t in tk]))
                            for fa, fb in zip_longest(steps[0], steps[1]):
                                if fa is not None:
                                    fa()
                                if fb is not None:
                                    fb()
                        P.barrier()
                    if hgstop == "prep":
                        continue
                    with ExitStack() as es_p:
                        pv = [PS(es_p, "gpv%d" % i, [128, 128]) for i in range(2)]
                        wv, wvk = load_w(es_p, win_c_d, 6144 + h * 128, 128, "gv")
                        for t in range(NT):
                            sl = t % 2
                            proj_tm(wv, wvk, 0, 128, pv[sl], ("gpv", sl), t)
                            P.op("act", lambda e, t=t, sl=sl: e.copy(out=vtok[:, t, :], in_=pv[sl][:, 0:128]),
                                 reads=[("gpv", sl)], writes=[("gv", t)])
                        P.barrier()
                    if hgstop == "v":
                        continue
                    with ExitStack() as es_u:
                        NVB = 16
                        U = SB(es_u, "gU", [128, 128, NCH], F32)
                        ebpo = SB(es_u, "ebpo", [128, NCH], F32)
                        ebrep = SB(es_u, "ebrep", [128, NVB, NCH], F32)
                        ktk = [SB(es_u, "gktk%d" % i, [128, 128], BF16) for i in range(2)]
                        tpk = [PS(es_u, "gtpk%d" % i, [128, 128], BF16) for i in range(2)]
                        Ups = [[PS(es_u, "gUps%d%d" % (i, hf), [128, 128]) for hf in range(2)] for i in range(2)]
                        for d in range(2):
                            (qd, qn, kd, kn) = QK[d]
                            if d == 0:
                                P.op("pool", lambda e: e.tensor_copy(out=ebpo[:], in_=ebj[:, 0, :]), writes=["ebpo"])
                            else:
                                P.op("pool", lambda e: e.tensor_copy(out=ebpo[:, 0:4], in_=rev(ebj[:, 1, :], 0, 4)),
                                     writes=["ebpo"])
                                P.op("pool", lambda e: e.tensor_copy(out=ebpo[:, 4:NCH], in_=rev(ebj[:, 1, :], 4, NCH - 4)),
                                     reads=["ebpo"], writes=["ebpo"])
                            P.op("pool", lambda e: e.memset(ebpo[:, 0:1], 0.0), reads=["ebpo"], writes=["ebpo"])
                            P.op("pool", lambda e: e.tensor_copy(
                                out=ebrep[:], in_=ebpo[:].unsqueeze(1).broadcast_to([128, NVB, NCH])),
                                reads=["ebpo"], writes=["ebrep"])
                            for t in range(NT):
                                i = t % 2
                                ts_ = slice(t * 128, (t + 1) * 128)
                                P.op("pe", lambda e, i=i: e.transpose(out=tpk[i][:], in_=kd[:, ts_], identity=ident_b[:]),
                                     reads=[(kn, t), "ident_b"], writes=[("gtpk", i)])
                                P.op("act", lambda e, i=i: e.copy(out=ktk[i][:], in_=tpk[i][:]),
                                     reads=[("gtpk", i)], writes=[("gktk", i)])
                                for hf in range(2):
                                    rs_ = slice(64 * hf, 64 * hf + 64)
                                    P.op("pe", lambda e, i=i, hf=hf, rs_=rs_: e.matmul(
                                        Ups[i][hf][:], lhsT=ktk[i][rs_, :], rhs=vtok[rs_, t, :], start=True,
                                        stop=True),
                                        reads=[("gktk", i), ("gv", t)], writes=[("gUps", i, hf)])
                                for hf in range(2):
                                    j = 2 * t + hf
                                    p_ = POS[d][j]
                                    if hf == 0:
                                        P.op("act", lambda e, i=i, hf=hf, p_=p_, d=d, j=j: e.activation(
                                            out=U[:, :, p_], in_=Ups[i][hf][:], func=AF.Copy, scale=ebj[:, d, j:j + 1]),
                                            reads=[("gUps", i, hf)], writes=[("gUp", p_)])
                                    else:
                                        P.op("dve", lambda e, i=i, hf=hf, p_=p_, d=d, j=j: e.tensor_scalar(
                                            out=U[:, :, p_], in0=Ups[i][hf][:], scalar1=ebj[:, d, j:j + 1],
                                            scalar2=None, op0=ALU.mult),
                                            reads=[("gUps", i, hf)], writes=[("gUp", p_)])
                            ukeys = [("gUp", p_) for p_ in range(NCH)]
                            for vb in range(128 // NVB):
                                Ub = U[:, vb * NVB:(vb + 1) * NVB, :].rearrange("p v j -> p (v j)")
                                P.op("dve", lambda e, Ub=Ub: e.tensor_tensor_scan(
                                    out=Ub, data0=ebrep[:].rearrange("p v j -> p (v j)"), data1=Ub, initial=0.0,
                                    op0=ALU.mult, op1=ALU.add),
                                    reads=(ukeys + ["ebrep"] if vb == 0 else []), writes=[("gUs", vb)])
                            skeys = [("gUs", vb) for vb in range(128 // NVB)]
                            hn = NCH // 2
                            Uperm = U[:].rearrange("p v j -> p j v")
                            P.op("act", lambda e, d=d: e.copy(out=SH[d][:, 0:hn, :], in_=Uperm[:, 0:hn, :]),
                                 reads=skeys, writes=[("SH", d, 0)])
                            P.op("pool", lambda e, d=d: e.tensor_copy(out=SH[d][:, hn:NCH, :], in_=Uperm[:, hn:NCH, :]),
                                 reads=skeys, writes=[("SH", d, 1)])
                            tk_ = list(P.lastw.get(("SH", d, 0), [])) + list(P.lastw.get(("SH", d, 1), []))
                            for p_ in range(NCH):
                                P.lastw[("gUp", p_)] = list(tk_)
                                P.readers[("gUp", p_)] = {}
                        P.barrier()
                    if hgstop == "u":
                        continue
                    with ExitStack() as es_c:
                        wz, wzk = load_w(es_c, win_c_d, 8192 + h * 128, 128, "gz")
                        NB = 3
                        AT = [SB(es_c, "gAT%d" % i, [128, 2, 4, 128], BF16) for i in range(NB)]
                        ot = [SB(es_c, "got%d" % i, [128, 4, 128], F32) for i in range(NB)]
                        sq = [SB(es_c, "gsq%d" % i, [128, 4, 128], F32) for i in range(NB)]
                        szl = [SB(es_c, "gszl%d" % i, [128, 4, 128], F32) for i in range(NB)]
                        yv = [SB(es_c, "gyv%d" % i, [128, 4, 128], F32) for i in range(NB)]
                        ybf = [SB(es_c, "gybf%d" % i, [128, 4, 128], BF16) for i in range(NB)]
                        ysb = [SB(es_c, "gysb%d" % i, [128, 4, 128], BF16) for i in range(NB)]
                        sm = [SB(es_c, "gsm%d" % i, [128, 3, 4], F32) for i in range(NB)]
                        pA = PS(es_c, "gpA", [128, 2, 4, 128])
                        po = PS(es_c, "gpo", [128, 4, 2, 128])
                        pz = PS(es_c, "gpz", [128, 4, 128])
                        tpy = PS(es_c, "gtpy", [128, 4, 128], BF16)
                        def S1(g):
                            i = g % NB
                            tl = [2 + 4 * g + tt for tt in range(4)]
                            for d in range(2):
                                (qd, qn, kd, kn) = QK[d]
                                for tt, t in enumerate(tl):
                                    ts_ = slice(t * 128, (t + 1) * 128)
                                    P.op("pe", lambda e, d=d, tt=tt, ts_=ts_, kd=kd, qd=qd: e.matmul(
                                        pA[:, d, tt, :], lhsT=kd[:, ts_], rhs=qd[:, ts_], start=True, stop=True),
                                        reads=[(kn, t), (qn, t)], writes=["gpA"], pe_chain=(d + tt > 0))
                            for d in range(2):
                                P.op("dve", lambda e, d=d, i=i: e.tensor_tensor(
                                    out=AT[i][:, d, :, :], in0=pA[:, d, :, :],
                                    in1=consts[:, 3 + d, :].unsqueeze(1).broadcast_to([128, 4, 128]), op=ALU.mult),
                                    reads=["gpA", "consts"], writes=[("gAT", i, d)])

                        def S2(g):
                            i = g % NB
                            tl = [2 + 4 * g + tt for tt in range(4)]
                            first = True
                            for tt, t in enumerate(tl):
                                for hf in range(2):
                                    j = 2 * t + hf
                                    cs_ = slice(j * 64, (j + 1) * 64)
                                    rs_ = slice(64 * hf, 64 * hf + 64)
                                    for d in range(2):
                                        P.op("pe", lambda e, d=d, i=i, hf=hf, tt=tt, rs_=rs_, t=t: e.matmul(
                                            po[0:64, tt, hf, :], lhsT=AT[i][rs_, d, tt, rs_], rhs=vtok[rs_, t, :],
                                            start=(d == 0), stop=False),
                                            reads=[("gAT", i, 0), ("gAT", i, 1), ("gv", t)], writes=["gpo"],
                                            pe_chain=(not first))
                                        first = False
                                    for d in range(2):
                                        (qd, qn, kd, kn) = QK[d]
                                        pm = POS[d][j] - 1
                                        P.op("pe", lambda e, d=d, hf=hf, tt=tt, cs_=cs_, pm=pm, qd=qd: e.matmul(
                                            po[0:64, tt, hf, :], lhsT=qd[:, cs_], rhs=SH[d][:, pm, :],
                                            start=False, stop=(d == 1)),
                                            reads=[(qn, t), ("SH", d, 0), ("SH", d, 1)], writes=["gpo"],
                                            pe_chain=True)
                            for hf in range(2):
                                rs_ = slice(64 * hf, 64 * hf + 64)
                                P.op("act", lambda e, i=i, hf=hf, rs_=rs_: e.copy(out=ot[i][rs_, :, :],
                                                                                  in_=po[0:64, :, hf, :]),
                                     reads=["gpo"], writes=[("got", i, hf)])
                            for tt, t in enumerate(tl):
                                proj_tm(wz, wzk, 0, 128, pz[:, tt, :], "gpz", t)
                            P.op("act", lambda e, i=i: e.activation(out=szl[i][:], in_=pz[:], func=AF.Silu),
                                 reads=["gpz"], writes=[("gszl", i)])

                        def S3(g):
                            i = g % NB
                            t0 = (2 + 4 * g) * 128
                            okeys = [("got", i, 0), ("got", i, 1)]
                            if dbg_o is not None:
                                P.dma("sp", dbg_o[t0:t0 + 512, h * 128:(h + 1) * 128].rearrange("(a p) v -> p a v", p=128),
                                      ot[i][:], reads=okeys, writes=[("dbgo1", g % 4)])
                            P.op("dve", lambda e, i=i: e.tensor_tensor(out=sq[i][:], in0=ot[i][:], in1=ot[i][:],
                                                                       op=ALU.mult),
                                 reads=okeys, writes=[("gsq", i)])
                            P.op("dve", lambda e, i=i: e.tensor_reduce(out=sm[i][:, 0, :], in_=sq[i][:], axis=AX.X,
                                                                       op=ALU.add),
                                 reads=[("gsq", i)], writes=[("gsm", i)])
                            P.op("act", lambda e, i=i: e.activation(out=sm[i][:, 1, :], in_=sm[i][:, 0, :], func=AF.Sqrt,
                                                                    scale=1.0 / 128, bias=EPS),
                                 reads=[("gsm", i)], writes=[("gsm", i)])
                            P.op("dve", lambda e, i=i: e.reciprocal(out=sm[i][:, 2, :], in_=sm[i][:, 1, :]),
                                 reads=[("gsm", i)], writes=[("gsm", i)])
                            P.op("dve", lambda e, i=i: e.tensor_tensor(
                                out=yv[i][:], in0=ot[i][:], in1=sm[i][:, 2, :].unsqueeze(2).broadcast_to([128, 4, 128]),
                                op=ALU.mult),
                                reads=okeys + [("gsm", i)], writes=[("gyv", i)])
                            P.op("pool", lambda e, i=i: e.tensor_tensor(
                                out=yv[i][:], in0=yv[i][:], in1=hnh[:].unsqueeze(1).broadcast_to([128, 4, 128]),
                                op=ALU.mult),
                                reads=[("gyv", i), "hnh"], writes=[("gyv", i)])
                            P.op("pool", lambda e, i=i: e.tensor_tensor(out=ybf[i][:], in0=yv[i][:], in1=szl[i][:],
                                                                        op=ALU.mult),
                                 reads=[("gyv", i), ("gszl", i)], writes=[("gybf", i)])

                        def S4(g):
                            i = g % NB
                            t0 = (2 + 4 * g) * 128
                            for tt in range(4):
                                P.op("pe", lambda e, i=i, tt=tt: e.transpose(out=tpy[:, tt, :], in_=ybf[i][:, tt, :],
                                                                             identity=ident_b[:]),
                                     reads=[("gybf", i), "ident_b"], writes=["gtpy"], pe_chain=(tt > 0))
                            P.op("act", lambda e, i=i: e.copy(out=ysb[i][:], in_=tpy[:]), reads=["gtpy"],
                                 writes=[("gysb", i)])
                            P.dma("sp" if i == 0 else "act", yT_d[h, :, t0:t0 + 512],
                                  ysb[i][:].rearrange("p a t -> p (a t)"), reads=[("gysb", i)],
                                  writes=[("yT_d", h, g)], semkey=("gysb", i))

                        NG = 8
                        for step in range(NG + 2):
                            if step < NG:
                                S1(step)
                                S2(step)
                            if 0 <= step - 1 < NG:
                                S3(step - 1)
                            if 0 <= step - 2 < NG:
                                S4(step - 2)
                        if dbg_o is not None:
                            P.finish("sp", [("dbgo1", i) for i in range(4)])
                        P.barrier()
            P.barrier()

    if dbg and "oa" in dbg:
        na_stage(dbg.get("_chunks", [0]), dbg_d["oa"])
    elif dbg and "hb" in dbg:
        ml_stage([0], dbg_d["hb"])
    elif dbg and "x1" in dbg:
        na_stage(list(range(8)))
        ml_stage(list(range(4)))
        out_stage(wout_ab_d, x_d, ctx_d, gate0, x1_d, ctx1_d, "x1")
        P.finish("sp", [("x1", t) for t in range(NT)])
    elif dbg and "o1" in dbg:
        pass
    elif not dbg or "_stop" in dbg:
        stop = (dbg or {}).get("_stop", "end")
        order = ["norm0", "na", "ml", "out0", "norm1", "hg", "end"]
        lvl = order.index(stop)
        if lvl >= 1:
            na_stage(list(range(8)))
        if lvl >= 2:
            ml_stage(list(range(4)))
        if lvl >= 3:
            out_stage(wout_ab_d, x_d, ctx_d, gate0, x1_d, ctx1_d, "x1")
    L0.close()
    P.barrier()
    L1 = ExitStack()
    gate1 = {0: SB(L1, "gate1_0", [128, D], F32)}
    if dbg and "o1" in dbg:
        x1_in = dt_in("x1_in", [T_LAT, D])
        ctx1_in = dt_in("ctx1_in", [T_CTX, D])
        norm_stage(1, x1_in, ctx1_in, gate1)
        hg_stage(dbg.get("_heads", [0]), dbg_d["o1"])
    elif not dbg or "_stop" in dbg:
        stop = (dbg or {}).get("_stop", "end")
        lvl = ["norm0", "na", "ml", "out0", "norm1", "hg", "end"].index(stop)
        if lvl >= 4:
            norm_stage(1, x1_d, ctx1_d, gate1)
        if lvl >= 5:
            hg_stage((dbg or {}).get("_heads", list(range(16))))
        if lvl >= 6:
            out_stage(wout_c_d, x1_d, None, gate1, out_d, None, "out")
            P.finish("sp", [("out", t) for t in range(2, NT)])
        else:
            P.dma("sp", out_d[0:128, :], x_d[0:128, :], writes=["probe_out"])
            P.finish("sp", ["probe_out"])
    L1.close()
    G.close()
    return nc


def host_inputs(inputs, b):
    cc = np.zeros((128, 16), np.float32)
    cb = np.asarray(inputs["c"][b], np.float32).reshape(8, 128)
    cx = np.asarray(inputs["c_ctx"], np.float32).reshape(8, 128)
    for k in range(8):
        cc[:, 2 * k] = cb[k]
        cc[:, 2 * k + 1] = cx[k]
    m = {
        "x": np.ascontiguousarray(inputs["x"][b], dtype=np.float32),
        "ctx": np.ascontiguousarray(inputs["ctx"][b], dtype=np.float32),
        "cc": cc,
        "w_ada": np.ascontiguousarray(inputs["w_ada"], dtype=np.float32),
        "b_ada": np.ascontiguousarray(inputs["b_ada"], dtype=np.float32),
        "norm_w": np.ascontiguousarray(inputs["norm_w"], dtype=np.float32),
        "ident": np.eye(128, dtype=np.float32),
        "consts": host_consts(),
        "w_in_ab": np.ascontiguousarray(inputs["w_in_ab"][0], dtype=np.float32),
        "qkw": np.stack([np.tile(np.asarray(inputs["q_norm_a"][0], np.float32), 2),
                         np.tile(np.asarray(inputs["k_norm_a"][0], np.float32), 2)], axis=1),
        "nab": host_nab(np.asarray(inputs["rpb_a"][0], np.float32)),
        "rope": host_rope(),
        "b_gate": np.asarray(inputs["b_gate_ab"], np.float32).reshape(1, 16),
        "h_norm_b": np.asarray(inputs["h_norm_b"], np.float32).reshape(1, D),
        "w_out_ab": np.ascontiguousarray(inputs["w_out_ab"][0], dtype=np.float32),
        "w_in_c": np.ascontiguousarray(inputs["w_in_c"][0], dtype=np.float32),
        "w_out_c": np.ascontiguousarray(inputs["w_out_c"][0], dtype=np.float32),
        "lbc": np.ascontiguousarray(np.asarray(inputs["lb_c"], np.float32).reshape(2, 16, 128).transpose(2, 1, 0)),
        "h_norm_c": np.asarray(inputs["h_norm_c"], np.float32).reshape(1, 2 * D),
        "smask": np.tile((np.arange(512) % 64 != 0).astype(np.float32)[None, :], (128, 1)),
    }
    return m


def host_rope():
    p = np.arange(128)
    freq = (10000.0 ** (-(p % 64).astype(np.float64) / 64.0))[:, None]
    pos = np.arange(64, dtype=np.float64)[None, :]
    ang = (pos.astype(np.float32) * freq.astype(np.float32)).astype(np.float32)
    cos = np.cos(ang).astype(np.float32)
    sin = np.sin(ang).astype(np.float32) * np.where(p < 64, -1.0, 1.0)[:, None].astype(np.float32)
    return np.stack([cos, sin, cos / 16, sin / 16], axis=1).astype(np.float32)


def host_consts():
    c = np.zeros((128, 6, 128), np.float32)
    p = np.arange(128)
    same = (p[:, None] // 64 == p[None, :] // 64)
    c[:, 0, :] = same
    c[:, 1, :] = (p[:, None] <= p[None, :])
    c[:, 2, :] = (p[:, None] >= p[None, :])
    c[:, 3, :] = same & (p[:, None] <= p[None, :])
    c[:, 4, :] = same & (p[:, None] >= p[None, :])
    return c


_NAB_CACHE = {}


def host_nab(rpb):
    NEG = np.float32(-30000.0)
    types = [(10, 10 + dj) for dj in range(-2, 3)]
    types += [(0, j) for j in range(4)] + [(1, j) for j in range(4)]
    types += [(30, 28 + j) for j in range(4)] + [(31, 28 + j) for j in range(4)]
    out = np.empty((16, len(types), 128, 128), np.float32)
    p = np.arange(128)
    for ti, (i, j) in enumerate(types):
        kr = (2 * j + p // 64)[:, None]
        kc = (p % 64)[:, None]
        qr = (2 * i + p // 64)[None, :]
        qc = (p % 64)[None, :]
        r0 = np.clip(qr - 4, 0, 56)
        c0 = np.clip(qc - 8, 0, 48)
        valid = (kr >= r0) & (kr < r0 + 8) & (kc >= c0) & (kc < c0 + 16)
        ri = np.clip(kr - qr + 7, 0, 14)
        ci = np.clip(kc - qc + 15, 0, 30)
        g = rpb[:, ri, ci]
        out[:, ti] = np.where(valid[None], g, NEG)
    return out


def kernel(**inputs):
    nc = build()
    in_maps = [host_inputs(inputs, b) for b in range(8)]
    res = run_bass_kernel_spmd(nc, in_maps, core_ids=list(range(8)))
    return np.stack([np.asarray(r["out"], np.float32) for r in res.results], axis=0)
```

```python
import numpy as np
import ml_dtypes
import concourse.bass as bass
import concourse.mybir as mybir
from concourse.bass_utils import run_bass_kernel_spmd

F32 = mybir.dt.float32
BF16 = mybir.dt.bfloat16
AF = mybir.ActivationFunctionType
ALU = mybir.AluOpType
AX = mybir.AxisListType

D = 1024
T_LAT = 4096
T_CTX = 256
T_ALL = T_LAT + T_CTX
NT = T_ALL // 128
EPS = 1e-6


class KeyList(list):
    pass


def _flat(keys):
    out = []
    for k in keys:
        if isinstance(k, KeyList):
            out.extend(k)
        else:
            out.append(k)
    return out


class Prog:
    def __init__(self, nc):
        self.nc = nc
        self.eng = {"pe": nc.tensor, "act": nc.scalar, "dve": nc.vector, "pool": nc.gpsimd, "sp": nc.sync}
        self.csem = {}
        self.ccnt = {}
        for e in ("pe", "act", "dve", "pool"):
            self.csem[e] = nc.alloc_semaphore("c_" + e)
            self.ccnt[e] = 0
        self.seen = {e: {} for e in self.eng}
        self.lastw = {}
        self.readers = {}
        self.dsem = {}
        self.dpool = []
        for i in range(40):
            self.dpool.append([nc.alloc_semaphore("d%d" % i), 0])
        self.nbuf = 0
        self.ninst = 0

    def sb(self, name, shape, dt):
        return self.nc.alloc_sbuf_tensor("s_" + name, list(shape), dt)

    def ps(self, name, shape, dt=F32):
        return self.nc.alloc_psum_tensor("p_" + name, list(shape), dt)

    def _deps(self, reads, writes, wadd=()):
        toks = []
        for k in reads:
            toks.extend(self.lastw.get(k, ()))
        for k in writes:
            toks.extend(self.lastw.get(k, ()))
            toks.extend(self.readers.get(k, {}).values())
        for k in wadd:
            toks.extend(self.readers.get(k, {}).values())
        return toks

    def _wait(self, e, toks, skip_sem=None):
        need = {}
        for (sem, val) in toks:
            if skip_sem is not None and sem.name == skip_sem:
                continue
            if self.seen[e].get(sem.name, 0) >= val:
                continue
            if need.get(sem.name, (None, 0))[1] < val:
                need[sem.name] = (sem, val)
        for name, (sem, val) in need.items():
            self.eng[e].wait_ge(sem, val)
            self.seen[e][name] = val

    def _commit(self, tok, reads, writes, wadd=()):
        for k in writes:
            self.lastw[k] = [tok]
            self.readers[k] = {}
        for k in wadd:
            self.lastw.setdefault(k, []).append(tok)
        for k in reads:
            if k in writes:
                continue
            r = self.readers.setdefault(k, {})
            o = r.get(tok[0].name)
            if o is None or o[1] < tok[1]:
                r[tok[0].name] = tok

    def op(self, e, fn, reads=(), writes=(), pe_chain=False):
        reads = _flat(reads)
        toks = self._deps(reads, writes)
        self._wait(e, toks, skip_sem=(self.csem[e].name if pe_chain else None))
        ins = fn(self.eng[e])
        self.ccnt[e] += 1
        ins.then_inc(self.csem[e], 1)
        tok = (self.csem[e], self.ccnt[e])
        self._commit(tok, reads, writes)
        self.ninst += 1
        return ins

    def dma(self, e, out, in_, reads=(), writes=(), semkey=None, wadd=(), **kw):
        toks = self._deps(reads, writes, wadd)
        if semkey is None:
            semkey = (tuple(writes) + tuple(reads))[0]
        if semkey not in self.dsem:
            self.dsem[semkey] = self.dpool[len(self.dsem) % len(self.dpool)]
        ent = self.dsem[semkey]
        if ent[1] > 0:
            toks.append((ent[0], ent[1]))
        self._wait(e, toks)
        ins = self.eng[e].dma_start(out=out, in_=in_, **kw)
        ent[1] += 16
        ins.then_inc(ent[0], 16)
        tok = (ent[0], ent[1])
        self._commit(tok, reads, writes, wadd)
        self.ninst += 1
        return ins

    def barrier(self):
        toks = [(self.csem[f], self.ccnt[f]) for f in self.csem if self.ccnt[f] > 0]
        toks += [(ent[0], ent[1]) for ent in self.dpool if ent[1] > 0]
        for e in self.eng:
            self._wait(e, toks)

    def finish(self, e, keys):
        toks = []
        for k in keys:
            toks.extend(self.lastw.get(k, ()))
        self._wait(e, toks)


def interleave(gens):
    active = list(gens)
    while active:
        for g in list(active):
            try:
                next(g)
            except StopIteration:
                active.remove(g)


def build(dbg=None):
    from contextlib import ExitStack
    nc = bass.Bass("TRN2", target_bir_lowering=False)
    P = Prog(nc)
    dbg_d = {}
    if dbg:
        for name, shape in dbg.items():
            if name.startswith("_"):
                continue
            dbg_d[name] = nc.dram_tensor("dbg_" + name, list(shape), F32, kind="ExternalOutput").ap()
    dt_in = lambda name, shape, dt=F32: nc.dram_tensor(name, list(shape), dt, kind="ExternalInput").ap()
    x_d = dt_in("x", [T_LAT, D])
    ctx_d = dt_in("ctx", [T_CTX, D])
    cc_d = dt_in("cc", [128, 16])
    wada_d = dt_in("w_ada", [2, D, 3 * D])
    bada_d = dt_in("b_ada", [2, 3 * D])
    normw_d = dt_in("norm_w", [2, D])
    ident_d = dt_in("ident", [128, 128])
    consts_d = dt_in("consts", [128, 6, 128])
    win_ab_d = dt_in("w_in_ab", [D, 9232])
    qkw_d = dt_in("qkw", [128, 2])
    nab_d = dt_in("nab", [16, 21, 128, 128])
    yT_d = nc.dram_tensor("yT_scr", [16, 128, T_ALL], BF16, kind="Internal").ap()
    rope_d = dt_in("rope", [128, 4, 64])
    bgate_d = dt_in("b_gate", [1, 16])
    hnb_d = dt_in("h_norm_b", [1, D])
    Hf_d = nc.dram_tensor("Hf_scr", [NT, 128, 256], F32, kind="Internal").ap()
    wout_ab_d = dt_in("w_out_ab", [2 * D, D])
    win_c_d = dt_in("w_in_c", [D, 10240])
    wout_c_d = dt_in("w_out_c", [2 * D, D])
    lbc_d = dt_in("lbc", [128, 16, 2])
    hnc_d = dt_in("h_norm_c", [1, 2 * D])
    smask_d = dt_in("smask", [128, 512])
    if dbg and "x1" in dbg:
        x1_d, ctx1_d = dbg_d["x1"], dbg_d["ctx1"]
    else:
        x1_d = nc.dram_tensor("x1_scr", [T_LAT, D], F32, kind="Internal").ap()
        ctx1_d = nc.dram_tensor("ctx1_scr", [T_CTX, D], F32, kind="Internal").ap()
    out_d = nc.dram_tensor("out", [T_LAT, D], F32, kind="ExternalOutput").ap()

    uid = [0]

    def SB(es, name, shape, dt):
        uid[0] += 1
        return es.enter_context(nc.sbuf_tensor("s%d_%s" % (uid[0], name), list(shape), dt))

    def PS(es, name, shape, dt=F32):
        uid[0] += 1
        return es.enter_context(nc.psum_tensor("p%d_%s" % (uid[0], name), list(shape), dt))

    G = ExitStack()
    ident_f = SB(G, "ident_f", [128, 128], F32)
    ident_b = SB(G, "ident_b", [128, 128], BF16)
    P.dma("sp", ident_f[:], ident_d[:, :], writes=["ident_f"])
    P.op("dve", lambda e: e.tensor_copy(out=ident_b[:], in_=ident_f[:]), reads=["ident_f"], writes=["ident_b"])
    ones_f = SB(G, "ones_f", [128, 128], F32)
    P.op("dve", lambda e: e.memset(ones_f[:], 1.0), writes=["ones_f"])
    cc = SB(G, "cc", [128, 16], F32)
    sc = SB(G, "sc", [128, 16], F32)
    P.dma("sp", cc[:], cc_d[:, :], writes=["cc"])
    P.op("act", lambda e: e.activation(out=sc[:], in_=cc[:], func=AF.Silu), reads=["cc"], writes=["sc"])
    hT = SB(G, "hT", [128, 8, T_ALL], BF16)
    wstage = SB(G, "wstage", [128, 8, 256], F32)
    consts = SB(G, "consts", [128, 6, 128], F32)
    P.dma("sp", consts[:], consts_d[:, :, :], writes=["consts"])
    bones = consts[:, 0, :]

    def norm_stage(l, xsrc, csrc, gate_tiles):
        with ExitStack() as es:
            screp = SB(es, "screp", [128, 16, 128], F32)
            for kj in range(16):
                P.op("dve", lambda e, kj=kj: e.tensor_scalar(out=screp[:, kj, :], in0=ones_f[:],
                                                             scalar1=sc[:, kj:kj + 1], scalar2=None, op0=ALU.mult),
                     reads=["sc", "ones_f"], writes=[("screp", kj)])
            wada = SB(es, "wada", [128, 8, 512], F32)
            brow = SB(es, "brow", [128, 3 * D], F32)
            nwb = SB(es, "nwb", [128, D], F32)
            shf = [SB(es, "shf%d" % j, [128, D], F32) for j in range(2)]
            mA = [SB(es, "mA%d" % j, [128, D], F32) for j in range(2)]
            mps = PS(es, "mps", [128, 512])
            P.dma("sp", brow[:], bada_d[l:l + 1, :].partition_broadcast(128), writes=["brow"])
            P.dma("act", nwb[:], normw_d[l:l + 1, :].partition_broadcast(128), writes=["nwb"])
            for blk in range(6):
                for k in range(8):
                    P.dma("sp" if k % 2 == 0 else "act", wada[:, k, :],
                          wada_d[l, k * 128:(k + 1) * 128, blk * 512:(blk + 1) * 512], writes=[("wada", k)])
                sec, half = blk // 2, blk % 2
                for j in range(2):
                    if sec == 2 and j not in gate_tiles:
                        continue
                    for k in range(8):
                        P.op("pe", lambda e, k=k, j=j: e.matmul(
                            mps[:], lhsT=screp[:, 2 * k + j, :], rhs=wada[:, k, :], start=(k == 0), stop=(k == 7)),
                            reads=[("screp", 2 * k + j), ("wada", k)], writes=["mps"], pe_chain=True)
                    dst = (shf[j], mA[j], gate_tiles.get(j))[sec]
                    dkey = (("shf", j), ("mA", j), ("gateh", j))[sec]
                    c0 = blk * 512
                    P.op("dve", lambda e, dst=dst, half=half, c0=c0: e.tensor_tensor(
                        out=dst[:, half * 512:(half + 1) * 512], in0=mps[:], in1=brow[:, c0:c0 + 512], op=ALU.add),
                        reads=["mps", "brow"], writes=[dkey + (half,)])
            for j in range(2):
                P.op("dve", lambda e, j=j: e.scalar_tensor_tensor(
                    out=mA[j][:], in0=mA[j][:], scalar=1.0, in1=nwb[:], op0=ALU.add, op1=ALU.mult),
                    reads=[("mA", j, 0), ("mA", j, 1), "nwb"], writes=[("mA", j, 0), ("mA", j, 1)])
            for j in gate_tiles:
                P.op("dve", lambda e, j=j: e.tensor_copy(out=gate_tiles[j][:, 0:1], in_=gate_tiles[j][:, 0:1]),
                     reads=[("gateh", j, 0), ("gateh", j, 1)], writes=[("gate", j)])
            xin = [SB(es, "xin%d" % i, [128, D], F32) for i in range(2)]
            junk = SB(es, "junk", [128, D], F32)
            ssq = SB(es, "ssq", [128, 2], F32)
            rstd = SB(es, "rstd", [128, 2], F32)
            hm = [SB(es, "hm%d" % i, [128, D], F32) for i in range(2)]
            hb = [SB(es, "hbf%d" % i, [128, D], BF16) for i in range(2)]
            tps = [PS(es, "tps%d" % i, [128, 8, 128], BF16) for i in range(2)]
            for t in range(NT):
                i = t % 2
                j = 1 if t < 2 else 0
                src = csrc[t * 128:(t + 1) * 128, :] if t < 2 else xsrc[(t - 2) * 128:(t - 1) * 128, :]
                P.dma("sp" if i == 0 else "act", xin[i][:], src, reads=[("x1", t)] if l == 1 else [],
                      writes=[("xin", i)])
                P.op("act", lambda e, i=i: e.activation(out=junk[:], in_=xin[i][:], func=AF.Square,
                                                        accum_out=ssq[:, i:i + 1]),
                     reads=[("xin", i)], writes=["junk", ("ssq", i)])
                P.op("act", lambda e, i=i: e.activation(out=ssq[:, i:i + 1], in_=ssq[:, i:i + 1], func=AF.Sqrt,
                                                        scale=1.0 / D, bias=EPS),
                     reads=[("ssq", i)], writes=[("ssq", i)])
                P.op("dve", lambda e, i=i: e.reciprocal(out=rstd[:, i:i + 1], in_=ssq[:, i:i + 1]),
                     reads=[("ssq", i)], writes=[("rstd", i)])
                P.op("dve", lambda e, i=i, j=j: e.scalar_tensor_tensor(
                    out=hm[i][:], in0=xin[i][:], scalar=rstd[:, i:i + 1], in1=mA[j][:],
                    op0=ALU.mult, op1=ALU.mult),
                    reads=[("xin", i), ("rstd", i), ("mA", j, 0), ("mA", j, 1)], writes=[("hm", i)])
                P.op("pool", lambda e, i=i, j=j: e.tensor_tensor(out=hb[i][:], in0=hm[i][:], in1=shf[j][:],
                                                                 op=ALU.add),
                     reads=[("hm", i), ("shf", j, 0), ("shf", j, 1)], writes=[("hb", i)])
                for c in range(8):
                    P.op("pe", lambda e, i=i, c=c: e.transpose(out=tps[i][:, c, :],
                                                               in_=hb[i][:, c * 128:(c + 1) * 128],
                                                               identity=ident_b[:]),
                         reads=[("hb", i), "ident_b"], writes=[("tps", i)], pe_chain=True)
                P.op("act", lambda e, i=i, t=t: e.copy(out=hT[:, :, t * 128:(t + 1) * 128], in_=tps[i][:]),
                     reads=[("tps", i)], writes=[("hT", t)])
            P.barrier()

    L0 = ExitStack()
    gate0 = {j: SB(L0, "gate0_%d" % j, [128, D], F32) for j in range(2)}
    norm_stage(0, x_d, ctx_d, gate0)


    def load_w(es_w, wd, col0, ncols, tag, segs=None):
        if segs is None:
            segs = [(col0, ncols)]
        pieces = []
        for (c0, n) in segs:
            while n > 0:
                m_ = min(n, 256)
                pieces.append((c0, m_))
                c0 += m_
                n -= m_
        ncols = sum(n for _, n in pieces)
        wb = SB(es_w, "wb_" + tag, [128, 8, ncols], BF16)
        groups, cur, curn = [], [], 0
        for pc in pieces:
            if curn + pc[1] > 256:
                groups.append(cur)
                cur, curn = [], 0
            cur.append(pc)
            curn += pc[1]
        groups.append(cur)
        o0 = 0
        for gi, grp in enumerate(groups):
            gn = sum(n for _, n in grp)
            for k in range(8):
                o = 0
                for si, (c0, n) in enumerate(grp):
                    P.dma("sp" if k % 2 == 0 else "act", wstage[:, k, o:o + n], wd[k * 128:(k + 1) * 128, c0:c0 + n],
                          writes=([("wstage", k)] if si == 0 else []), wadd=([] if si == 0 else [("wstage", k)]),
                          semkey=("wstage", k, si))
                    o += n
            for k in range(8):
                P.op("pool" if k % 2 == 0 else "dve",
                     lambda e, k=k, o0=o0, gn=gn: e.tensor_copy(out=wb[:, k, o0:o0 + gn], in_=wstage[:, k, 0:gn]),
                     reads=[("wstage", k)], writes=([("wb_" + tag, k)] if gi == 0 else []),
                     ) if gi == 0 else P.op("pool" if k % 2 == 0 else "dve",
                     lambda e, k=k, o0=o0, gn=gn: e.tensor_copy(out=wb[:, k, o0:o0 + gn], in_=wstage[:, k, 0:gn]),
                     reads=[("wstage", k)], writes=[("wb_" + tag, k, gi)])
            o0 += gn
        keys = [("wb_" + tag, k) for k in range(8)]
        extra = [("wb_" + tag, k, gi) for k in range(8) for gi in range(1, len(groups))]
        return wb, [KeyList([("wb_" + tag, k)] + [("wb_" + tag, k, gi) for gi in range(1, len(groups))]) for k in range(8)]

    TOKG = [(0, 256)] + [(256 + 512 * g, 512) for g in range(8)]

    def proj_fm(wb, wkeys, c0, pst, pkey, t0, n):
        for k in range(8):
            P.op("pe", lambda e, k=k: e.matmul(pst[:, 0:n], lhsT=wb[:, k, c0:c0 + 128], rhs=hT[:, k, t0:t0 + n],
                                               start=(k == 0), stop=(k == 7)),
                 reads=[wkeys[k]] + [("hT", t) for t in range(t0 // 128, (t0 + n) // 128)], writes=[pkey],
                 pe_chain=True)

    def proj_tm(wb, wkeys, c0, ncols, pst, pkey, t):
        for k in range(8):
            P.op("pe", lambda e, k=k: e.matmul(pst[:, 0:ncols], lhsT=hT[:, k, t * 128:(t + 1) * 128],
                                               rhs=wb[:, k, c0:c0 + ncols], start=(k == 0), stop=(k == 7)),
                 reads=[wkeys[k], ("hT", t)], writes=[pkey], pe_chain=True)

    def na_keytiles(t):
        if t < 2:
            return [(0, None), (1, None)]
        i = t - 2
        if i == 0:
            nb = [(j, 5 + j) for j in range(4)]
        elif i == 1:
            nb = [(j, 9 + j) for j in range(4)]
        elif i == 30:
            nb = [(28 + j, 13 + j) for j in range(4)]
        elif i == 31:
            nb = [(28 + j, 17 + j) for j in range(4)]
        else:
            nb = [(i + dj, dj + 2) for dj in range(-2, 3)]
        return [(2 + j, ty) for (j, ty) in nb] + [(0, None), (1, None)]

    def na_stage(chunks, dbg_oa=None):
        with ExitStack() as es:
            qkw = SB(es, "qkw", [128, 2], F32)
            P.dma("sp", qkw[:], qkw_d[:, :], writes=["qkw"])
            P.op("act", lambda e: e.mul(out=qkw[:, 0:1], in_=qkw[:, 0:1], mul=0.125), reads=["qkw"], writes=["qkw"])
            qT = SB(es, "qT", [128, T_ALL], BF16)
            kT = SB(es, "kT", [128, T_ALL], BF16)
            vaug = SB(es, "vaug", [128, NT, 2, 65], BF16)
            yTc = SB(es, "yTc", [128, T_ALL], BF16)
            biasf = SB(es, "biasf", [128, 2, 21, 128], F32)
            biasb = SB(es, "biasb", [128, 2, 21, 128], BF16)
            sq = SB(es, "sq", [128, 512], F32)
            rs = SB(es, "rs", [128, 512], F32)
            PT = SB(es, "PT", [128, 8, 128], BF16)
            rden = SB(es, "rden", [128, 2], F32)
            oa = SB(es, "oa", [128, 128], F32)
            sz = SB(es, "sz", [128, 128], F32)
            yab = SB(es, "yab", [128, 128], BF16)
            pp = PS(es, "pp", [128, 512])
            P.op("dve", lambda e: e.memset(vaug[:], 1.0), writes=[("vaug", t) for t in range(NT)])
            for c in chunks:
                with ExitStack() as es_w:
                    wq, wqk = load_w(es_w, win_ab_d, c * 128, 128, "q")
                    wk, wkk = load_w(es_w, win_ab_d, 1024 + c * 128, 128, "k")
                    wv, wvk = load_w(es_w, win_ab_d, 2048 + c * 128, 128, "v")
                    wz, wzk = load_w(es_w, win_ab_d, 3072 + c * 128, 128, "z")
                    for hh in range(2):
                        P.dma("sp" if hh == 0 else "act", biasf[:, hh, :, :],
                              nab_d[2 * c + hh].rearrange("t k q -> k t q"), writes=[("biasf", hh)])
                        P.op("pool", lambda e, hh=hh: e.tensor_copy(out=biasb[:, hh, :, :], in_=biasf[:, hh, :, :]),
                             reads=[("biasf", hh)], writes=[("biasb", hh)])
                    es_qk = ExitStack()
                    ssp = PS(es_qk, "ssp", [128, 512])
                    for (dst, dname, wb_, wk_, col) in ((qT, "qT", wq, wqk, 0), (kT, "kT", wk, wkk, 1)):
                        for (t0, n) in TOKG:
                            proj_fm(wb_, wk_, 0, pp, "pp", t0, n)
                            P.op("act", lambda e, n=n: e.activation(out=sq[:, 0:n], in_=pp[:, 0:n], func=AF.Square),
                                 reads=["pp"], writes=["sq"])
                            P.op("pe", lambda e, n=n: e.matmul(ssp[:, 0:n], lhsT=bones, rhs=sq[:, 0:n],
                                                               start=True, stop=True),
                                 reads=["sq", "consts"], writes=["ssp"])
                            P.op("act", lambda e, n=n: e.activation(out=rs[:, 0:n], in_=ssp[:, 0:n], func=AF.Sqrt,
                                                                    scale=1.0 / 64, bias=EPS),
                                 reads=["ssp"], writes=["rs"])
                            P.op("dve", lambda e, n=n: e.reciprocal(out=rs[:, 0:n], in_=rs[:, 0:n]),
                                 reads=["rs"], writes=["rs"])
                            P.op("dve", lambda e, n=n, t0=t0, dst=dst, col=col: e.scalar_tensor_tensor(
                                out=dst[:, t0:t0 + n], in0=pp[:, 0:n], scalar=qkw[:, col:col + 1], in1=rs[:, 0:n],
                                op0=ALU.mult, op1=ALU.mult),
                                reads=["pp", "rs", "qkw"],
                                writes=[(dname, t) for t in range(t0 // 128, (t0 + n) // 128)])
                    P.barrier()
                    es_qk.close()
                    for t in range(NT):
                        proj_tm(wv, wvk, 0, 128, pp, "pp", t)
                        P.op("act", lambda e, t=t: e.copy(out=vaug[:, t, :, 0:64],
                                                          in_=pp[:, 0:128].rearrange("p (h d) -> p h d", h=2)),
                             reads=["pp"], writes=[("vaug", t)])
                    P.barrier()
                    with ExitStack() as es_a:
                        st2 = [PS(es_a, "st%d" % i, [128, 8, 128]) for i in range(2)]
                        num2 = [PS(es_a, "num%d" % i, [128, 128]) for i in range(2)]
                        tp = PS(es_a, "tp", [128, 128], BF16)
                        PT2 = [SB(es_a, "PT%d" % i, [128, 8, 128], BF16) for i in range(2)]
                        oa2 = [SB(es_a, "oa%d" % i, [128, 128], F32) for i in range(2)]
                        sz2 = [SB(es_a, "sz%d" % i, [128, 128], F32) for i in range(2)]
                        yab2 = [SB(es_a, "yab%d" % i, [128, 128], BF16) for i in range(2)]
                        rden2 = SB(es_a, "rden2", [128, 4], F32)

                        def head_gen(t, hh, kts):
                            nk = len(kts)
                            pb = 64 * hh
                            st_, num_, PT_ = st2[hh], num2[hh], PT2[hh]
                            i = t % 2
                            for n_, (kt, ty) in enumerate(kts):
                                P.op("pe", lambda e: e.matmul(
                                    st_[:, n_, :], lhsT=kT[pb:pb + 64, kt * 128:(kt + 1) * 128],
                                    rhs=qT[pb:pb + 64, t * 128:(t + 1) * 128], start=True, stop=(ty is None)),
                                    reads=[("kT", kt), ("qT", t)], writes=[("st", hh)], pe_chain=True)
                                yield
                                if ty is not None:
                                    P.op("pe", lambda e: e.matmul(
                                        st_[:, n_, :], lhsT=ident_b[:], rhs=biasb[:, hh, ty, :], start=False, stop=True),
                                        reads=["ident_b", ("biasb", hh)], writes=[("st", hh)], pe_chain=True)
                                    yield
                            P.op("act", lambda e: e.activation(out=PT_[:, 0:nk, :], in_=st_[:, 0:nk, :], func=AF.Exp),
                                 reads=[("st", hh)], writes=[("PT", hh)])
                            yield
                            for n_, (kt, ty) in enumerate(kts):
                                P.op("pe", lambda e: e.matmul(
                                    num_[:, 0:65], lhsT=PT_[:, n_, :], rhs=vaug[:, kt, hh, :],
                                    start=(n_ == 0), stop=(n_ == nk - 1)),
                                    reads=[("PT", hh), ("vaug", kt)], writes=[("num", hh)], pe_chain=True)
                                yield
                            rd = rden2[:, 2 * i + hh:2 * i + hh + 1]
                            P.op("dve", lambda e: e.reciprocal(out=rd, in_=num_[:, 64:65]),
                                 reads=[("num", hh)], writes=[("rden", i, hh)])
                            yield
                            P.op("dve", lambda e: e.tensor_scalar(
                                out=oa2[i][:, hh * 64:(hh + 1) * 64], in0=num_[:, 0:64], scalar1=rd,
                                scalar2=None, op0=ALU.mult),
                                reads=[("num", hh), ("rden", i, hh)], writes=[("oa", i, hh)])
                            yield

                        def tail(t):
                            i = t % 2
                            if dbg_oa is not None:
                                P.dma("sp", dbg_oa[t * 128:(t + 1) * 128, c * 128:(c + 1) * 128], oa2[i][:],
                                      reads=[("oa", i, 0), ("oa", i, 1)], writes=[("dbgoa", t % 4)])
                            P.op("pool", lambda e: e.tensor_tensor(out=yab2[i][:], in0=oa2[i][:], in1=sz2[i][:],
                                                                   op=ALU.mult),
                                 reads=[("oa", i, 0), ("oa", i, 1), ("sz", i)], writes=[("yab", i)])
                            P.op("pe", lambda e: e.transpose(out=tp[:], in_=yab2[i][:], identity=ident_b[:]),
                                 reads=[("yab", i), "ident_b"], writes=["tp"])
                            P.op("act", lambda e: e.copy(out=yTc[:, t * 128:(t + 1) * 128], in_=tp[:]),
                                 reads=["tp"], writes=[("yTc", t)])

                        pend = None
                        for t in range(NT):
                            kts = na_keytiles(t)
                            i = t % 2
                            proj_tm(wz, wzk, 0, 128, pp, "pp", t)
                            P.op("act", lambda e, i=i: e.activation(out=sz2[i][:], in_=pp[:, 0:128], func=AF.Tanh,
                                                                    scale=0.5),
                                 reads=["pp"], writes=[("sz", i)])
                            P.op("dve", lambda e, i=i: e.tensor_scalar(out=sz2[i][:], in0=sz2[i][:], scalar1=0.5,
                                                                       scalar2=0.5, op0=ALU.mult, op1=ALU.add),
                                 reads=[("sz", i)], writes=[("sz", i)])
                            P.op("dve", lambda e, i=i: e.tensor_tensor(out=sz2[i][:], in0=sz2[i][:], in1=pp[:, 0:128],
                                                                       op=ALU.mult),
                                 reads=[("sz", i), "pp"], writes=[("sz", i)])
                            interleave([head_gen(t, 0, kts), head_gen(t, 1, kts)])
                            if pend is not None:
                                tail(pend)
                            pend = t
                        tail(pend)
                        P.barrier()
                    P.dma("sp", yT_d[c], yTc[:], reads=[("yTc", t) for t in range(NT)], writes=[("yT_d", c)])
                    if dbg and "qk" in dbg:
                        stg = SB(es_w, "dbgstg", [128, T_ALL], F32)
                        for ii, (src, nm) in enumerate(((qT, "qT"), (kT, "kT"))):
                            P.op("dve", lambda e, src=src: e.tensor_copy(out=stg[:], in_=src[:]),
                                 reads=[(nm, t) for t in range(NT)], writes=["dbgstg"])
                            P.dma("sp", dbg_d["qk"][ii * 128:(ii + 1) * 128, :], stg[:], reads=["dbgstg"],
                                  writes=[("dbgqk", ii)])
                        P.finish("sp", [("dbgqk", 0), ("dbgqk", 1), ("dbgqk", 2)])
                    P.barrier()
            if dbg_oa is not None:
                P.finish("sp", [("dbgoa", i) for i in range(4)])
            P.barrier()

    def ml_stage(heads, dbg_hb=None):
        with ExitStack() as es:
            rope = SB(es, "rope", [128, 4, 64], F32)
            P.dma("sp", rope[:], rope_d[:, :, :], writes=["rope"])
            bgate = SB(es, "bgate", [128, 16], F32)
            P.dma("act", bgate[:], bgate_d[0:1, :].partition_broadcast(128), writes=["bgate"])
            hnb = SB(es, "hnb", [128, D], F32)
            P.dma("sp", hnb[:], hnb_d[0:1, :].partition_broadcast(128), writes=["hnb"])
            Gt = SB(es, "Gt", [128, NT, 16], F32)
            LF = SB(es, "LF", [128, NT, 8], F32)
            Wt = SB(es, "Wt", [128, NT, 8], F32)
            Bs = SB(es, "Bs", [128, NT, 16], F32)
            EB = SB(es, "EB", [128, NT, 8], F32)
            EBL = SB(es, "EBL", [128, NT, 8], F32)
            with ExitStack() as es_w:
                pg = PS(es_w, "pg", [128, 16])
                wg, wgk = load_w(es_w, win_ab_d, 9216, 16, "g")
                for t in range(NT):
                    proj_tm(wg, wgk, 0, 16, pg, "pg", t)
                    P.op("dve", lambda e, t=t: e.tensor_tensor(out=Gt[:, t, :], in0=pg[:, 0:16], in1=bgate[:],
                                                               op=ALU.add),
                         reads=["pg", "bgate"], writes=[("Gt", t)])
                gkeys = [("Gt", t) for t in range(NT)]
                for d in range(2):
                    P.op("act", lambda e, d=d: e.activation(out=LF[:, :, 4 * d:4 * d + 4],
                                                            in_=Gt[:, :, 4 + 8 * d:8 + 8 * d], func=AF.Exp, scale=-1.0),
                         reads=gkeys, writes=[("LF", d)])
                    P.op("act", lambda e, d=d: e.activation(out=LF[:, :, 4 * d:4 * d + 4],
                                                            in_=LF[:, :, 4 * d:4 * d + 4], func=AF.Ln, bias=1.0),
                         reads=[("LF", d)], writes=[("LF", d)])
                    P.op("dve", lambda e, d=d: e.tensor_scalar(out=LF[:, :, 4 * d:4 * d + 4],
                                                               in0=LF[:, :, 4 * d:4 * d + 4], scalar1=-1.0,
                                                               scalar2=None, op0=ALU.mult),
                         reads=[("LF", d)], writes=[("LF", d)])
                for t in range(NT):
                    P.op("pe", lambda e, t=t: e.matmul(pg[:, 0:4], lhsT=consts[:, 1, :], rhs=LF[:, t, 0:4],
                                                       start=True, stop=True),
                         reads=["consts", ("LF", 0)], writes=["pg"])
                    P.op("pe", lambda e, t=t: e.matmul(pg[:, 4:8], lhsT=consts[:, 2, :], rhs=LF[:, t, 4:8],
                                                       start=True, stop=True),
                         reads=["consts", ("LF", 1)], writes=["pg"], pe_chain=True)
                    P.op("pe", lambda e, t=t: e.matmul(pg[:, 8:16], lhsT=ones_f[:], rhs=LF[:, t, 0:8],
                                                       start=True, stop=True),
                         reads=["ones_f", ("LF", 0), ("LF", 1)], writes=["pg"], pe_chain=True)
                    for d in range(2):
                        P.op("dve", lambda e, t=t, d=d: e.tensor_tensor(
                            out=Wt[:, t, 4 * d:4 * d + 4], in0=Gt[:, t, 8 * d:8 * d + 4], in1=pg[:, 4 * d:4 * d + 4],
                            op=ALU.subtract),
                            reads=["pg", ("Gt", t)], writes=[("Wt", t)])
                    P.op("dve", lambda e, t=t: e.tensor_copy(out=Bs[:, t, :], in_=pg[:, 0:16]),
                         reads=["pg"], writes=[("Bs", t)])
                P.op("act", lambda e: e.activation(out=Wt[:], in_=Wt[:], func=AF.Exp),
                     reads=[("Wt", t) for t in range(NT)], writes=["Wt"])
                P.op("act", lambda e: e.activation(out=EB[:], in_=Bs[:, :, 0:8], func=AF.Exp),
                     reads=[("Bs", t) for t in range(NT)], writes=["EB"])
                P.op("act", lambda e: e.activation(out=EBL[:], in_=Bs[:, :, 8:16], func=AF.Exp),
                     reads=[("Bs", t) for t in range(NT)], writes=["EBL"])
                P.barrier()
            for h in heads:
                with ExitStack() as es_h:
                    qT = SB(es_h, "mqT", [128, 2, T_ALL], BF16)
                    kT = SB(es_h, "mkT", [128, 2, T_ALL], BF16)
                    vaug = SB(es_h, "mvaug", [128, NT, 257], BF16)
                    P.op("pool", lambda e: e.memset(vaug[:], 1.0), writes=[("mv", t) for t in range(NT)])
                    with ExitStack() as es_p:
                        pp = PS(es_p, "mpp", [128, 512])
                        pp2 = PS(es_p, "mpp2", [128, 512])
                        t1 = SB(es_p, "t1", [128, 512], F32)
                        t2 = SB(es_p, "t2", [128, 512], F32)
                        for (dst, dn, cbase, ti) in ((qT, "mqT", 4096 + h * 256, 0), (kT, "mkT", 5120 + h * 256, 2)):
                            for cch in range(2):
                                with ExitStack() as es_w:
                                    c0 = cbase + cch * 128
                                    w, wk_ = load_w(es_w, win_ab_d, c0, 128, "a")
                                    wsw, wswk = load_w(es_w, win_ab_d, 0, 0, "b", segs=[(c0 + 64, 64), (c0, 64)])
                                    for (t0, n) in TOKG:
                                        okeys = [(dn, cch, t) for t in range(t0 // 128, (t0 + n) // 128)]
                                        proj_fm(w, wk_, 0, pp, "mpp", t0, n)
                                        if t0 < 256:
                                            P.op("act", lambda e, n=n, t0=t0, dst=dst, cch=cch, ti=ti: e.mul(
                                                out=dst[:, cch, t0:t0 + n], in_=pp[:, 0:n],
                                                mul=(1.0 if ti == 0 else 1.0 / 16)),
                                                reads=["mpp"], writes=okeys)
                                            continue
                                        proj_fm(wsw, wswk, 0, pp2, "mpp2", t0, n)
                                        r0 = (t0 - 256) // 64
                                        if cch == 0:
                                            cosv = rope[:, ti, r0:r0 + 8].unsqueeze(2).broadcast_to([128, 8, 64])
                                            sinv = rope[:, ti + 1, r0:r0 + 8].unsqueeze(2).broadcast_to([128, 8, 64])
                                        else:
                                            cosv = rope[:, ti, :].unsqueeze(1).broadcast_to([128, 8, 64])
                                            sinv = rope[:, ti + 1, :].unsqueeze(1).broadcast_to([128, 8, 64])
                                        P.op("dve", lambda e, cosv=cosv: e.tensor_tensor(
                                            out=t1[:].rearrange("p (r c) -> p r c", r=8),
                                            in0=pp[:].rearrange("p (r c) -> p r c", r=8), in1=cosv, op=ALU.mult),
                                            reads=["mpp", "rope"], writes=["t1"])
                                        P.op("dve", lambda e, sinv=sinv: e.tensor_tensor(
                                            out=t2[:].rearrange("p (r c) -> p r c", r=8),
                                            in0=pp2[:].rearrange("p (r c) -> p r c", r=8), in1=sinv, op=ALU.mult),
                                            reads=["mpp2", "rope"], writes=["t2"])
                                        P.op("pool", lambda e, t0=t0, dst=dst, cch=cch: e.tensor_tensor(
                                            out=dst[:, cch, t0:t0 + 512], in0=t1[:], in1=t2[:], op=ALU.add),
                                            reads=["t1", "t2"], writes=okeys)
                                    P.barrier()
                        with ExitStack() as es_w:
                            wv, wvk = load_w(es_w, win_ab_d, 6144 + h * 256, 256, "a")
                            for t in range(NT):
                                proj_tm(wv, wvk, 0, 256, pp, "mpp", t)
                                P.op("act", lambda e, t=t: e.copy(out=vaug[:, t, 0:256], in_=pp[:, 0:256]),
                                     reads=["mpp"], writes=[("mv", t)])
                            P.barrier()
                    if dbg and "mqk" in dbg:
                        with ExitStack() as es_d:
                            stg = SB(es_d, "dbgstg", [128, T_ALL], F32)
                            for ii, (src, nm, cch) in enumerate(((qT, "mqT", 0), (qT, "mqT", 1), (kT, "mkT", 0), (kT, "mkT", 1))):
                                P.op("dve", lambda e, src=src, cch=cch: e.tensor_copy(out=stg[:], in_=src[:, cch, :]),
                                     reads=[(nm, cch, t) for t in range(NT)], writes=["dbgstg"])
                                P.dma("sp", dbg_d["mqk"][ii * 128:(ii + 1) * 128, :], stg[:], reads=["dbgstg"],
                                      writes=[("dbgqk", ii)])
                            P.finish("sp", [("dbgqk", ii) for ii in range(4)])
                            P.barrier()
                    with ExitStack() as es_c:
                        wo, wok = load_w(es_c, win_ab_d, 0, 0, "oz", segs=[(7168 + h * 256, 256), (8192 + h * 256, 256)])
                        Cf = SB(es_c, "Cf", [128, 2, 257], F32)
                        Ct = SB(es_c, "Ct", [128, 2, 257], F32)
                        Cb = SB(es_c, "Cb", [128, 2, 257], BF16)
                        STm = SB(es_c, "STm", [128, 128], BF16)
                        kt = SB(es_c, "kt", [128, 256], BF16)
                        sm = SB(es_c, "sm", [128, 4], F32)
                        Hin = SB(es_c, "Hin", [128, 256], F32)
                        Hs = SB(es_c, "Hs", [128, 256], F32)
                        sig = SB(es_c, "sig", [128, 256], F32)
                        szb = SB(es_c, "szb", [128, 256], F32)
                        hbg = SB(es_c, "hbg", [128, 256], F32)
                        ybf = SB(es_c, "ybf", [128, 256], BF16)
                        ysb = SB(es_c, "ysb", [128, 2, 128], BF16)
                        STp = PS(es_c, "STp", [128, 128])
                        acc = PS(es_c, "acc", [128, 512])
                        cacc = PS(es_c, "cacc", [128, 2, 512])
                        po = PS(es_c, "po", [128, 512])
                        tpk = PS(es_c, "tpk", [128, 2, 128], BF16)
                        tpy = PS(es_c, "tpy", [128, 2, 128], BF16)
                        pcon = SB(es_c, "pcon", [128, 2], F32)
                        P.op("pool", lambda e: e.memset(pcon[:, 0:1], -1.0), writes=["pcon"])
                        P.op("pool", lambda e: e.memset(pcon[:, 1:2], -0.5), reads=["pcon"], writes=["pcon"])
                        acc2 = [acc, PS(es_c, "acc1", [128, 512])]
                        STm2 = [STm, SB(es_c, "STm1", [128, 128], BF16)]
                        Cb2 = [Cb, SB(es_c, "Cb1", [128, 2, 257], BF16)]

                        def ml_head(d, hd, n, t):
                            ts_ = slice(t * 128, (t + 1) * 128)
                            i = n % 2
                            for c in range(2):
                                P.op("pe", lambda e, c=c: e.matmul(STp[:], lhsT=kT[:, c, ts_], rhs=qT[:, c, ts_],
                                                                   start=(c == 0), stop=(c == 1)),
                                     reads=[("mkT", c, t), ("mqT", c, t)], writes=["STp"], pe_chain=True)
                            for c in range(2):
                                P.op("pe", lambda e, c=c: e.transpose(out=tpk[:, c, :], in_=kT[:, c, ts_],
                                                                      identity=ident_b[:]),
                                     reads=[("mkT", c, t), "ident_b"], writes=["tpk"], pe_chain=True)
                            P.op("dve", lambda e: e.scalar_tensor_tensor(
                                out=STm2[i][:], in0=STp[:], scalar=Wt[:, t, hd:hd + 1], in1=consts[:, 1 + d, :],
                                op0=ALU.mult, op1=ALU.mult),
                                reads=["STp", "Wt", "consts"], writes=[("STm", i)])
                            P.op("dve", lambda e: e.tensor_scalar(
                                out=kt[:], in0=tpk[:].rearrange("p c k -> p (c k)"), scalar1=Wt[:, t, hd:hd + 1],
                                scalar2=None, op0=ALU.mult),
                                reads=["tpk", "Wt"], writes=["kt"])
                            for c in range(2):
                                P.op("pe", lambda e, c=c: e.matmul(cacc[:, c, 0:257], lhsT=kt[:, c * 128:(c + 1) * 128],
                                                                   rhs=vaug[:, t, :], start=True, stop=True),
                                     reads=["kt", ("mv", t)], writes=["cacc"], pe_chain=(c == 1))
                            P.op("pe", lambda e: e.matmul(acc2[i][:, 0:257], lhsT=STm2[i][:], rhs=vaug[:, t, :],
                                                          start=True, stop=False),
                                 reads=[("STm", i), ("mv", t)], writes=[("acc", i)])
                            for c in range(2):
                                P.op("pe", lambda e, c=c: e.matmul(acc2[i][:, 0:257], lhsT=qT[:, c, ts_],
                                                                   rhs=Cb2[i][:, c, :], start=False, stop=(c == 1)),
                                     reads=[("mqT", c, t), ("Cb", i)], writes=[("acc", i)], pe_chain=True)
                            P.op("dve", lambda e: e.tensor_tensor(out=Ct[:], in0=cacc[:, :, 0:257], in1=Cf[:],
                                                                  op=ALU.add),
                                 reads=["cacc", "Cf"], writes=["Ct"])
                            P.op("dve", lambda e: e.tensor_scalar(
                                out=Cf[:], in0=Ct[:], scalar1=EBL[:, t, hd:hd + 1], scalar2=None, op0=ALU.mult),
                                reads=["Ct", "EBL"], writes=["Cf"])
                            P.op("pool", lambda e: e.tensor_scalar(
                                out=Cb2[1 - i][:], in0=Ct[:], scalar1=EBL[:, t, hd:hd + 1], scalar2=None, op0=ALU.mult),
                                reads=["Ct", "EBL"], writes=[("Cb", 1 - i)])

                        def ml_tail(d, hd, n, t):
                            ts_ = slice(t * 128, (t + 1) * 128)
                            i = n % 2
                            acc_ = acc2[i]
                            akey = ("acc", i)
                            P.op("act", lambda e: e.activation(
                                out=sm[:, 0:1], in_=acc_[:, 256:257], func=AF.Abs, scale=EB[:, t, hd:hd + 1]),
                                reads=[akey, "EB"], writes=["sm"])
                            P.op("dve", lambda e: e.tensor_scalar(out=sm[:, 1:2], in0=sm[:, 0:1], scalar1=1.0,
                                                                  scalar2=None, op0=ALU.max),
                                 reads=["sm"], writes=["sm"])
                            P.op("dve", lambda e: e.reciprocal(out=sm[:, 2:3], in_=sm[:, 1:2]),
                                 reads=["sm"], writes=["sm"])
                            P.op("dve", lambda e: e.tensor_tensor(
                                out=sm[:, 3:4], in0=sm[:, 2:3], in1=EB[:, t, hd:hd + 1], op=ALU.mult),
                                reads=["sm", "EB"], writes=["sm"])
                            if d == 0:
                                P.op("act", lambda e: e.activation(out=Hs[:], in_=acc_[:, 0:256], func=AF.Copy,
                                                                   scale=sm[:, 3:4]),
                                     reads=[akey, "sm"], writes=["Hs"])
                                P.dma("sp", Hf_d[t], Hs[:], reads=["Hs"], writes=[("Hf_d", t)])
                                return
                            P.dma("sp", Hin[:], Hf_d[t], reads=[("Hf_d", t)], writes=["Hin"])
                            P.op("dve", lambda e: e.scalar_tensor_tensor(
                                out=Hs[:], in0=acc_[:, 0:256], scalar=sm[:, 3:4], in1=Hin[:],
                                op0=ALU.mult, op1=ALU.add),
                                reads=[akey, "sm", "Hin"], writes=["Hs"])
                            if dbg_hb is not None:
                                P.dma("act", dbg_hb[t * 128:(t + 1) * 128, h * 256:(h + 1) * 256], Hs[:],
                                      reads=["Hs"], writes=[("dbghb", t % 4)])
                            proj_tm(wo, wok, 0, 512, po, "po", t)
                            P.op("act", lambda e: e.activation(out=sig[:], in_=po[:, 0:256], func=AF.Sigmoid),
                                 reads=["po"], writes=["sig"])
                            P.op("act", lambda e: e.activation(out=szb[:], in_=po[:, 256:512], func=AF.Silu),
                                 reads=["po"], writes=["szb"])
                            P.op("pool", lambda e: e.tensor_tensor(out=hbg[:], in0=Hs[:], in1=sig[:], op=ALU.mult),
                                 reads=["Hs", "sig"], writes=["hbg"])
                            P.op("act", lambda e: e.activation(out=sig[:], in_=hbg[:], func=AF.Square,
                                                               accum_out=sm[:, 0:1]),
                                 reads=["hbg", "sm"], writes=["sig", "sm"])
                            P.op("pool", lambda e: e.tensor_scalar(out=sm[:, 1:2], in0=sm[:, 0:1], scalar1=1.0 / 256,
                                                                   scalar2=EPS, op0=ALU.mult, op1=ALU.add),
                                 reads=["sm"], writes=["sm"])
                            P.op("pool", lambda e: e.tensor_tensor(out=sm[:, 2:3], in0=sm[:, 1:2], in1=pcon[:, 1:2],
                                                                   op=ALU.pow),
                                 reads=["sm", "pcon"], writes=["sm"])
                            P.op("dve", lambda e: e.scalar_tensor_tensor(
                                out=hbg[:], in0=hbg[:], scalar=sm[:, 2:3], in1=hnb[:, h * 256:(h + 1) * 256],
                                op0=ALU.mult, op1=ALU.mult),
                                reads=["hbg", "sm", "hnb"], writes=["hbg"])
                            P.op("pool", lambda e: e.tensor_tensor(out=ybf[:], in0=hbg[:], in1=szb[:], op=ALU.mult),
                                 reads=["hbg", "szb"], writes=["ybf"])
                            for c in range(2):
                                P.op("pe", lambda e, c=c: e.transpose(out=tpy[:, c, :], in_=ybf[:, c * 128:(c + 1) * 128],
                                                                      identity=ident_b[:]),
                                     reads=["ybf", "ident_b"], writes=["tpy"], pe_chain=True)
                            P.op("act", lambda e: e.copy(out=ysb[:], in_=tpy[:]), reads=["tpy"], writes=["ysb"])
                            P.dma("sp", yT_d[8 + 2 * h:10 + 2 * h, :, ts_].rearrange("c p t -> p c t"), ysb[:],
                                  reads=["ysb"], writes=[("yT_d", 8 + 2 * h, t)])

                        for d in range(2):
                            hd = 4 * d + h
                            order = [0, 1] + list(range(2, NT)) if d == 0 else [1, 0] + list(range(NT - 1, 1, -1))
                            P.op("dve", lambda e: e.memset(Cf[:], 0.0), writes=["Cf"])
                            P.op("pool", lambda e: e.memset(Cb2[0][:], 0.0), writes=[("Cb", 0)])
                            pend = None
                            for n, t in enumerate(order):
                                ml_head(d, hd, n, t)
                                if pend is not None:
                                    ml_tail(d, hd, *pend)
                                pend = (n, t)
                            ml_tail(d, hd, *pend)
                        if dbg_hb is not None:
                            P.finish("act", [("dbghb", i) for i in range(4)])
                        P.barrier()
            P.barrier()

    def out_stage(wout_d, xsrc, csrc, gates, dst_x, dst_c, outkey):
        with ExitStack() as es:
            wo_b = SB(es, "wo_b", [128, 16, D], BF16)
            for q4 in range(4):
                for kg in range(2):
                    for k in range(8):
                        kc = kg * 8 + k
                        P.dma("sp" if k % 2 == 0 else "act", wstage[:, k, :],
                              wout_d[kc * 128:(kc + 1) * 128, q4 * 256:(q4 + 1) * 256], writes=[("wstage", k)],
                              semkey=("wstage", k, 0))
                        P.op("pool" if k % 2 == 0 else "dve", lambda e, k=k, kc=kc, q4=q4: e.tensor_copy(
                            out=wo_b[:, kc, q4 * 256:(q4 + 1) * 256], in_=wstage[:, k, :]),
                            reads=[("wstage", k)], writes=[("wo_b", kc, q4)])
            yt = [SB(es, "yt%d" % i, [128, 16, 128], BF16) for i in range(2)]
            xt = [SB(es, "xt%d" % i, [128, D], F32) for i in range(2)]
            tmp = SB(es, "otmp", [128, D], F32)
            xo = [SB(es, "xo%d" % i, [128, D], F32) for i in range(2)]
            py = [PS(es, "py%d" % i, [128, 512]) for i in range(2)]
            tiles = list(range(NT)) if dst_c is not None else list(range(2, NT))
            for n_, t in enumerate(tiles):
                i = n_ % 2
                j = 1 if t < 2 else 0
                ts_ = slice(t * 128, (t + 1) * 128)
                P.dma("sp", yt[i][:], yT_d[:, :, ts_].rearrange("c p t -> p c t"),
                      reads=[("yT_d", c) for c in range(16)], writes=[("yt", i)])
                src = csrc[ts_, :] if t < 2 else xsrc[(t - 2) * 128:(t - 1) * 128, :]
                P.dma("act", xt[i][:], src, reads=[("x1", t)] if outkey == "out" else [], writes=[("xt", i)])
                for half in range(2):
                    for kc in range(16):
                        P.op("pe", lambda e, kc=kc, half=half, i=i: e.matmul(
                            py[half][:], lhsT=yt[i][:, kc, :], rhs=wo_b[:, kc, half * 512:(half + 1) * 512],
                            start=(kc == 0), stop=(kc == 15)),
                            reads=[("yt", i), ("wo_b", kc, 2 * half), ("wo_b", kc, 2 * half + 1)], writes=[("py", half)], pe_chain=True)
                    hs = slice(half * 512, (half + 1) * 512)
                    P.op("dve", lambda e, half=half, hs=hs, j=j: e.tensor_tensor(
                        out=tmp[:, hs], in0=py[half][:], in1=gates[j][:, hs], op=ALU.mult),
                        reads=[("py", half), ("gate", j)], writes=[("otmp", half)])
                    P.op("pool", lambda e, hs=hs, i=i: e.tensor_tensor(out=xo[i][:, hs], in0=tmp[:, hs],
                                                                       in1=xt[i][:, hs], op=ALU.add),
                         reads=[("otmp", half), ("xt", i)], writes=[("xo", i, half)])
                dst = dst_c[ts_, :] if t < 2 else dst_x[(t - 2) * 128:(t - 1) * 128, :]
                P.dma("sp", dst, xo[i][:], reads=[("xo", i, 0), ("xo", i, 1)], writes=[(outkey, t)],
                      semkey=("xo", i))
            P.barrier()

    def hg_stage(heads, dbg_o=None):
        from itertools import zip_longest
        NCH = T_ALL // 64
        ORD = [list(range(NCH)), [3, 2, 1, 0] + list(range(NCH - 1, 3, -1))]
        POS = [{j: p for p, j in enumerate(o_)} for o_ in ORD]
        hgstop = (dbg or {}).get("_hgstop")

        def rev(ap2d, lo, n):
            v = ap2d[:, lo:lo + n]
            return bass.AP(v.tensor, v.offset + (n - 1) * v.ap[-1][0], [list(v.ap[0]), [-v.ap[-1][0], n]])

        with ExitStack() as es:
            lbc = SB(es, "lbc", [128, 16, 2], F32)
            P.dma("sp", lbc[:], lbc_d[:, :, :], writes=["lbc"])
            lb = SB(es, "lb", [128, 16], F32)
            oml = SB(es, "oml", [128, 16], F32)
            P.op("dve", lambda e: e.tensor_tensor(out=lb[:], in0=lbc[:, :, 1], in1=lbc[:, :, 0], op=ALU.subtract),
                 reads=["lbc"], writes=["lb"])
            P.op("act", lambda e: e.activation(out=lb[:], in_=lb[:], func=AF.Sigmoid), reads=["lb"], writes=["lb"])
            P.op("dve", lambda e: e.tensor_scalar(out=oml[:], in0=lb[:], scalar1=-1.0, scalar2=1.0, op0=ALU.mult,
                                                  op1=ALU.add), reads=["lb"], writes=["oml"])
            smask = SB(es, "smask", [128, 512], F32)
            P.dma("sp", smask[:], smask_d[:, :], writes=["smask"])
            omh = SB(es, "omh", [128, 16], F32)
            lbh = SB(es, "lbh", [128, 16], F32)
            P.op("dve", lambda e: e.tensor_scalar(out=omh[:], in0=oml[:], scalar1=0.5, scalar2=None, op0=ALU.mult),
                 reads=["oml"], writes=["omh"])
            P.op("dve", lambda e: e.tensor_tensor(out=lbh[:], in0=lb[:], in1=omh[:], op=ALU.add),
                 reads=["lb", "omh"], writes=["lbh"])
            for h in heads:
                with ExitStack() as es_h:
                    hnh = SB(es_h, "hnh", [128, 128], F32)
                    P.dma("act", hnh[:], hnc_d[0:1, h * 128:(h + 1) * 128].partition_broadcast(128), writes=["hnh"])
                    qf = SB(es_h, "gqf", [128, T_ALL], BF16)
                    kf = SB(es_h, "gkf", [128, T_ALL], BF16)
                    qb = SB(es_h, "gqb", [128, T_ALL], BF16)
                    kb = SB(es_h, "gkb", [128, T_ALL], BF16)
                    QK = ((qf, "gqf", kf, "gkf"), (qb, "gqb", kb, "gkb"))
                    ebj = SB(es_h, "ebj", [128, 2, NCH], F32)
                    vtok = SB(es_h, "gv", [128, NT, 128], BF16)
                    SH = [SB(es_h, "SH%d" % d, [128, NCH, 128], BF16) for d in range(2)]
                    with ExitStack() as es_p:
                        pp = [PS(es_p, "gpp%d" % i, [128, 512]) for i in range(3)]
                        wq, wqk = load_w(es_p, win_c_d, h * 128, 128, "gq")
                        wf_ = [load_w(es_p, win_c_d, 2048 + h * 128, 128, "gff"),
                               load_w(es_p, win_c_d, 4096 + h * 128, 128, "gfb")]
                        qs = SB(es_p, "gqs", [128, 512], F32)
                        A = [SB(es_p, "gA%d" % d, [128, 512], F32) for d in range(2)]
                        B = [SB(es_p, "gB%d" % d, [128, 512], F32) for d in range(2)]
                        C = [SB(es_p, "gC%d" % d, [128, 512], F32) for d in range(2)]
                        E = [SB(es_p, "gE%d" % d, [128, 512], F32) for d in range(2)]
                        TL = SB(es_p, "gTL", [128, 8], F32)
                        for (t0, n) in TOKG:
                            nch = n // 64
                            j0 = t0 // 64
                            tk = [t for t in range(t0 // 128, (t0 + n) // 128)]
                            proj_fm(wq, wqk, 0, pp[2], "gpp2", t0, n)
                            P.op("act", lambda e, n=n: e.activation(out=qs[:, 0:n], in_=pp[2][:, 0:n], func=AF.Tanh,
                                                                    scale=0.5),
                                 reads=["gpp2"], writes=["gqs"])
                            P.op("dve", lambda e, n=n: e.scalar_tensor_tensor(
                                out=qs[:, 0:n], in0=qs[:, 0:n], scalar=1.0, in1=pp[2][:, 0:n], op0=ALU.add, op1=ALU.mult),
                                reads=["gqs", "gpp2"], writes=["gqs"])
                            steps = [[], []]
                            for d in range(2):
                                (w_, wk_) = wf_[d]
                                (qd, qn, kd, kn) = QK[d]
                                Ad, Bd, Cd, Ed = A[d], B[d], C[d], E[d]
                                kA, kB, kC, kE, kP = "gA%d" % d, "gB%d" % d, "gC%d" % d, "gE%d" % d, "gpp%d" % d
                                C3 = Cd[:, 0:n].rearrange("p (c l) -> p c l", l=64)
                                L = steps[d]
                                L.append(lambda w_=w_, wk_=wk_, d=d, kP=kP: proj_fm(w_, wk_, 0, pp[d], kP, t0, n))
                                L.append(lambda Ad=Ad, d=d, kP=kP, kA=kA: P.op(
                                    "act", lambda e: e.activation(out=Ad[:, 0:n], in_=pp[d][:, 0:n], func=AF.Tanh,
                                                                  scale=0.5),
                                    reads=[kP], writes=[kA]))
                                L.append(lambda Ad=Ad, kA=kA: P.op("dve", lambda e: e.tensor_scalar(
                                    out=Ad[:, 0:n], in0=Ad[:, 0:n], scalar1=omh[:, h:h + 1], scalar2=lbh[:, h:h + 1],
                                    op0=ALU.mult, op1=ALU.add), reads=[kA, "omh", "lbh"], writes=[kA]))
                                L.append(lambda Ad=Ad, Bd=Bd, kA=kA, kB=kB: P.op(
                                    "act", lambda e: e.activation(out=Bd[:, 0:n], in_=Ad[:, 0:n], func=AF.Ln),
                                    reads=[kA], writes=[kB]))
                                L.append(lambda Ad=Ad, kA=kA, kB=kB: P.op("pool", lambda e: e.tensor_scalar(
                                    out=Ad[:, 0:n], in0=Ad[:, 0:n], scalar1=-1.0, scalar2=1.0, op0=ALU.mult,
                                    op1=ALU.add), reads=[kA, kB], writes=[kA]))
                                L.append(lambda Bd=Bd, Cd=Cd, kB=kB, kC=kC: P.op("dve", lambda e: e.tensor_tensor_scan(
                                    out=Cd[:, 0:n], data0=smask[:, 0:n], data1=Bd[:, 0:n], initial=0.0,
                                    op0=ALU.mult, op1=ALU.add), reads=["smask", kB], writes=[kC]))
                                if d == 1:
                                    L.append(lambda Bd=Bd, Cd=Cd, kB=kB, kC=kC: P.op("pool", lambda e: e.tensor_tensor(
                                        out=Bd[:, 0:n], in0=Bd[:, 0:n], in1=Cd[:, 0:n], op=ALU.subtract),
                                        reads=[kB, kC], writes=[kB]))
                                    L.append(lambda C3=C3, kC=kC: P.op(
                                        "act", lambda e: e.copy(out=TL[:, 0:nch], in_=C3[:, :, 63]),
                                        reads=[kC], writes=["gTL"]))
                                    L.append(lambda Bd=Bd, C3=C3, kB=kB, kC=kC: P.op("dve", lambda e: e.tensor_tensor(
                                        out=C3, in0=Bd[:, 0:n].rearrange("p (c l) -> p c l", l=64),
                                        in1=TL[:, 0:nch].unsqueeze(2).broadcast_to([128, nch, 64]), op=ALU.add),
                                        reads=[kB, "gTL"], writes=[kC]))
                                    L.append(lambda: P.op("act", lambda e: e.activation(
                                        out=ebj[:, 1, j0:j0 + nch], in_=TL[:, 0:nch], func=AF.Exp),
                                        reads=["gTL"], writes=[("ebj", 1, t0)]))
                                else:
                                    L.append(lambda C3=C3, kC=kC: P.op("act", lambda e: e.activation(
                                        out=ebj[:, 0, j0:j0 + nch], in_=C3[:, :, 63], func=AF.Exp),
                                        reads=[kC], writes=[("ebj", 0, t0)]))
                                L.append(lambda Cd=Cd, Ed=Ed, kC=kC, kE=kE: P.op(
                                    "act", lambda e: e.activation(out=Ed[:, 0:n], in_=Cd[:, 0:n], func=AF.Exp),
                                    reads=[kC], writes=[kE]))
                                L.append(lambda Ed=Ed, qd=qd, qn=qn, kE=kE: P.op("dve", lambda e: e.scalar_tensor_tensor(
                                    out=qd[:, t0:t0 + n], in0=qs[:, 0:n], scalar=0.5, in1=Ed[:, 0:n], op0=ALU.mult,
                                    op1=ALU.mult),
                                    reads=["gqs", kE], writes=[(qn, t) for t in tk]))
                                L.append(lambda Bd=Bd, Cd=Cd, kB=kB, kC=kC: P.op(
                                    "act", lambda e: e.activation(out=Bd[:, 0:n], in_=Cd[:, 0:n], func=AF.Exp, scale=-1.0),
                                    reads=[kC], writes=[kB]))
                                L.append(lambda Ad=Ad, Bd=Bd, kd=kd, kn=kn, kA=kA, kB=kB: P.op(
                                    "pool", lambda e: e.tensor_tensor(out=kd[:, t0:t0 + n], in0=Ad[:, 0:n],
                                                                      in1=Bd[:, 0:n], op=ALU.mult),
                                    reads=[kA, kB], writes=[(kn, t) for t in tk]))
                            for fa, fb in zip_longest(steps[0], steps[1]):
                                if fa is not None:
                                    fa()
                                if fb is not None:
                                    fb()
                        P.barrier()
                    if hgstop == "prep":
                        continue
                    with ExitStack() as es_p:
                        pv = [PS(es_p, "gpv%d" % i, [128, 128]) for i in range(2)]
                        wv, wvk = load_w(es_p, win_c_d, 6144 + h * 128, 128, "gv")
                        for t in range(NT):
                            sl = t % 2
                            proj_tm(wv, wvk, 0, 128, pv[sl], ("gpv", sl), t)
                            P.op("act", lambda e, t=t, sl=sl: e.copy(out=vtok[:, t, :], in_=pv[sl][:, 0:128]),
                                 reads=[("gpv", sl)], writes=[("gv", t)])
                        P.barrier()
                    if hgstop == "v":
                        continue
                    with ExitStack() as es_u:
                        NVB = 16
                        U = SB(es_u, "gU", [128, 128, NCH], F32)
                        ebpo = SB(es_u, "ebpo", [128, NCH], F32)
                        ebrep = SB(es_u, "ebrep", [128, NVB, NCH], F32)
                        ktk = [SB(es_u, "gktk%d" % i, [128, 128], BF16) for i in range(2)]
                        tpk = [PS(es_u, "gtpk%d" % i, [128, 128], BF16) for i in range(2)]
                        Ups = [[PS(es_u, "gUps%d%d" % (i, hf), [128, 128]) for hf in range(2)] for i in range(2)]
                        for d in range(2):
                            (qd, qn, kd, kn) = QK[d]
                            if d == 0:
                                P.op("pool", lambda e: e.tensor_copy(out=ebpo[:], in_=ebj[:, 0, :]), writes=["ebpo"])
                            else:
                                P.op("pool", lambda e: e.tensor_copy(out=ebpo[:, 0:4], in_=rev(ebj[:, 1, :], 0, 4)),
                                     writes=["ebpo"])
                                P.op("pool", lambda e: e.tensor_copy(out=ebpo[:, 4:NCH], in_=rev(ebj[:, 1, :], 4, NCH - 4)),
                                     reads=["ebpo"], writes=["ebpo"])
                            P.op("pool", lambda e: e.memset(ebpo[:, 0:1], 0.0), reads=["ebpo"], writes=["ebpo"])
                            P.op("pool", lambda e: e.tensor_copy(
                                out=ebrep[:], in_=ebpo[:].unsqueeze(1).broadcast_to([128, NVB, NCH])),
                                reads=["ebpo"], writes=["ebrep"])
                            for t in range(NT):
                                i = t % 2
                                ts_ = slice(t * 128, (t + 1) * 128)
                                P.op("pe", lambda e, i=i: e.transpose(out=tpk[i][:], in_=kd[:, ts_], identity=ident_b[:]),
                                     reads=[(kn, t), "ident_b"], writes=[("gtpk", i)])
                                P.op("act", lambda e, i=i: e.copy(out=ktk[i][:], in_=tpk[i][:]),
                                     reads=[("gtpk", i)], writes=[("gktk", i)])
                                for hf in range(2):
                                    rs_ = slice(64 * hf, 64 * hf + 64)
                                    P.op("pe", lambda e, i=i, hf=hf, rs_=rs_: e.matmul(
                                        Ups[i][hf][:], lhsT=ktk[i][rs_, :], rhs=vtok[rs_, t, :], start=True,
                                        stop=True),
                                        reads=[("gktk", i), ("gv", t)], writes=[("gUps", i, hf)])
                                for hf in range(2):
                                    j = 2 * t + hf
                                    p_ = POS[d][j]
                                    if hf == 0:
                                        P.op("act", lambda e, i=i, hf=hf, p_=p_, d=d, j=j: e.activation(
                                            out=U[:, :, p_], in_=Ups[i][hf][:], func=AF.Copy, scale=ebj[:, d, j:j + 1]),
                                            reads=[("gUps", i, hf)], writes=[("gUp", p_)])
                                    else:
                                        P.op("dve", lambda e, i=i, hf=hf, p_=p_, d=d, j=j: e.tensor_scalar(
                                            out=U[:, :, p_], in0=Ups[i][hf][:], scalar1=ebj[:, d, j:j + 1],
                                            scalar2=None, op0=ALU.mult),
                                            reads=[("gUps", i, hf)], writes=[("gUp", p_)])
                            ukeys = [("gUp", p_) for p_ in range(NCH)]
                            for vb in range(128 // NVB):
                                Ub = U[:, vb * NVB:(vb + 1) * NVB, :].rearrange("p v j -> p (v j)")
                                P.op("dve", lambda e, Ub=Ub: e.tensor_tensor_scan(
                                    out=Ub, data0=ebrep[:].rearrange("p v j -> p (v j)"), data1=Ub, initial=0.0,
                                    op0=ALU.mult, op1=ALU.add),
                                    reads=(ukeys + ["ebrep"] if vb == 0 else []), writes=[("gUs", vb)])
                            skeys = [("gUs", vb) for vb in range(128 // NVB)]
                            hn = NCH // 2
                            Uperm = U[:].rearrange("p v j -> p j v")
                            P.op("act", lambda e, d=d: e.copy(out=SH[d][:, 0:hn, :], in_=Uperm[:, 0:hn, :]),
                                 reads=skeys, writes=[("SH", d, 0)])
                            P.op("pool", lambda e, d=d: e.tensor_copy(out=SH[d][:, hn:NCH, :], in_=Uperm[:, hn:NCH, :]),
                                 reads=skeys, writes=[("SH", d, 1)])
                            tk_ = list(P.lastw.get(("SH", d, 0), [])) + list(P.lastw.get(("SH", d, 1), []))
                            for p_ in range(NCH):
                                P.lastw[("gUp", p_)] = list(tk_)
                                P.readers[("gUp", p_)] = {}
                        P.barrier()
                    if hgstop == "u":
                        continue
                    with ExitStack() as es_c:
                        wz, wzk = load_w(es_c, win_c_d, 8192 + h * 128, 128, "gz")
                        NB = 3
                        AT = [SB(es_c, "gAT%d" % i, [128, 2, 4, 128], BF16) for i in range(NB)]
                        ot = [SB(es_c, "got%d" % i, [128, 4, 128], F32) for i in range(NB)]
                        sq = [SB(es_c, "gsq%d" % i, [128, 4, 128], F32) for i in range(NB)]
                        szl = [SB(es_c, "gszl%d" % i, [128, 4, 128], F32) for i in range(NB)]
                        yv = [SB(es_c, "gyv%d" % i, [128, 4, 128], F32) for i in range(NB)]
                        ybf = [SB(es_c, "gybf%d" % i, [128, 4, 128], BF16) for i in range(NB)]
                        ysb = [SB(es_c, "gysb%d" % i, [128, 4, 128], BF16) for i in range(NB)]
                        sm = [SB(es_c, "gsm%d" % i, [128, 3, 4], F32) for i in range(NB)]
                        pA = PS(es_c, "gpA", [128, 2, 4, 128])
                        po = PS(es_c, "gpo", [128, 4, 2, 128])
                        pz = PS(es_c, "gpz", [128, 4, 128])
                        tpy = PS(es_c, "gtpy", [128, 4, 128], BF16)
                        def S1(g):
                            i = g % NB
                            tl = [2 + 4 * g + tt for tt in range(4)]
                            for d in range(2):
                                (qd, qn, kd, kn) = QK[d]
                                for tt, t in enumerate(tl):
                                    ts_ = slice(t * 128, (t + 1) * 128)
                                    P.op("pe", lambda e, d=d, tt=tt, ts_=ts_, kd=kd, qd=qd: e.matmul(
                                        pA[:, d, tt, :], lhsT=kd[:, ts_], rhs=qd[:, ts_], start=True, stop=True),
                                        reads=[(kn, t), (qn, t)], writes=["gpA"], pe_chain=(d + tt > 0))
                            for d in range(2):
                                P.op("dve", lambda e, d=d, i=i: e.tensor_tensor(
                                    out=AT[i][:, d, :, :], in0=pA[:, d, :, :],
                                    in1=consts[:, 3 + d, :].unsqueeze(1).broadcast_to([128, 4, 128]), op=ALU.mult),
                                    reads=["gpA", "consts"], writes=[("gAT", i, d)])

                        def S2(g):
                            i = g % NB
                            tl = [2 + 4 * g + tt for tt in range(4)]
                            first = True
                            for tt, t in enumerate(tl):
                                for hf in range(2):
                                    j = 2 * t + hf
                                    cs_ = slice(j * 64, (j + 1) * 64)
                                    rs_ = slice(64 * hf, 64 * hf + 64)
                                    for d in range(2):
                                        P.op("pe", lambda e, d=d, i=i, hf=hf, tt=tt, rs_=rs_, t=t: e.matmul(
                                            po[0:64, tt, hf, :], lhsT=AT[i][rs_, d, tt, rs_], rhs=vtok[rs_, t, :],
                                            start=(d == 0), stop=False),
                                            reads=[("gAT", i, 0), ("gAT", i, 1), ("gv", t)], writes=["gpo"],
                                            pe_chain=(not first))
                                        first = False
                                    for d in range(2):
                                        (qd, qn, kd, kn) = QK[d]
                                        pm = POS[d][j] - 1
                                        P.op("pe", lambda e, d=d, hf=hf, tt=tt, cs_=cs_, pm=pm, qd=qd: e.matmul(
                                            po[0:64, tt, hf, :], lhsT=qd[:, cs_], rhs=SH[d][:, pm, :],
                                            start=False, stop=(d == 1)),
                                            reads=[(qn, t), ("SH", d, 0), ("SH", d, 1)], writes=["gpo"],
                                            pe_chain=True)
                            for hf in range(2):
                                rs_ = slice(64 * hf, 64 * hf + 64)
                                P.op("act", lambda e, i=i, hf=hf, rs_=rs_: e.copy(out=ot[i][rs_, :, :],
                                                                                  in_=po[0:64, :, hf, :]),
                                     reads=["gpo"], writes=[("got", i, hf)])
                            for tt, t in enumerate(tl):
                                proj_tm(wz, wzk, 0, 128, pz[:, tt, :], "gpz", t)
                            P.op("act", lambda e, i=i: e.activation(out=szl[i][:], in_=pz[:], func=AF.Silu),
                                 reads=["gpz"], writes=[("gszl", i)])

                        def S3(g):
                            i = g % NB
                            t0 = (2 + 4 * g) * 128
                            okeys = [("got", i, 0), ("got", i, 1)]
                            if dbg_o is not None:
                                P.dma("sp", dbg_o[t0:t0 + 512, h * 128:(h + 1) * 128].rearrange("(a p) v -> p a v", p=128),
                                      ot[i][:], reads=okeys, writes=[("dbgo1", g % 4)])
                            P.op("dve", lambda e, i=i: e.tensor_tensor(out=sq[i][:], in0=ot[i][:], in1=ot[i][:],
                                                                       op=ALU.mult),
                                 reads=okeys, writes=[("gsq", i)])
                            P.op("dve", lambda e, i=i: e.tensor_reduce(out=sm[i][:, 0, :], in_=sq[i][:], axis=AX.X,
                                                                       op=ALU.add),
                                 reads=[("gsq", i)], writes=[("gsm", i)])
                            P.op("act", lambda e, i=i: e.activation(out=sm[i][:, 1, :], in_=sm[i][:, 0, :], func=AF.Sqrt,
                                                                    scale=1.0 / 128, bias=EPS),
                                 reads=[("gsm", i)], writes=[("gsm", i)])
                            P.op("dve", lambda e, i=i: e.reciprocal(out=sm[i][:, 2, :], in_=sm[i][:, 1, :]),
                                 reads=[("gsm", i)], writes=[("gsm", i)])
                            P.op("dve", lambda e, i=i: e.tensor_tensor(
                                out=yv[i][:], in0=ot[i][:], in1=sm[i][:, 2, :].unsqueeze(2).broadcast_to([128, 4, 128]),
                                op=ALU.mult),
                                reads=okeys + [("gsm", i)], writes=[("gyv", i)])
                            P.op("pool", lambda e, i=i: e.tensor_tensor(
                                out=yv[i][:], in0=yv[i][:], in1=hnh[:].unsqueeze(1).broadcast_to([128, 4, 128]),
                                op=ALU.mult),
                                reads=[("gyv", i), "hnh"], writes=[("gyv", i)])
                            P.op("pool", lambda e, i=i: e.tensor_tensor(out=ybf[i][:], in0=yv[i][:], in1=szl[i][:],
                                                                        op=ALU.mult),
                                 reads=[("gyv", i), ("gszl", i)], writes=[("gybf", i)])

                        def S4(g):
                            i = g % NB
                            t0 = (2 + 4 * g) * 128
                            for tt in range(4):
                                P.op("pe", lambda e, i=i, tt=tt: e.transpose(out=tpy[:, tt, :], in_=ybf[i][:, tt, :],
                                                                             identity=ident_b[:]),
                                     reads=[("gybf", i), "ident_b"], writes=["gtpy"], pe_chain=(tt > 0))
                            P.op("act", lambda e, i=i: e.copy(out=ysb[i][:], in_=tpy[:]), reads=["gtpy"],
                                 writes=[("gysb", i)])
                            P.dma("sp" if i == 0 else "act", yT_d[h, :, t0:t0 + 512],
                                  ysb[i][:].rearrange("p a t -> p (a t)"), reads=[("gysb", i)],
                                  writes=[("yT_d", h, g)], semkey=("gysb", i))

                        NG = 8
                        for step in range(NG + 2):
                            if step < NG:
                                S1(step)
                                S2(step)
                            if 0 <= step - 1 < NG:
                                S3(step - 1)
                            if 0 <= step - 2 < NG:
                                S4(step - 2)
                        if dbg_o is not None:
                            P.finish("sp", [("dbgo1", i) for i in range(4)])
                        P.barrier()
            P.barrier()

    if dbg and "oa" in dbg:
        na_stage(dbg.get("_chunks", [0]), dbg_d["oa"])
    elif dbg and "hb" in dbg:
        ml_stage([0], dbg_d["hb"])
    elif dbg and "x1" in dbg:
        na_stage(list(range(8)))
        ml_stage(list(range(4)))
        out_stage(wout_ab_d, x_d, ctx_d, gate0, x1_d, ctx1_d, "x1")
        P.finish("sp", [("x1", t) for t in range(NT)])
    elif dbg and "o1" in dbg:
        pass
    elif not dbg or "_stop" in dbg:
        stop = (dbg or {}).get("_stop", "end")
        order = ["norm0", "na", "ml", "out0", "norm1", "hg", "end"]
        lvl = order.index(stop)
        if lvl >= 1:
            na_stage(list(range(8)))
        if lvl >= 2:
            ml_stage(list(range(4)))
        if lvl >= 3:
            out_stage(wout_ab_d, x_d, ctx_d, gate0, x1_d, ctx1_d, "x1")
    L0.close()
    P.barrier()
    L1 = ExitStack()
    gate1 = {0: SB(L1, "gate1_0", [128, D], F32)}
    if dbg and "o1" in dbg:
        x1_in = dt_in("x1_in", [T_LAT, D])
        ctx1_in = dt_in("ctx1_in", [T_CTX, D])
        norm_stage(1, x1_in, ctx1_in, gate1)
        hg_stage(dbg.get("_heads", [0]), dbg_d["o1"])
    elif not dbg or "_stop" in dbg:
        stop = (dbg or {}).get("_stop", "end")
        lvl = ["norm0", "na", "ml", "out0", "norm1", "hg", "end"].index(stop)
        if lvl >= 4:
            norm_stage(1, x1_d, ctx1_d, gate1)
        if lvl >= 5:
            hg_stage((dbg or {}).get("_heads", list(range(16))))
        if lvl >= 6:
            out_stage(wout_c_d, x1_d, None, gate1, out_d, None, "out")
            P.finish("sp", [("out", t) for t in range(2, NT)])
        else:
            P.dma("sp", out_d[0:128, :], x_d[0:128, :], writes=["probe_out"])
            P.finish("sp", ["probe_out"])
    L1.close()
    G.close()
    nc._prog_counts = dict(P.ccnt)
    return nc


def host_inputs(inputs, b):
    cc = np.zeros((128, 16), np.float32)
    cb = np.asarray(inputs["c"][b], np.float32).reshape(8, 128)
    cx = np.asarray(inputs["c_ctx"], np.float32).reshape(8, 128)
    for k in range(8):
        cc[:, 2 * k] = cb[k]
        cc[:, 2 * k + 1] = cx[k]
    m = {
        "x": np.ascontiguousarray(inputs["x"][b], dtype=np.float32),
        "ctx": np.ascontiguousarray(inputs["ctx"][b], dtype=np.float32),
        "cc": cc,
        "w_ada": np.ascontiguousarray(inputs["w_ada"], dtype=np.float32),
        "b_ada": np.ascontiguousarray(inputs["b_ada"], dtype=np.float32),
        "norm_w": np.ascontiguousarray(inputs["norm_w"], dtype=np.float32),
        "ident": np.eye(128, dtype=np.float32),
        "consts": host_consts(),
        "w_in_ab": np.ascontiguousarray(inputs["w_in_ab"][0], dtype=np.float32),
        "qkw": np.stack([np.tile(np.asarray(inputs["q_norm_a"][0], np.float32), 2),
                         np.tile(np.asarray(inputs["k_norm_a"][0], np.float32), 2)], axis=1),
        "nab": host_nab(np.asarray(inputs["rpb_a"][0], np.float32)),
        "rope": host_rope(),
        "b_gate": np.asarray(inputs["b_gate_ab"], np.float32).reshape(1, 16),
        "h_norm_b": np.asarray(inputs["h_norm_b"], np.float32).reshape(1, D),
        "w_out_ab": np.ascontiguousarray(inputs["w_out_ab"][0], dtype=np.float32),
        "w_in_c": np.ascontiguousarray(inputs["w_in_c"][0], dtype=np.float32),
        "w_out_c": np.ascontiguousarray(inputs["w_out_c"][0], dtype=np.float32),
        "lbc": np.ascontiguousarray(np.asarray(inputs["lb_c"], np.float32).reshape(2, 16, 128).transpose(2, 1, 0)),
        "h_norm_c": np.asarray(inputs["h_norm_c"], np.float32).reshape(1, 2 * D),
        "smask": np.tile((np.arange(512) % 64 != 0).astype(np.float32)[None, :], (128, 1)),
    }
    return m


def host_rope():
    p = np.arange(128)
    freq = (10000.0 ** (-(p % 64).astype(np.float64) / 64.0))[:, None]
    pos = np.arange(64, dtype=np.float64)[None, :]
    ang = (pos.astype(np.float32) * freq.astype(np.float32)).astype(np.float32)
    cos = np.cos(ang).astype(np.float32)
    sin = np.sin(ang).astype(np.float32) * np.where(p < 64, -1.0, 1.0)[:, None].astype(np.float32)
    return np.stack([cos, sin, cos / 16, sin / 16], axis=1).astype(np.float32)


def host_consts():
    c = np.zeros((128, 6, 128), np.float32)
    p = np.arange(128)
    same = (p[:, None] // 64 == p[None, :] // 64)
    c[:, 0, :] = same
    c[:, 1, :] = (p[:, None] <= p[None, :])
    c[:, 2, :] = (p[:, None] >= p[None, :])
    c[:, 3, :] = same & (p[:, None] <= p[None, :])
    c[:, 4, :] = same & (p[:, None] >= p[None, :])
    return c


_NAB_CACHE = {}


def host_nab(rpb):
    NEG = np.float32(-30000.0)
    types = [(10, 10 + dj) for dj in range(-2, 3)]
    types += [(0, j) for j in range(4)] + [(1, j) for j in range(4)]
    types += [(30, 28 + j) for j in range(4)] + [(31, 28 + j) for j in range(4)]
    out = np.empty((16, len(types), 128, 128), np.float32)
    p = np.arange(128)
    for ti, (i, j) in enumerate(types):
        kr = (2 * j + p // 64)[:, None]
        kc = (p % 64)[:, None]
        qr = (2 * i + p // 64)[None, :]
        qc = (p % 64)[None, :]
        r0 = np.clip(qr - 4, 0, 56)
        c0 = np.clip(qc - 8, 0, 48)
        valid = (kr >= r0) & (kr < r0 + 8) & (kc >= c0) & (kc < c0 + 16)
        ri = np.clip(kr - qr + 7, 0, 14)
        ci = np.clip(kc - qc + 15, 0, 30)
        g = rpb[:, ri, ci]
        out[:, ti] = np.where(valid[None], g, NEG)
    return out


def kernel(**inputs):
    nc = build()
    in_maps = [host_inputs(inputs, b) for b in range(8)]
    res = run_bass_kernel_spmd(nc, in_maps, core_ids=list(range(8)))
    return np.stack([np.asarray(r["out"], np.float32) for r in res.results], axis=0)
```

```python
import numpy as np
import ml_dtypes
import concourse.bass as bass
import concourse.mybir as mybir
from concourse.bass_utils import run_bass_kernel_spmd

F32 = mybir.dt.float32
BF16 = mybir.dt.bfloat16
AF = mybir.ActivationFunctionType
ALU = mybir.AluOpType
AX = mybir.AxisListType

D = 1024
T_LAT = 4096
T_CTX = 256
T_ALL = T_LAT + T_CTX
NT = T_ALL // 128
EPS = 1e-6


class KeyList(list):
    pass


def _flat(keys):
    out = []
    for k in keys:
        if isinstance(k, KeyList):
            out.extend(k)
        else:
            out.append(k)
    return out


class Prog:
    def __init__(self, nc):
        self.nc = nc
        self.eng = {"pe": nc.tensor, "act": nc.scalar, "dve": nc.vector, "pool": nc.gpsimd, "sp": nc.sync}
        self.csem = {}
        self.ccnt = {}
        for e in ("pe", "act", "dve", "pool"):
            self.csem[e] = nc.alloc_semaphore("c_" + e)
            self.ccnt[e] = 0
        self.seen = {e: {} for e in self.eng}
        self.lastw = {}
        self.readers = {}
        self.dsem = {}
        self.dpool = []
        for i in range(40):
            self.dpool.append([nc.alloc_semaphore("d%d" % i), 0])
        self.nbuf = 0
        self.ninst = 0

    def sb(self, name, shape, dt):
        return self.nc.alloc_sbuf_tensor("s_" + name, list(shape), dt)

    def ps(self, name, shape, dt=F32):
        return self.nc.alloc_psum_tensor("p_" + name, list(shape), dt)

    def _deps(self, reads, writes, wadd=()):
        toks = []
        for k in reads:
            toks.extend(self.lastw.get(k, ()))
        for k in writes:
            toks.extend(self.lastw.get(k, ()))
            toks.extend(self.readers.get(k, {}).values())
        for k in wadd:
            toks.extend(self.readers.get(k, {}).values())
        return toks

    def _wait(self, e, toks, skip_sem=None):
        need = {}
        for (sem, val) in toks:
            if skip_sem is not None and sem.name == skip_sem:
                continue
            if self.seen[e].get(sem.name, 0) >= val:
                continue
            if need.get(sem.name, (None, 0))[1] < val:
                need[sem.name] = (sem, val)
        for name, (sem, val) in need.items():
            self.eng[e].wait_ge(sem, val)
            self.seen[e][name] = val

    def _commit(self, tok, reads, writes, wadd=()):
        for k in writes:
            self.lastw[k] = [tok]
            self.readers[k] = {}
        for k in wadd:
            self.lastw.setdefault(k, []).append(tok)
        for k in reads:
            if k in writes:
                continue
            r = self.readers.setdefault(k, {})
            o = r.get(tok[0].name)
            if o is None or o[1] < tok[1]:
                r[tok[0].name] = tok

    def op(self, e, fn, reads=(), writes=(), pe_chain=False):
        reads = _flat(reads)
        toks = self._deps(reads, writes)
        self._wait(e, toks, skip_sem=(self.csem[e].name if pe_chain else None))
        ins = fn(self.eng[e])
        self.ccnt[e] += 1
        ins.then_inc(self.csem[e], 1)
        tok = (self.csem[e], self.ccnt[e])
        self._commit(tok, reads, writes)
        self.ninst += 1
        return ins

    def dma(self, e, out, in_, reads=(), writes=(), semkey=None, wadd=(), **kw):
        toks = self._deps(reads, writes, wadd)
        if semkey is None:
            semkey = (tuple(writes) + tuple(reads))[0]
        if semkey not in self.dsem:
            self.dsem[semkey] = self.dpool[len(self.dsem) % len(self.dpool)]
        ent = self.dsem[semkey]
        if ent[1] > 0:
            toks.append((ent[0], ent[1]))
        self._wait(e, toks)
        ins = self.eng[e].dma_start(out=out, in_=in_, **kw)
        ent[1] += 16
        ins.then_inc(ent[0], 16)
        tok = (ent[0], ent[1])
        self._commit(tok, reads, writes, wadd)
        self.ninst += 1
        return ins

    def barrier(self):
        toks = [(self.csem[f], self.ccnt[f]) for f in self.csem if self.ccnt[f] > 0]
        toks += [(ent[0], ent[1]) for ent in self.dpool if ent[1] > 0]
        for e in self.eng:
            self._wait(e, toks)

    def finish(self, e, keys):
        toks = []
        for k in keys:
            toks.extend(self.lastw.get(k, ()))
        self._wait(e, toks)


def interleave(gens):
    active = list(gens)
    while active:
        for g in list(active):
            try:
                next(g)
            except StopIteration:
                active.remove(g)


def build(dbg=None):
    from contextlib import ExitStack
    nc = bass.Bass("TRN2", target_bir_lowering=False)
    P = Prog(nc)
    dbg_d = {}
    if dbg:
        for name, shape in dbg.items():
            if name.startswith("_"):
                continue
            dbg_d[name] = nc.dram_tensor("dbg_" + name, list(shape), F32, kind="ExternalOutput").ap()
    dt_in = lambda name, shape, dt=F32: nc.dram_tensor(name, list(shape), dt, kind="ExternalInput").ap()
    x_d = dt_in("x", [T_LAT, D])
    ctx_d = dt_in("ctx", [T_CTX, D])
    cc_d = dt_in("cc", [128, 16])
    wada_d = dt_in("w_ada", [2, D, 3 * D])
    bada_d = dt_in("b_ada", [2, 3 * D])
    normw_d = dt_in("norm_w", [2, D])
    ident_d = dt_in("ident", [128, 128])
    consts_d = dt_in("consts", [128, 6, 128])
    win_ab_d = dt_in("w_in_ab", [D, 9232])
    qkw_d = dt_in("qkw", [128, 2])
    nab_d = dt_in("nab", [16, 21, 128, 128])
    yT_d = nc.dram_tensor("yT_scr", [16, 128, T_ALL], BF16, kind="Internal").ap()
    rope_d = dt_in("rope", [128, 4, 64])
    bgate_d = dt_in("b_gate", [1, 16])
    hnb_d = dt_in("h_norm_b", [1, D])
    Hf_d = nc.dram_tensor("Hf_scr", [NT, 128, 256], F32, kind="Internal").ap()
    wout_ab_d = dt_in("w_out_ab", [2 * D, D])
    win_c_d = dt_in("w_in_c", [D, 10240])
    wout_c_d = dt_in("w_out_c", [2 * D, D])
    lbc_d = dt_in("lbc", [128, 16, 2])
    hnc_d = dt_in("h_norm_c", [1, 2 * D])
    smask_d = dt_in("smask", [128, 512])
    if dbg and "x1" in dbg:
        x1_d, ctx1_d = dbg_d["x1"], dbg_d["ctx1"]
    else:
        x1_d = nc.dram_tensor("x1_scr", [T_LAT, D], F32, kind="Internal").ap()
        ctx1_d = nc.dram_tensor("ctx1_scr", [T_CTX, D], F32, kind="Internal").ap()
    out_d = nc.dram_tensor("out", [T_LAT, D], F32, kind="ExternalOutput").ap()

    uid = [0]

    def SB(es, name, shape, dt):
        uid[0] += 1
        return es.enter_context(nc.sbuf_tensor("s%d_%s" % (uid[0], name), list(shape), dt))

    def PS(es, name, shape, dt=F32):
        uid[0] += 1
        return es.enter_context(nc.psum_tensor("p%d_%s" % (uid[0], name), list(shape), dt))

    G = ExitStack()
    ident_f = SB(G, "ident_f", [128, 128], F32)
    ident_b = SB(G, "ident_b", [128, 128], BF16)
    P.dma("sp", ident_f[:], ident_d[:, :], writes=["ident_f"])
    P.op("dve", lambda e: e.tensor_copy(out=ident_b[:], in_=ident_f[:]), reads=["ident_f"], writes=["ident_b"])
    ones_f = SB(G, "ones_f", [128, 128], F32)
    P.op("dve", lambda e: e.memset(ones_f[:], 1.0), writes=["ones_f"])
    cc = SB(G, "cc", [128, 16], F32)
    sc = SB(G, "sc", [128, 16], F32)
    P.dma("sp", cc[:], cc_d[:, :], writes=["cc"])
    P.op("act", lambda e: e.activation(out=sc[:], in_=cc[:], func=AF.Silu), reads=["cc"], writes=["sc"])
    hT = SB(G, "hT", [128, 8, T_ALL], BF16)
    wstage = SB(G, "wstage", [128, 8, 256], F32)
    consts = SB(G, "consts", [128, 6, 128], F32)
    P.dma("sp", consts[:], consts_d[:, :, :], writes=["consts"])
    bones = consts[:, 0, :]

    def norm_stage(l, xsrc, csrc, gate_tiles):
        with ExitStack() as es:
            screp = SB(es, "screp", [128, 16, 128], F32)
            for kj in range(16):
                P.op("dve", lambda e, kj=kj: e.tensor_scalar(out=screp[:, kj, :], in0=ones_f[:],
                                                             scalar1=sc[:, kj:kj + 1], scalar2=None, op0=ALU.mult),
                     reads=["sc", "ones_f"], writes=[("screp", kj)])
            wada = SB(es, "wada", [128, 8, 512], F32)
            brow = SB(es, "brow", [128, 3 * D], F32)
            nwb = SB(es, "nwb", [128, D], F32)
            shf = [SB(es, "shf%d" % j, [128, D], F32) for j in range(2)]
            mA = [SB(es, "mA%d" % j, [128, D], F32) for j in range(2)]
            mps = PS(es, "mps", [128, 512])
            P.dma("sp", brow[:], bada_d[l:l + 1, :].partition_broadcast(128), writes=["brow"])
            P.dma("act", nwb[:], normw_d[l:l + 1, :].partition_broadcast(128), writes=["nwb"])
            for blk in range(6):
                for k in range(8):
                    P.dma("sp" if k % 2 == 0 else "act", wada[:, k, :],
                          wada_d[l, k * 128:(k + 1) * 128, blk * 512:(blk + 1) * 512], writes=[("wada", k)])
                sec, half = blk // 2, blk % 2
                for j in range(2):
                    if sec == 2 and j not in gate_tiles:
                        continue
                    for k in range(8):
                        P.op("pe", lambda e, k=k, j=j: e.matmul(
                            mps[:], lhsT=screp[:, 2 * k + j, :], rhs=wada[:, k, :], start=(k == 0), stop=(k == 7)),
                            reads=[("screp", 2 * k + j), ("wada", k)], writes=["mps"], pe_chain=True)
                    dst = (shf[j], mA[j], gate_tiles.get(j))[sec]
                    dkey = (("shf", j), ("mA", j), ("gateh", j))[sec]
                    c0 = blk * 512
                    P.op("dve", lambda e, dst=dst, half=half, c0=c0: e.tensor_tensor(
                        out=dst[:, half * 512:(half + 1) * 512], in0=mps[:], in1=brow[:, c0:c0 + 512], op=ALU.add),
                        reads=["mps", "brow"], writes=[dkey + (half,)])
            for j in range(2):
                P.op("dve", lambda e, j=j: e.scalar_tensor_tensor(
                    out=mA[j][:], in0=mA[j][:], scalar=1.0, in1=nwb[:], op0=ALU.add, op1=ALU.mult),
                    reads=[("mA", j, 0), ("mA", j, 1), "nwb"], writes=[("mA", j, 0), ("mA", j, 1)])
            for j in gate_tiles:
                P.op("dve", lambda e, j=j: e.tensor_copy(out=gate_tiles[j][:, 0:1], in_=gate_tiles[j][:, 0:1]),
                     reads=[("gateh", j, 0), ("gateh", j, 1)], writes=[("gate", j)])
            xin = [SB(es, "xin%d" % i, [128, D], F32) for i in range(2)]
            junk = SB(es, "junk", [128, D], F32)
            ssq = SB(es, "ssq", [128, 2], F32)
            rstd = SB(es, "rstd", [128, 2], F32)
            hm = [SB(es, "hm%d" % i, [128, D], F32) for i in range(2)]
            hb = [SB(es, "hbf%d" % i, [128, D], BF16) for i in range(2)]
            tps = [PS(es, "tps%d" % i, [128, 8, 128], BF16) for i in range(2)]
            for t in range(NT):
                i = t % 2
                j = 1 if t < 2 else 0
                src = csrc[t * 128:(t + 1) * 128, :] if t < 2 else xsrc[(t - 2) * 128:(t - 1) * 128, :]
                P.dma("sp" if i == 0 else "act", xin[i][:], src, reads=[("x1", t)] if l == 1 else [],
                      writes=[("xin", i)])
                P.op("act", lambda e, i=i: e.activation(out=junk[:], in_=xin[i][:], func=AF.Square,
                                                        accum_out=ssq[:, i:i + 1]),
                     reads=[("xin", i)], writes=["junk", ("ssq", i)])
                P.op("act", lambda e, i=i: e.activation(out=ssq[:, i:i + 1], in_=ssq[:, i:i + 1], func=AF.Sqrt,
                                                        scale=1.0 / D, bias=EPS),
                     reads=[("ssq", i)], writes=[("ssq", i)])
                P.op("dve", lambda e, i=i: e.reciprocal(out=rstd[:, i:i + 1], in_=ssq[:, i:i + 1]),
                     reads=[("ssq", i)], writes=[("rstd", i)])
                P.op("dve", lambda e, i=i, j=j: e.scalar_tensor_tensor(
                    out=hm[i][:], in0=xin[i][:], scalar=rstd[:, i:i + 1], in1=mA[j][:],
                    op0=ALU.mult, op1=ALU.mult),
                    reads=[("xin", i), ("rstd", i), ("mA", j, 0), ("mA", j, 1)], writes=[("hm", i)])
                P.op("pool", lambda e, i=i, j=j: e.tensor_tensor(out=hb[i][:], in0=hm[i][:], in1=shf[j][:],
                                                                 op=ALU.add),
                     reads=[("hm", i), ("shf", j, 0), ("shf", j, 1)], writes=[("hb", i)])
                for c in range(8):
                    P.op("pe", lambda e, i=i, c=c: e.transpose(out=tps[i][:, c, :],
                                                               in_=hb[i][:, c * 128:(c + 1) * 128],
                                                               identity=ident_b[:]),
                         reads=[("hb", i), "ident_b"], writes=[("tps", i)], pe_chain=True)
                P.op("act", lambda e, i=i, t=t: e.copy(out=hT[:, :, t * 128:(t + 1) * 128], in_=tps[i][:]),
                     reads=[("tps", i)], writes=[("hT", t)])
            P.barrier()

    L0 = ExitStack()
    gate0 = {j: SB(L0, "gate0_%d" % j, [128, D], F32) for j in range(2)}
    norm_stage(0, x_d, ctx_d, gate0)


    def load_w(es_w, wd, col0, ncols, tag, segs=None):
        if segs is None:
            segs = [(col0, ncols)]
        pieces = []
        for (c0, n) in segs:
            while n > 0:
                m_ = min(n, 256)
                pieces.append((c0, m_))
                c0 += m_
                n -= m_
        ncols = sum(n for _, n in pieces)
        wb = SB(es_w, "wb_" + tag, [128, 8, ncols], BF16)
        groups, cur, curn = [], [], 0
        for pc in pieces:
            if curn + pc[1] > 256:
                groups.append(cur)
                cur, curn = [], 0
            cur.append(pc)
            curn += pc[1]
        groups.append(cur)
        o0 = 0
        for gi, grp in enumerate(groups):
            gn = sum(n for _, n in grp)
            for k in range(8):
                o = 0
                for si, (c0, n) in enumerate(grp):
                    P.dma("sp" if k % 2 == 0 else "act", wstage[:, k, o:o + n], wd[k * 128:(k + 1) * 128, c0:c0 + n],
                          writes=([("wstage", k)] if si == 0 else []), wadd=([] if si == 0 else [("wstage", k)]),
                          semkey=("wstage", k, si))
                    o += n
            for k in range(8):
                P.op("pool" if k % 2 == 0 else "dve",
                     lambda e, k=k, o0=o0, gn=gn: e.tensor_copy(out=wb[:, k, o0:o0 + gn], in_=wstage[:, k, 0:gn]),
                     reads=[("wstage", k)], writes=([("wb_" + tag, k)] if gi == 0 else []),
                     ) if gi == 0 else P.op("pool" if k % 2 == 0 else "dve",
                     lambda e, k=k, o0=o0, gn=gn: e.tensor_copy(out=wb[:, k, o0:o0 + gn], in_=wstage[:, k, 0:gn]),
                     reads=[("wstage", k)], writes=[("wb_" + tag, k, gi)])
            o0 += gn
        keys = [("wb_" + tag, k) for k in range(8)]
        extra = [("wb_" + tag, k, gi) for k in range(8) for gi in range(1, len(groups))]
        return wb, [KeyList([("wb_" + tag, k)] + [("wb_" + tag, k, gi) for gi in range(1, len(groups))]) for k in range(8)]

    TOKG = [(0, 256)] + [(256 + 512 * g, 512) for g in range(8)]

    def proj_fm(wb, wkeys, c0, pst, pkey, t0, n):
        for k in range(8):
            P.op("pe", lambda e, k=k: e.matmul(pst[:, 0:n], lhsT=wb[:, k, c0:c0 + 128], rhs=hT[:, k, t0:t0 + n],
                                               start=(k == 0), stop=(k == 7)),
                 reads=[wkeys[k]] + [("hT", t) for t in range(t0 // 128, (t0 + n) // 128)], writes=[pkey],
                 pe_chain=True)

    def proj_tm(wb, wkeys, c0, ncols, pst, pkey, t):
        for k in range(8):
            P.op("pe", lambda e, k=k: e.matmul(pst[:, 0:ncols], lhsT=hT[:, k, t * 128:(t + 1) * 128],
                                               rhs=wb[:, k, c0:c0 + ncols], start=(k == 0), stop=(k == 7)),
                 reads=[wkeys[k], ("hT", t)], writes=[pkey], pe_chain=True)

    def na_keytiles(t):
        if t < 2:
            return [(0, None), (1, None)]
        i = t - 2
        if i == 0:
            nb = [(j, 5 + j) for j in range(4)]
        elif i == 1:
            nb = [(j, 9 + j) for j in range(4)]
        elif i == 30:
            nb = [(28 + j, 13 + j) for j in range(4)]
        elif i == 31:
            nb = [(28 + j, 17 + j) for j in range(4)]
        else:
            nb = [(i + dj, dj + 2) for dj in range(-2, 3)]
        return [(2 + j, ty) for (j, ty) in nb] + [(0, None), (1, None)]

    def na_stage(chunks, dbg_oa=None):
        with ExitStack() as es:
            qkw = SB(es, "qkw", [128, 2], F32)
            P.dma("sp", qkw[:], qkw_d[:, :], writes=["qkw"])
            P.op("act", lambda e: e.mul(out=qkw[:, 0:1], in_=qkw[:, 0:1], mul=0.125), reads=["qkw"], writes=["qkw"])
            qT = SB(es, "qT", [128, T_ALL], BF16)
            kT = SB(es, "kT", [128, T_ALL], BF16)
            vaug = SB(es, "vaug", [128, NT, 2, 65], BF16)
            yTc = SB(es, "yTc", [128, T_ALL], BF16)
            biasf = SB(es, "biasf", [128, 2, 21, 128], F32)
            biasb = SB(es, "biasb", [128, 2, 21, 128], BF16)
            sq = SB(es, "sq", [128, 512], F32)
            rs = SB(es, "rs", [128, 512], F32)
            PT = SB(es, "PT", [128, 8, 128], BF16)
            rden = SB(es, "rden", [128, 2], F32)
            oa = SB(es, "oa", [128, 128], F32)
            sz = SB(es, "sz", [128, 128], F32)
            yab = SB(es, "yab", [128, 128], BF16)
            pp = PS(es, "pp", [128, 512])
            P.op("dve", lambda e: e.memset(vaug[:], 1.0), writes=[("vaug", t) for t in range(NT)])
            for c in chunks:
                with ExitStack() as es_w:
                    wq, wqk = load_w(es_w, win_ab_d, c * 128, 128, "q")
                    wk, wkk = load_w(es_w, win_ab_d, 1024 + c * 128, 128, "k")
                    wv, wvk = load_w(es_w, win_ab_d, 2048 + c * 128, 128, "v")
                    wz, wzk = load_w(es_w, win_ab_d, 3072 + c * 128, 128, "z")
                    for hh in range(2):
                        P.dma("sp" if hh == 0 else "act", biasf[:, hh, :, :],
                              nab_d[2 * c + hh].rearrange("t k q -> k t q"), writes=[("biasf", hh)])
                        P.op("pool", lambda e, hh=hh: e.tensor_copy(out=biasb[:, hh, :, :], in_=biasf[:, hh, :, :]),
                             reads=[("biasf", hh)], writes=[("biasb", hh)])
                    es_qk = ExitStack()
                    ssp = PS(es_qk, "ssp", [128, 512])
                    for (dst, dname, wb_, wk_, col) in ((qT, "qT", wq, wqk, 0), (kT, "kT", wk, wkk, 1)):
                        for (t0, n) in TOKG:
                            proj_fm(wb_, wk_, 0, pp, "pp", t0, n)
                            P.op("act", lambda e, n=n: e.activation(out=sq[:, 0:n], in_=pp[:, 0:n], func=AF.Square),
                                 reads=["pp"], writes=["sq"])
                            P.op("pe", lambda e, n=n: e.matmul(ssp[:, 0:n], lhsT=bones, rhs=sq[:, 0:n],
                                                               start=True, stop=True),
                                 reads=["sq", "consts"], writes=["ssp"])
                            P.op("act", lambda e, n=n: e.activation(out=rs[:, 0:n], in_=ssp[:, 0:n], func=AF.Sqrt,
                                                                    scale=1.0 / 64, bias=EPS),
                                 reads=["ssp"], writes=["rs"])
                            P.op("dve", lambda e, n=n: e.reciprocal(out=rs[:, 0:n], in_=rs[:, 0:n]),
                                 reads=["rs"], writes=["rs"])
                            P.op("dve", lambda e, n=n, t0=t0, dst=dst, col=col: e.scalar_tensor_tensor(
                                out=dst[:, t0:t0 + n], in0=pp[:, 0:n], scalar=qkw[:, col:col + 1], in1=rs[:, 0:n],
                                op0=ALU.mult, op1=ALU.mult),
                                reads=["pp", "rs", "qkw"],
                                writes=[(dname, t) for t in range(t0 // 128, (t0 + n) // 128)])
                    P.barrier()
                    es_qk.close()
                    for t in range(NT):
                        proj_tm(wv, wvk, 0, 128, pp, "pp", t)
                        P.op("act", lambda e, t=t: e.copy(out=vaug[:, t, :, 0:64],
                                                          in_=pp[:, 0:128].rearrange("p (h d) -> p h d", h=2)),
                             reads=["pp"], writes=[("vaug", t)])
                    P.barrier()
                    with ExitStack() as es_a:
                        st2 = [PS(es_a, "st%d" % i, [128, 8, 128]) for i in range(2)]
                        num2 = [PS(es_a, "num%d" % i, [128, 128]) for i in range(2)]
                        tp = PS(es_a, "tp", [128, 128], BF16)
                        PT2 = [SB(es_a, "PT%d" % i, [128, 8, 128], BF16) for i in range(2)]
                        oa2 = [SB(es_a, "oa%d" % i, [128, 128], F32) for i in range(2)]
                        sz2 = [SB(es_a, "sz%d" % i, [128, 128], F32) for i in range(2)]
                        yab2 = [SB(es_a, "yab%d" % i, [128, 128], BF16) for i in range(2)]
                        rden2 = SB(es_a, "rden2", [128, 4], F32)

                        def head_gen(t, hh, kts):
                            nk = len(kts)
                            pb = 64 * hh
                            st_, num_, PT_ = st2[hh], num2[hh], PT2[hh]
                            i = t % 2
                            for n_, (kt, ty) in enumerate(kts):
                                P.op("pe", lambda e: e.matmul(
                                    st_[:, n_, :], lhsT=kT[pb:pb + 64, kt * 128:(kt + 1) * 128],
                                    rhs=qT[pb:pb + 64, t * 128:(t + 1) * 128], start=True, stop=(ty is None)),
                                    reads=[("kT", kt), ("qT", t)], writes=[("st", hh)], pe_chain=True)
                                yield
                                if ty is not None:
                                    P.op("pe", lambda e: e.matmul(
                                        st_[:, n_, :], lhsT=ident_b[:], rhs=biasb[:, hh, ty, :], start=False, stop=True),
                                        reads=["ident_b", ("biasb", hh)], writes=[("st", hh)], pe_chain=True)
                                    yield
                            P.op("act", lambda e: e.activation(out=PT_[:, 0:nk, :], in_=st_[:, 0:nk, :], func=AF.Exp),
                                 reads=[("st", hh)], writes=[("PT", hh)])
                            yield
                            for n_, (kt, ty) in enumerate(kts):
                                P.op("pe", lambda e: e.matmul(
                                    num_[:, 0:65], lhsT=PT_[:, n_, :], rhs=vaug[:, kt, hh, :],
                                    start=(n_ == 0), stop=(n_ == nk - 1)),
                                    reads=[("PT", hh), ("vaug", kt)], writes=[("num", hh)], pe_chain=True)
                                yield
                            rd = rden2[:, 2 * i + hh:2 * i + hh + 1]
                            P.op("dve", lambda e: e.reciprocal(out=rd, in_=num_[:, 64:65]),
                                 reads=[("num", hh)], writes=[("rden", i, hh)])
                            yield
                            P.op("dve", lambda e: e.tensor_scalar(
                                out=oa2[i][:, hh * 64:(hh + 1) * 64], in0=num_[:, 0:64], scalar1=rd,
                                scalar2=None, op0=ALU.mult),
                                reads=[("num", hh), ("rden", i, hh)], writes=[("oa", i, hh)])
                            yield

                        def tail(t):
                            i = t % 2
                            if dbg_oa is not None:
                                P.dma("sp", dbg_oa[t * 128:(t + 1) * 128, c * 128:(c + 1) * 128], oa2[i][:],
                                      reads=[("oa", i, 0), ("oa", i, 1)], writes=[("dbgoa", t % 4)])
                            P.op("pool", lambda e: e.tensor_tensor(out=yab2[i][:], in0=oa2[i][:], in1=sz2[i][:],
                                                                   op=ALU.mult),
                                 reads=[("oa", i, 0), ("oa", i, 1), ("sz", i)], writes=[("yab", i)])
                            P.op("pe", lambda e: e.transpose(out=tp[:], in_=yab2[i][:], identity=ident_b[:]),
                                 reads=[("yab", i), "ident_b"], writes=["tp"])
                            P.op("act", lambda e: e.copy(out=yTc[:, t * 128:(t + 1) * 128], in_=tp[:]),
                                 reads=["tp"], writes=[("yTc", t)])

                        pend = None
                        for t in range(NT):
                            kts = na_keytiles(t)
                            i = t % 2
                            proj_tm(wz, wzk, 0, 128, pp, "pp", t)
                            P.op("act", lambda e, i=i: e.activation(out=sz2[i][:], in_=pp[:, 0:128], func=AF.Tanh,
                                                                    scale=0.5),
                                 reads=["pp"], writes=[("sz", i)])
                            P.op("dve", lambda e, i=i: e.tensor_scalar(out=sz2[i][:], in0=sz2[i][:], scalar1=0.5,
                                                                       scalar2=0.5, op0=ALU.mult, op1=ALU.add),
                                 reads=[("sz", i)], writes=[("sz", i)])
                            P.op("dve", lambda e, i=i: e.tensor_tensor(out=sz2[i][:], in0=sz2[i][:], in1=pp[:, 0:128],
                                                                       op=ALU.mult),
                                 reads=[("sz", i), "pp"], writes=[("sz", i)])
                            interleave([head_gen(t, 0, kts), head_gen(t, 1, kts)])
                            if pend is not None:
                                tail(pend)
                            pend = t
                        tail(pend)
                        P.barrier()
                    P.dma("sp", yT_d[c], yTc[:], reads=[("yTc", t) for t in range(NT)], writes=[("yT_d", c)])
                    if dbg and "qk" in dbg:
                        stg = SB(es_w, "dbgstg", [128, T_ALL], F32)
                        for ii, (src, nm) in enumerate(((qT, "qT"), (kT, "kT"))):
                            P.op("dve", lambda e, src=src: e.tensor_copy(out=stg[:], in_=src[:]),
                                 reads=[(nm, t) for t in range(NT)], writes=["dbgstg"])
                            P.dma("sp", dbg_d["qk"][ii * 128:(ii + 1) * 128, :], stg[:], reads=["dbgstg"],
                                  writes=[("dbgqk", ii)])
                        P.finish("sp", [("dbgqk", 0), ("dbgqk", 1), ("dbgqk", 2)])
                    P.barrier()
            if dbg_oa is not None:
                P.finish("sp", [("dbgoa", i) for i in range(4)])
            P.barrier()

    def ml_stage(heads, dbg_hb=None):
        with ExitStack() as es:
            rope = SB(es, "rope", [128, 4, 64], F32)
            P.dma("sp", rope[:], rope_d[:, :, :], writes=["rope"])
            bgate = SB(es, "bgate", [128, 16], F32)
            P.dma("act", bgate[:], bgate_d[0:1, :].partition_broadcast(128), writes=["bgate"])
            hnb = SB(es, "hnb", [128, D], F32)
            P.dma("sp", hnb[:], hnb_d[0:1, :].partition_broadcast(128), writes=["hnb"])
            Gt = SB(es, "Gt", [128, NT, 16], F32)
            LF = SB(es, "LF", [128, NT, 8], F32)
            Wt = SB(es, "Wt", [128, NT, 8], F32)
            Bs = SB(es, "Bs", [128, NT, 16], F32)
            EB = SB(es, "EB", [128, NT, 8], F32)
            EBL = SB(es, "EBL", [128, NT, 8], F32)
            with ExitStack() as es_w:
                pg = PS(es_w, "pg", [128, 16])
                wg, wgk = load_w(es_w, win_ab_d, 9216, 16, "g")
                for t in range(NT):
                    proj_tm(wg, wgk, 0, 16, pg, "pg", t)
                    P.op("dve", lambda e, t=t: e.tensor_tensor(out=Gt[:, t, :], in0=pg[:, 0:16], in1=bgate[:],
                                                               op=ALU.add),
                         reads=["pg", "bgate"], writes=[("Gt", t)])
                gkeys = [("Gt", t) for t in range(NT)]
                for d in range(2):
                    P.op("act", lambda e, d=d: e.activation(out=LF[:, :, 4 * d:4 * d + 4],
                                                            in_=Gt[:, :, 4 + 8 * d:8 + 8 * d], func=AF.Exp, scale=-1.0),
                         reads=gkeys, writes=[("LF", d)])
                    P.op("act", lambda e, d=d: e.activation(out=LF[:, :, 4 * d:4 * d + 4],
                                                            in_=LF[:, :, 4 * d:4 * d + 4], func=AF.Ln, bias=1.0),
                         reads=[("LF", d)], writes=[("LF", d)])
                    P.op("dve", lambda e, d=d: e.tensor_scalar(out=LF[:, :, 4 * d:4 * d + 4],
                                                               in0=LF[:, :, 4 * d:4 * d + 4], scalar1=-1.0,
                                                               scalar2=None, op0=ALU.mult),
                         reads=[("LF", d)], writes=[("LF", d)])
                for t in range(NT):
                    P.op("pe", lambda e, t=t: e.matmul(pg[:, 0:4], lhsT=consts[:, 1, :], rhs=LF[:, t, 0:4],
                                                       start=True, stop=True),
                         reads=["consts", ("LF", 0)], writes=["pg"])
                    P.op("pe", lambda e, t=t: e.matmul(pg[:, 4:8], lhsT=consts[:, 2, :], rhs=LF[:, t, 4:8],
                                                       start=True, stop=True),
                         reads=["consts", ("LF", 1)], writes=["pg"], pe_chain=True)
                    P.op("pe", lambda e, t=t: e.matmul(pg[:, 8:16], lhsT=ones_f[:], rhs=LF[:, t, 0:8],
                                                       start=True, stop=True),
                         reads=["ones_f", ("LF", 0), ("LF", 1)], writes=["pg"], pe_chain=True)
                    for d in range(2):
                        P.op("dve", lambda e, t=t, d=d: e.tensor_tensor(
                            out=Wt[:, t, 4 * d:4 * d + 4], in0=Gt[:, t, 8 * d:8 * d + 4], in1=pg[:, 4 * d:4 * d + 4],
                            op=ALU.subtract),
                            reads=["pg", ("Gt", t)], writes=[("Wt", t)])
                    P.op("dve", lambda e, t=t: e.tensor_copy(out=Bs[:, t, :], in_=pg[:, 0:16]),
                         reads=["pg"], writes=[("Bs", t)])
                P.op("act", lambda e: e.activation(out=Wt[:], in_=Wt[:], func=AF.Exp),
                     reads=[("Wt", t) for t in range(NT)], writes=["Wt"])
                P.op("act", lambda e: e.activation(out=EB[:], in_=Bs[:, :, 0:8], func=AF.Exp),
                     reads=[("Bs", t) for t in range(NT)], writes=["EB"])
                P.op("act", lambda e: e.activation(out=EBL[:], in_=Bs[:, :, 8:16], func=AF.Exp),
                     reads=[("Bs", t) for t in range(NT)], writes=["EBL"])
                P.barrier()
            for h in heads:
                with ExitStack() as es_h:
                    qT = SB(es_h, "mqT", [128, 2, T_ALL], BF16)
                    kT = SB(es_h, "mkT", [128, 2, T_ALL], BF16)
                    vaug = SB(es_h, "mvaug", [128, NT, 257], BF16)
                    P.op("pool", lambda e: e.memset(vaug[:], 1.0), writes=[("mv", t) for t in range(NT)])
                    with ExitStack() as es_p:
                        pp = PS(es_p, "mpp", [128, 512])
                        pp2 = PS(es_p, "mpp2", [128, 512])
                        t1 = SB(es_p, "t1", [128, 512], F32)
                        t2 = SB(es_p, "t2", [128, 512], F32)
                        for (dst, dn, cbase, ti) in ((qT, "mqT", 4096 + h * 256, 0), (kT, "mkT", 5120 + h * 256, 2)):
                            for cch in range(2):
                                with ExitStack() as es_w:
                                    c0 = cbase + cch * 128
                                    w, wk_ = load_w(es_w, win_ab_d, c0, 128, "a")
                                    wsw, wswk = load_w(es_w, win_ab_d, 0, 0, "b", segs=[(c0 + 64, 64), (c0, 64)])
                                    for (t0, n) in TOKG:
                                        okeys = [(dn, cch, t) for t in range(t0 // 128, (t0 + n) // 128)]
                                        proj_fm(w, wk_, 0, pp, "mpp", t0, n)
                                        if t0 < 256:
                                            P.op("act", lambda e, n=n, t0=t0, dst=dst, cch=cch, ti=ti: e.mul(
                                                out=dst[:, cch, t0:t0 + n], in_=pp[:, 0:n],
                                                mul=(1.0 if ti == 0 else 1.0 / 16)),
                                                reads=["mpp"], writes=okeys)
                                            continue
                                        proj_fm(wsw, wswk, 0, pp2, "mpp2", t0, n)
                                        r0 = (t0 - 256) // 64
                                        if cch == 0:
                                            cosv = rope[:, ti, r0:r0 + 8].unsqueeze(2).broadcast_to([128, 8, 64])
                                            sinv = rope[:, ti + 1, r0:r0 + 8].unsqueeze(2).broadcast_to([128, 8, 64])
                                        else:
                                            cosv = rope[:, ti, :].unsqueeze(1).broadcast_to([128, 8, 64])
                                            sinv = rope[:, ti + 1, :].unsqueeze(1).broadcast_to([128, 8, 64])
                                        P.op("dve", lambda e, cosv=cosv: e.tensor_tensor(
                                            out=t1[:].rearrange("p (r c) -> p r c", r=8),
                                            in0=pp[:].rearrange("p (r c) -> p r c", r=8), in1=cosv, op=ALU.mult),
                                            reads=["mpp", "rope"], writes=["t1"])
                                        P.op("dve", lambda e, sinv=sinv: e.tensor_tensor(
                                            out=t2[:].rearrange("p (r c) -> p r c", r=8),
                                            in0=pp2[:].rearrange("p (r c) -> p r c", r=8), in1=sinv, op=ALU.mult),
                                            reads=["mpp2", "rope"], writes=["t2"])
                                        P.op("pool", lambda e, t0=t0, dst=dst, cch=cch: e.tensor_tensor(
                                            out=dst[:, cch, t0:t0 + 512], in0=t1[:], in1=t2[:], op=ALU.add),
                                            reads=["t1", "t2"], writes=okeys)
                                    P.barrier()
                        with ExitStack() as es_w:
                            wv, wvk = load_w(es_w, win_ab_d, 6144 + h * 256, 256, "a")
                            for t in range(NT):
                                proj_tm(wv, wvk, 0, 256, pp, "mpp", t)
                                P.op("act", lambda e, t=t: e.copy(out=vaug[:, t, 0:256], in_=pp[:, 0:256]),
                                     reads=["mpp"], writes=[("mv", t)])
                            P.barrier()
                    if dbg and "mqk" in dbg:
                        with ExitStack() as es_d:
                            stg = SB(es_d, "dbgstg", [128, T_ALL], F32)
                            for ii, (src, nm, cch) in enumerate(((qT, "mqT", 0), (qT, "mqT", 1), (kT, "mkT", 0), (kT, "mkT", 1))):
                                P.op("dve", lambda e, src=src, cch=cch: e.tensor_copy(out=stg[:], in_=src[:, cch, :]),
                                     reads=[(nm, cch, t) for t in range(NT)], writes=["dbgstg"])
                                P.dma("sp", dbg_d["mqk"][ii * 128:(ii + 1) * 128, :], stg[:], reads=["dbgstg"],
                                      writes=[("dbgqk", ii)])
                            P.finish("sp", [("dbgqk", ii) for ii in range(4)])
                            P.barrier()
                    with ExitStack() as es_c:
                        wo, wok = load_w(es_c, win_ab_d, 0, 0, "oz", segs=[(7168 + h * 256, 256), (8192 + h * 256, 256)])
                        Cf = SB(es_c, "Cf", [128, 2, 257], F32)
                        Ct = SB(es_c, "Ct", [128, 2, 257], F32)
                        Cb = SB(es_c, "Cb", [128, 2, 257], BF16)
                        STm = SB(es_c, "STm", [128, 128], BF16)
                        kt = SB(es_c, "kt", [128, 256], BF16)
                        sm = SB(es_c, "sm", [128, 4], F32)
                        Hin = SB(es_c, "Hin", [128, 256], F32)
                        Hs = SB(es_c, "Hs", [128, 256], F32)
                        sig = SB(es_c, "sig", [128, 256], F32)
                        szb = SB(es_c, "szb", [128, 256], F32)
                        hbg = SB(es_c, "hbg", [128, 256], F32)
                        ybf = SB(es_c, "ybf", [128, 256], BF16)
                        ysb = SB(es_c, "ysb", [128, 2, 128], BF16)
                        STp = PS(es_c, "STp", [128, 128])
                        acc = PS(es_c, "acc", [128, 512])
                        cacc = PS(es_c, "cacc", [128, 2, 512])
                        po = PS(es_c, "po", [128, 512])
                        tpk = PS(es_c, "tpk", [128, 2, 128], BF16)
                        tpy = PS(es_c, "tpy", [128, 2, 128], BF16)
                        pcon = SB(es_c, "pcon", [128, 2], F32)
                        P.op("pool", lambda e: e.memset(pcon[:, 0:1], -1.0), writes=["pcon"])
                        P.op("pool", lambda e: e.memset(pcon[:, 1:2], -0.5), reads=["pcon"], writes=["pcon"])
                        acc2 = [acc, PS(es_c, "acc1", [128, 512])]
                        STm2 = [STm, SB(es_c, "STm1", [128, 128], BF16)]
                        Cb2 = [Cb, SB(es_c, "Cb1", [128, 2, 257], BF16)]

                        def ml_head(d, hd, n, t):
                            ts_ = slice(t * 128, (t + 1) * 128)
                            i = n % 2
                            for c in range(2):
                                P.op("pe", lambda e, c=c: e.matmul(STp[:], lhsT=kT[:, c, ts_], rhs=qT[:, c, ts_],
                                                                   start=(c == 0), stop=(c == 1)),
                                     reads=[("mkT", c, t), ("mqT", c, t)], writes=["STp"], pe_chain=True)
                            for c in range(2):
                                P.op("pe", lambda e, c=c: e.transpose(out=tpk[:, c, :], in_=kT[:, c, ts_],
                                                                      identity=ident_b[:]),
                                     reads=[("mkT", c, t), "ident_b"], writes=["tpk"], pe_chain=True)
                            P.op("dve", lambda e: e.scalar_tensor_tensor(
                                out=STm2[i][:], in0=STp[:], scalar=Wt[:, t, hd:hd + 1], in1=consts[:, 1 + d, :],
                                op0=ALU.mult, op1=ALU.mult),
                                reads=["STp", "Wt", "consts"], writes=[("STm", i)])
                            P.op("dve", lambda e: e.tensor_scalar(
                                out=kt[:], in0=tpk[:].rearrange("p c k -> p (c k)"), scalar1=Wt[:, t, hd:hd + 1],
                                scalar2=None, op0=ALU.mult),
                                reads=["tpk", "Wt"], writes=["kt"])
                            for c in range(2):
                                P.op("pe", lambda e, c=c: e.matmul(cacc[:, c, 0:257], lhsT=kt[:, c * 128:(c + 1) * 128],
                                                                   rhs=vaug[:, t, :], start=True, stop=True),
                                     reads=["kt", ("mv", t)], writes=["cacc"], pe_chain=(c == 1))
                            P.op("pe", lambda e: e.matmul(acc2[i][:, 0:257], lhsT=STm2[i][:], rhs=vaug[:, t, :],
                                                          start=True, stop=False),
                                 reads=[("STm", i), ("mv", t)], writes=[("acc", i)])
                            for c in range(2):
                                P.op("pe", lambda e, c=c: e.matmul(acc2[i][:, 0:257], lhsT=qT[:, c, ts_],
                                                                   rhs=Cb2[i][:, c, :], start=False, stop=(c == 1)),
                                     reads=[("mqT", c, t), ("Cb", i)], writes=[("acc", i)], pe_chain=True)
                            P.op("dve", lambda e: e.tensor_tensor(out=Ct[:], in0=cacc[:, :, 0:257], in1=Cf[:],
                                                                  op=ALU.add),
                                 reads=["cacc", "Cf"], writes=["Ct"])
                            P.op("dve", lambda e: e.tensor_scalar(
                                out=Cf[:], in0=Ct[:], scalar1=EBL[:, t, hd:hd + 1], scalar2=None, op0=ALU.mult),
                                reads=["Ct", "EBL"], writes=["Cf"])
                            P.op("pool", lambda e: e.tensor_scalar(
                                out=Cb2[1 - i][:], in0=Ct[:], scalar1=EBL[:, t, hd:hd + 1], scalar2=None, op0=ALU.mult),
                                reads=["Ct", "EBL"], writes=[("Cb", 1 - i)])

                        def ml_tail(d, hd, n, t):
                            ts_ = slice(t * 128, (t + 1) * 128)
                            i = n % 2
                            acc_ = acc2[i]
                            akey = ("acc", i)
                            P.op("act", lambda e: e.activation(
                                out=sm[:, 0:1], in_=acc_[:, 256:257], func=AF.Abs, scale=EB[:, t, hd:hd + 1]),
                                reads=[akey, "EB"], writes=["sm"])
                            P.op("dve", lambda e: e.tensor_scalar(out=sm[:, 1:2], in0=sm[:, 0:1], scalar1=1.0,
                                                                  scalar2=None, op0=ALU.max),
                                 reads=["sm"], writes=["sm"])
                            P.op("dve", lambda e: e.reciprocal(out=sm[:, 2:3], in_=sm[:, 1:2]),
                                 reads=["sm"], writes=["sm"])
                            P.op("dve", lambda e: e.tensor_tensor(
                                out=sm[:, 3:4], in0=sm[:, 2:3], in1=EB[:, t, hd:hd + 1], op=ALU.mult),
                                reads=["sm", "EB"], writes=["sm"])
                            if d == 0:
                                P.op("act", lambda e: e.activation(out=Hs[:], in_=acc_[:, 0:256], func=AF.Copy,
                                                                   scale=sm[:, 3:4]),
                                     reads=[akey, "sm"], writes=["Hs"])
                                P.dma("sp", Hf_d[t], Hs[:], reads=["Hs"], writes=[("Hf_d", t)])
                                return
                            P.dma("sp", Hin[:], Hf_d[t], reads=[("Hf_d", t)], writes=["Hin"])
                            P.op("dve", lambda e: e.scalar_tensor_tensor(
                                out=Hs[:], in0=acc_[:, 0:256], scalar=sm[:, 3:4], in1=Hin[:],
                                op0=ALU.mult, op1=ALU.add),
                                reads=[akey, "sm", "Hin"], writes=["Hs"])
                            if dbg_hb is not None:
                                P.dma("act", dbg_hb[t * 128:(t + 1) * 128, h * 256:(h + 1) * 256], Hs[:],
                                      reads=["Hs"], writes=[("dbghb", t % 4)])
                            proj_tm(wo, wok, 0, 512, po, "po", t)
                            P.op("act", lambda e: e.activation(out=sig[:], in_=po[:, 0:256], func=AF.Sigmoid),
                                 reads=["po"], writes=["sig"])
                            P.op("act", lambda e: e.activation(out=szb[:], in_=po[:, 256:512], func=AF.Silu),
                                 reads=["po"], writes=["szb"])
                            P.op("pool", lambda e: e.tensor_tensor(out=hbg[:], in0=Hs[:], in1=sig[:], op=ALU.mult),
                                 reads=["Hs", "sig"], writes=["hbg"])
                            P.op("act", lambda e: e.activation(out=sig[:], in_=hbg[:], func=AF.Square,
                                                               accum_out=sm[:, 0:1]),
                                 reads=["hbg", "sm"], writes=["sig", "sm"])
                            P.op("pool", lambda e: e.tensor_scalar(out=sm[:, 1:2], in0=sm[:, 0:1], scalar1=1.0 / 256,
                                                                   scalar2=EPS, op0=ALU.mult, op1=ALU.add),
                                 reads=["sm"], writes=["sm"])
                            P.op("pool", lambda e: e.tensor_tensor(out=sm[:, 2:3], in0=sm[:, 1:2], in1=pcon[:, 1:2],
                                                                   op=ALU.pow),
                                 reads=["sm", "pcon"], writes=["sm"])
                            P.op("dve", lambda e: e.scalar_tensor_tensor(
                                out=hbg[:], in0=hbg[:], scalar=sm[:, 2:3], in1=hnb[:, h * 256:(h + 1) * 256],
                                op0=ALU.mult, op1=ALU.mult),
                                reads=["hbg", "sm", "hnb"], writes=["hbg"])
                            P.op("pool", lambda e: e.tensor_tensor(out=ybf[:], in0=hbg[:], in1=szb[:], op=ALU.mult),
                                 reads=["hbg", "szb"], writes=["ybf"])
                            for c in range(2):
                                P.op("pe", lambda e, c=c: e.transpose(out=tpy[:, c, :], in_=ybf[:, c * 128:(c + 1) * 128],
                                                                      identity=ident_b[:]),
                                     reads=["ybf", "ident_b"], writes=["tpy"], pe_chain=True)
                            P.op("act", lambda e: e.copy(out=ysb[:], in_=tpy[:]), reads=["tpy"], writes=["ysb"])
                            P.dma("sp", yT_d[8 + 2 * h:10 + 2 * h, :, ts_].rearrange("c p t -> p c t"), ysb[:],
                                  reads=["ysb"], writes=[("yT_d", 8 + 2 * h, t)])

                        for d in range(2):
                            hd = 4 * d + h
                            order = [0, 1] + list(range(2, NT)) if d == 0 else [1, 0] + list(range(NT - 1, 1, -1))
                            P.op("dve", lambda e: e.memset(Cf[:], 0.0), writes=["Cf"])
                            P.op("pool", lambda e: e.memset(Cb2[0][:], 0.0), writes=[("Cb", 0)])
                            pend = None
                            for n, t in enumerate(order):
                                ml_head(d, hd, n, t)
                                if pend is not None:
                                    ml_tail(d, hd, *pend)
                                pend = (n, t)
                            ml_tail(d, hd, *pend)
                        if dbg_hb is not None:
                            P.finish("act", [("dbghb", i) for i in range(4)])
                        P.barrier()
            P.barrier()

    def out_stage(wout_d, xsrc, csrc, gates, dst_x, dst_c, outkey):
        with ExitStack() as es:
            wo_b = SB(es, "wo_b", [128, 16, D], BF16)
            for q4 in range(4):
                for kg in range(2):
                    for k in range(8):
                        kc = kg * 8 + k
                        P.dma("sp" if k % 2 == 0 else "act", wstage[:, k, :],
                              wout_d[kc * 128:(kc + 1) * 128, q4 * 256:(q4 + 1) * 256], writes=[("wstage", k)],
                              semkey=("wstage", k, 0))
                        P.op("pool" if k % 2 == 0 else "dve", lambda e, k=k, kc=kc, q4=q4: e.tensor_copy(
                            out=wo_b[:, kc, q4 * 256:(q4 + 1) * 256], in_=wstage[:, k, :]),
                            reads=[("wstage", k)], writes=[("wo_b", kc, q4)])
            yt = [SB(es, "yt%d" % i, [128, 16, 128], BF16) for i in range(2)]
            xt = [SB(es, "xt%d" % i, [128, D], F32) for i in range(2)]
            tmp = SB(es, "otmp", [128, D], F32)
            xo = [SB(es, "xo%d" % i, [128, D], F32) for i in range(2)]
            py = [PS(es, "py%d" % i, [128, 512]) for i in range(2)]
            tiles = list(range(NT)) if dst_c is not None else list(range(2, NT))
            for n_, t in enumerate(tiles):
                i = n_ % 2
                j = 1 if t < 2 else 0
                ts_ = slice(t * 128, (t + 1) * 128)
                P.dma("sp", yt[i][:], yT_d[:, :, ts_].rearrange("c p t -> p c t"),
                      reads=[("yT_d", c) for c in range(16)], writes=[("yt", i)])
                src = csrc[ts_, :] if t < 2 else xsrc[(t - 2) * 128:(t - 1) * 128, :]
                P.dma("act", xt[i][:], src, reads=[("x1", t)] if outkey == "out" else [], writes=[("xt", i)])
                for half in range(2):
                    for kc in range(16):
                        P.op("pe", lambda e, kc=kc, half=half, i=i: e.matmul(
                            py[half][:], lhsT=yt[i][:, kc, :], rhs=wo_b[:, kc, half * 512:(half + 1) * 512],
                            start=(kc == 0), stop=(kc == 15)),
                            reads=[("yt", i), ("wo_b", kc, 2 * half), ("wo_b", kc, 2 * half + 1)], writes=[("py", half)], pe_chain=True)
                    hs = slice(half * 512, (half + 1) * 512)
                    P.op("dve", lambda e, half=half, hs=hs, j=j: e.tensor_tensor(
                        out=tmp[:, hs], in0=py[half][:], in1=gates[j][:, hs], op=ALU.mult),
                        reads=[("py", half), ("gate", j)], writes=[("otmp", half)])
                    P.op("pool", lambda e, hs=hs, i=i: e.tensor_tensor(out=xo[i][:, hs], in0=tmp[:, hs],
                                                                       in1=xt[i][:, hs], op=ALU.add),
                         reads=[("otmp", half), ("xt", i)], writes=[("xo", i, half)])
                dst = dst_c[ts_, :] if t < 2 else dst_x[(t - 2) * 128:(t - 1) * 128, :]
                P.dma("sp", dst, xo[i][:], reads=[("xo", i, 0), ("xo", i, 1)], writes=[(outkey, t)],
                      semkey=("xo", i))
            P.barrier()

    def hg_stage(heads, dbg_o=None):
        from itertools import zip_longest
        NCH = T_ALL // 64
        ORD = [list(range(NCH)), [3, 2, 1, 0] + list(range(NCH - 1, 3, -1))]
        POS = [{j: p for p, j in enumerate(o_)} for o_ in ORD]
        hgstop = (dbg or {}).get("_hgstop")

        def rev(ap2d, lo, n):
            v = ap2d[:, lo:lo + n]
            return bass.AP(v.tensor, v.offset + (n - 1) * v.ap[-1][0], [list(v.ap[0]), [-v.ap[-1][0], n]])

        with ExitStack() as es:
            lbc = SB(es, "lbc", [128, 16, 2], F32)
            P.dma("sp", lbc[:], lbc_d[:, :, :], writes=["lbc"])
            lb = SB(es, "lb", [128, 16], F32)
            oml = SB(es, "oml", [128, 16], F32)
            P.op("dve", lambda e: e.tensor_tensor(out=lb[:], in0=lbc[:, :, 1], in1=lbc[:, :, 0], op=ALU.subtract),
                 reads=["lbc"], writes=["lb"])
            P.op("act", lambda e: e.activation(out=lb[:], in_=lb[:], func=AF.Sigmoid), reads=["lb"], writes=["lb"])
            P.op("dve", lambda e: e.tensor_scalar(out=oml[:], in0=lb[:], scalar1=-1.0, scalar2=1.0, op0=ALU.mult,
                                                  op1=ALU.add), reads=["lb"], writes=["oml"])
            smask = SB(es, "smask", [128, 512], F32)
            P.dma("sp", smask[:], smask_d[:, :], writes=["smask"])
            omh = SB(es, "omh", [128, 16], F32)
            lbh = SB(es, "lbh", [128, 16], F32)
            P.op("dve", lambda e: e.tensor_scalar(out=omh[:], in0=oml[:], scalar1=0.5, scalar2=None, op0=ALU.mult),
                 reads=["oml"], writes=["omh"])
            P.op("dve", lambda e: e.tensor_tensor(out=lbh[:], in0=lb[:], in1=omh[:], op=ALU.add),
                 reads=["lb", "omh"], writes=["lbh"])
            for h in heads:
                with ExitStack() as es_h:
                    hnh = SB(es_h, "hnh", [128, 128], F32)
                    P.dma("act", hnh[:], hnc_d[0:1, h * 128:(h + 1) * 128].partition_broadcast(128), writes=["hnh"])
                    qf = SB(es_h, "gqf", [128, T_ALL], BF16)
                    kf = SB(es_h, "gkf", [128, T_ALL], BF16)
                    qb = SB(es_h, "gqb", [128, T_ALL], BF16)
                    kb = SB(es_h, "gkb", [128, T_ALL], BF16)
                    QK = ((qf, "gqf", kf, "gkf"), (qb, "gqb", kb, "gkb"))
                    ebj = SB(es_h, "ebj", [128, 2, NCH], F32)
                    vtok = SB(es_h, "gv", [128, NT, 128], BF16)
                    SH = [SB(es_h, "SH%d" % d, [128, NCH, 128], BF16) for d in range(2)]
                    with ExitStack() as es_p:
                        pp = [PS(es_p, "gpp%d" % i, [128, 512]) for i in range(3)]
                        wq, wqk = load_w(es_p, win_c_d, h * 128, 128, "gq")
                        wf_ = [load_w(es_p, win_c_d, 2048 + h * 128, 128, "gff"),
                               load_w(es_p, win_c_d, 4096 + h * 128, 128, "gfb")]
                        qs = SB(es_p, "gqs", [128, 512], F32)
                        A = [SB(es_p, "gA%d" % d, [128, 512], F32) for d in range(2)]
                        B = [SB(es_p, "gB%d" % d, [128, 512], F32) for d in range(2)]
                        C = [SB(es_p, "gC%d" % d, [128, 512], F32) for d in range(2)]
                        E = [SB(es_p, "gE%d" % d, [128, 512], F32) for d in range(2)]
                        TL = SB(es_p, "gTL", [128, 8], F32)
                        for (t0, n) in TOKG:
                            nch = n // 64
                            j0 = t0 // 64
                            tk = [t for t in range(t0 // 128, (t0 + n) // 128)]
                            proj_fm(wq, wqk, 0, pp[2], "gpp2", t0, n)
                            P.op("act", lambda e, n=n: e.activation(out=qs[:, 0:n], in_=pp[2][:, 0:n], func=AF.Tanh,
                                                                    scale=0.5),
                                 reads=["gpp2"], writes=["gqs"])
                            P.op("dve", lambda e, n=n: e.scalar_tensor_tensor(
                                out=qs[:, 0:n], in0=qs[:, 0:n], scalar=1.0, in1=pp[2][:, 0:n], op0=ALU.add, op1=ALU.mult),
                                reads=["gqs", "gpp2"], writes=["gqs"])
                            steps = [[], []]
                            for d in range(2):
                                (w_, wk_) = wf_[d]
                                (qd, qn, kd, kn) = QK[d]
                                Ad, Bd, Cd, Ed = A[d], B[d], C[d], E[d]
                                kA, kB, kC, kE, kP = "gA%d" % d, "gB%d" % d, "gC%d" % d, "gE%d" % d, "gpp%d" % d
                                C3 = Cd[:, 0:n].rearrange("p (c l) -> p c l", l=64)
                                L = steps[d]
                                L.append(lambda w_=w_, wk_=wk_, d=d, kP=kP: proj_fm(w_, wk_, 0, pp[d], kP, t0, n))
                                L.append(lambda Ad=Ad, d=d, kP=kP, kA=kA: P.op(
                                    "act", lambda e: e.activation(out=Ad[:, 0:n], in_=pp[d][:, 0:n], func=AF.Tanh,
                                                                  scale=0.5),
                                    reads=[kP], writes=[kA]))
                                L.append(lambda Ad=Ad, kA=kA: P.op("dve", lambda e: e.tensor_scalar(
                                    out=Ad[:, 0:n], in0=Ad[:, 0:n], scalar1=omh[:, h:h + 1], scalar2=lbh[:, h:h + 1],
                                    op0=ALU.mult, op1=ALU.add), reads=[kA, "omh", "lbh"], writes=[kA]))
                                L.append(lambda Ad=Ad, Bd=Bd, kA=kA, kB=kB: P.op(
                                    "act", lambda e: e.activation(out=Bd[:, 0:n], in_=Ad[:, 0:n], func=AF.Ln),
                                    reads=[kA], writes=[kB]))
                                L.append(lambda Ad=Ad, kA=kA, kB=kB: P.op("pool", lambda e: e.tensor_scalar(
                                    out=Ad[:, 0:n], in0=Ad[:, 0:n], scalar1=-1.0, scalar2=1.0, op0=ALU.mult,
                                    op1=ALU.add), reads=[kA, kB], writes=[kA]))
                                L.append(lambda Bd=Bd, Cd=Cd, kB=kB, kC=kC: P.op("dve", lambda e: e.tensor_tensor_scan(
                                    out=Cd[:, 0:n], data0=smask[:, 0:n], data1=Bd[:, 0:n], initial=0.0,
                                    op0=ALU.mult, op1=ALU.add), reads=["smask", kB], writes=[kC]))
                                if d == 1:
                                    L.append(lambda Bd=Bd, Cd=Cd, kB=kB, kC=kC: P.op("pool", lambda e: e.tensor_tensor(
                                        out=Bd[:, 0:n], in0=Bd[:, 0:n], in1=Cd[:, 0:n], op=ALU.subtract),
                                        reads=[kB, kC], writes=[kB]))
                                    L.append(lambda C3=C3, kC=kC: P.op(
                                        "act", lambda e: e.copy(out=TL[:, 0:nch], in_=C3[:, :, 63]),
                                        reads=[kC], writes=["gTL"]))
                                    L.append(lambda Bd=Bd, C3=C3, kB=kB, kC=kC: P.op("dve", lambda e: e.tensor_tensor(
                                        out=C3, in0=Bd[:, 0:n].rearrange("p (c l) -> p c l", l=64),
                                        in1=TL[:, 0:nch].unsqueeze(2).broadcast_to([128, nch, 64]), op=ALU.add),
                                        reads=[kB, "gTL"], writes=[kC]))
                                    L.append(lambda: P.op("act", lambda e: e.activation(
                                        out=ebj[:, 1, j0:j0 + nch], in_=TL[:, 0:nch], func=AF.Exp),
                                        reads=["gTL"], writes=[("ebj", 1, t0)]))
                                else:
                                    L.append(lambda C3=C3, kC=kC: P.op("act", lambda e: e.activation(
                                        out=ebj[:, 0, j0:j0 + nch], in_=C3[:, :, 63], func=AF.Exp),
                                        reads=[kC], writes=[("ebj", 0, t0)]))
                                L.append(lambda Cd=Cd, Ed=Ed, kC=kC, kE=kE: P.op(
                                    "act", lambda e: e.activation(out=Ed[:, 0:n], in_=Cd[:, 0:n], func=AF.Exp),
                                    reads=[kC], writes=[kE]))
                                L.append(lambda Ed=Ed, qd=qd, qn=qn, kE=kE: P.op("dve", lambda e: e.scalar_tensor_tensor(
                                    out=qd[:, t0:t0 + n], in0=qs[:, 0:n], scalar=0.5, in1=Ed[:, 0:n], op0=ALU.mult,
                                    op1=ALU.mult),
                                    reads=["gqs", kE], writes=[(qn, t) for t in tk]))
                                L.append(lambda Bd=Bd, Cd=Cd, kB=kB, kC=kC: P.op(
                                    "act", lambda e: e.activation(out=Bd[:, 0:n], in_=Cd[:, 0:n], func=AF.Exp, scale=-1.0),
                                    reads=[kC], writes=[kB]))
                                L.append(lambda Ad=Ad, Bd=Bd, kd=kd, kn=kn, kA=kA, kB=kB: P.op(
                                    "pool", lambda e: e.tensor_tensor(out=kd[:, t0:t0 + n], in0=Ad[:, 0:n],
                                                                      in1=Bd[:, 0:n], op=ALU.mult),
                                    reads=[kA, kB], writes=[(kn, t) for t in tk]))
                            for fa, fb in zip_longest(steps[0], steps[1]):
                                if fa is not None:
                                    fa()
                                if fb is not None:
                                    fb()
                        P.barrier()
                    if hgstop == "prep":
                        continue
                    with ExitStack() as es_p:
                        pv = [PS(es_p, "gpv%d" % i, [128, 128]) for i in range(2)]
                        wv, wvk = load_w(es_p, win_c_d, 6144 + h * 128, 128, "gv")
                        for t in range(NT):
                            sl = t % 2
                            proj_tm(wv, wvk, 0, 128, pv[sl], ("gpv", sl), t)
                            P.op("act", lambda e, t=t, sl=sl: e.copy(out=vtok[:, t, :], in_=pv[sl][:, 0:128]),
                                 reads=[("gpv", sl)], writes=[("gv", t)])
                        P.barrier()
                    if hgstop == "v":
                        continue
                    with ExitStack() as es_u:
                        NVB = 16
                        U = SB(es_u, "gU", [128, 128, NCH], F32)
                        ebpo = SB(es_u, "ebpo", [128, NCH], F32)
                        ebrep = SB(es_u, "ebrep", [128, NVB, NCH], F32)
                        ktk = [SB(es_u, "gktk%d" % i, [128, 4, 128], BF16) for i in range(2)]
                        tpk = [PS(es_u, "gtpk%d" % i, [128, 4, 128], BF16) for i in range(2)]
                        Ups = [[PS(es_u, "gUps%d%d" % (i, hf), [128, 4, 128]) for hf in range(2)] for i in range(2)]
                        UGROUPS = [[0, 1]] + [[2 + 4 * g + tt for tt in range(4)] for g in range(8)]

                        def strided(ap3, n, step):
                            return bass.AP(ap3.tensor, ap3.offset, [list(ap3.ap[0]), list(ap3.ap[1]), [step, n]])

                        for d in range(2):
                            (qd, qn, kd, kn) = QK[d]
                            if d == 0:
                                P.op("pool", lambda e: e.tensor_copy(out=ebpo[:], in_=ebj[:, 0, :]), writes=["ebpo"])
                            else:
                                P.op("pool", lambda e: e.tensor_copy(out=ebpo[:, 0:4], in_=rev(ebj[:, 1, :], 0, 4)),
                                     writes=["ebpo"])
                                P.op("pool", lambda e: e.tensor_copy(out=ebpo[:, 4:NCH], in_=rev(ebj[:, 1, :], 4, NCH - 4)),
                                     reads=["ebpo"], writes=["ebpo"])
                            P.op("pool", lambda e: e.memset(ebpo[:, 0:1], 0.0), reads=["ebpo"], writes=["ebpo"])
                            P.op("pool", lambda e: e.tensor_copy(
                                out=ebrep[:], in_=ebpo[:].unsqueeze(1).broadcast_to([128, NVB, NCH])),
                                reads=["ebpo"], writes=["ebrep"])
                            for gi, tl in enumerate(UGROUPS):
                                i = gi % 2
                                L_ = len(tl)
                                for tt, t in enumerate(tl):
                                    ts_ = slice(t * 128, (t + 1) * 128)
                                    P.op("pe", lambda e, i=i, tt=tt, ts_=ts_: e.transpose(
                                        out=tpk[i][:, tt, :], in_=kd[:, ts_], identity=ident_b[:]),
                                        reads=[(kn, t), "ident_b"], writes=[("gtpk", i)], pe_chain=(tt > 0))
                                P.op("act", lambda e, i=i, L_=L_: e.copy(out=ktk[i][:, 0:L_, :], in_=tpk[i][:, 0:L_, :]),
                                     reads=[("gtpk", i)], writes=[("gktk", i)])
                                for tt, t in enumerate(tl):
                                    for hf in range(2):
                                        rs_ = slice(64 * hf, 64 * hf + 64)
                                        P.op("pe", lambda e, i=i, hf=hf, rs_=rs_, tt=tt, t=t: e.matmul(
                                            Ups[i][hf][:, tt, :], lhsT=ktk[i][rs_, tt, :], rhs=vtok[rs_, t, :],
                                            start=True, stop=True),
                                            reads=[("gktk", i), ("gv", t)], writes=[("gUps", i, hf)],
                                            pe_chain=(tt > 0))
                                for hf in range(2):
                                    j0 = 2 * tl[0] + hf
                                    p0 = POS[d][j0]
                                    pstep = POS[d][j0 + 2] - p0
                                    outv = strided(U[:, :, p0:p0 + 1], L_, pstep)
                                    ebv = ebj[:, d, j0:j0 + 1].unsqueeze(1)
                                    ebv = bass.AP(ebv.tensor, ebv.offset, [list(ebv.ap[0]), [0, 128], [2, L_]])
                                    P.op("dve", lambda e, i=i, hf=hf, outv=outv, ebv=ebv, L_=L_: e.tensor_tensor(
                                        out=outv, in0=Ups[i][hf][:, 0:L_, :].rearrange("p a v -> p v a"), in1=ebv,
                                        op=ALU.mult),
                                        reads=[("gUps", i, hf)], writes=[("gUp", gi, hf)])
                            ukeys = [("gUp", gi, hf) for gi in range(len(UGROUPS)) for hf in range(2)]
                            for vb in range(128 // NVB):
                                Ub = U[:, vb * NVB:(vb + 1) * NVB, :].rearrange("p v j -> p (v j)")
                                P.op("dve", lambda e, Ub=Ub: e.tensor_tensor_scan(
                                    out=Ub, data0=ebrep[:].rearrange("p v j -> p (v j)"), data1=Ub, initial=0.0,
                                    op0=ALU.mult, op1=ALU.add),
                                    reads=(ukeys + ["ebrep"] if vb == 0 else []), writes=[("gUs", vb)])
                            skeys = [("gUs", vb) for vb in range(128 // NVB)]
                            hn = NCH // 2
                            Uperm = U[:].rearrange("p v j -> p j v")
                            P.op("act", lambda e, d=d: e.copy(out=SH[d][:, 0:hn, :], in_=Uperm[:, 0:hn, :]),
                                 reads=skeys, writes=[("SH", d, 0)])
                            P.op("pool", lambda e, d=d: e.tensor_copy(out=SH[d][:, hn:NCH, :], in_=Uperm[:, hn:NCH, :]),
                                 reads=skeys, writes=[("SH", d, 1)])
                            tk_ = list(P.lastw.get(("SH", d, 0), [])) + list(P.lastw.get(("SH", d, 1), []))
                            for uk in ukeys:
                                P.lastw[uk] = list(tk_)
                                P.readers[uk] = {}
                        P.barrier()
                    if hgstop == "u":
                        continue
                    with ExitStack() as es_c:
                        wz, wzk = load_w(es_c, win_c_d, 8192 + h * 128, 128, "gz")
                        NB = 3
                        AT = [SB(es_c, "gAT%d" % i, [128, 2, 4, 128], BF16) for i in range(NB)]
                        ot = [SB(es_c, "got%d" % i, [128, 4, 128], F32) for i in range(NB)]
                        sq = [SB(es_c, "gsq%d" % i, [128, 4, 128], F32) for i in range(NB)]
                        szl = [SB(es_c, "gszl%d" % i, [128, 4, 128], F32) for i in range(NB)]
                        yv = [SB(es_c, "gyv%d" % i, [128, 4, 128], F32) for i in range(NB)]
                        ybf = [SB(es_c, "gybf%d" % i, [128, 4, 128], BF16) for i in range(NB)]
                        ysb = [SB(es_c, "gysb%d" % i, [128, 4, 128], BF16) for i in range(NB)]
                        sm = [SB(es_c, "gsm%d" % i, [128, 3, 4], F32) for i in range(NB)]
                        pA = PS(es_c, "gpA", [128, 2, 4, 128])
                        po = PS(es_c, "gpo", [128, 4, 2, 128])
                        pz = PS(es_c, "gpz", [128, 4, 128])
                        tpy = PS(es_c, "gtpy", [128, 4, 128], BF16)
                        def S1(g):
                            i = g % NB
                            tl = [2 + 4 * g + tt for tt in range(4)]
                            for d in range(2):
                                (qd, qn, kd, kn) = QK[d]
                                for tt, t in enumerate(tl):
                                    ts_ = slice(t * 128, (t + 1) * 128)
                                    P.op("pe", lambda e, d=d, tt=tt, ts_=ts_, kd=kd, qd=qd: e.matmul(
                                        pA[:, d, tt, :], lhsT=kd[:, ts_], rhs=qd[:, ts_], start=True, stop=True),
                                        reads=[(kn, t), (qn, t)], writes=["gpA"], pe_chain=(d + tt > 0))
                            for d in range(2):
                                P.op("dve", lambda e, d=d, i=i: e.tensor_tensor(
                                    out=AT[i][:, d, :, :], in0=pA[:, d, :, :],
                                    in1=consts[:, 3 + d, :].unsqueeze(1).broadcast_to([128, 4, 128]), op=ALU.mult),
                                    reads=["gpA", "consts"], writes=[("gAT", i, d)])

                        def S2(g):
                            i = g % NB
                            tl = [2 + 4 * g + tt for tt in range(4)]
                            first = True
                            for tt, t in enumerate(tl):
                                for hf in range(2):
                                    j = 2 * t + hf
                                    cs_ = slice(j * 64, (j + 1) * 64)
                                    rs_ = slice(64 * hf, 64 * hf + 64)
                                    for d in range(2):
                                        P.op("pe", lambda e, d=d, i=i, hf=hf, tt=tt, rs_=rs_, t=t: e.matmul(
                                            po[0:64, tt, hf, :], lhsT=AT[i][rs_, d, tt, rs_], rhs=vtok[rs_, t, :],
                                            start=(d == 0), stop=False),
                                            reads=[("gAT", i, 0), ("gAT", i, 1), ("gv", t)], writes=["gpo"],
                                            pe_chain=(not first))
                                        first = False
                                    for d in range(2):
                                        (qd, qn, kd, kn) = QK[d]
                                        pm = POS[d][j] - 1
                                        P.op("pe", lambda e, d=d, hf=hf, tt=tt, cs_=cs_, pm=pm, qd=qd: e.matmul(
                                            po[0:64, tt, hf, :], lhsT=qd[:, cs_], rhs=SH[d][:, pm, :],
                                            start=False, stop=(d == 1)),
                                            reads=[(qn, t), ("SH", d, 0), ("SH", d, 1)], writes=["gpo"],
                                            pe_chain=True)
                            for hf in range(2):
                                rs_ = slice(64 * hf, 64 * hf + 64)
                                P.op("act", lambda e, i=i, hf=hf, rs_=rs_: e.copy(out=ot[i][rs_, :, :],
                                                                                  in_=po[0:64, :, hf, :]),
                                     reads=["gpo"], writes=[("got", i, hf)])
                            for tt, t in enumerate(tl):
                                proj_tm(wz, wzk, 0, 128, pz[:, tt, :], "gpz", t)
                            P.op("act", lambda e, i=i: e.activation(out=szl[i][:], in_=pz[:], func=AF.Silu),
                                 reads=["gpz"], writes=[("gszl", i)])

                        def S3(g):
                            i = g % NB
                            t0 = (2 + 4 * g) * 128
                            okeys = [("got", i, 0), ("got", i, 1)]
                            if dbg_o is not None:
                                P.dma("sp", dbg_o[t0:t0 + 512, h * 128:(h + 1) * 128].rearrange("(a p) v -> p a v", p=128),
                                      ot[i][:], reads=okeys, writes=[("dbgo1", g % 4)])
                            P.op("dve", lambda e, i=i: e.tensor_tensor(out=sq[i][:], in0=ot[i][:], in1=ot[i][:],
                                                                       op=ALU.mult),
                                 reads=okeys, writes=[("gsq", i)])
                            P.op("dve", lambda e, i=i: e.tensor_reduce(out=sm[i][:, 0, :], in_=sq[i][:], axis=AX.X,
                                                                       op=ALU.add),
                                 reads=[("gsq", i)], writes=[("gsm", i)])
                            P.op("act", lambda e, i=i: e.activation(out=sm[i][:, 1, :], in_=sm[i][:, 0, :], func=AF.Sqrt,
                                                                    scale=1.0 / 128, bias=EPS),
                                 reads=[("gsm", i)], writes=[("gsm", i)])
                            P.op("dve", lambda e, i=i: e.reciprocal(out=sm[i][:, 2, :], in_=sm[i][:, 1, :]),
                                 reads=[("gsm", i)], writes=[("gsm", i)])
                            P.op("dve", lambda e, i=i: e.tensor_tensor(
                                out=yv[i][:], in0=ot[i][:], in1=sm[i][:, 2, :].unsqueeze(2).broadcast_to([128, 4, 128]),
                                op=ALU.mult),
                                reads=okeys + [("gsm", i)], writes=[("gyv", i)])
                            P.op("pool", lambda e, i=i: e.tensor_tensor(
                                out=yv[i][:], in0=yv[i][:], in1=hnh[:].unsqueeze(1).broadcast_to([128, 4, 128]),
                                op=ALU.mult),
                                reads=[("gyv", i), "hnh"], writes=[("gyv", i)])
                            P.op("pool", lambda e, i=i: e.tensor_tensor(out=ybf[i][:], in0=yv[i][:], in1=szl[i][:],
                                                                        op=ALU.mult),
                                 reads=[("gyv", i), ("gszl", i)], writes=[("gybf", i)])

                        def S4(g):
                            i = g % NB
                            t0 = (2 + 4 * g) * 128
                            for tt in range(4):
                                P.op("pe", lambda e, i=i, tt=tt: e.transpose(out=tpy[:, tt, :], in_=ybf[i][:, tt, :],
                                                                             identity=ident_b[:]),
                                     reads=[("gybf", i), "ident_b"], writes=["gtpy"], pe_chain=(tt > 0))
                            P.op("act", lambda e, i=i: e.copy(out=ysb[i][:], in_=tpy[:]), reads=["gtpy"],
                                 writes=[("gysb", i)])
                            P.dma("sp" if i == 0 else "act", yT_d[h, :, t0:t0 + 512],
                                  ysb[i][:].rearrange("p a t -> p (a t)"), reads=[("gysb", i)],
                                  writes=[("yT_d", h, g)], semkey=("gysb", i))

                        NG = 8
                        for step in range(NG + 2):
                            if step < NG:
                                S1(step)
                                S2(step)
                            if 0 <= step - 1 < NG:
                                S3(step - 1)
                            if 0 <= step - 2 < NG:
                                S4(step - 2)
                        if dbg_o is not None:
                            P.finish("sp", [("dbgo1", i) for i in range(4)])
                        P.barrier()
            P.barrier()

    if dbg and "oa" in dbg:
        na_stage(dbg.get("_chunks", [0]), dbg_d["oa"])
    elif dbg and "hb" in dbg:
        ml_stage([0], dbg_d["hb"])
    elif dbg and "x1" in dbg:
        na_stage(list(range(8)))
        ml_stage(list(range(4)))
        out_stage(wout_ab_d, x_d, ctx_d, gate0, x1_d, ctx1_d, "x1")
        P.finish("sp", [("x1", t) for t in range(NT)])
    elif dbg and "o1" in dbg:
        pass
    elif not dbg or "_stop" in dbg:
        stop = (dbg or {}).get("_stop", "end")
        order = ["norm0", "na", "ml", "out0", "norm1", "hg", "end"]
        lvl = order.index(stop)
        if lvl >= 1:
            na_stage(list(range(8)))
        if lvl >= 2:
            ml_stage(list(range(4)))
        if lvl >= 3:
            out_stage(wout_ab_d, x_d, ctx_d, gate0, x1_d, ctx1_d, "x1")
    L0.close()
    P.barrier()
    L1 = ExitStack()
    gate1 = {0: SB(L1, "gate1_0", [128, D], F32)}
    if dbg and "o1" in dbg:
        x1_in = dt_in("x1_in", [T_LAT, D])
        ctx1_in = dt_in("ctx1_in", [T_CTX, D])
        norm_stage(1, x1_in, ctx1_in, gate1)
        hg_stage(dbg.get("_heads", [0]), dbg_d["o1"])
    elif not dbg or "_stop" in dbg:
        stop = (dbg or {}).get("_stop", "end")
        lvl = ["norm0", "na", "ml", "out0", "norm1", "hg", "end"].index(stop)
        if lvl >= 4:
            norm_stage(1, x1_d, ctx1_d, gate1)
        if lvl >= 5:
            hg_stage((dbg or {}).get("_heads", list(range(16))))
        if lvl >= 6:
            out_stage(wout_c_d, x1_d, None, gate1, out_d, None, "out")
            P.finish("sp", [("out", t) for t in range(2, NT)])
        else:
            P.dma("sp", out_d[0:128, :], x_d[0:128, :], writes=["probe_out"])
            P.finish("sp", ["probe_out"])
    L1.close()
    G.close()
    nc._prog_counts = dict(P.ccnt)
    return nc


def host_inputs(inputs, b):
    cc = np.zeros((128, 16), np.float32)
    cb = np.asarray(inputs["c"][b], np.float32).reshape(8, 128)
    cx = np.asarray(inputs["c_ctx"], np.float32).reshape(8, 128)
    for k in range(8):
        cc[:, 2 * k] = cb[k]
        cc[:, 2 * k + 1] = cx[k]
    m = {
        "x": np.ascontiguousarray(inputs["x"][b], dtype=np.float32),
        "ctx": np.ascontiguousarray(inputs["ctx"][b], dtype=np.float32),
        "cc": cc,
        "w_ada": np.ascontiguousarray(inputs["w_ada"], dtype=np.float32),
        "b_ada": np.ascontiguousarray(inputs["b_ada"], dtype=np.float32),
        "norm_w": np.ascontiguousarray(inputs["norm_w"], dtype=np.float32),
        "ident": np.eye(128, dtype=np.float32),
        "consts": host_consts(),
        "w_in_ab": np.ascontiguousarray(inputs["w_in_ab"][0], dtype=np.float32),
        "qkw": np.stack([np.tile(np.asarray(inputs["q_norm_a"][0], np.float32), 2),
                         np.tile(np.asarray(inputs["k_norm_a"][0], np.float32), 2)], axis=1),
        "nab": host_nab(np.asarray(inputs["rpb_a"][0], np.float32)),
        "rope": host_rope(),
        "b_gate": np.asarray(inputs["b_gate_ab"], np.float32).reshape(1, 16),
        "h_norm_b": np.asarray(inputs["h_norm_b"], np.float32).reshape(1, D),
        "w_out_ab": np.ascontiguousarray(inputs["w_out_ab"][0], dtype=np.float32),
        "w_in_c": np.ascontiguousarray(inputs["w_in_c"][0], dtype=np.float32),
        "w_out_c": np.ascontiguousarray(inputs["w_out_c"][0], dtype=np.float32),
        "lbc": np.ascontiguousarray(np.asarray(inputs["lb_c"], np.float32).reshape(2, 16, 128).transpose(2, 1, 0)),
        "h_norm_c": np.asarray(inputs["h_norm_c"], np.float32).reshape(1, 2 * D),
        "smask": np.tile((np.arange(512) % 64 != 0).astype(np.float32)[None, :], (128, 1)),
    }
    return m


def host_rope():
    p = np.arange(128)
    freq = (10000.0 ** (-(p % 64).astype(np.float64) / 64.0))[:, None]
    pos = np.arange(64, dtype=np.float64)[None, :]
    ang = (pos.astype(np.float32) * freq.astype(np.float32)).astype(np.float32)
    cos = np.cos(ang).astype(np.float32)
    sin = np.sin(ang).astype(np.float32) * np.where(p < 64, -1.0, 1.0)[:, None].astype(np.float32)
    return np.stack([cos, sin, cos / 16, sin / 16], axis=1).astype(np.float32)


def host_consts():
    c = np.zeros((128, 6, 128), np.float32)
    p = np.arange(128)
    same = (p[:, None] // 64 == p[None, :] // 64)
    c[:, 0, :] = same
    c[:, 1, :] = (p[:, None] <= p[None, :])
    c[:, 2, :] = (p[:, None] >= p[None, :])
    c[:, 3, :] = same & (p[:, None] <= p[None, :])
    c[:, 4, :] = same & (p[:, None] >= p[None, :])
    return c


_NAB_CACHE = {}


def host_nab(rpb):
    NEG = np.float32(-30000.0)
    types = [(10, 10 + dj) for dj in range(-2, 3)]
    types += [(0, j) for j in range(4)] + [(1, j) for j in range(4)]
    types += [(30, 28 + j) for j in range(4)] + [(31, 28 + j) for j in range(4)]
    out = np.empty((16, len(types), 128, 128), np.float32)
    p = np.arange(128)
    for ti, (i, j) in enumerate(types):
        kr = (2 * j + p // 64)[:, None]
        kc = (p % 64)[:, None]
        qr = (2 * i + p // 64)[None, :]
        qc = (p % 64)[None, :]
        r0 = np.clip(qr - 4, 0, 56)
        c0 = np.clip(qc - 8, 0, 48)
        valid = (kr >= r0) & (kr < r0 + 8) & (kc >= c0) & (kc < c0 + 16)
        ri = np.clip(kr - qr + 7, 0, 14)
        ci = np.clip(kc - qc + 15, 0, 30)
        g = rpb[:, ri, ci]
        out[:, ti] = np.where(valid[None], g, NEG)
    return out


def kernel(**inputs):
    nc = build()
    in_maps = [host_inputs(inputs, b) for b in range(8)]
    res = run_bass_kernel_spmd(nc, in_maps, core_ids=list(range(8)))
    return np.stack([np.asarray(r["out"], np.float32) for r in res.results], axis=0)
```
